# Optimizing a Trainium2 kernel written in Bass

```python
import math
import jax, jax.numpy as jnp
from jax import lax
import numpy as np

D_MODEL = 1024
BATCH = 8
SEQ = 2048
DEPTH = 2
DEC_BATCH = 8
DEC_SEQ = 8192
PAST_LEN = 128

N_EVEN = (DEPTH + 1) // 2
N_ODD = DEPTH // 2
D_FF = 4 * D_MODEL
EPS = 1e-6

POOL_WINDOWS = (2, 4, 8, 16)
POOL_GROUP = D_MODEL // 8
D_POOL = POOL_GROUP * len(POOL_WINDOWS)
MLA_HEADS = 8
QK_NOPE = 64
QK_ROPE = 32
V_HEAD = 64
Q_LORA = D_MODEL // 4
KV_LORA = D_MODEL // 8
ROPE_THETA = 10000.0
Q_BLOCK = 128
D_MLA_OUT = MLA_HEADS * V_HEAD
D_IN_EVEN = D_POOL + Q_LORA + KV_LORA + QK_ROPE
D_HYENA = 3 * D_MODEL // 4
HYENA_ORDER = 2
SHORT_WIDTH = 3
POS_BANDS = 16
POS_EMB = 1 + 2 * POS_BANDS
FILTER_HIDDEN = 64
DECAY_TARGET = 1e-2
FAST_DECAY_PCT = 0.3
SLOW_DECAY_PCT = 1.5
FNET_GROUPS = 4
FNET_GROUP = D_MODEL // 16
D_FNET = FNET_GROUPS * FNET_GROUP
D_IN_ODD = (HYENA_ORDER + 1) * D_HYENA + D_FNET
D_MIX = D_MODEL

kernel_name = "hybrid_pool_mla_hyena_fnet_encoder"


def rms_norm(x, g):
    xf = x.astype(jnp.float32)
    y = xf * lax.rsqrt(jnp.mean(xf * xf, axis=-1, keepdims=True) + EPS)
    return (y * g.astype(jnp.float32)).astype(x.dtype)


def centred_pool_residual(u, window):
    L = u.shape[1]
    before = window // 2
    after = window - 1 - before
    uf = u.astype(jnp.float32)
    up = jnp.pad(uf, ((0, 0), (before + 1, after), (0, 0)))
    c = jnp.cumsum(up, axis=1)
    s = c[:, window:window + L] - c[:, :L]
    pos = jnp.arange(L)
    cnt = (jnp.minimum(pos + after, L - 1) - jnp.maximum(pos - before, 0) + 1).astype(jnp.float32)
    return s / cnt[None, :, None] - uf


def pool_mixer(a, pool_w, pool_scale):
    outs = []
    for gi, w in enumerate(POOL_WINDOWS):
        ag = a[..., gi * POOL_GROUP:(gi + 1) * POOL_GROUP]
        r = centred_pool_residual(ag, w).astype(a.dtype)
        outs.append(jnp.einsum('blc,cd->bld', r, pool_w[gi]))
    return jnp.concatenate(outs, axis=-1) * pool_scale


def rope_tables(L):
    inv = ROPE_THETA ** (-jnp.arange(0, QK_ROPE, 2, dtype=jnp.float32) / QK_ROPE)
    ang = jnp.arange(L, dtype=jnp.float32)[:, None] * inv[None, :]
    return jnp.cos(ang), jnp.sin(ang)


def apply_rope(x, cos, sin):
    xf = x.astype(jnp.float32)
    x1, x2 = jnp.split(xf, 2, axis=-1)
    return jnp.concatenate([x1 * cos - x2 * sin, x2 * cos + x1 * sin], axis=-1).astype(x.dtype)


def mla_mixer(cq, ckv, kr, q_norm, w_uq, kv_norm, w_ukv):
    B, L, _ = cq.shape
    q = jnp.einsum('blr,rf->blf', rms_norm(cq, q_norm), w_uq).reshape(B, L, MLA_HEADS, QK_NOPE + QK_ROPE)
    kv = jnp.einsum('blr,rf->blf', rms_norm(ckv, kv_norm), w_ukv).reshape(B, L, MLA_HEADS, QK_NOPE + V_HEAD)
    q_nope, q_rope = q[..., :QK_NOPE], q[..., QK_NOPE:]
    k_nope, v = kv[..., :QK_NOPE], kv[..., QK_NOPE:]
    cos, sin = rope_tables(L)
    q_rope = apply_rope(q_rope, cos[None, :, None], sin[None, :, None])
    k_rope = apply_rope(kr, cos[None], sin[None])
    scale = (QK_NOPE + QK_ROPE) ** -0.5
    nb = L // Q_BLOCK
    qn_b = q_nope.reshape(B, nb, Q_BLOCK, MLA_HEADS, QK_NOPE).transpose(1, 0, 2, 3, 4)
    qr_b = q_rope.reshape(B, nb, Q_BLOCK, MLA_HEADS, QK_ROPE).transpose(1, 0, 2, 3, 4)

    def block(args):
        qn, qr = args
        s = jnp.einsum('bqhd,bkhd->bhqk', qn, k_nope) + jnp.einsum('bqhr,bkr->bhqk', qr, k_rope)
        p = jax.nn.softmax(s.astype(jnp.float32) * scale, axis=-1).astype(v.dtype)
        return jnp.einsum('bhqk,bkhd->bqhd', p, v)

    o = lax.map(block, (qn_b, qr_b))
    return o.transpose(1, 0, 2, 3, 4).reshape(B, L, D_MLA_OUT)


def even_mixer(h, w_in, pool_w, pool_scale, q_norm, w_uq, kv_norm, w_ukv, w_out):
    z = jnp.einsum('bld,df->blf', h, w_in)
    a = z[..., :D_POOL]
    cq = z[..., D_POOL:D_POOL + Q_LORA]
    ckv = z[..., D_POOL + Q_LORA:D_POOL + Q_LORA + KV_LORA]
    kr = z[..., D_POOL + Q_LORA + KV_LORA:]
    mixed = jnp.concatenate([pool_mixer(a, pool_w, pool_scale),
                             mla_mixer(cq, ckv, kr, q_norm, w_uq, kv_norm, w_ukv)], axis=-1)
    return jnp.einsum('blf,fd->bld', mixed, w_out)


def hyena_filter_spectra(L, w1, b1, w2, b2, w3, freq):
    f32 = lambda a: a.astype(jnp.float32)
    t = jnp.linspace(0.0, 1.0, L, dtype=jnp.float32)
    bands = jnp.linspace(1e-4, POS_BANDS - 1, POS_BANDS, dtype=jnp.float32)
    ang = (2.0 * math.pi / L) * jnp.arange(L, dtype=jnp.float32)[:, None] * bands[None, :]
    feat = jnp.concatenate([t[:, None], jnp.cos(ang), -jnp.sin(ang)], axis=-1)
    fr = f32(freq)
    hid = jnp.sin(fr * (feat @ f32(w1) + f32(b1)))
    hid = jnp.sin(fr * (hid @ f32(w2) + f32(b2)))
    hf = (hid @ f32(w3)).reshape(L, HYENA_ORDER, 2, D_HYENA)
    deltas = jnp.abs(jnp.linspace(math.log(DECAY_TARGET) / SLOW_DECAY_PCT,
                                  math.log(DECAY_TARGET) / FAST_DECAY_PCT, D_HYENA, dtype=jnp.float32))
    hf = hf * jnp.exp(-t[:, None] * deltas[None, :])[:, None, None, :]
    hf = hf * lax.rsqrt(jnp.sum(hf * hf, axis=0, keepdims=True) + EPS)
    h_fwd, h_bwd = hf[:, :, 0], hf[:, :, 1]
    k_full = jnp.concatenate([h_fwd, jnp.zeros((1, HYENA_ORDER, D_HYENA), jnp.float32), h_bwd[:0:-1]], axis=0)
    return jnp.fft.rfft(k_full, axis=0)


def long_conv(u, k_f, bias):
    L = u.shape[1]
    y = jnp.fft.irfft(jnp.fft.rfft(u, n=2 * L, axis=1) * k_f[None], n=2 * L, axis=1)[:, :L]
    return y + u * bias.astype(jnp.float32)


def hyena_mixer(z, short_w, short_b, filt_w1, filt_b1, filt_w2, filt_b2, filt_w3, filt_freq, hyena_bias):
    B, L, _ = z.shape
    pad = SHORT_WIDTH // 2
    zp = jnp.pad(z, ((0, 0), (pad, pad), (0, 0)))
    zc = sum(zp[:, j:j + L] * short_w[j] for j in range(SHORT_WIDTH)) + short_b
    zc = zc.astype(jnp.float32)
    v, x1, x2 = jnp.split(zc, HYENA_ORDER + 1, axis=-1)
    k_f = hyena_filter_spectra(L, filt_w1, filt_b1, filt_w2, filt_b2, filt_w3, filt_freq)
    u = v
    for o, gate in enumerate((x1, x2)):
        u = gate * long_conv(u, k_f[:, o], hyena_bias[o])
    return u.astype(z.dtype)


def fnet_mixer(f, fnet_w):
    B, L, _ = f.shape
    fg = f.astype(jnp.float32).reshape(B, L, FNET_GROUPS, FNET_GROUP)
    m = jnp.fft.fftn(fg, axes=(1, 3), norm='ortho').real.astype(f.dtype)
    return jnp.einsum('blgc,gcd->blgd', m, fnet_w).reshape(B, L, D_FNET)


def odd_mixer(h, w_in, short_w, short_b, filt_w1, filt_b1, filt_w2, filt_b2, filt_w3, filt_freq,
              hyena_bias, fnet_w, w_out):
    z = jnp.einsum('bld,df->blf', h, w_in)
    zh = z[..., :(HYENA_ORDER + 1) * D_HYENA]
    f = z[..., (HYENA_ORDER + 1) * D_HYENA:]
    mixed = jnp.concatenate([
        hyena_mixer(zh, short_w, short_b, filt_w1, filt_b1, filt_w2, filt_b2, filt_w3, filt_freq, hyena_bias),
        fnet_mixer(f, fnet_w)], axis=-1)
    return jnp.einsum('blf,fd->bld', mixed, w_out)


def trunk(x, norm_mix, norm_mlp, w_ff_in, w_ff_out,
          even_w_in, pool_w, pool_scale, mla_q_norm, mla_w_uq, mla_kv_norm, mla_w_ukv, even_w_out,
          odd_w_in, short_w, short_b, filt_w1, filt_b1, filt_w2, filt_b2, filt_w3, filt_freq,
          hyena_bias, fnet_w, odd_w_out):
    for layer in range(DEPTH):
        i = layer // 2
        h = rms_norm(x, norm_mix[layer, 0])
        if layer % 2 == 0:
            m = even_mixer(h, even_w_in[i], pool_w[i], pool_scale[i], mla_q_norm[i], mla_w_uq[i],
                           mla_kv_norm[i], mla_w_ukv[i], even_w_out[i])
        else:
            m = odd_mixer(h, odd_w_in[i], short_w[i], short_b[i], filt_w1[i], filt_b1[i], filt_w2[i],
                          filt_b2[i], filt_w3[i], filt_freq[i], hyena_bias[i], fnet_w[i], odd_w_out[i])
        x = x + rms_norm(m, norm_mix[layer, 1])
        h = rms_norm(x, norm_mlp[layer, 0])
        f = jnp.einsum('bld,df->blf', h, w_ff_in[layer])
        f = jnp.einsum('blf,fd->bld', jnp.square(jax.nn.relu(f)), w_ff_out[layer])
        x = x + rms_norm(f, norm_mlp[layer, 1])
    return x


def setup_inputs(seed: int = 0) -> dict:
    key = jax.random.key(seed)
    ks = iter(jax.random.split(key, 40))
    nrm = lambda shape, scale: scale * jax.random.normal(next(ks), shape, jnp.float32)
    gain = lambda shape: 1.0 + 0.1 * jax.random.normal(next(ks), shape, jnp.float32)
    return {
        "x_prompt": nrm((BATCH, SEQ, D_MODEL), 1.0),
        "x_sample": nrm((DEC_BATCH, DEC_SEQ, D_MODEL), 1.0),
        "norm_mix": gain((DEPTH, 2, D_MODEL)),
        "norm_mlp": gain((DEPTH, 2, D_MODEL)),
        "w_ff_in": nrm((DEPTH, D_MODEL, D_FF), D_MODEL ** -0.5),
        "w_ff_out": nrm((DEPTH, D_FF, D_MODEL), D_FF ** -0.5),
        "even_w_in": nrm((N_EVEN, D_MODEL, D_IN_EVEN), D_MODEL ** -0.5),
        "pool_w": nrm((N_EVEN, len(POOL_WINDOWS), POOL_GROUP, POOL_GROUP), POOL_GROUP ** -0.5),
        "pool_scale": gain((N_EVEN, D_POOL)),
        "mla_q_norm": gain((N_EVEN, Q_LORA)),
        "mla_w_uq": nrm((N_EVEN, Q_LORA, MLA_HEADS * (QK_NOPE + QK_ROPE)), Q_LORA ** -0.5),
        "mla_kv_norm": gain((N_EVEN, KV_LORA)),
        "mla_w_ukv": nrm((N_EVEN, KV_LORA, MLA_HEADS * (QK_NOPE + V_HEAD)), KV_LORA ** -0.5),
        "even_w_out": nrm((N_EVEN, D_MIX, D_MODEL), D_MIX ** -0.5),
        "odd_w_in": nrm((N_ODD, D_MODEL, D_IN_ODD), D_MODEL ** -0.5),
        "short_w": nrm((N_ODD, SHORT_WIDTH, (HYENA_ORDER + 1) * D_HYENA), SHORT_WIDTH ** -0.5),
        "short_b": nrm((N_ODD, (HYENA_ORDER + 1) * D_HYENA), 0.01),
        "filt_w1": nrm((N_ODD, POS_EMB, FILTER_HIDDEN), POS_EMB ** -0.5),
        "filt_b1": nrm((N_ODD, FILTER_HIDDEN), 0.1),
        "filt_w2": nrm((N_ODD, FILTER_HIDDEN, FILTER_HIDDEN), FILTER_HIDDEN ** -0.5),
        "filt_b2": nrm((N_ODD, FILTER_HIDDEN), 0.1),
        "filt_w3": nrm((N_ODD, FILTER_HIDDEN, HYENA_ORDER * 2 * D_HYENA), FILTER_HIDDEN ** -0.5),
        "filt_freq": gain((N_ODD, FILTER_HIDDEN)),
        "hyena_bias": nrm((N_ODD, HYENA_ORDER, D_HYENA), 0.5),
        "fnet_w": nrm((N_ODD, FNET_GROUPS, FNET_GROUP, FNET_GROUP), FNET_GROUP ** -0.5),
        "odd_w_out": nrm((N_ODD, D_MIX, D_MODEL), D_MIX ** -0.5),
    }


def reference(x_prompt, x_sample, norm_mix, norm_mlp, w_ff_in, w_ff_out,
              even_w_in, pool_w, pool_scale, mla_q_norm, mla_w_uq, mla_kv_norm, mla_w_ukv, even_w_out,
              odd_w_in, short_w, short_b, filt_w1, filt_b1, filt_w2, filt_b2, filt_w3, filt_freq,
              hyena_bias, fnet_w, odd_w_out):
    y_prompt = trunk(x_prompt, norm_mix, norm_mlp, w_ff_in, w_ff_out,
                     even_w_in, pool_w, pool_scale, mla_q_norm, mla_w_uq, mla_kv_norm, mla_w_ukv, even_w_out,
                     odd_w_in, short_w, short_b, filt_w1, filt_b1, filt_w2, filt_b2, filt_w3, filt_freq,
                     hyena_bias, fnet_w, odd_w_out)
    y_sample = trunk(x_sample, norm_mix, norm_mlp, w_ff_in, w_ff_out,
                     even_w_in, pool_w, pool_scale, mla_q_norm, mla_w_uq, mla_kv_norm, mla_w_ukv, even_w_out,
                     odd_w_in, short_w, short_b, filt_w1, filt_b1, filt_w2, filt_b2, filt_w3, filt_freq,
                     hyena_bias, fnet_w, odd_w_out)
    return (y_prompt, y_sample)
```

```python
import contextlib
import math
import os
import numpy as np
import concourse.bass as bass
import concourse.mybir as mybir
from concourse.bass_utils import run_bass_kernel_spmd

F32 = mybir.dt.float32
BF16 = mybir.dt.bfloat16
AF = mybir.ActivationFunctionType
ALU = mybir.AluOpType
AX = mybir.AxisListType

D = 1024
EPS = 1e-6
LP, LS = 2048, 8192
ENGS = ('pe', 'act', 'dve', 'pool', 'sp')
NRING = 8


class Buf:
    __slots__ = ('w', 'r')

    def __init__(self):
        self.w = None
        self.r = []


class Op:
    __slots__ = ('eng', 'fn', 'deps', 'signal', 'sigval', 'is_dma', 'dsem', 'dval', 'ringdep')

    def __init__(self, eng, fn, is_dma):
        self.eng = eng
        self.fn = fn
        self.deps = []
        self.signal = False
        self.sigval = 0
        self.is_dma = is_dma
        self.dsem = None
        self.dval = 0
        self.ringdep = None


class _Rec:
    def __getattr__(self, name):
        def f(*a, **k):
            self.__dict__['call'] = (name, a, k)
            return None
        return f


class Prog:
    def __init__(self, nc):
        self.nc = nc
        self.ops = {e: [] for e in ENGS}
        self.ndma = {e: 0 for e in ENGS}
        self.ringops = {e: [None] * NRING for e in ENGS}

    def _rec(self, eng, fn, reads, writes, is_dma):
        r = _Rec()
        fn(r)
        op = Op(eng, r.call, is_dma)
        deps = []
        for t in reads:
            if t.w is not None:
                deps.append(t.w)
        for t in writes:
            if t.w is not None:
                deps.append(t.w)
            deps.extend(t.r)
        for t in reads:
            t.r.append(op)
        for t in writes:
            t.w = op
            t.r = []
        seen = set()
        for d in deps:
            if id(d) in seen or d is op:
                continue
            seen.add(id(d))
            if (not d.is_dma) and d.eng == eng and eng == 'pe':
                continue
            op.deps.append(d)
            if not d.is_dma:
                d.signal = True
        if is_dma:
            j = self.ndma[eng]
            self.ndma[eng] = j + 1
            op.dsem = (eng, j % NRING)
            op.dval = 16 * (j // NRING + 1)
            op.ringdep = self.ringops[eng][j % NRING]
            self.ringops[eng][j % NRING] = op
        self.ops[eng].append(op)
        return op

    def op(self, eng, fn, reads=(), writes=()):
        return self._rec(eng, fn, reads, writes, False)

    def dma(self, eng, out, in_, reads=(), writes=()):
        return self._rec(eng, lambda e: e.dma_start(out=out, in_=in_), reads, writes, True)

    def barrier(self):
        lasts = []
        for e in ENGS:
            for o in reversed(self.ops[e]):
                if o.fn is not None and not o.is_dma:
                    lasts.append(o)
                    break
            for d in self.ringops[e]:
                if d is not None:
                    lasts.append(d)
        for e in ENGS:
            op = Op(e, None, False)
            for d in lasts:
                if d.is_dma or d.eng != e:
                    op.deps.append(d)
                    if not d.is_dma:
                        d.signal = True
            self.ops[e].append(op)

    def emit(self):
        nc = self.nc
        with contextlib.ExitStack() as st:
            csem = {e: st.enter_context(nc.semaphore('c_' + e)) for e in ENGS}
            dsem = {}
            for e in ENGS:
                for i in range(min(NRING, self.ndma[e])):
                    dsem[(e, i)] = st.enter_context(nc.semaphore('d_%s%d' % (e, i)))
            for e in ENGS:
                c = 0
                for o in self.ops[e]:
                    if o.signal:
                        c += 1
                        o.sigval = c
            st.enter_context(nc.allow_non_contiguous_dma(reason='small strided tables / layout loads'))
            block = st.enter_context(nc.Block())
            handles = {'pe': 'tensor', 'act': 'scalar', 'dve': 'vector', 'pool': 'gpsimd', 'sp': 'sync'}

            def make(e):
                def body(eng):
                    waited = {}
                    for o in self.ops[e]:
                        ws = []
                        for d in o.deps:
                            if d.is_dma:
                                ws.append((('d',) + d.dsem, dsem[d.dsem], d.dval))
                            else:
                                ws.append((('c', d.eng), csem[d.eng], d.sigval))
                        if o.ringdep is not None:
                            d = o.ringdep
                            ws.append((('d',) + d.dsem, dsem[d.dsem], d.dval))
                        for key, sem, val in ws:
                            if waited.get(key, 0) >= val:
                                continue
                            waited[key] = val
                            eng.wait_ge(sem, val)
                        if o.fn is None:
                            continue
                        name_, a_, k_ = o.fn
                        ins = getattr(eng, name_)(*a_, **k_)
                        if o.is_dma:
                            ins.then_inc(dsem[o.dsem], 16)
                        elif o.signal:
                            ins.then_inc(csem[e], 1)
                    for i in range(NRING):
                        d = self.ringops[e][i]
                        if d is not None and waited.get(('d',) + d.dsem, 0) < d.dval:
                            eng.wait_ge(dsem[d.dsem], d.dval)
                return body

            for e in ENGS:
                if self.ops[e]:
                    getattr(block, handles[e])(make(e))


_UID = [0]


class _Stop(Exception):
    pass


def _chk(tag):
    if os.environ.get('KSTOP') == tag:
        raise _Stop()


class RB:
    def __init__(self, st, nc, name, shape, dt, n, psum=False):
        alloc = nc.psum_tensor if psum else nc.sbuf_tensor
        _UID[0] += 1
        self.t = [st.enter_context(alloc('%s_%d_%d' % (name, _UID[0], i), shape, dt)) for i in range(n)]
        self.b = [Buf() for _ in range(n)]
        self.i = 0

    def next(self):
        k = self.i % len(self.t)
        self.i += 1
        return self.t[k], self.b[k]


POOL_WINDOWS = (2, 4, 8, 16)


def _band_tables(L):
    out = np.zeros((128, 4, 5, 128), np.float32)
    for g, w in enumerate(POOL_WINDOWS):
        before = w // 2
        after = w - 1 - before

        def fill(kind, t_tile, s_tile):
            for tl in range(128):
                t = t_tile * 128 + tl
                lo = max(t - before, 0)
                hi = min(t + after, L - 1)
                cnt = hi - lo + 1
                for s in range(lo, hi + 1):
                    sl = s - s_tile * 128
                    if 0 <= sl < 128:
                        out[sl, g, kind, tl] += 1.0 / cnt
                if s_tile == t_tile:
                    out[tl, g, kind, tl] -= 1.0
        nt = L // 128
        fill(0, 2, 2)
        fill(1, 2, 1)
        fill(2, 2, 3)
        fill(3, 0, 0)
        fill(4, nt - 1, nt - 1)
    return out


def _consts():
    c = {}
    c['ident'] = np.eye(128, dtype=np.float32)
    inv = 10000.0 ** (-np.arange(0, 32, 2, dtype=np.float32) / 32)
    ang = np.arange(LS, dtype=np.float32)[None, :] * inv[:, None]
    cs, sn = np.cos(ang).astype(np.float32), np.sin(ang).astype(np.float32)
    c['ropeC'] = np.concatenate([cs, cs], 0)
    c['ropeS'] = np.concatenate([-sn, sn], 0)
    for nm, L in (('p', LP), ('s', LS)):
        c['band_' + nm] = _band_tables(L)
        N = 2 * L
        N2 = L // 64
        a = np.arange(64)[:, None].astype(np.float64)
        ap = np.arange(64)[None, :].astype(np.float64)
        th = np.pi * (2 * ap + 1) * a / 128.0
        c['F1'] = np.concatenate([np.cos(th), -np.sin(th)], 1).astype(np.float32)
        c['G1re_' + nm] = ((2.0 / N) * np.cos(th).T).astype(np.float32)
        c['G1im_' + nm] = ((2.0 / N) * (-np.sin(th)).T).astype(np.float32)
        b = np.arange(N2)[:, None].astype(np.float64)
        tt = np.pi * (2 * ap + 1) * b / N
        tre, tim = np.cos(tt), -np.sin(tt)
        TA = np.concatenate([tre, tre], 1)
        TB = np.concatenate([tim, tim], 1)
        c['TA_' + nm] = np.repeat(TA[:, None, :], 4, 1).astype(np.float32)
        c['TB_' + nm] = np.repeat(TB[:, None, :], 4, 1).astype(np.float32)
        tcr, tci = np.cos(tt).T, np.sin(tt).T
        c['TcA_' + nm] = np.repeat(np.concatenate([tcr, tcr], 1)[:, None, :], 4, 1).astype(np.float32)
        c['TcB_' + nm] = np.repeat(np.concatenate([tci, tci], 1)[:, None, :], 4, 1).astype(np.float32)
        bb = np.arange(N2)[None, :].astype(np.float64)
        f2 = 2 * np.pi * b * bb / N2
        f2re, f2im = np.cos(f2), -np.sin(f2)
        c['F2_' + nm] = np.stack([f2re, f2im, -f2im], 1).astype(np.float32)
        c['FA_' + nm] = np.concatenate([f2re, -f2im], 1).astype(np.float32)
        c['FB_' + nm] = np.concatenate([f2im, f2re], 1).astype(np.float32)
        tf = 2 * np.pi * b * ap / L
        fre, fim = np.cos(tf), -np.sin(tf)
        c['TfA_' + nm] = np.repeat(np.concatenate([fre, fre], 1)[:, None, :], 4, 1).astype(np.float32)
        c['TfB_' + nm] = np.repeat(np.concatenate([fim, fim], 1)[:, None, :], 4, 1).astype(np.float32)
        t = np.linspace(0.0, 1.0, L, dtype=np.float32)
        bands = np.linspace(1e-4, 15, 16, dtype=np.float32)
        angf = (np.float32(2.0 * math.pi / L) * np.arange(L, dtype=np.float32)[:, None] * bands[None, :]).astype(np.float32)
        feat = np.concatenate([t[:, None], np.cos(angf), -np.sin(angf)], -1).astype(np.float32)
        c['featT_' + nm] = np.ascontiguousarray(feat.T)
        c['trow_' + nm] = t[None, :].copy()
    ff = 2 * np.pi * a * ap / 64.0
    c['F1fA'] = np.concatenate([np.cos(ff), -np.sin(ff)], 1).astype(np.float32)
    c['F1fB'] = np.concatenate([-np.sin(ff), -np.cos(ff)], 1).astype(np.float32)
    dd = np.arange(64)[:, None] * np.arange(64)[None, :]
    c64, s64 = np.cos(2 * np.pi * dd / 64.0), np.sin(2 * np.pi * dd / 64.0)
    z = np.zeros((64, 64))
    c['DC'] = np.block([[c64, z], [z, c64]]).astype(np.float32)
    c['DS'] = np.block([[s64, z], [z, s64]]).astype(np.float32)
    deltas = np.abs(np.linspace(math.log(1e-2) / 1.5, math.log(1e-2) / 0.3, 768, dtype=np.float32))
    c['negd'] = np.ascontiguousarray((-deltas).reshape(6, 128).T).astype(np.float32)
    return c


_CONST_CACHE = {}


def _get_consts():
    if not _CONST_CACHE:
        _CONST_CACHE.update(_consts())
    return _CONST_CACHE


def build(consts, stop_after=99, dbg=None):
    nc = bass.Bass("TRN2", target_bir_lowering=False)

    def sb(name, shape, dt):
        _UID[0] += 1
        return nc.sbuf_tensor('%s_%d' % (name, _UID[0]), shape, dt)
    I = {}

    def inp(name, shape):
        I[name] = nc.dram_tensor(name, list(shape), F32, kind="ExternalInput").ap()
        return I[name]

    specs = {
        "x_p": (LP, D), "x_s": (LS, D), "norm_mix": (2, 2, D), "norm_mlp": (2, 2, D),
        "w_ff_in": (2, D, 4096), "w_ff_out": (2, 4096, D), "even_w_in": (1, D, 928),
        "pool_w": (1, 4, 128, 128), "pool_scale": (1, 512), "mla_q_norm": (1, 256),
        "mla_w_uq": (1, 256, 768), "mla_kv_norm": (1, 128), "mla_w_ukv": (1, 128, 1024),
        "even_w_out": (1, D, D), "odd_w_in": (1, D, 2560), "short_w": (1, 3, 2304),
        "short_b": (1, 2304), "filt_w1": (1, 33, 64), "filt_b1": (1, 64), "filt_w2": (1, 64, 64),
        "filt_b2": (1, 64), "filt_w3": (1, 64, 3072), "filt_freq": (1, 64), "hyena_bias": (1, 2, 768),
        "fnet_w": (1, 4, 64, 64), "odd_w_out": (1, D, D),
    }
    for k, v in specs.items():
        inp(k, v)
    for k, v in consts.items():
        inp('c_' + k, v.shape)
    y_p = nc.dram_tensor("y_p", [LP, D], F32, kind="ExternalOutput").ap()
    y_s = nc.dram_tensor("y_s", [LS, D], F32, kind="ExternalOutput").ap()

    def scratch(name, shape, dt):
        return nc.dram_tensor(name, list(shape), dt, kind="Internal").ap()

    SEQ = []
    for nm, L, xin, yout in (('p', LP, I['x_p'], y_p), ('s', LS, I['x_s'], y_s)):
        s = dict(nm=nm, L=L, x=xin, y=yout, N2=L // 64)
        s['QT'] = scratch('QT' + nm, [8, 96, L], BF16)
        s['KN'] = scratch('KN' + nm, [512, L], BF16)
        s['KR'] = scratch('KR' + nm, [32, L], BF16)
        s['V'] = scratch('V' + nm, [8, 128, L // 128, 64], BF16)
        s['mixT'] = scratch('mixT' + nm, [D, L], BF16)
        s['X1'] = scratch('X1' + nm, [L, D], F32)
        s['X2'] = scratch('X2' + nm, [L, D], F32)
        s['zcT'] = scratch('zcT' + nm, [2304, L], F32)
        s['fcT'] = scratch('fcT' + nm, [512, L], F32)
        s['hfT'] = scratch('hfT' + nm, [3072, L], F32)
        s['invd'] = scratch('invd' + nm, [3072], F32)
        s['mix2'] = scratch('mix2' + nm, [768, L], BF16)
        s['mfT'] = scratch('mfT' + nm, [256, L], BF16)
        s['X3'] = scratch('X3' + nm, [L, D], F32)
        for k in ('QT', 'KN', 'KR', 'V', 'mixT', 'X1', 'X2', 'zcT', 'fcT', 'hfT', 'invd', 'mix2', 'mfT', 'X3'):
            s['b_' + k] = Buf()
        SEQ.append(s)

    P = Prog(nc)
    NOB = Buf

    with contextlib.ExitStack() as top:
        def gt(name, shape, dt):
            return top.enter_context(sb(name, shape, dt))

        PS = RB(top, nc, 'ps', [128, 512], F32, 6, psum=True)
        PST = RB(top, nc, 'pst', [128, 1024], BF16, 2, psum=True)
        ident = gt('ident', [128, 128], BF16)
        onesb = gt('onesb', [128, 128], BF16)
        onesf = gt('onesf', [128, 128], F32)
        epsT = gt('epsT', [128, 2], F32)
        stg = RB(top, nc, 'stg', [128, 1024], F32, 2)
        bconst = Buf()
        P.op('pool', lambda e: e.memset(onesb[:], 1.0), writes=[bconst])
        P.op('pool', lambda e: e.memset(onesf[:], 1.0), writes=[bconst])
        P.op('pool', lambda e: e.memset(epsT[:, 0:1], EPS), writes=[bconst])
        P.op('pool', lambda e: e.memset(epsT[:, 1:2], 96.0 * EPS), writes=[bconst])
        _castc = [0]

        def cast_engine():
            _castc[0] += 1
            return ('act', 'pool')[_castc[0] % 2]

        def copy_op(eng, out, in_, reads, writes):
            if eng == 'act':
                P.op('act', lambda e: e.copy(out=out, in_=in_), reads, writes)
            else:
                P.op(eng, lambda e: e.tensor_copy(out=out, in_=in_), reads, writes)

        def loadw(dst, src, dbuf, np_=128):
            cols = dst.shape[-1]
            assert len(dst.shape) == 2 and len(src.shape) == 2, (dst.shape, src.shape)
            for c0 in range(0, cols, 1024):
                cw = min(1024, cols - c0)
                t, b = stg.next()
                P.dma('sp', t[0:np_, 0:cw], src[:, c0:c0 + cw], writes=[b])
                copy_op(cast_engine(), dst[:, c0:c0 + cw], t[0:np_, 0:cw], [b], [dbuf])

        loadw(ident[:], I['c_ident'], bconst)

        def rms_transpose(ph, xt, bx, nj, gcol, hT, bh, tmp):
            ssq, bs = tmp['ssq'].next()
            junk, bj = tmp['junk'].next()
            for j in range(nj):
                P.op('act', (lambda j: lambda e: e.activation(out=junk[:], in_=xt[:, j, :], func=AF.Square,
                                                               accum_out=ssq[:, j:j + 1]))(j), [bx], [bj, bs])
            P.op('act', lambda e: e.activation(out=ssq[:, 4:4 + nj], in_=ssq[:, 0:nj], func=AF.Sqrt,
                                               bias=epsT[:, 0:1], scale=1.0 / D), [bs, bconst], [bs])
            P.op('dve', lambda e: e.reciprocal(out=ssq[:, 8:8 + nj], in_=ssq[:, 4:4 + nj]), [bs], [bs])
            xn, bn = tmp['xn'].next()
            for j in range(nj):
                P.op('dve', (lambda j: lambda e: e.tensor_scalar(out=xn[:, j, :], in0=xt[:, j, :],
                                                                 scalar1=ssq[:, 8 + j:9 + j], scalar2=None,
                                                                 op0=ALU.mult))(j), [bx, bs], [bn])
            for dc in range(8):
                pt, bp = PST.next()
                for j in range(nj):
                    P.op('pe', (lambda j, dc, pt: lambda e: e.transpose(pt[:, j * 128:(j + 1) * 128],
                                                                        xn[:, j, dc * 128:(dc + 1) * 128], ident[:]))(j, dc, pt),
                         [bn, bconst], [bp])
                P.op('act', (lambda dc, pt: lambda e: e.activation(out=hT[:, dc, :], in_=pt[:, 0:nj * 128], func=AF.Copy,
                                                                   scale=gcol[:, dc:dc + 1]))(dc, pt), [bp, bconst], [bh])

        def epilogue(pss, xt, bx, j, grow, tmp):
            ssq, bs = tmp['ssq2'].next()
            junk, bj = tmp['junk'].next()
            for h in range(2):
                P.op('act', (lambda h: lambda e: e.activation(out=junk[:, 0:512], in_=pss[h][0][:], func=AF.Square,
                                                               accum_out=ssq[:, h:h + 1]))(h), [pss[h][1]], [bj, bs])
            P.op('dve', lambda e: e.tensor_tensor(out=ssq[:, 2:3], in0=ssq[:, 0:1], in1=ssq[:, 1:2], op=ALU.add), [bs], [bs])
            P.op('act', lambda e: e.activation(out=ssq[:, 3:4], in_=ssq[:, 2:3], func=AF.Sqrt, bias=epsT[:, 0:1],
                                               scale=1.0 / D), [bs, bconst], [bs])
            P.op('dve', lambda e: e.reciprocal(out=ssq[:, 4:5], in_=ssq[:, 3:4]), [bs], [bs])
            for h in range(2):
                t, bt = tmp['ep'].next()
                P.op('dve', (lambda h, t: lambda e: e.scalar_tensor_tensor(out=t[:], in0=pss[h][0][:], scalar=ssq[:, 4:5],
                                                                           in1=grow[:, h * 512:(h + 1) * 512], op0=ALU.mult,
                                                                           op1=ALU.mult))(h, t), [pss[h][1], bs, bconst], [bt])
                P.op('pool', (lambda h, t: lambda e: e.tensor_tensor(out=xt[:, j, h * 512:(h + 1) * 512], in0=t[:],
                                                                     in1=xt[:, j, h * 512:(h + 1) * 512], op=ALU.add))(h, t),
                     [bt, bx], [bx])

        def colvec(ph, name, src1d, ncol, bufc):
            t = ph.enter_context(sb(name, [128, ncol], F32))
            P.dma('pool', t[:], src1d.rearrange("(c p) -> p c", p=128), writes=[bufc])
            return t

        def rowbc(ph, name, src1d, n, bufc, npart=128):
            t = ph.enter_context(sb(name, [npart, n], F32))
            P.dma('pool', t[:], src1d.rearrange("(o n) -> o n", o=1).partition_broadcast(npart), writes=[bufc])
            return t

        def ph1():
            try:
                ph1_()
            except _Stop:
                pass
            P.barrier()

        def ph1_():
            with contextlib.ExitStack() as ph:
                try:
                    ph1_body(ph)
                except _Stop:
                    pass

        def ph1_body(ph):
            if True:
                T = lambda name, shape, dt: ph.enter_context(sb(name, shape, dt))
                bw = Buf()
                win = T('win', [128, 8, 928], BF16)
                winsw = T('winsw', [128, 8, 32], BF16)
                wuq = T('wuq', [128, 2, 768], BF16)
                wuqsw = T('wuqsw', [128, 2, 768], BF16)
                wuk = T('wuk', [128, 512], BF16)
                wuv = T('wuv', [128, 512], BF16)
                poolw = T('poolw', [128, 4, 128], BF16)
                ewi = I['even_w_in'][0].rearrange("(c p) f -> p c f", p=128)
                for dc in range(8):
                    loadw(win[:, dc, :], ewi[:, dc, :], bw)
                    loadw(winsw[:, dc, 0:16], ewi[:, dc, 912:928], bw)
                    loadw(winsw[:, dc, 16:32], ewi[:, dc, 896:912], bw)
                uq = I['mla_w_uq'][0].rearrange("(c p) f -> p c f", p=128)
                for rc in range(2):
                    loadw(wuq[:, rc, :], uq[:, rc, :], bw)
                    loadw(wuqsw[:, rc, :], uq[:, rc, :], bw)
                    for h in range(8):
                        loadw(wuqsw[:, rc, h * 96 + 64:h * 96 + 80], uq[:, rc, h * 96 + 80:h * 96 + 96], bw)
                        loadw(wuqsw[:, rc, h * 96 + 80:h * 96 + 96], uq[:, rc, h * 96 + 64:h * 96 + 80], bw)
                ukv = I['mla_w_ukv'][0]
                for h in range(8):
                    loadw(wuk[:, h * 64:(h + 1) * 64], ukv[:, h * 128:h * 128 + 64], bw)
                    loadw(wuv[:, h * 64:(h + 1) * 64], ukv[:, h * 128 + 64:h * 128 + 128], bw)
                for g in range(4):
                    loadw(poolw[:, g, :], I['pool_w'][0, g], bw)
                g0col = colvec(ph, 'g0col', I['norm_mix'][0, 0], 8, bw)
                pscol = colvec(ph, 'pscol', I['pool_scale'][0], 4, bw)
                qncol = colvec(ph, 'qncol', I['mla_q_norm'][0], 2, bw)
                kvcol = colvec(ph, 'kvcol', I['mla_kv_norm'][0], 1, bw)
                band = T('band', [128, 4 * 5 * 128], BF16)
                _chk('w')
                tmp = dict(ssq=RB(ph, nc, 'ssq', [128, 12], F32, 2), junk=RB(ph, nc, 'junk', [128, 1024], F32, 1),
                           xn=RB(ph, nc, 'xn', [128, 4, 1024], BF16, 1))
                XT = RB(ph, nc, 'xt', [128, 4, 1024], F32, 1)
                HT = RB(ph, nc, 'hT', [128, 8, 512], BF16, 1)
                atok = T('atok', [128, 64, 512], BF16)
                SQ = RB(ph, nc, 'sq', [128, 3, 512], BF16, 2)
                CG = RB(ph, nc, 'cg', [128, 3, 512], BF16, 2)
                RQ = RB(ph, nc, 'rq', [128, 2, 512], F32, 1)
                RT = RB(ph, nc, 'rt', [96, 4, 512], F32, 1)
                RK = RB(ph, nc, 'rk', [32, 2, 512], F32, 1)
                KT = RB(ph, nc, 'kt', [32, 2, 512], F32, 1)
                KRO = RB(ph, nc, 'kro', [32, 512], BF16, 2)
                QTs = RB(ph, nc, 'qTs', [96, 512], BF16, 3)
                QTm = RB(ph, nc, 'qTm', [96, 2, 512], F32, 1)
                KS = RB(ph, nc, 'ks', [128, 512], BF16, 2)
                VS = RB(ph, nc, 'vs', [128, 512], BF16, 2)
                RC = RB(ph, nc, 'rc', [128, 12], F32, 2)
                RTs = RB(ph, nc, 'rTs', [128, 512], BF16, 2)
                MS = RB(ph, nc, 'ms', [128, 512], BF16, 2)
                for s in SEQ:
                    L = s['L']
                    nt = L // 128
                    batok = [Buf() for _ in range(nt)]
                    bband = Buf()
                    loadw(band[:], I['c_band_' + s['nm']].rearrange("p g k t -> p (g k t)"), bband)
                    for ci in range(L // 512):
                        c0 = ci * 512
                        xt, bx = XT.next()
                        P.dma('sp', xt[:], s['x'][c0:c0 + 512, :].rearrange("(j p) d -> p j d", p=128), writes=[bx])
                        hT, bh = HT.next()
                        rms_transpose(ph, xt, bx, 4, g0col, hT, bh, tmp)
                        _chk('rms')
                        for j in range(4):
                            ps, bp = PS.next()
                            for dc in range(8):
                                P.op('pe', (lambda j, dc, ps: lambda e: e.matmul(ps[:], lhsT=hT[:, dc, j * 128:(j + 1) * 128],
                                                                                 rhs=win[:, dc, 0:512], start=(dc == 0), stop=(dc == 7)))(j, dc, ps),
                                     [bh, bw], [bp])
                            copy_op('act', atok[:, ci * 4 + j, :], ps[:], [bp], [batok[ci * 4 + j]])
                        _chk('atok')
                        sq, bsq = SQ.next()
                        cg, bcg = CG.next()
                        for k3, (lo, ncol) in enumerate(((512, qncol[:, 0:1]), (640, qncol[:, 1:2]), (768, kvcol[:, 0:1]))):
                            ps, bp = PS.next()
                            for dc in range(8):
                                P.op('pe', (lambda dc, ps, lo: lambda e: e.matmul(ps[:], lhsT=win[:, dc, lo:lo + 128], rhs=hT[:, dc, :],
                                                                                  start=(dc == 0), stop=(dc == 7)))(dc, ps, lo), [bh, bw], [bp])
                            P.op('act', (lambda k3, ps: lambda e: e.activation(out=sq[:, k3, :], in_=ps[:], func=AF.Square))(k3, ps), [bp], [bsq])
                            P.op('dve', (lambda k3, ps, ncol: lambda e: e.tensor_scalar(out=cg[:, k3, :], in0=ps[:], scalar1=ncol, scalar2=None,
                                                                                        op0=ALU.mult))(k3, ps, ncol), [bp, bw, bsq], [bcg])
                        _chk('lat')
                        rk, brk = RK.next()
                        P.dma('pool', rk[:, 0, :], I['c_ropeC'][:, c0:c0 + 512], writes=[brk])
                        P.dma('pool', rk[:, 1, :], I['c_ropeS'][:, c0:c0 + 512], writes=[brk])
                        kt, bkt = KT.next()
                        for k2, (wt, lo) in enumerate(((win, 896), (winsw, 0))):
                            ps, bp = PS.next()
                            for dc in range(8):
                                P.op('pe', (lambda dc, ps, wt, lo: lambda e: e.matmul(ps[0:32, :], lhsT=wt[:, dc, lo:lo + 32], rhs=hT[:, dc, :],
                                                                                      start=(dc == 0), stop=(dc == 7)))(dc, ps, wt, lo), [bh, bw], [bp])
                            P.op('dve', (lambda k2, ps: lambda e: e.tensor_tensor(out=kt[:, k2, :], in0=ps[0:32, :], in1=rk[:, k2, :],
                                                                                  op=ALU.mult))(k2, ps), [bp, brk], [bkt])
                        kro, bkro = KRO.next()
                        P.op('pool', lambda e, kt=kt, kro=kro: e.tensor_tensor(out=kro[:], in0=kt[:, 0, :], in1=kt[:, 1, :], op=ALU.add), [bkt], [bkro])
                        P.dma('pool', s['KR'][:, c0:c0 + 512], kro[:], reads=[bkro], writes=[s['b_KR']])
                        _chk('krope')
                        rq, brq = RQ.next()
                        ps, bp = PS.next()
                        for rc in range(2):
                            P.op('pe', (lambda rc, ps: lambda e: e.matmul(ps[:], lhsT=onesb[:], rhs=sq[:, rc, :], start=(rc == 0), stop=(rc == 1)))(rc, ps),
                                 [bsq, bconst], [bp])
                        P.op('act', lambda e, ps=ps, rq=rq: e.activation(out=rq[:, 0, :], in_=ps[:], func=AF.Sqrt, bias=epsT[:, 1:2], scale=96.0 / 256.0),
                             [bp, bconst], [brq])
                        P.op('dve', lambda e, rq=rq: e.reciprocal(out=rq[:, 0, :], in_=rq[:, 0, :]), [brq], [brq])
                        ps, bp = PS.next()
                        P.op('pe', lambda e, ps=ps, sq=sq: e.matmul(ps[:], lhsT=onesb[:], rhs=sq[:, 2, :], start=True, stop=True), [bsq, bconst], [bp])
                        P.op('act', lambda e, ps=ps, rq=rq: e.activation(out=rq[:, 1, :], in_=ps[:], func=AF.Sqrt, bias=epsT[:, 0:1], scale=1.0 / 128.0),
                             [bp, bconst], [brq])
                        P.op('dve', lambda e, rq=rq: e.reciprocal(out=rq[:, 1, :], in_=rq[:, 1, :]), [brq], [brq])
                        rc_, brc = RC.next()
                        ps, bp = PS.next()
                        for j in range(4):
                            P.op('pe', (lambda j, ps: lambda e: e.matmul(ps[:, j:j + 1], lhsT=sq[:, 2, j * 128:(j + 1) * 128], rhs=onesb[:, 0:1],
                                                                         start=True, stop=True))(j, ps), [bsq, bconst], [bp])
                        P.op('act', lambda e, ps=ps, rc_=rc_: e.activation(out=rc_[:, 0:4], in_=ps[:, 0:4], func=AF.Sqrt, bias=epsT[:, 0:1], scale=1.0 / 128.0),
                             [bp, bconst], [brc])
                        P.op('dve', lambda e, rc_=rc_: e.reciprocal(out=rc_[:, 4:8], in_=rc_[:, 0:4]), [brc], [brc])
                        rt, brt = RT.next()
                        P.dma('pool', rt[64:96, 0, :], I['c_ropeC'][:, c0:c0 + 512], writes=[brt])
                        P.dma('pool', rt[64:96, 1, :], I['c_ropeS'][:, c0:c0 + 512], writes=[brt])
                        for k2 in range(2):
                            P.op('pool', (lambda k2: lambda e, rt=rt, rq=rq: e.tensor_tensor(out=rt[64:96, 2 + k2, :], in0=rt[64:96, k2, :],
                                                                                             in1=rq[64:96, 0, :], op=ALU.mult))(k2), [brt, brq], [brt])
                        _chk('rstd')
                        for h in range(8):
                            pq, bpq = PS.next()
                            pw, bpw = PS.next()
                            for rc in range(2):
                                P.op('pe', (lambda rc, pq, h: lambda e: e.matmul(pq[0:96, :], lhsT=wuq[:, rc, h * 96:(h + 1) * 96], rhs=cg[:, rc, :],
                                                                                 start=(rc == 0), stop=(rc == 1)))(rc, pq, h), [bcg, bw], [bpq])
                            for rc in range(2):
                                P.op('pe', (lambda rc, pw, h: lambda e: e.matmul(pw[0:96, :], lhsT=wuqsw[:, rc, h * 96:(h + 1) * 96], rhs=cg[:, rc, :],
                                                                                 start=(rc == 0), stop=(rc == 1)))(rc, pw, h), [bcg, bw], [bpw])
                            qs, bqs = QTs.next()
                            qm, bqm = QTm.next()
                            P.op('dve', lambda e, qs=qs, pq=pq, rq=rq: e.tensor_tensor(out=qs[0:64, :], in0=pq[0:64, :], in1=rq[0:64, 0, :], op=ALU.mult),
                                 [bpq, brq], [bqs])
                            P.op('dve', lambda e, qm=qm, pq=pq, rt=rt: e.tensor_tensor(out=qm[64:96, 0, :], in0=pq[64:96, :], in1=rt[64:96, 2, :], op=ALU.mult),
                                 [bpq, brt], [bqm])
                            P.op('dve', lambda e, qm=qm, pw=pw, rt=rt: e.tensor_tensor(out=qm[64:96, 1, :], in0=pw[64:96, :], in1=rt[64:96, 3, :], op=ALU.mult),
                                 [bpw, brt], [bqm])
                            P.op('pool', lambda e, qs=qs, qm=qm: e.tensor_tensor(out=qs[64:96, :], in0=qm[64:96, 0, :], in1=qm[64:96, 1, :], op=ALU.add),
                                 [bqm], [bqs])
                            P.dma('sp', s['QT'][h, :, c0:c0 + 512], qs[:], reads=[bqs], writes=[s['b_QT']])
                        _chk('q')
                        for hp in range(4):
                            ps, bp = PS.next()
                            P.op('pe', lambda e, ps=ps, hp=hp, cg=cg: e.matmul(ps[:], lhsT=wuk[:, hp * 128:(hp + 1) * 128], rhs=cg[:, 2, :], start=True, stop=True),
                                 [bcg, bw], [bp])
                            ks, bks = KS.next()
                            P.op('dve', lambda e, ks=ks, ps=ps, rq=rq: e.tensor_tensor(out=ks[:], in0=ps[:], in1=rq[:, 1, :], op=ALU.mult), [bp, brq], [bks])
                            P.dma('sp', s['KN'][hp * 128:(hp + 1) * 128, c0:c0 + 512], ks[:], reads=[bks], writes=[s['b_KN']])
                        _chk('k')
                        for j in range(4):
                            ps, bp = PS.next()
                            P.op('pe', lambda e, ps=ps, j=j, cg=cg: e.matmul(ps[:], lhsT=cg[:, 2, j * 128:(j + 1) * 128], rhs=wuv[:], start=True, stop=True),
                                 [bcg, bw], [bp])
                            vs, bvs = VS.next()
                            P.op('act', lambda e, vs=vs, ps=ps, rc_=rc_, j=j: e.activation(out=vs[:], in_=ps[:], func=AF.Copy, scale=rc_[:, 4 + j:5 + j]),
                                 [bp, brc], [bvs])
                            P.dma('sp', s['V'][:, :, ci * 4 + j, :].rearrange("h p d -> p h d"), vs[:].rearrange("p (h d) -> p h d", h=8), reads=[bvs], writes=[s['b_V']])
                    _chk('v')
                    for sp in range(L // 512):
                        for g in range(4):
                            ps, bp = PS.next()
                            for jt in range(4):
                                t = sp * 4 + jt
                                terms = []
                                if t > 0:
                                    terms.append((t - 1, 1))
                                terms.append((t, 3 if t == 0 else (4 if t == nt - 1 else 0)))
                                if t < nt - 1:
                                    terms.append((t + 1, 2))
                                for k, (st_, kind) in enumerate(terms):
                                    off = (g * 5 + kind) * 128
                                    P.op('pe', lambda e, ps=ps, jt=jt, st_=st_, g=g, off=off, k=k, n=len(terms): e.matmul(
                                        ps[:, jt * 128:(jt + 1) * 128], lhsT=atok[:, st_, g * 128:(g + 1) * 128], rhs=band[:, off:off + 128],
                                        start=(k == 0), stop=(k == n - 1)), [batok[st_], bband], [bp])
                            rts, brts = RTs.next()
                            copy_op('dve', rts[:], ps[:], [bp], [brts])
                            ps2, bp2 = PS.next()
                            P.op('pe', lambda e, ps2=ps2, g=g, rts=rts: e.matmul(ps2[:], lhsT=poolw[:, g, :], rhs=rts[:], start=True, stop=True), [brts, bw], [bp2])
                            ms, bms = MS.next()
                            P.op('act', lambda e, ms=ms, ps2=ps2, g=g: e.activation(out=ms[:], in_=ps2[:], func=AF.Copy, scale=pscol[:, g:g + 1]), [bp2, bw], [bms])
                            P.dma('sp', s['mixT'][g * 128:(g + 1) * 128, sp * 512:(sp + 1) * 512], ms[:], reads=[bms], writes=[s['b_mixT']])
            P.barrier()

        def ph2():
            with contextlib.ExitStack() as ph:
                KT = RB(ph, nc, 'aK', [96, LS], BF16, 2)
                QT = RB(ph, nc, 'aQ', [96, LS], BF16, 2)
                VT = RB(ph, nc, 'aV', [128, LS // 128, 128], BF16, 2)
                PT = RB(ph, nc, 'aP', [128, 512], BF16, 3)
                RS = RB(ph, nc, 'aR', [128, 512], F32, 2)
                BC = RB(ph, nc, 'aB', [64, 512], F32, 2)
                OS = RB(ph, nc, 'aO', [64, 512], BF16, 2)
                for i in range(2):
                    P.op('pool', lambda e, i=i: e.memset(VT.t[i][:, :, 64:128], 1.0), writes=[VT.b[i]])

                class _Sub:
                    def __init__(self, lo, hi):
                        self.t = PS.t[lo:hi]
                        self.b = PS.b[lo:hi]
                        self.i = 0
                    next = RB.next
                POs = _Sub(0, 2)
                PSs = _Sub(2, 6)
                for s in SEQ:
                    L = s['L']
                    nk = L // 128
                    for h in range(8):
                        kt, bk = KT.next()
                        qt, bq = QT.next()
                        vt, bv = VT.next()
                        P.dma('sp', kt[0:64, 0:L], s['KN'][h * 64:(h + 1) * 64, :], reads=[s['b_KN']], writes=[bk])
                        P.dma('sp', kt[64:96, 0:L], s['KR'][:, :], reads=[s['b_KR']], writes=[bk])
                        P.dma('sp', qt[:, 0:L], s['QT'][h], reads=[s['b_QT']], writes=[bq])
                        P.dma('pool', vt[:, 0:nk, 0:64], s['V'][h],
                              reads=[s['b_V']], writes=[bv])
                        for qc in range(L // 512):
                            po, bpo = POs.next()
                            for k in range(nk):
                                pss, bps = PSs.next()
                                P.op('pe', lambda e, pss=pss, kt=kt, qt=qt, k=k, qc=qc: e.matmul(pss[:], lhsT=kt[:, k * 128:(k + 1) * 128],
                                                                                                 rhs=qt[:, qc * 512:(qc + 1) * 512], start=True, stop=True),
                                     [bk, bq], [bps])
                                pt, bpt = PT.next()
                                P.op('act', lambda e, pt=pt, pss=pss: e.activation(out=pt[:], in_=pss[:], func=AF.Exp), [bps], [bpt])
                                P.op('pe', lambda e, po=po, vt=vt, pt=pt, k=k, nk=nk: e.matmul(po[:], lhsT=vt[:, k, :], rhs=pt[:], start=(k == 0), stop=(k == nk - 1)),
                                     [bv, bpt], [bpo])
                            rs, brs = RS.next()
                            P.op('dve', lambda e, rs=rs, po=po: e.reciprocal(out=rs[64:65, :], in_=po[64:65, :]), [bpo], [brs])
                            pb, bpb = PSs.next()
                            P.op('pe', lambda e, pb=pb, rs=rs: e.matmul(pb[0:64, :], lhsT=onesf[64:65, 0:64], rhs=rs[64:65, :], start=True, stop=True),
                                 [brs, bconst], [bpb])
                            bc, bbc = BC.next()
                            copy_op('act', bc[:], pb[0:64, :], [bpb], [bbc])
                            os_, bos = OS.next()
                            P.op('dve', lambda e, os_=os_, po=po, bc=bc: e.tensor_tensor(out=os_[:], in0=po[0:64, :], in1=bc[:], op=ALU.mult), [bpo, bbc], [bos])
                            P.dma('pool', s['mixT'][512 + h * 64:512 + (h + 1) * 64, qc * 512:(qc + 1) * 512], os_[:], reads=[bos], writes=[s['b_mixT']])
            P.barrier()

        def ph_out(layer):
            with contextlib.ExitStack() as ph:
                T = lambda name, shape, dt: ph.enter_context(sb(name, shape, dt))
                bw = Buf()
                wout = T('wout', [128, 8, 1024], BF16)
                wsrc = (I['even_w_out'] if layer == 0 else I['odd_w_out'])[0].rearrange("(c p) f -> p c f", p=128)
                for c in range(8):
                    loadw(wout[:, c, :], wsrc[:, c, :], bw)
                grow = rowbc(ph, 'grow', I['norm_mix'][layer, 1], D, bw)
                if layer == 1:
                    fwb = T('fwb', [128, 2, 128], BF16)
                    P.op('pool', lambda e: e.memset(fwb[:], 0.0), writes=[bw])
                    for g in range(4):
                        pp = (g % 2) * 64
                        loadw(fwb[pp:pp + 64, g // 2, pp:pp + 64], I['fnet_w'][0, g], bw, np_=64)
                tmp = dict(ssq2=RB(ph, nc, 'ssq2', [128, 8], F32, 2), junk=RB(ph, nc, 'junk', [128, 1024], F32, 1),
                           ep=RB(ph, nc, 'ep', [128, 512], F32, 3))
                XT = RB(ph, nc, 'xt', [128, 4, 1024], F32, 2)
                MX = RB(ph, nc, 'mx', [128, 8, 512], BF16, 2)
                MF = RB(ph, nc, 'mf', [128, 2, 512], BF16, 2)
                for s in SEQ:
                    L = s['L']
                    xin, bxin = (s['x'], Buf()) if layer == 0 else (s['X2'], s['b_X2'])
                    xout, bxout = (s['X1'], s['b_X1']) if layer == 0 else (s['X3'], s['b_X3'])
                    for ci in range(L // 512):
                        c0 = ci * 512
                        xt, bx = XT.next()
                        P.dma('sp', xt[:], xin[c0:c0 + 512, :].rearrange("(j p) d -> p j d", p=128), reads=[bxin], writes=[bx])
                        mx, bm = MX.next()
                        if layer == 0:
                            P.dma('pool', mx[:], s['mixT'][:, c0:c0 + 512].rearrange("(c p) t -> p c t", p=128), reads=[s['b_mixT']], writes=[bm])
                        else:
                            P.dma('pool', mx[:, 0:6, :], s['mix2'][:, c0:c0 + 512].rearrange("(c p) t -> p c t", p=128), reads=[s['b_mix2']], writes=[bm])
                            mf, bmf = MF.next()
                            P.dma('pool', mf[:], s['mfT'][:, c0:c0 + 512].rearrange("(c p) t -> p c t", p=128), reads=[s['b_mfT']], writes=[bmf])
                            for j2 in range(2):
                                ps, bp = PS.next()
                                P.op('pe', lambda e, ps=ps, j2=j2, mf=mf: e.matmul(ps[:], lhsT=fwb[:, j2, :], rhs=mf[:, j2, :], start=True, stop=True), [bmf, bw], [bp])
                                copy_op('act', mx[:, 6 + j2, :], ps[:], [bp], [bm])
                        for j in range(4):
                            pss = []
                            for hf in range(2):
                                ps, bp = PS.next()
                                for c in range(8):
                                    P.op('pe', lambda e, ps=ps, c=c, j=j, hf=hf, mx=mx: e.matmul(ps[:], lhsT=mx[:, c, j * 128:(j + 1) * 128],
                                                                                                 rhs=wout[:, c, hf * 512:(hf + 1) * 512], start=(c == 0), stop=(c == 7)),
                                         [bm, bw], [bp])
                                pss.append((ps, bp))
                            epilogue(pss, xt, bx, j, grow, tmp)
                        P.dma('sp', xout[c0:c0 + 512, :].rearrange("(j p) d -> p j d", p=128), xt[:], reads=[bx], writes=[bxout])
            P.barrier()

        def ph_mlp(layer):
            TT = 256
            nj = TT // 128
            with contextlib.ExitStack() as ph:
                T = lambda name, shape, dt: ph.enter_context(sb(name, shape, dt))
                bw = Buf()
                wfi = T('wfi', [128, 8, 4096], BF16)
                wfo = T('wfo', [128, 32, 1024], BF16)
                s1 = I['w_ff_in'][layer].rearrange("(c p) f -> p c f", p=128)
                s2 = I['w_ff_out'][layer].rearrange("(c p) f -> p c f", p=128)
                for c in range(8):
                    loadw(wfi[:, c, :], s1[:, c, :], bw)
                for c in range(32):
                    loadw(wfo[:, c, :], s2[:, c, :], bw)
                gcol = colvec(ph, 'gcol', I['norm_mlp'][layer, 0], 8, bw)
                grow = rowbc(ph, 'grow', I['norm_mlp'][layer, 1], D, bw)
                tmp = dict(ssq=RB(ph, nc, 'ssq', [128, 12], F32, 2), ssq2=RB(ph, nc, 'ssq2', [128, 8], F32, 2),
                           junk=RB(ph, nc, 'junk', [128, 1024], F32, 1), xn=RB(ph, nc, 'xn', [128, nj, 1024], BF16, 1),
                           ep=RB(ph, nc, 'ep', [128, 512], F32, 2))
                XT = RB(ph, nc, 'xt', [128, nj, 1024], F32, 2)
                HT = RB(ph, nc, 'hT', [128, 8, TT], BF16, 1)
                F1 = RB(ph, nc, 'f1', [128, 32, TT], BF16, 1)
                RL = RB(ph, nc, 'rl', [128, TT], F32, 3)
                for s in SEQ:
                    L = s['L']
                    xin, bxin = (s['X1'], s['b_X1']) if layer == 0 else (s['X3'], s['b_X3'])
                    xout, bxout = (s['X2'], s['b_X2']) if layer == 0 else (s['y'], Buf())
                    for ci in range(L // TT):
                        c0 = ci * TT
                        xt, bx = XT.next()
                        P.dma('sp', xt[:], xin[c0:c0 + TT, :].rearrange("(j p) d -> p j d", p=128), reads=[bxin], writes=[bx])
                        hT, bh = HT.next()
                        rms_transpose(ph, xt, bx, nj, gcol, hT, bh, tmp)
                        f1, bf1 = F1.next()
                        for fc in range(32):
                            ps, bp = PS.next()
                            for dc in range(8):
                                P.op('pe', lambda e, ps=ps, dc=dc, fc=fc, hT=hT: e.matmul(ps[:, 0:TT], lhsT=wfi[:, dc, fc * 128:(fc + 1) * 128], rhs=hT[:, dc, :],
                                                                                          start=(dc == 0), stop=(dc == 7)), [bh, bw], [bp])
                            rl, brl = RL.next()
                            P.op('act', lambda e, rl=rl, ps=ps: e.activation(out=rl[:], in_=ps[:, 0:TT], func=AF.Relu), [bp], [brl])
                            eng = 'pool' if fc % 2 else 'dve'
                            P.op(eng, lambda e, rl=rl, f1=f1, fc=fc: e.tensor_tensor(out=f1[:, fc, :], in0=rl[:], in1=rl[:], op=ALU.mult), [brl], [bf1])
                        for j in range(nj):
                            pss = []
                            for hf in range(2):
                                ps, bp = PS.next()
                                for fc in range(32):
                                    P.op('pe', lambda e, ps=ps, fc=fc, j=j, hf=hf, f1=f1: e.matmul(ps[:], lhsT=f1[:, fc, j * 128:(j + 1) * 128],
                                                                                                   rhs=wfo[:, fc, hf * 512:(hf + 1) * 512], start=(fc == 0), stop=(fc == 31)),
                                         [bf1, bw], [bp])
                                pss.append((ps, bp))
                            epilogue(pss, xt, bx, j, grow, tmp)
                        P.dma('pool', xout[c0:c0 + TT, :].rearrange("(j p) d -> p j d", p=128), xt[:], reads=[bx], writes=[bxout])
            P.barrier()

        def ph5():
            with contextlib.ExitStack() as ph:
                T = lambda name, shape, dt: ph.enter_context(sb(name, shape, dt))
                bw = Buf()
                owin = T('owin', [128, 8, 2560], BF16)
                src = I['odd_w_in'][0].rearrange("(c p) f -> p c f", p=128)
                for c in range(8):
                    loadw(owin[:, c, :], src[:, c, :], bw)
                gcol = colvec(ph, 'gcol', I['norm_mix'][1, 0], 8, bw)
                swc = T('swc', [128, 3, 18], F32)
                for j in range(3):
                    P.dma('pool', swc[:, j, :], I['short_w'][0, j].rearrange("(c p) -> p c", p=128), writes=[bw])
                sbc = colvec(ph, 'sbc', I['short_b'][0], 18, bw)
                DCt = T('DCt', [128, 2, 128], F32)
                loadw(DCt[:, 0, :], I['c_DC'], bw)
                loadw(DCt[:, 1, :], I['c_DS'], bw)
                tmp = dict(ssq=RB(ph, nc, 'ssq', [128, 12], F32, 2), junk=RB(ph, nc, 'junk', [128, 1024], F32, 1),
                           xn=RB(ph, nc, 'xn', [128, 4, 1024], BF16, 1))
                XT = RB(ph, nc, 'xt', [128, 4, 1024], F32, 2)
                HT = RB(ph, nc, 'hT', [128, 8, 512], BF16, 2)
                WN = RB(ph, nc, 'wn', [128, 514], F32, 3)
                OT = RB(ph, nc, 'ot', [128, 512], F32, 3)
                FS = RB(ph, nc, 'fs', [128, 512], F32, 2)
                FO = RB(ph, nc, 'fo', [128, 512], F32, 3)
                carry = T('carry', [128, 18, 2], F32)
                bcar = Buf()
                for s in SEQ:
                    L = s['L']
                    fsc = 1.0 / math.sqrt(64.0 * L)
                    P.op('pool', lambda e: e.memset(carry[:], 0.0), writes=[bcar])
                    nch = L // 512
                    for ci in range(nch + 1):
                        c0 = ci * 512
                        last = (ci == nch)
                        if not last:
                            xt, bx = XT.next()
                            P.dma('sp', xt[:], s['X2'][c0:c0 + 512, :].rearrange("(j p) d -> p j d", p=128), reads=[s['b_X2']], writes=[bx])
                            hT, bh = HT.next()
                            rms_transpose(ph, xt, bx, 4, gcol, hT, bh, tmp)
                        for fc in range(18):
                            wn, bwn = WN.next()
                            P.op('pool', lambda e, wn=wn, fc=fc: e.tensor_copy(out=wn[:, 0:2], in_=carry[:, fc, :]), [bcar], [bwn])
                            if not last:
                                ps, bp = PS.next()
                                for dc in range(8):
                                    P.op('pe', lambda e, ps=ps, dc=dc, fc=fc, hT=hT: e.matmul(ps[:], lhsT=owin[:, dc, fc * 128:(fc + 1) * 128], rhs=hT[:, dc, :],
                                                                                              start=(dc == 0), stop=(dc == 7)), [bh, bw], [bp])
                                copy_op('act', wn[:, 2:514], ps[:], [bp], [bwn])
                                nw = 512
                            else:
                                P.op('pool', lambda e, wn=wn: e.memset(wn[:, 2:3], 0.0), [], [bwn])
                                nw = 1
                            ot, bo = OT.next()
                            P.op('dve', lambda e, ot=ot, wn=wn, fc=fc, nw=nw: e.tensor_scalar(out=ot[:, 0:nw], in0=wn[:, 0:nw], scalar1=swc[:, 0, fc:fc + 1],
                                                                                              scalar2=sbc[:, fc:fc + 1], op0=ALU.mult, op1=ALU.add), [bwn, bw], [bo])
                            for j in (1, 2):
                                P.op('dve', lambda e, ot=ot, wn=wn, fc=fc, nw=nw, j=j: e.scalar_tensor_tensor(out=ot[:, 0:nw], in0=wn[:, j:j + nw], scalar=swc[:, j, fc:fc + 1],
                                                                                                              in1=ot[:, 0:nw], op0=ALU.mult, op1=ALU.add), [bwn, bw, bo], [bo])
                            if not last:
                                P.op('pool', lambda e, wn=wn, fc=fc: e.tensor_copy(out=carry[:, fc, :], in_=wn[:, 512:514]), [bwn], [bcar])
                            if ci == 0:
                                P.dma('sp', s['zcT'][fc * 128:(fc + 1) * 128, 0:511], ot[:, 1:512], reads=[bo], writes=[s['b_zcT']])
                            elif not last:
                                P.dma('sp', s['zcT'][fc * 128:(fc + 1) * 128, c0 - 1:c0 + 511], ot[:, 0:512], reads=[bo], writes=[s['b_zcT']])
                            else:
                                P.dma('sp', s['zcT'][fc * 128:(fc + 1) * 128, L - 1:L], ot[:, 0:1], reads=[bo], writes=[s['b_zcT']])
                        if last:
                            continue
                        for f2 in range(2):
                            ps, bp = PS.next()
                            for dc in range(8):
                                P.op('pe', lambda e, ps=ps, dc=dc, f2=f2, hT=hT: e.matmul(ps[:], lhsT=owin[:, dc, 2304 + f2 * 128:2304 + (f2 + 1) * 128], rhs=hT[:, dc, :],
                                                                                          start=(dc == 0), stop=(dc == 7)), [bh, bw], [bp])
                            fs, bfs = FS.next()
                            copy_op('act', fs[:], ps[:], [bp], [bfs])
                            for k2 in range(2):
                                ps2, bp2 = PS.next()
                                P.op('pe', lambda e, ps2=ps2, k2=k2, fs=fs: e.matmul(ps2[:], lhsT=DCt[:, k2, :], rhs=fs[:], start=True, stop=True), [bfs, bw], [bp2])
                                fo, bfo = FO.next()
                                P.op('act', lambda e, fo=fo, ps2=ps2: e.activation(out=fo[:], in_=ps2[:], func=AF.Copy, scale=fsc), [bp2], [bfo])
                                r0 = k2 * 256 + f2 * 128
                                P.dma('pool', s['fcT'][r0:r0 + 128, c0:c0 + 512], fo[:], reads=[bfo], writes=[s['b_fcT']])
            P.barrier()

        def ph6():
            PI = math.pi
            with contextlib.ExitStack() as ph:
                T = lambda name, shape, dt: ph.enter_context(sb(name, shape, dt))
                bw = Buf()
                w1 = T('fw1', [33, 64], F32)
                w2 = T('fw2', [64, 64], F32)
                w3 = T('fw3', [64, 3072], F32)
                P.dma('pool', w1[:], I['filt_w1'][0], writes=[bw])
                P.dma('pool', w2[:], I['filt_w2'][0], writes=[bw])
                P.dma('pool', w3[:], I['filt_w3'][0], writes=[bw])
                cols = T('fcols', [64, 8], F32)
                for k, nm_ in enumerate(('filt_b1', 'filt_b2', 'filt_freq')):
                    P.dma('pool', cols[:, k:k + 1], I[nm_][0].rearrange("(p o) -> p o", o=1), writes=[bw])
                P.op('dve', lambda e: e.tensor_tensor(out=cols[:, 3:4], in0=cols[:, 2:3], in1=cols[:, 0:1], op=ALU.mult), [bw], [bw])
                P.op('dve', lambda e: e.tensor_tensor(out=cols[:, 4:5], in0=cols[:, 2:3], in1=cols[:, 1:2], op=ALU.mult), [bw], [bw])
                negd = T('negd', [128, 6], F32)
                P.dma('pool', negd[:], I['c_negd'], writes=[bw])
                FT = RB(ph, nc, 'ft', [33, 512], F32, 2)
                TB = RB(ph, nc, 'tb', [128, 512], F32, 2)
                DT = RB(ph, nc, 'dt', [128, 6, 512], F32, 2)
                AR = RB(ph, nc, 'ar', [64, 3, 512], F32, 2)
                H1 = RB(ph, nc, 'h1', [64, 512], F32, 2)
                H2 = RB(ph, nc, 'h2', [64, 512], F32, 2)
                HF = RB(ph, nc, 'hf', [128, 512], F32, 3)
                junk = T('fjunk', [128, 512], F32)
                bjunk = Buf()
                ssq = T('fssq', [128, 24, 16], F32)
                tot = T('ftot', [128, 3, 24], F32)
                bss = Buf()

                def sinlayer(ps, bp, fb, out, bo):
                    ar, ba = AR.next()
                    P.op('dve', lambda e: e.tensor_scalar(out=ar[:, 0, :], in0=ps[0:64, :], scalar1=cols[:, 2:3], scalar2=cols[:, fb:fb + 1],
                                                          op0=ALU.mult, op1=ALU.add), [bp, bw], [ba])
                    P.op('dve', lambda e: e.tensor_scalar(out=ar[:, 1, :], in0=ar[:, 0, :], scalar1=PI, scalar2=2 * PI, op0=ALU.is_gt, op1=ALU.mult), [ba], [ba])
                    P.op('dve', lambda e: e.tensor_tensor(out=ar[:, 0, :], in0=ar[:, 0, :], in1=ar[:, 1, :], op=ALU.subtract), [ba], [ba])
                    P.op('dve', lambda e: e.tensor_scalar(out=ar[:, 1, :], in0=ar[:, 0, :], scalar1=-PI, scalar2=2 * PI, op0=ALU.is_lt, op1=ALU.mult), [ba], [ba])
                    P.op('dve', lambda e: e.tensor_tensor(out=ar[:, 0, :], in0=ar[:, 0, :], in1=ar[:, 1, :], op=ALU.add), [ba], [ba])
                    P.op('act', lambda e: e.activation(out=out[:], in_=ar[:, 0, :], func=AF.Sin, scale=0.999999), [ba], [bo])

                for s in SEQ:
                    L = s['L']
                    nch = L // 512
                    for ci in range(nch):
                        c0 = ci * 512
                        ft, bft = FT.next()
                        P.dma('sp', ft[:], I['c_featT_' + s['nm']][:, c0:c0 + 512], writes=[bft])
                        tb, btb = TB.next()
                        P.dma('sp', tb[:], I['c_trow_' + s['nm']][:, c0:c0 + 512].partition_broadcast(128), writes=[btb])
                        dt_, bdt = DT.next()
                        for k in range(6):
                            P.op('act', lambda e, k=k, dt_=dt_, tb=tb: e.activation(out=dt_[:, k, :], in_=tb[:], func=AF.Exp, scale=negd[:, k:k + 1]), [btb, bw], [bdt])
                        ps, bp = PS.next()
                        P.op('pe', lambda e, ps=ps, ft=ft: e.matmul(ps[0:64, :], lhsT=w1[:], rhs=ft[:], start=True, stop=True), [bft, bw], [bp])
                        h1, bh1 = H1.next()
                        sinlayer(ps, bp, 3, h1, bh1)
                        ps, bp = PS.next()
                        P.op('pe', lambda e, ps=ps, h1=h1: e.matmul(ps[0:64, :], lhsT=w2[:], rhs=h1[:], start=True, stop=True), [bh1, bw], [bp])
                        h2, bh2 = H2.next()
                        sinlayer(ps, bp, 4, h2, bh2)
                        for jc in range(24):
                            ps, bp = PS.next()
                            P.op('pe', lambda e, ps=ps, jc=jc, h2=h2: e.matmul(ps[:], lhsT=w3[:, jc * 128:(jc + 1) * 128], rhs=h2[:], start=True, stop=True), [bh2, bw], [bp])
                            hf, bhf = HF.next()
                            P.op('dve', lambda e, hf=hf, ps=ps, dt_=dt_, jc=jc: e.tensor_tensor(out=hf[:], in0=ps[:], in1=dt_[:, jc % 6, :], op=ALU.mult), [bp, bdt], [bhf])
                            P.op('act', lambda e, hf=hf, jc=jc, ci=ci: e.activation(out=junk[:], in_=hf[:], func=AF.Square, accum_out=ssq[:, jc, ci:ci + 1]), [bhf], [bjunk, bss])
                            P.dma('sp', s['hfT'][jc * 128:(jc + 1) * 128, c0:c0 + 512], hf[:], reads=[bhf], writes=[s['b_hfT']])
                    P.op('dve', lambda e, nch=nch: e.tensor_reduce(out=tot[:, 0, :], in_=ssq[:, :, 0:nch], axis=AX.X, op=ALU.add), [bss], [bss])
                    P.op('act', lambda e: e.activation(out=tot[:, 1, :], in_=tot[:, 0, :], func=AF.Sqrt, bias=epsT[:, 0:1], scale=1.0), [bss, bconst], [bss])
                    P.op('dve', lambda e: e.reciprocal(out=tot[:, 2, :], in_=tot[:, 1, :]), [bss], [bss])
                    P.dma('sp', s['invd'].rearrange("(c p) -> p c", p=128), tot[:, 2, :], reads=[bss], writes=[s['b_invd']])
            P.barrier()

        def ph7():
            CB = 4
            with contextlib.ExitStack() as ph:
                T = lambda name, shape, dt: ph.enter_context(sb(name, shape, dt))
                bw = Buf()
                F1 = T('cF1', [64, 128], BF16)
                loadw(F1[:], I['c_F1'], bw, np_=64)
                F1fA = T('cF1fA', [64, 128], BF16)
                F1fB = T('cF1fB', [64, 128], BF16)
                loadw(F1fA[:], I['c_F1fA'], bw, np_=64)
                loadw(F1fB[:], I['c_F1fB'], bw, np_=64)
                hbb = rowbc(ph, 'hbb', I['hyena_bias'][0].rearrange("o c -> (o c)"), 1536, bw, npart=64)
                for s in SEQ:
                    L, N2, nm = s['L'], s['N2'], s['nm']
                    with contextlib.ExitStack() as p2:
                        T2 = lambda name, shape, dt: p2.enter_context(sb(name + nm, shape, dt))
                        bc_ = Buf()
                        G1re = T2('G1re', [64, 64], BF16)
                        G1im = T2('G1im', [64, 64], BF16)
                        loadw(G1re[:], I['c_G1re_' + nm], bc_, np_=64)
                        loadw(G1im[:], I['c_G1im_' + nm], bc_, np_=64)
                        TA = T2('TA', [N2, 4, 128], F32)
                        TBt = T2('TB', [N2, 4, 128], F32)
                        TfA = T2('TfA', [N2, 4, 128], F32)
                        TfB = T2('TfB', [N2, 4, 128], F32)
                        for t_, k_ in ((TA, 'TA_'), (TBt, 'TB_'), (TfA, 'TfA_'), (TfB, 'TfB_')):
                            P.dma('pool', t_[:], I['c_' + k_ + nm], writes=[bc_])
                        TcA = T2('TcA', [64, 4, 2 * N2], F32)
                        TcB = T2('TcB', [64, 4, 2 * N2], F32)
                        P.dma('pool', TcA[:], I['c_TcA_' + nm], writes=[bc_])
                        P.dma('pool', TcB[:], I['c_TcB_' + nm], writes=[bc_])
                        F2 = T2('F2', [N2, 3, N2], BF16)
                        FA = T2('FA', [N2, 2 * N2], BF16)
                        FB = T2('FB', [N2, 2 * N2], BF16)
                        loadw(F2[:].rearrange("p a b -> p (a b)"), I['c_F2_' + nm].rearrange("p a b -> p (a b)"), bc_, np_=N2)
                        loadw(FA[:], I['c_FA_' + nm], bc_, np_=N2)
                        loadw(FB[:], I['c_FB_' + nm], bc_, np_=N2)
                        invb = T2('invb', [64, 3072], F32)
                        P.dma('pool', invb[:], s['invd'].rearrange("(o n) -> o n", o=1).partition_broadcast(64), reads=[s['b_invd']], writes=[bc_])
                        XI = RB(p2, nc, 'xi' + nm, [64, 3, CB, N2], F32, 2)
                        HI = RB(p2, nc, 'hi' + nm, [64, 4, CB, N2], F32, 2)
                        HB = RB(p2, nc, 'hb' + nm, [64, 4, CB, N2], BF16, 2)
                        UB = RB(p2, nc, 'ub' + nm, [64, CB, N2], BF16, 3)
                        PP = RB(p2, nc, 'pp' + nm, [N2, 2, 4, 128], F32, 3)
                        YT = RB(p2, nc, 'yt' + nm, [N2, 4, 128], BF16, 6)
                        YF = RB(p2, nc, 'yf' + nm, [N2, 4, 128], F32, 6)
                        KK = RB(p2, nc, 'kk' + nm, [N2, 2, 2, CB, 128], F32, 2)
                        GG = RB(p2, nc, 'gg' + nm, [N2, 4, 128], BF16, 2)
                        VP = RB(p2, nc, 'vp' + nm, [64, 2, 4, 2 * N2], F32, 2)
                        VT_ = RB(p2, nc, 'vt' + nm, [64, 4, 2, N2], BF16, 2)
                        UT = RB(p2, nc, 'ut' + nm, [64, CB, N2], F32, 3)
                        MO = RB(p2, nc, 'mo' + nm, [N2, CB, 64], BF16, 2)

                        def tview(ps, n):
                            return ps[0:n, :].rearrange("p (t c) -> p t c", t=4)

                        def twiddle(ps, bp, A, B, n, w, outre, outim, bo, rd):
                            pp, bpp = PP.next() if n == N2 and w == 64 else VP.next()
                            src = ps
                            P.op('dve', lambda e: e.tensor_tensor(out=pp[0:n, 0], in0=src, in1=A, op=ALU.mult), [bp] + rd, [bpp])
                            P.op('dve', lambda e: e.tensor_tensor(out=pp[0:n, 1], in0=src, in1=B, op=ALU.mult), [bp] + rd, [bpp])
                            P.op('pool', lambda e: e.tensor_tensor(out=outre, in0=pp[0:n, 0, :, 0:w], in1=pp[0:n, 1, :, w:2 * w], op=ALU.subtract), [bpp], [bo])
                            P.op('pool', lambda e: e.tensor_tensor(out=outim, in0=pp[0:n, 1, :, 0:w], in1=pp[0:n, 0, :, w:2 * w], op=ALU.add), [bpp], [bo])

                        def tform_fwd(tiles, rd, A, B, out, bo):
                            ps, bp = PS.next()
                            pv = tview(ps, N2)
                            for t, tl in enumerate(tiles):
                                P.op('pe', lambda e, t=t, tl=tl: e.matmul(pv[:, t, :], lhsT=tl, rhs=F1[:], start=True, stop=True), rd + [bw], [bp])
                            twiddle(pv, bp, A, B, N2, 64, out[:, :, 0:64], out[:, :, 64:128], bo, [bc_])

                        def nform_fwd(yre_src, yim_src, rd, imsrc_re=None, imsrc_im=None):
                            ps, bp = PS.next()
                            pv = tview(ps, N2)
                            ire = yre_src if imsrc_re is None else imsrc_re
                            iim = yim_src if imsrc_im is None else imsrc_im
                            P.op('pe', lambda e: e.matmul(pv[:, :, 0:64], lhsT=F2[:, 0, :], rhs=yre_src, start=True, stop=False), rd + [bc_], [bp])
                            P.op('pe', lambda e: e.matmul(pv[:, :, 0:64], lhsT=F2[:, 2, :], rhs=yim_src, start=False, stop=True), rd + [bc_], [bp])
                            P.op('pe', lambda e: e.matmul(pv[:, :, 64:128], lhsT=F2[:, 0, :], rhs=iim, start=True, stop=False), rd + [bc_], [bp])
                            P.op('pe', lambda e: e.matmul(pv[:, :, 64:128], lhsT=F2[:, 1, :], rhs=ire, start=False, stop=True), rd + [bc_], [bp])
                            return pv, bp

                        def conv(ub, bub, kk, bkk, o):
                            yt, byt = YT.next()
                            tform_fwd([ub[:, t, :] for t in range(CB)], [bub], TA[:], TBt[:], yt, byt)
                            pv, bp = nform_fwd(yt[:, :, 0:64], yt[:, :, 64:128], [byt])
                            gg, bgg = GG.next()
                            twiddle(pv, bp, kk[:, o, 0], kk[:, o, 1], N2, 64, gg[:, :, 0:64], gg[:, :, 64:128], bgg, [bkk])
                            vt, bvt = VT_.next()
                            per = 512 // (2 * N2)
                            per = min(per, CB)
                            for g0 in range(0, CB, per):
                                ps, bp = PS.next()
                                pv2 = ps[0:64, 0:per * 2 * N2].rearrange("p (t c) -> p t c", t=per)
                                for t in range(per):
                                    P.op('pe', lambda e, t=t, g0=g0: e.matmul(pv2[:, t, :], lhsT=gg[:, g0 + t, 0:64], rhs=FA[:], start=True, stop=False), [bgg, bc_], [bp])
                                    P.op('pe', lambda e, t=t, g0=g0: e.matmul(pv2[:, t, :], lhsT=gg[:, g0 + t, 64:128], rhs=FB[:], start=False, stop=True), [bgg, bc_], [bp])
                                vp, bvp = VP.next()
                                P.op('dve', lambda e, g0=g0, pv2=pv2, vp=vp: e.tensor_tensor(out=vp[:, 0, 0:per, :], in0=pv2, in1=TcA[:, 0:per, :], op=ALU.mult), [bp, bc_], [bvp])
                                P.op('dve', lambda e, g0=g0, pv2=pv2, vp=vp: e.tensor_tensor(out=vp[:, 1, 0:per, :], in0=pv2, in1=TcB[:, 0:per, :], op=ALU.mult), [bp, bc_], [bvp])
                                P.op('pool', lambda e, g0=g0, vp=vp: e.tensor_tensor(out=vt[:, g0:g0 + per, 0, :], in0=vp[:, 0, 0:per, 0:N2], in1=vp[:, 1, 0:per, N2:2 * N2], op=ALU.subtract), [bvp], [bvt])
                                P.op('pool', lambda e, g0=g0, vp=vp: e.tensor_tensor(out=vt[:, g0:g0 + per, 1, :], in0=vp[:, 1, 0:per, 0:N2], in1=vp[:, 0, 0:per, N2:2 * N2], op=ALU.add), [bvp], [bvt])
                            ps, bp = PS.next()
                            pvo = ps[0:64, 0:CB * N2].rearrange("p (t c) -> p t c", t=CB)
                            P.op('pe', lambda e: e.matmul(pvo, lhsT=G1re[:], rhs=vt[:, :, 0, :], start=True, stop=False), [bvt, bc_], [bp])
                            P.op('pe', lambda e: e.matmul(pvo, lhsT=G1im[:], rhs=vt[:, :, 1, :], start=False, stop=True), [bvt, bc_], [bp])
                            return pvo, bp

                        def zc_view(r0):
                            return s['zcT'][r0:r0 + CB, :].rearrange("c (a b) -> a c b", b=N2)

                        for cb in range(768 // CB):
                            ch0 = cb * CB
                            xi, bxi = XI.next()
                            for k in range(3):
                                P.dma('sp', xi[:, k], zc_view(k * 768 + ch0), reads=[s['b_zcT']], writes=[bxi])
                            hi, bhi = HI.next()
                            for od in range(4):
                                P.dma('pool', hi[:, od], s['hfT'][od * 768 + ch0:od * 768 + ch0 + CB, :].rearrange("c (a b) -> a c b", b=N2),
                                      reads=[s['b_hfT']], writes=[bhi])
                            hb, bhb = HB.next()
                            for od in range(4):
                                for t in range(CB):
                                    col = od * 768 + ch0 + t
                                    eng = 'dve' if (od * CB + t) % 2 else 'pool'
                                    P.op(eng, lambda e, od=od, t=t, col=col, hb=hb, hi=hi: e.tensor_scalar(out=hb[:, od, t, :], in0=hi[:, od, t, :], scalar1=invb[:, col:col + 1],
                                                                                                          scalar2=None, op0=ALU.mult), [bhi, bc_], [bhb])
                            for od in (1, 3):
                                P.op('pool', lambda e, od=od, hb=hb: e.memset(hb[0:1, od, :, 0:1], 0.0), [], [bhb])
                            kk, bkk = KK.next()
                            for o in range(2):
                                yf, byf = YF.next()
                                yb, byb = YF.next()
                                tform_fwd([hb[:, 2 * o, t, :] for t in range(CB)], [bhb], TA[:], TBt[:], yf, byf)
                                tform_fwd([hb[:, 2 * o + 1, t, :] for t in range(CB)], [bhb], TA[:], TBt[:], yb, byb)
                                ysum, bys = YT.next()
                                ydif, byd = YT.next()
                                P.op('pool', lambda e, ysum=ysum, yf=yf, yb=yb: e.tensor_tensor(out=ysum[:], in0=yf[:], in1=yb[:], op=ALU.add), [byf, byb], [bys])
                                P.op('pool', lambda e, ydif=ydif, yf=yf, yb=yb: e.tensor_tensor(out=ydif[:], in0=yf[:], in1=yb[:], op=ALU.subtract), [byf, byb], [byd])
                                pv, bp = nform_fwd(ysum[:, :, 0:64], ysum[:, :, 64:128], [bys, byd], imsrc_re=ydif[:, :, 0:64], imsrc_im=ydif[:, :, 64:128])
                                for half in range(2):
                                    copy_op('act', kk[:, o, 0, :, half * 64:(half + 1) * 64], pv[:, :, 0:64], [bp], [bkk])
                                    copy_op('act', kk[:, o, 1, :, half * 64:(half + 1) * 64], pv[:, :, 64:128], [bp], [bkk])
                            ub, bub = UB.next()
                            copy_op('act', ub[:], xi[:, 0], [bxi], [bub])
                            ucur, bucur = xi[:, 0], bxi
                            for o in range(2):
                                pvo, bp = conv(ub, bub, kk, bkk, o)
                                ut, but = UT.next()
                                for t in range(CB):
                                    col = o * 768 + ch0 + t
                                    P.op('dve', lambda e, t=t, col=col, ut=ut, ucur=ucur, pvo=pvo: e.scalar_tensor_tensor(out=ut[:, t, :], in0=ucur[:, t, :], scalar=hbb[:, col:col + 1],
                                                                                                                          in1=pvo[:, t, :], op0=ALU.mult, op1=ALU.add), [bucur, bp, bw], [but])
                                ut2, but2 = UT.next()
                                P.op('pool', lambda e, ut=ut, ut2=ut2, o=o, xi=xi: e.tensor_tensor(out=ut2[:], in0=ut[:], in1=xi[:, 1 + o], op=ALU.mult), [but, bxi], [but2])
                                ub, bub = UB.next()
                                copy_op('act', ub[:], ut2[:], [but2], [bub])
                                ucur, bucur = ut2, but2
                            P.dma('sp', s['mix2'][ch0:ch0 + CB, :].rearrange("c (a b) -> a c b", b=N2), ub[:], reads=[bub], writes=[s['b_mix2']])

                        for cb in range(256 // CB):
                            q0 = cb * CB
                            xi, bxi = XI.next()
                            P.dma('sp', xi[:, 0], s['fcT'][q0:q0 + CB, :].rearrange("c (a b) -> a c b", b=N2), reads=[s['b_fcT']], writes=[bxi])
                            P.dma('sp', xi[:, 1], s['fcT'][256 + q0:256 + q0 + CB, :].rearrange("c (a b) -> a c b", b=N2), reads=[s['b_fcT']], writes=[bxi])
                            xc, bxc = UB.next()
                            xs_, bxs = UB.next()
                            copy_op('act', xc[:], xi[:, 0], [bxi], [bxc])
                            copy_op('pool', xs_[:], xi[:, 1], [bxi], [bxs])
                            ps, bp = PS.next()
                            pv = tview(ps, N2)
                            for t in range(CB):
                                P.op('pe', lambda e, t=t, xc=xc: e.matmul(pv[:, t, :], lhsT=xc[:, t, :], rhs=F1fA[:], start=True, stop=False), [bxc, bw], [bp])
                                P.op('pe', lambda e, t=t, xs_=xs_: e.matmul(pv[:, t, :], lhsT=xs_[:, t, :], rhs=F1fB[:], start=False, stop=True), [bxs, bw], [bp])
                            yt, byt = YT.next()
                            twiddle(pv, bp, TfA[:], TfB[:], N2, 64, yt[:, :, 0:64], yt[:, :, 64:128], byt, [bc_])
                            ps, bp = PS.next()
                            pz = ps[0:N2, 0:CB * 64].rearrange("p (t c) -> p t c", t=CB)
                            P.op('pe', lambda e, pz=pz, yt=yt: e.matmul(pz, lhsT=F2[:, 0, :], rhs=yt[:, :, 0:64], start=True, stop=False), [byt, bc_], [bp])
                            P.op('pe', lambda e, pz=pz, yt=yt: e.matmul(pz, lhsT=F2[:, 2, :], rhs=yt[:, :, 64:128], start=False, stop=True), [byt, bc_], [bp])
                            mo, bmo = MO.next()
                            copy_op('act', mo[:], pz, [bp], [bmo])
                            P.dma('sp', s['mfT'][q0:q0 + CB, :].rearrange("c (b a) -> b c a", a=64), mo[:], reads=[bmo], writes=[s['b_mfT']])
                    P.barrier()

        phases = [ph1, ph2, lambda: ph_out(0), lambda: ph_mlp(0), ph5, ph6, ph7, lambda: ph_out(1), lambda: ph_mlp(1)]
        for i, f in enumerate(phases):
            if i >= stop_after:
                break
            f()
        if dbg is not None:
            dbg(nc, P, SEQ, I)
        P.emit()
    return nc


_NC_CACHE = {}


def _prep_inputs(inputs):
    consts = _get_consts()
    base = {}
    for k, v in inputs.items():
        if k in ('x_prompt', 'x_sample'):
            continue
        base[k] = np.ascontiguousarray(np.asarray(v, dtype=np.float32))
    for k, v in consts.items():
        base['c_' + k] = np.ascontiguousarray(v)
    xp = np.asarray(inputs['x_prompt'], dtype=np.float32)
    xs = np.asarray(inputs['x_sample'], dtype=np.float32)
    in_maps = []
    for i in range(8):
        m = dict(base)
        m['x_p'] = np.ascontiguousarray(xp[i])
        m['x_s'] = np.ascontiguousarray(xs[i])
        in_maps.append(m)
    return consts, in_maps


def kernel(**inputs):
    consts, in_maps = _prep_inputs(inputs)
    if 'nc' not in _NC_CACHE:
        _NC_CACHE['nc'] = build(consts)
    nc = _NC_CACHE['nc']
    res = run_bass_kernel_spmd(nc, in_maps, core_ids=list(range(8)))
    yp = np.stack([np.asarray(r['y_p'], dtype=np.float32) for r in res.results], 0)
    ys = np.stack([np.asarray(r['y_s'], dtype=np.float32) for r in res.results], 0)
    return (yp, ys)
```

```python
import contextlib
import math
import os
import numpy as np
import concourse.bass as bass
import concourse.mybir as mybir
from concourse.bass_utils import run_bass_kernel_spmd

F32 = mybir.dt.float32
BF16 = mybir.dt.bfloat16
AF = mybir.ActivationFunctionType
ALU = mybir.AluOpType
AX = mybir.AxisListType

D = 1024
EPS = 1e-6
LP, LS = 2048, 8192
ENGS = ('pe', 'act', 'dve', 'pool', 'sp')
NRING = 8


class Buf:
    __slots__ = ('w', 'r')

    def __init__(self):
        self.w = None
        self.r = []


class Op:
    __slots__ = ('eng', 'fn', 'deps', 'signal', 'sigval', 'is_dma', 'dsem', 'dval', 'ringdep')

    def __init__(self, eng, fn, is_dma):
        self.eng = eng
        self.fn = fn
        self.deps = []
        self.signal = False
        self.sigval = 0
        self.is_dma = is_dma
        self.dsem = None
        self.dval = 0
        self.ringdep = None


class _Rec:
    def __getattr__(self, name):
        def f(*a, **k):
            self.__dict__['call'] = (name, a, k)
            return None
        return f


class Prog:
    def __init__(self, nc):
        self.nc = nc
        self.ops = {e: [] for e in ENGS}
        self.ndma = {e: 0 for e in ENGS}
        self.ringops = {e: [None] * NRING for e in ENGS}

    def _rec(self, eng, fn, reads, writes, is_dma):
        r = _Rec()
        fn(r)
        op = Op(eng, r.call, is_dma)
        deps = []
        for t in reads:
            if t.w is not None:
                deps.append(t.w)
        for t in writes:
            if t.w is not None:
                deps.append(t.w)
            deps.extend(t.r)
        for t in reads:
            t.r.append(op)
        for t in writes:
            t.w = op
            t.r = []
        seen = set()
        for d in deps:
            if id(d) in seen or d is op:
                continue
            seen.add(id(d))
            if (not d.is_dma) and d.eng == eng and eng == 'pe':
                continue
            op.deps.append(d)
            if not d.is_dma:
                d.signal = True
        if is_dma:
            j = self.ndma[eng]
            self.ndma[eng] = j + 1
            op.dsem = (eng, j % NRING)
            op.dval = 16 * (j // NRING + 1)
            op.ringdep = self.ringops[eng][j % NRING]
            self.ringops[eng][j % NRING] = op
        self.ops[eng].append(op)
        return op

    def op(self, eng, fn, reads=(), writes=()):
        return self._rec(eng, fn, reads, writes, False)

    def dma(self, eng, out, in_, reads=(), writes=()):
        return self._rec(eng, lambda e: e.dma_start(out=out, in_=in_), reads, writes, True)

    def barrier(self):
        lasts = []
        for e in ENGS:
            for o in reversed(self.ops[e]):
                if o.fn is not None and not o.is_dma:
                    lasts.append(o)
                    break
            for d in self.ringops[e]:
                if d is not None:
                    lasts.append(d)
        for e in ENGS:
            op = Op(e, None, False)
            for d in lasts:
                if d.is_dma or d.eng != e:
                    op.deps.append(d)
                    if not d.is_dma:
                        d.signal = True
            self.ops[e].append(op)

    def emit(self):
        nc = self.nc
        with contextlib.ExitStack() as st:
            csem = {e: st.enter_context(nc.semaphore('c_' + e)) for e in ENGS}
            dsem = {}
            for e in ENGS:
                for i in range(min(NRING, self.ndma[e])):
                    dsem[(e, i)] = st.enter_context(nc.semaphore('d_%s%d' % (e, i)))
            for e in ENGS:
                c = 0
                for o in self.ops[e]:
                    if o.signal:
                        c += 1
                        o.sigval = c
            st.enter_context(nc.allow_non_contiguous_dma(reason='small strided tables / layout loads'))
            block = st.enter_context(nc.Block())
            handles = {'pe': 'tensor', 'act': 'scalar', 'dve': 'vector', 'pool': 'gpsimd', 'sp': 'sync'}

            def make(e):
                def body(eng):
                    waited = {}
                    for o in self.ops[e]:
                        ws = []
                        for d in o.deps:
                            if d.is_dma:
                                ws.append((('d',) + d.dsem, dsem[d.dsem], d.dval))
                            else:
                                ws.append((('c', d.eng), csem[d.eng], d.sigval))
                        if o.ringdep is not None:
                            d = o.ringdep
                            ws.append((('d',) + d.dsem, dsem[d.dsem], d.dval))
                        for key, sem, val in ws:
                            if waited.get(key, 0) >= val:
                                continue
                            waited[key] = val
                            eng.wait_ge(sem, val)
                        if o.fn is None:
                            continue
                        name_, a_, k_ = o.fn
                        ins = getattr(eng, name_)(*a_, **k_)
                        if o.is_dma:
                            ins.then_inc(dsem[o.dsem], 16)
                        elif o.signal:
                            ins.then_inc(csem[e], 1)
                    for i in range(NRING):
                        d = self.ringops[e][i]
                        if d is not None and waited.get(('d',) + d.dsem, 0) < d.dval:
                            eng.wait_ge(dsem[d.dsem], d.dval)
                return body

            for e in ENGS:
                if self.ops[e]:
                    getattr(block, handles[e])(make(e))


_UID = [0]


class _Stop(Exception):
    pass


def _chk(tag):
    if os.environ.get('KSTOP') == tag:
        raise _Stop()


class RB:
    def __init__(self, st, nc, name, shape, dt, n, psum=False):
        alloc = nc.psum_tensor if psum else nc.sbuf_tensor
        _UID[0] += 1
        self.t = [st.enter_context(alloc('%s_%d_%d' % (name, _UID[0], i), shape, dt)) for i in range(n)]
        self.b = [Buf() for _ in range(n)]
        self.i = 0

    def next(self):
        k = self.i % len(self.t)
        self.i += 1
        return self.t[k], self.b[k]


POOL_WINDOWS = (2, 4, 8, 16)


def _band_tables(L):
    out = np.zeros((128, 4, 5, 128), np.float32)
    for g, w in enumerate(POOL_WINDOWS):
        before = w // 2
        after = w - 1 - before

        def fill(kind, t_tile, s_tile):
            for tl in range(128):
                t = t_tile * 128 + tl
                lo = max(t - before, 0)
                hi = min(t + after, L - 1)
                cnt = hi - lo + 1
                for s in range(lo, hi + 1):
                    sl = s - s_tile * 128
                    if 0 <= sl < 128:
                        out[sl, g, kind, tl] += 1.0 / cnt
                if s_tile == t_tile:
                    out[tl, g, kind, tl] -= 1.0
        nt = L // 128
        fill(0, 2, 2)
        fill(1, 2, 1)
        fill(2, 2, 3)
        fill(3, 0, 0)
        fill(4, nt - 1, nt - 1)
    return out


def _consts():
    c = {}
    c['ident'] = np.eye(128, dtype=np.float32)
    inv = 10000.0 ** (-np.arange(0, 32, 2, dtype=np.float32) / 32)
    ang = np.arange(LS, dtype=np.float32)[None, :] * inv[:, None]
    cs, sn = np.cos(ang).astype(np.float32), np.sin(ang).astype(np.float32)
    c['ropeC'] = np.concatenate([cs, cs], 0)
    c['ropeS'] = np.concatenate([-sn, sn], 0)
    for nm, L in (('p', LP), ('s', LS)):
        c['band_' + nm] = _band_tables(L)
        N = 2 * L
        N2 = L // 64
        a = np.arange(64)[:, None].astype(np.float64)
        ap = np.arange(64)[None, :].astype(np.float64)
        th = np.pi * (2 * ap + 1) * a / 128.0
        c['F1'] = np.concatenate([np.cos(th), -np.sin(th)], 1).astype(np.float32)
        c['G1re_' + nm] = ((2.0 / N) * np.cos(th).T).astype(np.float32)
        c['G1im_' + nm] = ((2.0 / N) * (-np.sin(th)).T).astype(np.float32)
        b = np.arange(N2)[:, None].astype(np.float64)
        tt = np.pi * (2 * ap + 1) * b / N
        tre, tim = np.cos(tt), -np.sin(tt)
        TA = np.concatenate([tre, tre], 1)
        TB = np.concatenate([tim, tim], 1)
        c['TA_' + nm] = np.repeat(TA[:, None, :], 4, 1).astype(np.float32)
        c['TB_' + nm] = np.repeat(TB[:, None, :], 4, 1).astype(np.float32)
        tcr, tci = np.cos(tt).T, np.sin(tt).T
        c['TcA_' + nm] = np.repeat(np.concatenate([tcr, tcr], 1)[:, None, :], 4, 1).astype(np.float32)
        c['TcB_' + nm] = np.repeat(np.concatenate([tci, tci], 1)[:, None, :], 4, 1).astype(np.float32)
        bb = np.arange(N2)[None, :].astype(np.float64)
        f2 = 2 * np.pi * b * bb / N2
        f2re, f2im = np.cos(f2), -np.sin(f2)
        c['F2_' + nm] = np.stack([f2re, f2im, -f2im], 1).astype(np.float32)
        c['FA_' + nm] = np.concatenate([f2re, -f2im], 1).astype(np.float32)
        c['FB_' + nm] = np.concatenate([f2im, f2re], 1).astype(np.float32)
        tf = 2 * np.pi * b * ap / L
        fre, fim = np.cos(tf), -np.sin(tf)
        c['TfA_' + nm] = np.repeat(np.concatenate([fre, fre], 1)[:, None, :], 4, 1).astype(np.float32)
        c['TfB_' + nm] = np.repeat(np.concatenate([fim, fim], 1)[:, None, :], 4, 1).astype(np.float32)
        t = np.linspace(0.0, 1.0, L, dtype=np.float32)
        bands = np.linspace(1e-4, 15, 16, dtype=np.float32)
        angf = (np.float32(2.0 * math.pi / L) * np.arange(L, dtype=np.float32)[:, None] * bands[None, :]).astype(np.float32)
        feat = np.concatenate([t[:, None], np.cos(angf), -np.sin(angf)], -1).astype(np.float32)
        c['featT_' + nm] = np.ascontiguousarray(feat.T)
        c['trow_' + nm] = t[None, :].copy()
    ff = 2 * np.pi * a * ap / 64.0
    c['F1fA'] = np.concatenate([np.cos(ff), -np.sin(ff)], 1).astype(np.float32)
    c['F1fB'] = np.concatenate([-np.sin(ff), -np.cos(ff)], 1).astype(np.float32)
    dd = np.arange(64)[:, None] * np.arange(64)[None, :]
    c64, s64 = np.cos(2 * np.pi * dd / 64.0), np.sin(2 * np.pi * dd / 64.0)
    z = np.zeros((64, 64))
    c['DC'] = np.block([[c64, z], [z, c64]]).astype(np.float32)
    c['DS'] = np.block([[s64, z], [z, s64]]).astype(np.float32)
    deltas = np.abs(np.linspace(math.log(1e-2) / 1.5, math.log(1e-2) / 0.3, 768, dtype=np.float32))
    c['negd'] = np.ascontiguousarray((-deltas).reshape(6, 128).T).astype(np.float32)
    return c


_CONST_CACHE = {}


def _get_consts():
    if not _CONST_CACHE:
        _CONST_CACHE.update(_consts())
    return _CONST_CACHE


def build(consts, stop_after=99, dbg=None):
    nc = bass.Bass("TRN2", target_bir_lowering=False)

    def sb(name, shape, dt):
        _UID[0] += 1
        return nc.sbuf_tensor('%s_%d' % (name, _UID[0]), shape, dt)
    I = {}

    def inp(name, shape):
        I[name] = nc.dram_tensor(name, list(shape), F32, kind="ExternalInput").ap()
        return I[name]

    specs = {
        "x_p": (LP, D), "x_s": (LS, D), "norm_mix": (2, 2, D), "norm_mlp": (2, 2, D),
        "w_ff_in": (2, D, 4096), "w_ff_out": (2, 4096, D), "even_w_in": (1, D, 928),
        "pool_w": (1, 4, 128, 128), "pool_scale": (1, 512), "mla_q_norm": (1, 256),
        "mla_w_uq": (1, 256, 768), "mla_kv_norm": (1, 128), "mla_w_ukv": (1, 128, 1024),
        "even_w_out": (1, D, D), "odd_w_in": (1, D, 2560), "short_w": (1, 3, 2304),
        "short_b": (1, 2304), "filt_w1": (1, 33, 64), "filt_b1": (1, 64), "filt_w2": (1, 64, 64),
        "filt_b2": (1, 64), "filt_w3": (1, 64, 3072), "filt_freq": (1, 64), "hyena_bias": (1, 2, 768),
        "fnet_w": (1, 4, 64, 64), "odd_w_out": (1, D, D),
    }
    for k, v in specs.items():
        inp(k, v)
    for k, v in consts.items():
        inp('c_' + k, v.shape)
    y_p = nc.dram_tensor("y_p", [LP, D], F32, kind="ExternalOutput").ap()
    y_s = nc.dram_tensor("y_s", [LS, D], F32, kind="ExternalOutput").ap()

    def scratch(name, shape, dt):
        return nc.dram_tensor(name, list(shape), dt, kind="Internal").ap()

    SEQ = []
    for nm, L, xin, yout in (('p', LP, I['x_p'], y_p), ('s', LS, I['x_s'], y_s)):
        s = dict(nm=nm, L=L, x=xin, y=yout, N2=L // 64)
        s['QT'] = scratch('QT' + nm, [8, 96, L], BF16)
        s['KN'] = scratch('KN' + nm, [512, L], BF16)
        s['KR'] = scratch('KR' + nm, [32, L], BF16)
        s['V'] = scratch('V' + nm, [8, 128, L // 128, 64], BF16)
        s['mixT'] = scratch('mixT' + nm, [D, L], BF16)
        s['X1'] = scratch('X1' + nm, [L, D], F32)
        s['X2'] = scratch('X2' + nm, [L, D], F32)
        s['zcT'] = scratch('zcT' + nm, [2304, L], F32)
        s['fcT'] = scratch('fcT' + nm, [512, L], F32)
        s['hfT'] = scratch('hfT' + nm, [3072, L], F32)
        s['invd'] = scratch('invd' + nm, [128, 32], F32)
        s['mix2'] = scratch('mix2' + nm, [768, L], BF16)
        s['mfT'] = scratch('mfT' + nm, [256, L], BF16)
        s['X3'] = scratch('X3' + nm, [L, D], F32)
        for k in ('QT', 'KN', 'KR', 'V', 'mixT', 'X1', 'X2', 'zcT', 'fcT', 'hfT', 'invd', 'mix2', 'mfT', 'X3'):
            s['b_' + k] = Buf()
        SEQ.append(s)

    P = Prog(nc)
    NOB = Buf

    with contextlib.ExitStack() as top:
        def gt(name, shape, dt):
            return top.enter_context(sb(name, shape, dt))

        PS = RB(top, nc, 'ps', [128, 512], F32, 6, psum=True)
        PST = RB(top, nc, 'pst', [128, 1024], BF16, 2, psum=True)
        ident = gt('ident', [128, 128], BF16)
        onesb = gt('onesb', [128, 128], BF16)
        onesf = gt('onesf', [128, 128], F32)
        epsT = gt('epsT', [128, 2], F32)
        stg = RB(top, nc, 'stg', [128, 1024], F32, 2)
        bconst = Buf()
        P.op('pool', lambda e: e.memset(onesb[:], 1.0), writes=[bconst])
        P.op('pool', lambda e: e.memset(onesf[:], 1.0), writes=[bconst])
        P.op('pool', lambda e: e.memset(epsT[:, 0:1], EPS), writes=[bconst])
        P.op('pool', lambda e: e.memset(epsT[:, 1:2], 96.0 * EPS), writes=[bconst])
        _castc = [0]

        def cast_engine():
            _castc[0] += 1
            return ('act', 'pool')[_castc[0] % 2]

        def copy_op(eng, out, in_, reads, writes):
            if eng == 'act':
                P.op('act', lambda e: e.copy(out=out, in_=in_), reads, writes)
            else:
                P.op(eng, lambda e: e.tensor_copy(out=out, in_=in_), reads, writes)

        def loadw(dst, src, dbuf, np_=128):
            cols = dst.shape[-1]
            assert len(dst.shape) == 2 and len(src.shape) == 2, (dst.shape, src.shape)
            for c0 in range(0, cols, 1024):
                cw = min(1024, cols - c0)
                t, b = stg.next()
                P.dma('sp', t[0:np_, 0:cw], src[:, c0:c0 + cw], writes=[b])
                copy_op(cast_engine(), dst[:, c0:c0 + cw], t[0:np_, 0:cw], [b], [dbuf])

        loadw(ident[:], I['c_ident'], bconst)

        def rms_transpose(ph, xt, bx, nj, gcol, hT, bh, tmp, bg):
            ssq, bs = tmp['ssq'].next()
            junk, bj = tmp['junk'].next()
            for j in range(nj):
                P.op('act', (lambda j: lambda e: e.activation(out=junk[:], in_=xt[:, j, :], func=AF.Square,
                                                               accum_out=ssq[:, j:j + 1]))(j), [bx], [bj, bs])
            P.op('act', lambda e: e.activation(out=ssq[:, 4:4 + nj], in_=ssq[:, 0:nj], func=AF.Sqrt,
                                               bias=epsT[:, 0:1], scale=1.0 / D), [bs, bconst], [bs])
            P.op('dve', lambda e: e.reciprocal(out=ssq[:, 8:8 + nj], in_=ssq[:, 4:4 + nj]), [bs], [bs])
            xn, bn = tmp['xn'].next()
            for j in range(nj):
                P.op('dve', (lambda j: lambda e: e.tensor_scalar(out=xn[:, j, :], in0=xt[:, j, :],
                                                                 scalar1=ssq[:, 8 + j:9 + j], scalar2=None,
                                                                 op0=ALU.mult))(j), [bx, bs], [bn])
            for dc in range(8):
                pt, bp = PST.next()
                for j in range(nj):
                    P.op('pe', (lambda j, dc, pt: lambda e: e.transpose(pt[:, j * 128:(j + 1) * 128],
                                                                        xn[:, j, dc * 128:(dc + 1) * 128], ident[:]))(j, dc, pt),
                         [bn, bconst], [bp])
                P.op('act', (lambda dc, pt: lambda e: e.activation(out=hT[:, dc, :], in_=pt[:, 0:nj * 128], func=AF.Copy,
                                                                   scale=gcol[:, dc:dc + 1]))(dc, pt), [bp, bg], [bh])

        def epilogue(pss, xt, bx, j, grow, tmp, bg):
            ssq, bs = tmp['ssq2'].next()
            junk, bj = tmp['junk'].next()
            for h in range(2):
                P.op('act', (lambda h: lambda e: e.activation(out=junk[:, 0:512], in_=pss[h][0][:], func=AF.Square,
                                                               accum_out=ssq[:, h:h + 1]))(h), [pss[h][1]], [bj, bs])
            P.op('dve', lambda e: e.tensor_tensor(out=ssq[:, 2:3], in0=ssq[:, 0:1], in1=ssq[:, 1:2], op=ALU.add), [bs], [bs])
            P.op('act', lambda e: e.activation(out=ssq[:, 3:4], in_=ssq[:, 2:3], func=AF.Sqrt, bias=epsT[:, 0:1],
                                               scale=1.0 / D), [bs, bconst], [bs])
            P.op('dve', lambda e: e.reciprocal(out=ssq[:, 4:5], in_=ssq[:, 3:4]), [bs], [bs])
            for h in range(2):
                t, bt = tmp['ep'].next()
                P.op('dve', (lambda h, t: lambda e: e.scalar_tensor_tensor(out=t[:], in0=pss[h][0][:], scalar=ssq[:, 4:5],
                                                                           in1=grow[:, h * 512:(h + 1) * 512], op0=ALU.mult,
                                                                           op1=ALU.mult))(h, t), [pss[h][1], bs, bg], [bt])
                P.op('pool', (lambda h, t: lambda e: e.tensor_tensor(out=xt[:, j, h * 512:(h + 1) * 512], in0=t[:],
                                                                     in1=xt[:, j, h * 512:(h + 1) * 512], op=ALU.add))(h, t),
                     [bt, bx], [bx])

        def colvec(ph, name, src1d, ncol, bufc):
            t = ph.enter_context(sb(name, [128, ncol], F32))
            P.dma('pool', t[:], src1d.rearrange("(c p) -> p c", p=128), writes=[bufc])
            return t

        def rowbc(ph, name, src1d, n, bufc, npart=128):
            t = ph.enter_context(sb(name, [npart, n], F32))
            P.dma('pool', t[:], src1d.rearrange("(o n) -> o n", o=1).partition_broadcast(npart), writes=[bufc])
            return t

        def ph1():
            try:
                ph1_()
            except _Stop:
                pass
            P.barrier()

        def ph1_():
            with contextlib.ExitStack() as ph:
                try:
                    ph1_body(ph)
                except _Stop:
                    pass

        def ph1_body(ph):
            if True:
                T = lambda name, shape, dt: ph.enter_context(sb(name, shape, dt))
                bw = Buf()
                win = T('win', [128, 8, 928], BF16)
                winsw = T('winsw', [128, 8, 32], BF16)
                wuq = T('wuq', [128, 2, 768], BF16)
                wuqsw = T('wuqsw', [128, 2, 768], BF16)
                wuk = T('wuk', [128, 512], BF16)
                wuv = T('wuv', [128, 512], BF16)
                poolw = T('poolw', [128, 4, 128], BF16)
                ewi = I['even_w_in'][0].rearrange("(c p) f -> p c f", p=128)
                for dc in range(8):
                    loadw(win[:, dc, :], ewi[:, dc, :], bw)
                    loadw(winsw[:, dc, 0:16], ewi[:, dc, 912:928], bw)
                    loadw(winsw[:, dc, 16:32], ewi[:, dc, 896:912], bw)
                uq = I['mla_w_uq'][0].rearrange("(c p) f -> p c f", p=128)
                for rc in range(2):
                    loadw(wuq[:, rc, :], uq[:, rc, :], bw)
                    loadw(wuqsw[:, rc, :], uq[:, rc, :], bw)
                    for h in range(8):
                        loadw(wuqsw[:, rc, h * 96 + 64:h * 96 + 80], uq[:, rc, h * 96 + 80:h * 96 + 96], bw)
                        loadw(wuqsw[:, rc, h * 96 + 80:h * 96 + 96], uq[:, rc, h * 96 + 64:h * 96 + 80], bw)
                ukv = I['mla_w_ukv'][0]
                for h in range(8):
                    loadw(wuk[:, h * 64:(h + 1) * 64], ukv[:, h * 128:h * 128 + 64], bw)
                    loadw(wuv[:, h * 64:(h + 1) * 64], ukv[:, h * 128 + 64:h * 128 + 128], bw)
                for g in range(4):
                    loadw(poolw[:, g, :], I['pool_w'][0, g], bw)
                g0col = colvec(ph, 'g0col', I['norm_mix'][0, 0], 8, bw)
                pscol = colvec(ph, 'pscol', I['pool_scale'][0], 4, bw)
                qncol = colvec(ph, 'qncol', I['mla_q_norm'][0], 2, bw)
                kvcol = colvec(ph, 'kvcol', I['mla_kv_norm'][0], 1, bw)
                band = T('band', [128, 4 * 5 * 128], BF16)
                _chk('w')
                tmp = dict(ssq=RB(ph, nc, 'ssq', [128, 12], F32, 2), junk=RB(ph, nc, 'junk', [128, 1024], F32, 1),
                           xn=RB(ph, nc, 'xn', [128, 4, 1024], BF16, 1))
                XT = RB(ph, nc, 'xt', [128, 4, 1024], F32, 1)
                HT = RB(ph, nc, 'hT', [128, 8, 512], BF16, 1)
                atok = T('atok', [128, 64, 512], BF16)
                SQ = RB(ph, nc, 'sq', [128, 3, 512], BF16, 2)
                CG = RB(ph, nc, 'cg', [128, 3, 512], BF16, 2)
                RQ = RB(ph, nc, 'rq', [128, 2, 512], F32, 1)
                RT = RB(ph, nc, 'rt', [96, 4, 512], F32, 1)
                RK = RB(ph, nc, 'rk', [32, 2, 512], F32, 1)
                KT = RB(ph, nc, 'kt', [32, 2, 512], F32, 1)
                KRO = RB(ph, nc, 'kro', [32, 512], BF16, 2)
                QTs = RB(ph, nc, 'qTs', [96, 512], BF16, 3)
                QTm = RB(ph, nc, 'qTm', [96, 2, 512], F32, 1)
                KS = RB(ph, nc, 'ks', [128, 512], BF16, 2)
                VS = RB(ph, nc, 'vs', [128, 512], BF16, 2)
                RC = RB(ph, nc, 'rc', [128, 12], F32, 2)
                RTs = RB(ph, nc, 'rTs', [128, 512], BF16, 2)
                MS = RB(ph, nc, 'ms', [128, 512], BF16, 2)
                for s in SEQ:
                    L = s['L']
                    nt = L // 128
                    batok = [Buf() for _ in range(nt)]
                    bband = Buf()
                    loadw(band[:], I['c_band_' + s['nm']].rearrange("p g k t -> p (g k t)"), bband)
                    for ci in range(L // 512):
                        c0 = ci * 512
                        xt, bx = XT.next()
                        P.dma('sp', xt[:], s['x'][c0:c0 + 512, :].rearrange("(j p) d -> p j d", p=128), writes=[bx])
                        hT, bh = HT.next()
                        rms_transpose(ph, xt, bx, 4, g0col, hT, bh, tmp, bw)
                        _chk('rms')
                        for j in range(4):
                            ps, bp = PS.next()
                            for dc in range(8):
                                P.op('pe', (lambda j, dc, ps: lambda e: e.matmul(ps[:], lhsT=hT[:, dc, j * 128:(j + 1) * 128],
                                                                                 rhs=win[:, dc, 0:512], start=(dc == 0), stop=(dc == 7)))(j, dc, ps),
                                     [bh, bw], [bp])
                            copy_op('act', atok[:, ci * 4 + j, :], ps[:], [bp], [batok[ci * 4 + j]])
                        _chk('atok')
                        sq, bsq = SQ.next()
                        cg, bcg = CG.next()
                        for k3, (lo, ncol) in enumerate(((512, qncol[:, 0:1]), (640, qncol[:, 1:2]), (768, kvcol[:, 0:1]))):
                            ps, bp = PS.next()
                            for dc in range(8):
                                P.op('pe', (lambda dc, ps, lo: lambda e: e.matmul(ps[:], lhsT=win[:, dc, lo:lo + 128], rhs=hT[:, dc, :],
                                                                                  start=(dc == 0), stop=(dc == 7)))(dc, ps, lo), [bh, bw], [bp])
                            P.op('act', (lambda k3, ps: lambda e: e.activation(out=sq[:, k3, :], in_=ps[:], func=AF.Square))(k3, ps), [bp], [bsq])
                            P.op('dve', (lambda k3, ps, ncol: lambda e: e.tensor_scalar(out=cg[:, k3, :], in0=ps[:], scalar1=ncol, scalar2=None,
                                                                                        op0=ALU.mult))(k3, ps, ncol), [bp, bw, bsq], [bcg])
                        _chk('lat')
                        rk, brk = RK.next()
                        P.dma('pool', rk[:, 0, :], I['c_ropeC'][:, c0:c0 + 512], writes=[brk])
                        P.dma('pool', rk[:, 1, :], I['c_ropeS'][:, c0:c0 + 512], writes=[brk])
                        kt, bkt = KT.next()
                        for k2, (wt, lo) in enumerate(((win, 896), (winsw, 0))):
                            ps, bp = PS.next()
                            for dc in range(8):
                                P.op('pe', (lambda dc, ps, wt, lo: lambda e: e.matmul(ps[0:32, :], lhsT=wt[:, dc, lo:lo + 32], rhs=hT[:, dc, :],
                                                                                      start=(dc == 0), stop=(dc == 7)))(dc, ps, wt, lo), [bh, bw], [bp])
                            P.op('dve', (lambda k2, ps: lambda e: e.tensor_tensor(out=kt[:, k2, :], in0=ps[0:32, :], in1=rk[:, k2, :],
                                                                                  op=ALU.mult))(k2, ps), [bp, brk], [bkt])
                        kro, bkro = KRO.next()
                        P.op('pool', lambda e, kt=kt, kro=kro: e.tensor_tensor(out=kro[:], in0=kt[:, 0, :], in1=kt[:, 1, :], op=ALU.add), [bkt], [bkro])
                        P.dma('pool', s['KR'][:, c0:c0 + 512], kro[:], reads=[bkro], writes=[s['b_KR']])
                        _chk('krope')
                        rq, brq = RQ.next()
                        ps, bp = PS.next()
                        for rc in range(2):
                            P.op('pe', (lambda rc, ps: lambda e: e.matmul(ps[:], lhsT=onesb[:], rhs=sq[:, rc, :], start=(rc == 0), stop=(rc == 1)))(rc, ps),
                                 [bsq, bconst], [bp])
                        P.op('act', lambda e, ps=ps, rq=rq: e.activation(out=rq[:, 0, :], in_=ps[:], func=AF.Sqrt, bias=epsT[:, 1:2], scale=96.0 / 256.0),
                             [bp, bconst], [brq])
                        P.op('dve', lambda e, rq=rq: e.reciprocal(out=rq[:, 0, :], in_=rq[:, 0, :]), [brq], [brq])
                        ps, bp = PS.next()
                        P.op('pe', lambda e, ps=ps, sq=sq: e.matmul(ps[:], lhsT=onesb[:], rhs=sq[:, 2, :], start=True, stop=True), [bsq, bconst], [bp])
                        P.op('act', lambda e, ps=ps, rq=rq: e.activation(out=rq[:, 1, :], in_=ps[:], func=AF.Sqrt, bias=epsT[:, 0:1], scale=1.0 / 128.0),
                             [bp, bconst], [brq])
                        P.op('dve', lambda e, rq=rq: e.reciprocal(out=rq[:, 1, :], in_=rq[:, 1, :]), [brq], [brq])
                        rc_, brc = RC.next()
                        ps, bp = PS.next()
                        for j in range(4):
                            P.op('pe', (lambda j, ps: lambda e: e.matmul(ps[:, j:j + 1], lhsT=sq[:, 2, j * 128:(j + 1) * 128], rhs=onesb[:, 0:1],
                                                                         start=True, stop=True))(j, ps), [bsq, bconst], [bp])
                        P.op('act', lambda e, ps=ps, rc_=rc_: e.activation(out=rc_[:, 0:4], in_=ps[:, 0:4], func=AF.Sqrt, bias=epsT[:, 0:1], scale=1.0 / 128.0),
                             [bp, bconst], [brc])
                        P.op('dve', lambda e, rc_=rc_: e.reciprocal(out=rc_[:, 4:8], in_=rc_[:, 0:4]), [brc], [brc])
                        rt, brt = RT.next()
                        P.dma('pool', rt[64:96, 0, :], I['c_ropeC'][:, c0:c0 + 512], writes=[brt])
                        P.dma('pool', rt[64:96, 1, :], I['c_ropeS'][:, c0:c0 + 512], writes=[brt])
                        for k2 in range(2):
                            P.op('pool', (lambda k2: lambda e, rt=rt, rq=rq: e.tensor_tensor(out=rt[64:96, 2 + k2, :], in0=rt[64:96, k2, :],
                                                                                             in1=rq[64:96, 0, :], op=ALU.mult))(k2), [brt, brq], [brt])
                        _chk('rstd')
                        for h in range(8):
                            pq, bpq = PS.next()
                            pw, bpw = PS.next()
                            for rc in range(2):
                                P.op('pe', (lambda rc, pq, h: lambda e: e.matmul(pq[0:96, :], lhsT=wuq[:, rc, h * 96:(h + 1) * 96], rhs=cg[:, rc, :],
                                                                                 start=(rc == 0), stop=(rc == 1)))(rc, pq, h), [bcg, bw], [bpq])
                            for rc in range(2):
                                P.op('pe', (lambda rc, pw, h: lambda e: e.matmul(pw[0:96, :], lhsT=wuqsw[:, rc, h * 96:(h + 1) * 96], rhs=cg[:, rc, :],
                                                                                 start=(rc == 0), stop=(rc == 1)))(rc, pw, h), [bcg, bw], [bpw])
                            qs, bqs = QTs.next()
                            qm, bqm = QTm.next()
                            P.op('dve', lambda e, qs=qs, pq=pq, rq=rq: e.tensor_tensor(out=qs[0:64, :], in0=pq[0:64, :], in1=rq[0:64, 0, :], op=ALU.mult),
                                 [bpq, brq], [bqs])
                            P.op('dve', lambda e, qm=qm, pq=pq, rt=rt: e.tensor_tensor(out=qm[64:96, 0, :], in0=pq[64:96, :], in1=rt[64:96, 2, :], op=ALU.mult),
                                 [bpq, brt], [bqm])
                            P.op('dve', lambda e, qm=qm, pw=pw, rt=rt: e.tensor_tensor(out=qm[64:96, 1, :], in0=pw[64:96, :], in1=rt[64:96, 3, :], op=ALU.mult),
                                 [bpw, brt], [bqm])
                            P.op('pool', lambda e, qs=qs, qm=qm: e.tensor_tensor(out=qs[64:96, :], in0=qm[64:96, 0, :], in1=qm[64:96, 1, :], op=ALU.add),
                                 [bqm], [bqs])
                            P.dma('sp', s['QT'][h, :, c0:c0 + 512], qs[:], reads=[bqs], writes=[s['b_QT']])
                        _chk('q')
                        for hp in range(4):
                            ps, bp = PS.next()
                            P.op('pe', lambda e, ps=ps, hp=hp, cg=cg: e.matmul(ps[:], lhsT=wuk[:, hp * 128:(hp + 1) * 128], rhs=cg[:, 2, :], start=True, stop=True),
                                 [bcg, bw], [bp])
                            ks, bks = KS.next()
                            P.op('dve', lambda e, ks=ks, ps=ps, rq=rq: e.tensor_tensor(out=ks[:], in0=ps[:], in1=rq[:, 1, :], op=ALU.mult), [bp, brq], [bks])
                            P.dma('sp', s['KN'][hp * 128:(hp + 1) * 128, c0:c0 + 512], ks[:], reads=[bks], writes=[s['b_KN']])
                        _chk('k')
                        for j in range(4):
                            ps, bp = PS.next()
                            P.op('pe', lambda e, ps=ps, j=j, cg=cg: e.matmul(ps[:], lhsT=cg[:, 2, j * 128:(j + 1) * 128], rhs=wuv[:], start=True, stop=True),
                                 [bcg, bw], [bp])
                            vs, bvs = VS.next()
                            P.op('act', lambda e, vs=vs, ps=ps, rc_=rc_, j=j: e.activation(out=vs[:], in_=ps[:], func=AF.Copy, scale=rc_[:, 4 + j:5 + j]),
                                 [bp, brc], [bvs])
                            P.dma('sp', s['V'][:, :, ci * 4 + j, :].rearrange("h p d -> p h d"), vs[:].rearrange("p (h d) -> p h d", h=8), reads=[bvs], writes=[s['b_V']])
                    _chk('v')
                    for sp in range(L // 512):
                        for g in range(4):
                            ps, bp = PS.next()
                            for jt in range(4):
                                t = sp * 4 + jt
                                terms = []
                                if t > 0:
                                    terms.append((t - 1, 1))
                                terms.append((t, 3 if t == 0 else (4 if t == nt - 1 else 0)))
                                if t < nt - 1:
                                    terms.append((t + 1, 2))
                                for k, (st_, kind) in enumerate(terms):
                                    off = (g * 5 + kind) * 128
                                    P.op('pe', lambda e, ps=ps, jt=jt, st_=st_, g=g, off=off, k=k, n=len(terms): e.matmul(
                                        ps[:, jt * 128:(jt + 1) * 128], lhsT=atok[:, st_, g * 128:(g + 1) * 128], rhs=band[:, off:off + 128],
                                        start=(k == 0), stop=(k == n - 1)), [batok[st_], bband], [bp])
                            rts, brts = RTs.next()
                            copy_op('dve', rts[:], ps[:], [bp], [brts])
                            ps2, bp2 = PS.next()
                            P.op('pe', lambda e, ps2=ps2, g=g, rts=rts: e.matmul(ps2[:], lhsT=poolw[:, g, :], rhs=rts[:], start=True, stop=True), [brts, bw], [bp2])
                            ms, bms = MS.next()
                            P.op('act', lambda e, ms=ms, ps2=ps2, g=g: e.activation(out=ms[:], in_=ps2[:], func=AF.Copy, scale=pscol[:, g:g + 1]), [bp2, bw], [bms])
                            P.dma('sp', s['mixT'][g * 128:(g + 1) * 128, sp * 512:(sp + 1) * 512], ms[:], reads=[bms], writes=[s['b_mixT']])
            P.barrier()

        def ph2():
            with contextlib.ExitStack() as ph:
                KT = RB(ph, nc, 'aK', [96, LS], BF16, 2)
                QT = RB(ph, nc, 'aQ', [96, LS], BF16, 2)
                VT = RB(ph, nc, 'aV', [128, LS // 128, 128], BF16, 2)
                PT = RB(ph, nc, 'aP', [128, 512], BF16, 3)
                RS = RB(ph, nc, 'aR', [128, 512], F32, 2)
                BC = RB(ph, nc, 'aB', [64, 512], F32, 2)
                OS = RB(ph, nc, 'aO', [64, 512], BF16, 2)
                for i in range(2):
                    P.op('pool', lambda e, i=i: e.memset(VT.t[i][:, :, 64:128], 1.0), writes=[VT.b[i]])

                class _Sub:
                    def __init__(self, lo, hi):
                        self.t = PS.t[lo:hi]
                        self.b = PS.b[lo:hi]
                        self.i = 0
                    next = RB.next
                POs = _Sub(0, 2)
                PSs = _Sub(2, 6)
                for s in SEQ:
                    L = s['L']
                    nk = L // 128
                    for h in range(8):
                        kt, bk = KT.next()
                        qt, bq = QT.next()
                        vt, bv = VT.next()
                        P.dma('sp', kt[0:64, 0:L], s['KN'][h * 64:(h + 1) * 64, :], reads=[s['b_KN']], writes=[bk])
                        P.dma('sp', kt[64:96, 0:L], s['KR'][:, :], reads=[s['b_KR']], writes=[bk])
                        P.dma('sp', qt[:, 0:L], s['QT'][h], reads=[s['b_QT']], writes=[bq])
                        P.dma('pool', vt[:, 0:nk, 0:64], s['V'][h],
                              reads=[s['b_V']], writes=[bv])
                        for qc in range(L // 512):
                            po, bpo = POs.next()
                            pend = []

                            def issue_s(k, kt=kt, qt=qt, qc=qc):
                                pss, bps = PSs.next()
                                P.op('pe', lambda e: e.matmul(pss[:], lhsT=kt[:, k * 128:(k + 1) * 128], rhs=qt[:, qc * 512:(qc + 1) * 512], start=True, stop=True),
                                     [bk, bq], [bps])
                                pend.append((pss, bps))
                            for k in range(min(2, nk)):
                                issue_s(k)
                            for k in range(nk):
                                if k + 2 < nk:
                                    issue_s(k + 2)
                                pss, bps = pend.pop(0)
                                pt, bpt = PT.next()
                                P.op('act', lambda e, pt=pt, pss=pss: e.activation(out=pt[:], in_=pss[:], func=AF.Exp), [bps], [bpt])
                                P.op('pe', lambda e, po=po, vt=vt, pt=pt, k=k, nk=nk: e.matmul(po[:], lhsT=vt[:, k, :], rhs=pt[:], start=(k == 0), stop=(k == nk - 1)),
                                     [bv, bpt], [bpo])
                            rs, brs = RS.next()
                            P.op('dve', lambda e, rs=rs, po=po: e.reciprocal(out=rs[64:65, :], in_=po[64:65, :]), [bpo], [brs])
                            pb, bpb = PSs.next()
                            P.op('pe', lambda e, pb=pb, rs=rs: e.matmul(pb[0:64, :], lhsT=onesf[64:65, 0:64], rhs=rs[64:65, :], start=True, stop=True),
                                 [brs, bconst], [bpb])
                            bc, bbc = BC.next()
                            copy_op('act', bc[:], pb[0:64, :], [bpb], [bbc])
                            os_, bos = OS.next()
                            P.op('dve', lambda e, os_=os_, po=po, bc=bc: e.tensor_tensor(out=os_[:], in0=po[0:64, :], in1=bc[:], op=ALU.mult), [bpo, bbc], [bos])
                            P.dma('pool', s['mixT'][512 + h * 64:512 + (h + 1) * 64, qc * 512:(qc + 1) * 512], os_[:], reads=[bos], writes=[s['b_mixT']])
            P.barrier()

        def ph_out(layer):
            with contextlib.ExitStack() as ph:
                T = lambda name, shape, dt: ph.enter_context(sb(name, shape, dt))
                bw = Buf()
                wout = T('wout', [128, 8, 1024], BF16)
                wsrc = (I['even_w_out'] if layer == 0 else I['odd_w_out'])[0].rearrange("(c p) f -> p c f", p=128)
                for c in range(8):
                    loadw(wout[:, c, :], wsrc[:, c, :], bw)
                grow = rowbc(ph, 'grow', I['norm_mix'][layer, 1], D, bw)
                if layer == 1:
                    fwb = T('fwb', [128, 2, 128], BF16)
                    P.op('pool', lambda e: e.memset(fwb[:], 0.0), writes=[bw])
                    for g in range(4):
                        pp = (g % 2) * 64
                        loadw(fwb[pp:pp + 64, g // 2, pp:pp + 64], I['fnet_w'][0, g], bw, np_=64)
                tmp = dict(ssq2=RB(ph, nc, 'ssq2', [128, 8], F32, 2), junk=RB(ph, nc, 'junk', [128, 1024], F32, 1),
                           ep=RB(ph, nc, 'ep', [128, 512], F32, 3))
                XT = RB(ph, nc, 'xt', [128, 4, 1024], F32, 2)
                MX = RB(ph, nc, 'mx', [128, 8, 512], BF16, 2)
                MF = RB(ph, nc, 'mf', [128, 2, 512], BF16, 2)
                for s in SEQ:
                    L = s['L']
                    xin, bxin = (s['x'], Buf()) if layer == 0 else (s['X2'], s['b_X2'])
                    xout, bxout = (s['X1'], s['b_X1']) if layer == 0 else (s['X3'], s['b_X3'])
                    for ci in range(L // 512):
                        c0 = ci * 512
                        xt, bx = XT.next()
                        P.dma('sp', xt[:], xin[c0:c0 + 512, :].rearrange("(j p) d -> p j d", p=128), reads=[bxin], writes=[bx])
                        mx, bm = MX.next()
                        if layer == 0:
                            P.dma('pool', mx[:], s['mixT'][:, c0:c0 + 512].rearrange("(c p) t -> p c t", p=128), reads=[s['b_mixT']], writes=[bm])
                        else:
                            P.dma('pool', mx[:, 0:6, :], s['mix2'][:, c0:c0 + 512].rearrange("(c p) t -> p c t", p=128), reads=[s['b_mix2']], writes=[bm])
                            mf, bmf = MF.next()
                            P.dma('pool', mf[:], s['mfT'][:, c0:c0 + 512].rearrange("(c p) t -> p c t", p=128), reads=[s['b_mfT']], writes=[bmf])
                            for j2 in range(2):
                                ps, bp = PS.next()
                                P.op('pe', lambda e, ps=ps, j2=j2, mf=mf: e.matmul(ps[:], lhsT=fwb[:, j2, :], rhs=mf[:, j2, :], start=True, stop=True), [bmf, bw], [bp])
                                copy_op('act', mx[:, 6 + j2, :], ps[:], [bp], [bm])
                        for j in range(4):
                            pss = []
                            for hf in range(2):
                                ps, bp = PS.next()
                                for c in range(8):
                                    P.op('pe', lambda e, ps=ps, c=c, j=j, hf=hf, mx=mx: e.matmul(ps[:], lhsT=mx[:, c, j * 128:(j + 1) * 128],
                                                                                                 rhs=wout[:, c, hf * 512:(hf + 1) * 512], start=(c == 0), stop=(c == 7)),
                                         [bm, bw], [bp])
                                pss.append((ps, bp))
                            epilogue(pss, xt, bx, j, grow, tmp, bw)
                        P.dma('sp', xout[c0:c0 + 512, :].rearrange("(j p) d -> p j d", p=128), xt[:], reads=[bx], writes=[bxout])
            P.barrier()

        def ph_mlp(layer):
            TT = 256
            nj = TT // 128
            with contextlib.ExitStack() as ph:
                T = lambda name, shape, dt: ph.enter_context(sb(name, shape, dt))
                bw = Buf()
                wfi = T('wfi', [128, 8, 4096], BF16)
                wfo = T('wfo', [128, 32, 1024], BF16)
                s1 = I['w_ff_in'][layer].rearrange("(c p) f -> p c f", p=128)
                s2 = I['w_ff_out'][layer].rearrange("(c p) f -> p c f", p=128)
                for c in range(8):
                    loadw(wfi[:, c, :], s1[:, c, :], bw)
                for c in range(32):
                    loadw(wfo[:, c, :], s2[:, c, :], bw)
                gcol = colvec(ph, 'gcol', I['norm_mlp'][layer, 0], 8, bw)
                grow = rowbc(ph, 'grow', I['norm_mlp'][layer, 1], D, bw)
                tmp = dict(ssq=RB(ph, nc, 'ssq', [128, 12], F32, 2), ssq2=RB(ph, nc, 'ssq2', [128, 8], F32, 2),
                           junk=RB(ph, nc, 'junk', [128, 1024], F32, 1), xn=RB(ph, nc, 'xn', [128, nj, 1024], BF16, 1),
                           ep=RB(ph, nc, 'ep', [128, 512], F32, 2))
                XT = RB(ph, nc, 'xt', [128, nj, 1024], F32, 2)
                HT = RB(ph, nc, 'hT', [128, 8, TT], BF16, 1)
                F1 = RB(ph, nc, 'f1', [128, 32, TT], BF16, 1)
                RL = RB(ph, nc, 'rl', [128, TT], F32, 3)
                for s in SEQ:
                    L = s['L']
                    xin, bxin = (s['X1'], s['b_X1']) if layer == 0 else (s['X3'], s['b_X3'])
                    xout, bxout = (s['X2'], s['b_X2']) if layer == 0 else (s['y'], Buf())
                    for ci in range(L // TT):
                        c0 = ci * TT
                        xt, bx = XT.next()
                        P.dma('sp', xt[:], xin[c0:c0 + TT, :].rearrange("(j p) d -> p j d", p=128), reads=[bxin], writes=[bx])
                        hT, bh = HT.next()
                        rms_transpose(ph, xt, bx, nj, gcol, hT, bh, tmp, bw)
                        f1, bf1 = F1.next()
                        for fc in range(32):
                            ps, bp = PS.next()
                            for dc in range(8):
                                P.op('pe', lambda e, ps=ps, dc=dc, fc=fc, hT=hT: e.matmul(ps[:, 0:TT], lhsT=wfi[:, dc, fc * 128:(fc + 1) * 128], rhs=hT[:, dc, :],
                                                                                          start=(dc == 0), stop=(dc == 7)), [bh, bw], [bp])
                            rl, brl = RL.next()
                            P.op('act', lambda e, rl=rl, ps=ps: e.activation(out=rl[:], in_=ps[:, 0:TT], func=AF.Relu), [bp], [brl])
                            eng = 'pool' if fc % 2 else 'dve'
                            P.op(eng, lambda e, rl=rl, f1=f1, fc=fc: e.tensor_tensor(out=f1[:, fc, :], in0=rl[:], in1=rl[:], op=ALU.mult), [brl], [bf1])
                        for j in range(nj):
                            pss = []
                            for hf in range(2):
                                ps, bp = PS.next()
                                for fc in range(32):
                                    P.op('pe', lambda e, ps=ps, fc=fc, j=j, hf=hf, f1=f1: e.matmul(ps[:], lhsT=f1[:, fc, j * 128:(j + 1) * 128],
                                                                                                   rhs=wfo[:, fc, hf * 512:(hf + 1) * 512], start=(fc == 0), stop=(fc == 31)),
                                         [bf1, bw], [bp])
                                pss.append((ps, bp))
                            epilogue(pss, xt, bx, j, grow, tmp, bw)
                        P.dma('pool', xout[c0:c0 + TT, :].rearrange("(j p) d -> p j d", p=128), xt[:], reads=[bx], writes=[bxout])
            P.barrier()

        def ph5():
            with contextlib.ExitStack() as ph:
                T = lambda name, shape, dt: ph.enter_context(sb(name, shape, dt))
                bw = Buf()
                owin = T('owin', [128, 8, 2560], BF16)
                src = I['odd_w_in'][0].rearrange("(c p) f -> p c f", p=128)
                for c in range(8):
                    loadw(owin[:, c, :], src[:, c, :], bw)
                gcol = colvec(ph, 'gcol', I['norm_mix'][1, 0], 8, bw)
                swc = T('swc', [128, 3, 18], F32)
                for j in range(3):
                    P.dma('pool', swc[:, j, :], I['short_w'][0, j].rearrange("(c p) -> p c", p=128), writes=[bw])
                sbc = colvec(ph, 'sbc', I['short_b'][0], 18, bw)
                DCt = T('DCt', [128, 2, 128], F32)
                loadw(DCt[:, 0, :], I['c_DC'], bw)
                loadw(DCt[:, 1, :], I['c_DS'], bw)
                tmp = dict(ssq=RB(ph, nc, 'ssq', [128, 12], F32, 2), junk=RB(ph, nc, 'junk', [128, 1024], F32, 1),
                           xn=RB(ph, nc, 'xn', [128, 4, 1024], BF16, 1))
                XT = RB(ph, nc, 'xt', [128, 4, 1024], F32, 2)
                HT = RB(ph, nc, 'hT', [128, 8, 512], BF16, 2)
                WN = RB(ph, nc, 'wn', [128, 514], F32, 3)
                OT = RB(ph, nc, 'ot', [128, 512], F32, 3)
                FS = RB(ph, nc, 'fs', [128, 512], F32, 2)
                FO = RB(ph, nc, 'fo', [128, 512], F32, 3)
                carry = T('carry', [128, 18, 2], F32)
                bcar = Buf()
                for s in SEQ:
                    L = s['L']
                    fsc = 1.0 / math.sqrt(64.0 * L)
                    P.op('pool', lambda e: e.memset(carry[:], 0.0), writes=[bcar])
                    nch = L // 512
                    for ci in range(nch + 1):
                        c0 = ci * 512
                        last = (ci == nch)
                        if not last:
                            xt, bx = XT.next()
                            P.dma('sp', xt[:], s['X2'][c0:c0 + 512, :].rearrange("(j p) d -> p j d", p=128), reads=[s['b_X2']], writes=[bx])
                            hT, bh = HT.next()
                            rms_transpose(ph, xt, bx, 4, gcol, hT, bh, tmp, bw)
                        for fc in range(18):
                            wn, bwn = WN.next()
                            P.op('pool', lambda e, wn=wn, fc=fc: e.tensor_copy(out=wn[:, 0:2], in_=carry[:, fc, :]), [bcar], [bwn])
                            if not last:
                                ps, bp = PS.next()
                                for dc in range(8):
                                    P.op('pe', lambda e, ps=ps, dc=dc, fc=fc, hT=hT: e.matmul(ps[:], lhsT=owin[:, dc, fc * 128:(fc + 1) * 128], rhs=hT[:, dc, :],
                                                                                              start=(dc == 0), stop=(dc == 7)), [bh, bw], [bp])
                                copy_op('act', wn[:, 2:514], ps[:], [bp], [bwn])
                                nw = 512
                            else:
                                P.op('pool', lambda e, wn=wn: e.memset(wn[:, 2:3], 0.0), [], [bwn])
                                nw = 1
                            ot, bo = OT.next()
                            P.op('dve', lambda e, ot=ot, wn=wn, fc=fc, nw=nw: e.tensor_scalar(out=ot[:, 0:nw], in0=wn[:, 0:nw], scalar1=swc[:, 0, fc:fc + 1],
                                                                                              scalar2=sbc[:, fc:fc + 1], op0=ALU.mult, op1=ALU.add), [bwn, bw], [bo])
                            for j in (1, 2):
                                P.op('dve', lambda e, ot=ot, wn=wn, fc=fc, nw=nw, j=j: e.scalar_tensor_tensor(out=ot[:, 0:nw], in0=wn[:, j:j + nw], scalar=swc[:, j, fc:fc + 1],
                                                                                                              in1=ot[:, 0:nw], op0=ALU.mult, op1=ALU.add), [bwn, bw, bo], [bo])
                            if not last:
                                P.op('pool', lambda e, wn=wn, fc=fc: e.tensor_copy(out=carry[:, fc, :], in_=wn[:, 512:514]), [bwn], [bcar])
                            if ci == 0:
                                P.dma('sp', s['zcT'][fc * 128:(fc + 1) * 128, 0:511], ot[:, 1:512], reads=[bo], writes=[s['b_zcT']])
                            elif not last:
                                P.dma('sp', s['zcT'][fc * 128:(fc + 1) * 128, c0 - 1:c0 + 511], ot[:, 0:512], reads=[bo], writes=[s['b_zcT']])
                            else:
                                P.dma('sp', s['zcT'][fc * 128:(fc + 1) * 128, L - 1:L], ot[:, 0:1], reads=[bo], writes=[s['b_zcT']])
                        if last:
                            continue
                        for f2 in range(2):
                            ps, bp = PS.next()
                            for dc in range(8):
                                P.op('pe', lambda e, ps=ps, dc=dc, f2=f2, hT=hT: e.matmul(ps[:], lhsT=owin[:, dc, 2304 + f2 * 128:2304 + (f2 + 1) * 128], rhs=hT[:, dc, :],
                                                                                          start=(dc == 0), stop=(dc == 7)), [bh, bw], [bp])
                            fs, bfs = FS.next()
                            copy_op('act', fs[:], ps[:], [bp], [bfs])
                            for k2 in range(2):
                                ps2, bp2 = PS.next()
                                P.op('pe', lambda e, ps2=ps2, k2=k2, fs=fs: e.matmul(ps2[:], lhsT=DCt[:, k2, :], rhs=fs[:], start=True, stop=True), [bfs, bw], [bp2])
                                fo, bfo = FO.next()
                                P.op('act', lambda e, fo=fo, ps2=ps2: e.activation(out=fo[:], in_=ps2[:], func=AF.Copy, scale=fsc), [bp2], [bfo])
                                r0 = k2 * 256 + f2 * 128
                                P.dma('pool', s['fcT'][r0:r0 + 128, c0:c0 + 512], fo[:], reads=[bfo], writes=[s['b_fcT']])
            P.barrier()

        def ph6():
            PI = math.pi
            with contextlib.ExitStack() as ph:
                T = lambda name, shape, dt: ph.enter_context(sb(name, shape, dt))
                bw = Buf()
                w1 = T('fw1', [33, 64], F32)
                w2 = T('fw2', [64, 64], F32)
                w3 = T('fw3', [64, 3072], F32)
                P.dma('pool', w1[:], I['filt_w1'][0], writes=[bw])
                P.dma('pool', w2[:], I['filt_w2'][0], writes=[bw])
                P.dma('pool', w3[:], I['filt_w3'][0], writes=[bw])
                cols = T('fcols', [64, 8], F32)
                for k, nm_ in enumerate(('filt_b1', 'filt_b2', 'filt_freq')):
                    P.dma('pool', cols[:, k:k + 1], I[nm_][0].rearrange("(p o) -> p o", o=1), writes=[bw])
                P.op('dve', lambda e: e.tensor_tensor(out=cols[:, 3:4], in0=cols[:, 2:3], in1=cols[:, 0:1], op=ALU.mult), [bw], [bw])
                P.op('dve', lambda e: e.tensor_tensor(out=cols[:, 4:5], in0=cols[:, 2:3], in1=cols[:, 1:2], op=ALU.mult), [bw], [bw])
                negd = T('negd', [128, 6], F32)
                P.dma('pool', negd[:], I['c_negd'], writes=[bw])
                FT = RB(ph, nc, 'ft', [33, 512], F32, 2)
                TB = RB(ph, nc, 'tb', [128, 512], F32, 2)
                DT = RB(ph, nc, 'dt', [128, 6, 512], F32, 2)
                AR = RB(ph, nc, 'ar', [64, 3, 512], F32, 2)
                H1 = RB(ph, nc, 'h1', [64, 512], F32, 2)
                H2 = RB(ph, nc, 'h2', [64, 512], F32, 2)
                HF = RB(ph, nc, 'hf', [128, 512], F32, 3)
                junk = T('fjunk', [128, 512], F32)
                bjunk = Buf()
                ssq = T('fssq', [128, 24, 16], F32)
                tot = T('ftot', [128, 3, 24], F32)
                bss = Buf()

                def sinlayer(ps, bp, fb, out, bo):
                    ar, ba = AR.next()
                    P.op('dve', lambda e: e.tensor_scalar(out=ar[:, 0, :], in0=ps[0:64, :], scalar1=cols[:, 2:3], scalar2=cols[:, fb:fb + 1],
                                                          op0=ALU.mult, op1=ALU.add), [bp, bw], [ba])
                    P.op('dve', lambda e: e.tensor_scalar(out=ar[:, 1, :], in0=ar[:, 0, :], scalar1=PI, scalar2=2 * PI, op0=ALU.is_gt, op1=ALU.mult), [ba], [ba])
                    P.op('dve', lambda e: e.tensor_tensor(out=ar[:, 0, :], in0=ar[:, 0, :], in1=ar[:, 1, :], op=ALU.subtract), [ba], [ba])
                    P.op('dve', lambda e: e.tensor_scalar(out=ar[:, 1, :], in0=ar[:, 0, :], scalar1=-PI, scalar2=2 * PI, op0=ALU.is_lt, op1=ALU.mult), [ba], [ba])
                    P.op('dve', lambda e: e.tensor_tensor(out=ar[:, 0, :], in0=ar[:, 0, :], in1=ar[:, 1, :], op=ALU.add), [ba], [ba])
                    P.op('act', lambda e: e.activation(out=out[:], in_=ar[:, 0, :], func=AF.Sin, scale=0.999999), [ba], [bo])

                for s in SEQ:
                    L = s['L']
                    nch = L // 512
                    for ci in range(nch):
                        c0 = ci * 512
                        ft, bft = FT.next()
                        P.dma('sp', ft[:], I['c_featT_' + s['nm']][:, c0:c0 + 512], writes=[bft])
                        tb, btb = TB.next()
                        P.dma('sp', tb[:], I['c_trow_' + s['nm']][:, c0:c0 + 512].partition_broadcast(128), writes=[btb])
                        dt_, bdt = DT.next()
                        for k in range(6):
                            P.op('act', lambda e, k=k, dt_=dt_, tb=tb: e.activation(out=dt_[:, k, :], in_=tb[:], func=AF.Exp, scale=negd[:, k:k + 1]), [btb, bw], [bdt])
                        ps, bp = PS.next()
                        P.op('pe', lambda e, ps=ps, ft=ft: e.matmul(ps[0:64, :], lhsT=w1[:], rhs=ft[:], start=True, stop=True), [bft, bw], [bp])
                        h1, bh1 = H1.next()
                        sinlayer(ps, bp, 3, h1, bh1)
                        ps, bp = PS.next()
                        P.op('pe', lambda e, ps=ps, h1=h1: e.matmul(ps[0:64, :], lhsT=w2[:], rhs=h1[:], start=True, stop=True), [bh1, bw], [bp])
                        h2, bh2 = H2.next()
                        sinlayer(ps, bp, 4, h2, bh2)
                        for jc in range(24):
                            ps, bp = PS.next()
                            P.op('pe', lambda e, ps=ps, jc=jc, h2=h2: e.matmul(ps[:], lhsT=w3[:, jc * 128:(jc + 1) * 128], rhs=h2[:], start=True, stop=True), [bh2, bw], [bp])
                            hf, bhf = HF.next()
                            P.op('dve', lambda e, hf=hf, ps=ps, dt_=dt_, jc=jc: e.tensor_tensor(out=hf[:], in0=ps[:], in1=dt_[:, jc % 6, :], op=ALU.mult), [bp, bdt], [bhf])
                            P.op('act', lambda e, hf=hf, jc=jc, ci=ci: e.activation(out=junk[:], in_=hf[:], func=AF.Square, accum_out=ssq[:, jc, ci:ci + 1]), [bhf], [bjunk, bss])
                            P.dma('sp', s['hfT'][jc * 128:(jc + 1) * 128, c0:c0 + 512], hf[:], reads=[bhf], writes=[s['b_hfT']])
                    P.op('dve', lambda e, nch=nch: e.tensor_reduce(out=tot[:, 0, :], in_=ssq[:, :, 0:nch], axis=AX.X, op=ALU.add), [bss], [bss])
                    P.op('act', lambda e: e.activation(out=tot[:, 1, :], in_=tot[:, 0, :], func=AF.Sqrt, bias=epsT[:, 0:1], scale=1.0), [bss, bconst], [bss])
                    P.op('dve', lambda e: e.reciprocal(out=tot[:, 2, :], in_=tot[:, 1, :]), [bss], [bss])
                    P.dma('sp', s['invd'][:, 0:24], tot[:, 2, :], reads=[bss], writes=[s['b_invd']])
            P.barrier()

        def ph7():
            CB = 4
            with contextlib.ExitStack() as ph:
                T = lambda name, shape, dt: ph.enter_context(sb(name, shape, dt))
                bw = Buf()
                F1 = T('cF1', [64, 128], BF16)
                loadw(F1[:], I['c_F1'], bw, np_=64)
                F1fA = T('cF1fA', [64, 128], BF16)
                F1fB = T('cF1fB', [64, 128], BF16)
                loadw(F1fA[:], I['c_F1fA'], bw, np_=64)
                loadw(F1fB[:], I['c_F1fB'], bw, np_=64)
                hbb = rowbc(ph, 'hbb', I['hyena_bias'][0].rearrange("o c -> (o c)"), 1536, bw, npart=64)
                for s in SEQ:
                    L, N2, nm = s['L'], s['N2'], s['nm']
                    G = 3 if nm == 's' else 6
                    with contextlib.ExitStack() as p2:
                        T2 = lambda name, shape, dt: p2.enter_context(sb(name + nm, shape, dt))
                        bc_ = Buf()
                        G1re = T2('G1re', [64, 64], BF16)
                        G1im = T2('G1im', [64, 64], BF16)
                        loadw(G1re[:], I['c_G1re_' + nm], bc_, np_=64)
                        loadw(G1im[:], I['c_G1im_' + nm], bc_, np_=64)
                        TA = T2('TA', [N2, 4, 128], F32)
                        TBt = T2('TB', [N2, 4, 128], F32)
                        TfA = T2('TfA', [N2, 4, 128], F32)
                        TfB = T2('TfB', [N2, 4, 128], F32)
                        for t_, k_ in ((TA, 'TA_'), (TBt, 'TB_'), (TfA, 'TfA_'), (TfB, 'TfB_')):
                            P.dma('pool', t_[:], I['c_' + k_ + nm], writes=[bc_])
                        TcA = T2('TcA', [64, 4, 2 * N2], F32)
                        TcB = T2('TcB', [64, 4, 2 * N2], F32)
                        P.dma('pool', TcA[:], I['c_TcA_' + nm], writes=[bc_])
                        P.dma('pool', TcB[:], I['c_TcB_' + nm], writes=[bc_])
                        F2 = T2('F2', [N2, 3, N2], BF16)
                        FA = T2('FA', [N2, 2 * N2], BF16)
                        FB = T2('FB', [N2, 2 * N2], BF16)
                        loadw(F2[:].rearrange("p a b -> p (a b)"), I['c_F2_' + nm].rearrange("p a b -> p (a b)"), bc_, np_=N2)
                        loadw(FA[:], I['c_FA_' + nm], bc_, np_=N2)
                        loadw(FB[:], I['c_FB_' + nm], bc_, np_=N2)
                        XI = RB(p2, nc, 'xi' + nm, [64, 3, CB, N2], F32, G)
                        HI = RB(p2, nc, 'hi' + nm, [64, 4, CB, N2], F32, 2)
                        IV = RB(p2, nc, 'iv' + nm, [64, 4, CB], F32, 3)
                        HB = RB(p2, nc, 'hb' + nm, [64, 4, CB, N2], BF16, G)
                        UB = RB(p2, nc, 'ub' + nm, [64, CB, N2], BF16, 2 * G)
                        PP = RB(p2, nc, 'pp' + nm, [N2, 2, 4, 128], F32, 3)
                        YT = RB(p2, nc, 'yt' + nm, [N2, 4, 128], BF16, 2 * G + 2)
                        YF = RB(p2, nc, 'yf' + nm, [N2, 4, 128], F32, 2 * G + 2)
                        KK = RB(p2, nc, 'kk' + nm, [N2, 2, 2, CB, 128], F32, G)
                        GG = RB(p2, nc, 'gg' + nm, [N2, 4, 128], BF16, G)
                        VP = RB(p2, nc, 'vp' + nm, [64, 2, 4, 2 * N2], F32, 2)
                        VT_ = RB(p2, nc, 'vt' + nm, [64, 4, 2, N2], BF16, G)
                        UT = RB(p2, nc, 'ut' + nm, [64, CB, N2], F32, 3 * G)
                        MO = RB(p2, nc, 'mo' + nm, [N2, CB, 64], BF16, 3)

                        def tview(ps, n):
                            return ps[0:n, :].rearrange("p (t c) -> p t c", t=4)

                        def twiddle(ps, bp, A, B, n, w, outre, outim, bo, rd):
                            pp, bpp = PP.next()
                            src = ps
                            P.op('dve', lambda e: e.tensor_tensor(out=pp[0:n, 0], in0=src, in1=A, op=ALU.mult), [bp] + rd, [bpp])
                            P.op('dve', lambda e: e.tensor_tensor(out=pp[0:n, 1], in0=src, in1=B, op=ALU.mult), [bp] + rd, [bpp])
                            P.op('pool', lambda e: e.tensor_tensor(out=outre, in0=pp[0:n, 0, :, 0:w], in1=pp[0:n, 1, :, w:2 * w], op=ALU.subtract), [bpp], [bo])
                            P.op('pool', lambda e: e.tensor_tensor(out=outim, in0=pp[0:n, 1, :, 0:w], in1=pp[0:n, 0, :, w:2 * w], op=ALU.add), [bpp], [bo])

                        def tform_fwd(tiles, rd, A, B, out, bo):
                            ps, bp = PS.next()
                            pv = tview(ps, N2)
                            for t, tl in enumerate(tiles):
                                P.op('pe', lambda e, t=t, tl=tl: e.matmul(pv[:, t, :], lhsT=tl, rhs=F1[:], start=True, stop=True), rd + [bw], [bp])
                            twiddle(pv, bp, A, B, N2, 64, out[:, :, 0:64], out[:, :, 64:128], bo, [bc_])

                        def nform_fwd(yre_src, yim_src, rd, imsrc_re=None, imsrc_im=None):
                            ps, bp = PS.next()
                            pv = tview(ps, N2)
                            ire = yre_src if imsrc_re is None else imsrc_re
                            iim = yim_src if imsrc_im is None else imsrc_im
                            P.op('pe', lambda e: e.matmul(pv[:, :, 0:64], lhsT=F2[:, 0, :], rhs=yre_src, start=True, stop=False), rd + [bc_], [bp])
                            P.op('pe', lambda e: e.matmul(pv[:, :, 0:64], lhsT=F2[:, 2, :], rhs=yim_src, start=False, stop=True), rd + [bc_], [bp])
                            P.op('pe', lambda e: e.matmul(pv[:, :, 64:128], lhsT=F2[:, 0, :], rhs=iim, start=True, stop=False), rd + [bc_], [bp])
                            P.op('pe', lambda e: e.matmul(pv[:, :, 64:128], lhsT=F2[:, 1, :], rhs=ire, start=False, stop=True), rd + [bc_], [bp])
                            return pv, bp

                        def conv(ub, bub, kk, bkk, o, res):
                            yt, byt = YT.next()
                            tform_fwd([ub[:, t, :] for t in range(CB)], [bub], TA[:], TBt[:], yt, byt)
                            yield
                            pv, bp = nform_fwd(yt[:, :, 0:64], yt[:, :, 64:128], [byt])
                            gg, bgg = GG.next()
                            twiddle(pv, bp, kk[:, o, 0], kk[:, o, 1], N2, 64, gg[:, :, 0:64], gg[:, :, 64:128], bgg, [bkk])
                            yield
                            vt, bvt = VT_.next()
                            per = min(512 // (2 * N2), CB)
                            for g0 in range(0, CB, per):
                                ps, bp = PS.next()
                                pv2 = ps[0:64, 0:per * 2 * N2].rearrange("p (t c) -> p t c", t=per)
                                for t in range(per):
                                    P.op('pe', lambda e, t=t, g0=g0: e.matmul(pv2[:, t, :], lhsT=gg[:, g0 + t, 0:64], rhs=FA[:], start=True, stop=False), [bgg, bc_], [bp])
                                    P.op('pe', lambda e, t=t, g0=g0: e.matmul(pv2[:, t, :], lhsT=gg[:, g0 + t, 64:128], rhs=FB[:], start=False, stop=True), [bgg, bc_], [bp])
                                vp, bvp = VP.next()
                                P.op('dve', lambda e, pv2=pv2, vp=vp: e.tensor_tensor(out=vp[:, 0, 0:per, :], in0=pv2, in1=TcA[:, 0:per, :], op=ALU.mult), [bp, bc_], [bvp])
                                P.op('dve', lambda e, pv2=pv2, vp=vp: e.tensor_tensor(out=vp[:, 1, 0:per, :], in0=pv2, in1=TcB[:, 0:per, :], op=ALU.mult), [bp, bc_], [bvp])
                                P.op('pool', lambda e, g0=g0, vp=vp: e.tensor_tensor(out=vt[:, g0:g0 + per, 0, :], in0=vp[:, 0, 0:per, 0:N2], in1=vp[:, 1, 0:per, N2:2 * N2], op=ALU.subtract), [bvp], [bvt])
                                P.op('pool', lambda e, g0=g0, vp=vp: e.tensor_tensor(out=vt[:, g0:g0 + per, 1, :], in0=vp[:, 1, 0:per, 0:N2], in1=vp[:, 0, 0:per, N2:2 * N2], op=ALU.add), [bvp], [bvt])
                                yield
                            ps, bp = PS.next()
                            pvo = ps[0:64, 0:CB * N2].rearrange("p (t c) -> p t c", t=CB)
                            P.op('pe', lambda e: e.matmul(pvo, lhsT=G1re[:], rhs=vt[:, :, 0, :], start=True, stop=False), [bvt, bc_], [bp])
                            P.op('pe', lambda e: e.matmul(pvo, lhsT=G1im[:], rhs=vt[:, :, 1, :], start=False, stop=True), [bvt, bc_], [bp])
                            res['pvo'] = (pvo, bp)

                        def zc_view(r0):
                            return s['zcT'][r0:r0 + CB, :].rearrange("c (a b) -> a c b", b=N2)

                        def hyena_gen(cb):
                            ch0 = cb * CB
                            xi, bxi = XI.next()
                            for k in range(3):
                                P.dma('sp', xi[:, k], zc_view(k * 768 + ch0), reads=[s['b_zcT']], writes=[bxi])
                            hi, bhi = HI.next()
                            for od in range(4):
                                P.dma('pool', hi[:, od], s['hfT'][od * 768 + ch0:od * 768 + ch0 + CB, :].rearrange("c (a b) -> a c b", b=N2),
                                      reads=[s['b_hfT']], writes=[bhi])
                            iv, biv = IV.next()
                            for od in range(4):
                                col = od * 768 + ch0
                                P.dma('pool', iv[:, od, :], s['invd'][col % 128:col % 128 + CB, col // 128:col // 128 + 1].rearrange("c o -> o c").partition_broadcast(64),
                                      reads=[s['b_invd']], writes=[biv])
                            hb, bhb = HB.next()
                            P.op('dve', lambda e: e.tensor_tensor(out=hb[:], in0=hi[:], in1=iv[:].unsqueeze(3).to_broadcast([64, 4, CB, N2]), op=ALU.mult),
                                 [bhi, biv], [bhb])
                            for od in (1, 3):
                                P.op('pool', lambda e, od=od: e.memset(hb[0:1, od, :, 0:1], 0.0), [], [bhb])
                            yield
                            kk, bkk = KK.next()
                            for o in range(2):
                                yf, byf = YF.next()
                                yb, byb = YF.next()
                                tform_fwd([hb[:, 2 * o, t, :] for t in range(CB)], [bhb], TA[:], TBt[:], yf, byf)
                                yield
                                tform_fwd([hb[:, 2 * o + 1, t, :] for t in range(CB)], [bhb], TA[:], TBt[:], yb, byb)
                                ysum, bys = YT.next()
                                ydif, byd = YT.next()
                                P.op('pool', lambda e: e.tensor_tensor(out=ysum[:], in0=yf[:], in1=yb[:], op=ALU.add), [byf, byb], [bys])
                                P.op('pool', lambda e: e.tensor_tensor(out=ydif[:], in0=yf[:], in1=yb[:], op=ALU.subtract), [byf, byb], [byd])
                                yield
                                pv, bp = nform_fwd(ysum[:, :, 0:64], ysum[:, :, 64:128], [bys, byd], imsrc_re=ydif[:, :, 0:64], imsrc_im=ydif[:, :, 64:128])
                                for half in range(2):
                                    copy_op('act', kk[:, o, 0, :, half * 64:(half + 1) * 64], pv[:, :, 0:64], [bp], [bkk])
                                    copy_op('act', kk[:, o, 1, :, half * 64:(half + 1) * 64], pv[:, :, 64:128], [bp], [bkk])
                                yield
                            ub, bub = UB.next()
                            copy_op('act', ub[:], xi[:, 0], [bxi], [bub])
                            ucur, bucur = xi[:, 0], bxi
                            for o in range(2):
                                res = {}
                                yield from conv(ub, bub, kk, bkk, o, res)
                                pvo, bp = res['pvo']
                                ut, but = UT.next()
                                bia = hbb[:, o * 768 + ch0:o * 768 + ch0 + CB].unsqueeze(2).to_broadcast([64, CB, N2])
                                P.op('pool', lambda e, ut=ut, ucur=ucur, bia=bia: e.tensor_tensor(out=ut[:], in0=ucur, in1=bia, op=ALU.mult), [bucur, bw], [but])
                                P.op('dve', lambda e, ut=ut, pvo=pvo: e.tensor_tensor(out=ut[:], in0=pvo, in1=ut[:], op=ALU.add), [bp, but], [but])
                                ut2, but2 = UT.next()
                                P.op('pool', lambda e, ut=ut, ut2=ut2, o=o: e.tensor_tensor(out=ut2[:], in0=ut[:], in1=xi[:, 1 + o], op=ALU.mult), [but, bxi], [but2])
                                ub, bub = UB.next()
                                copy_op('act', ub[:], ut2[:], [but2], [bub])
                                ucur, bucur = ut2[:], but2
                                yield
                            P.dma('sp', s['mix2'][ch0:ch0 + CB, :].rearrange("c (a b) -> a c b", b=N2), ub[:], reads=[bub], writes=[s['b_mix2']])

                        def fnet_gen(cb):
                            q0 = cb * CB
                            xi, bxi = XI.next()
                            P.dma('sp', xi[:, 0], s['fcT'][q0:q0 + CB, :].rearrange("c (a b) -> a c b", b=N2), reads=[s['b_fcT']], writes=[bxi])
                            P.dma('sp', xi[:, 1], s['fcT'][256 + q0:256 + q0 + CB, :].rearrange("c (a b) -> a c b", b=N2), reads=[s['b_fcT']], writes=[bxi])
                            xc, bxc = UB.next()
                            xs_, bxs = UB.next()
                            copy_op('act', xc[:], xi[:, 0], [bxi], [bxc])
                            copy_op('pool', xs_[:], xi[:, 1], [bxi], [bxs])
                            yield
                            ps, bp = PS.next()
                            pv = tview(ps, N2)
                            for t in range(CB):
                                P.op('pe', lambda e, t=t: e.matmul(pv[:, t, :], lhsT=xc[:, t, :], rhs=F1fA[:], start=True, stop=False), [bxc, bw], [bp])
                                P.op('pe', lambda e, t=t: e.matmul(pv[:, t, :], lhsT=xs_[:, t, :], rhs=F1fB[:], start=False, stop=True), [bxs, bw], [bp])
                            yt, byt = YT.next()
                            twiddle(pv, bp, TfA[:], TfB[:], N2, 64, yt[:, :, 0:64], yt[:, :, 64:128], byt, [bc_])
                            yield
                            ps, bp = PS.next()
                            pz = ps[0:N2, 0:CB * 64].rearrange("p (t c) -> p t c", t=CB)
                            P.op('pe', lambda e: e.matmul(pz, lhsT=F2[:, 0, :], rhs=yt[:, :, 0:64], start=True, stop=False), [byt, bc_], [bp])
                            P.op('pe', lambda e: e.matmul(pz, lhsT=F2[:, 2, :], rhs=yt[:, :, 64:128], start=False, stop=True), [byt, bc_], [bp])
                            mo, bmo = MO.next()
                            copy_op('act', mo[:], pz, [bp], [bmo])
                            P.dma('sp', s['mfT'][q0:q0 + CB, :].rearrange("c (b a) -> b c a", a=64), mo[:], reads=[bmo], writes=[s['b_mfT']])

                        def lockstep(gens, g):
                            it = iter(gens)
                            active = []
                            done = False
                            while True:
                                while not done and len(active) < g:
                                    nx = next(it, None)
                                    if nx is None:
                                        done = True
                                    else:
                                        active.append(nx)
                                if not active:
                                    break
                                for gg_ in list(active):
                                    try:
                                        next(gg_)
                                    except StopIteration:
                                        active.remove(gg_)

                        lockstep((hyena_gen(cb) for cb in range(768 // CB)), G)
                        lockstep((fnet_gen(cb) for cb in range(256 // CB)), G)
                    P.barrier()

        phases = [ph1, ph2, lambda: ph_out(0), lambda: ph_mlp(0), ph5, ph6, ph7, lambda: ph_out(1), lambda: ph_mlp(1)]
        for i, f in enumerate(phases):
            if i >= stop_after:
                break
            f()
        if dbg is not None:
            dbg(nc, P, SEQ, I)
        P.emit()
    return nc


_NC_CACHE = {}


def _prep_inputs(inputs):
    consts = _get_consts()
    base = {}
    for k, v in inputs.items():
        if k in ('x_prompt', 'x_sample'):
            continue
        base[k] = np.ascontiguousarray(np.asarray(v, dtype=np.float32))
    for k, v in consts.items():
        base['c_' + k] = np.ascontiguousarray(v)
    xp = np.asarray(inputs['x_prompt'], dtype=np.float32)
    xs = np.asarray(inputs['x_sample'], dtype=np.float32)
    in_maps = []
    for i in range(8):
        m = dict(base)
        m['x_p'] = np.ascontiguousarray(xp[i])
        m['x_s'] = np.ascontiguousarray(xs[i])
        in_maps.append(m)
    return consts, in_maps


def kernel(**inputs):
    consts, in_maps = _prep_inputs(inputs)
    if 'nc' not in _NC_CACHE:
        _NC_CACHE['nc'] = build(consts)
    nc = _NC_CACHE['nc']
    res = run_bass_kernel_spmd(nc, in_maps, core_ids=list(range(8)))
    yp = np.stack([np.asarray(r['y_p'], dtype=np.float32) for r in res.results], 0)
    ys = np.stack([np.asarray(r['y_s'], dtype=np.float32) for r in res.results], 0)
    return (yp, ys)
```

```python
import contextlib
import math
import os
import numpy as np
import concourse.bass as bass
import concourse.mybir as mybir
from concourse.bass_utils import run_bass_kernel_spmd

F32 = mybir.dt.float32
BF16 = mybir.dt.bfloat16
AF = mybir.ActivationFunctionType
ALU = mybir.AluOpType
AX = mybir.AxisListType

D = 1024
EPS = 1e-6
LP, LS = 2048, 8192
ENGS = ('pe', 'act', 'dve', 'pool', 'sp')
NRING = 8


class Buf:
    __slots__ = ('w', 'r')

    def __init__(self):
        self.w = None
        self.r = []


class Op:
    __slots__ = ('eng', 'fn', 'deps', 'signal', 'sigval', 'is_dma', 'dsem', 'dval', 'ringdep')

    def __init__(self, eng, fn, is_dma):
        self.eng = eng
        self.fn = fn
        self.deps = []
        self.signal = False
        self.sigval = 0
        self.is_dma = is_dma
        self.dsem = None
        self.dval = 0
        self.ringdep = None


class _Rec:
    def __getattr__(self, name):
        def f(*a, **k):
            self.__dict__['call'] = (name, a, k)
            return None
        return f


class Prog:
    def __init__(self, nc):
        self.nc = nc
        self.ops = {e: [] for e in ENGS}
        self.ndma = {e: 0 for e in ENGS}
        self.ringops = {e: [None] * NRING for e in ENGS}

    def _rec(self, eng, fn, reads, writes, is_dma, nowaw=False):
        r = _Rec()
        fn(r)
        op = Op(eng, r.call, is_dma)
        deps = []
        for t in reads:
            if t.w is not None:
                deps.append(t.w)
        for t in writes:
            if t.w is not None and not (nowaw and t.w.eng == eng and not t.w.is_dma):
                deps.append(t.w)
            deps.extend(t.r)
        for t in reads:
            t.r.append(op)
        for t in writes:
            t.w = op
            t.r = []
        seen = set()
        for d in deps:
            if id(d) in seen or d is op:
                continue
            seen.add(id(d))
            if (not d.is_dma) and d.eng == eng and eng == 'pe':
                continue
            op.deps.append(d)
            if not d.is_dma:
                d.signal = True
        if is_dma:
            j = self.ndma[eng]
            self.ndma[eng] = j + 1
            op.dsem = (eng, j % NRING)
            op.dval = 16 * (j // NRING + 1)
            op.ringdep = self.ringops[eng][j % NRING]
            self.ringops[eng][j % NRING] = op
        self.ops[eng].append(op)
        return op

    def op(self, eng, fn, reads=(), writes=(), nowaw=False):
        return self._rec(eng, fn, reads, writes, False, nowaw)

    def dma(self, eng, out, in_, reads=(), writes=()):
        return self._rec(eng, lambda e: e.dma_start(out=out, in_=in_), reads, writes, True)

    def barrier(self):
        lasts = []
        for e in ENGS:
            for o in reversed(self.ops[e]):
                if o.fn is not None and not o.is_dma:
                    lasts.append(o)
                    break
            for d in self.ringops[e]:
                if d is not None:
                    lasts.append(d)
        for e in ENGS:
            op = Op(e, None, False)
            for d in lasts:
                if d.is_dma or d.eng != e:
                    op.deps.append(d)
                    if not d.is_dma:
                        d.signal = True
            self.ops[e].append(op)

    def emit(self):
        nc = self.nc
        with contextlib.ExitStack() as st:
            csem = {e: st.enter_context(nc.semaphore('c_' + e)) for e in ENGS}
            dsem = {}
            for e in ENGS:
                for i in range(min(NRING, self.ndma[e])):
                    dsem[(e, i)] = st.enter_context(nc.semaphore('d_%s%d' % (e, i)))
            for e in ENGS:
                c = 0
                for o in self.ops[e]:
                    if o.signal:
                        c += 1
                        o.sigval = c
            st.enter_context(nc.allow_non_contiguous_dma(reason='small strided tables / layout loads'))
            block = st.enter_context(nc.Block())
            handles = {'pe': 'tensor', 'act': 'scalar', 'dve': 'vector', 'pool': 'gpsimd', 'sp': 'sync'}

            def make(e):
                def body(eng):
                    waited = {}
                    for o in self.ops[e]:
                        ws = []
                        for d in o.deps:
                            if d.is_dma:
                                ws.append((('d',) + d.dsem, dsem[d.dsem], d.dval))
                            else:
                                ws.append((('c', d.eng), csem[d.eng], d.sigval))
                        if o.ringdep is not None:
                            d = o.ringdep
                            ws.append((('d',) + d.dsem, dsem[d.dsem], d.dval))
                        for key, sem, val in ws:
                            if waited.get(key, 0) >= val:
                                continue
                            waited[key] = val
                            eng.wait_ge(sem, val)
                        if o.fn is None:
                            continue
                        name_, a_, k_ = o.fn
                        ins = getattr(eng, name_)(*a_, **k_)
                        if o.is_dma:
                            ins.then_inc(dsem[o.dsem], 16)
                        elif o.signal:
                            ins.then_inc(csem[e], 1)
                    for i in range(NRING):
                        d = self.ringops[e][i]
                        if d is not None and waited.get(('d',) + d.dsem, 0) < d.dval:
                            eng.wait_ge(dsem[d.dsem], d.dval)
                return body

            for e in ENGS:
                if self.ops[e]:
                    getattr(block, handles[e])(make(e))


_UID = [0]


class _Stop(Exception):
    pass


def _chk(tag):
    if os.environ.get('KSTOP') == tag:
        raise _Stop()


class RB:
    def __init__(self, st, nc, name, shape, dt, n, psum=False):
        alloc = nc.psum_tensor if psum else nc.sbuf_tensor
        _UID[0] += 1
        self.t = [st.enter_context(alloc('%s_%d_%d' % (name, _UID[0], i), shape, dt)) for i in range(n)]
        self.b = [Buf() for _ in range(n)]
        self.i = 0

    def next(self):
        k = self.i % len(self.t)
        self.i += 1
        return self.t[k], self.b[k]


POOL_WINDOWS = (2, 4, 8, 16)


def _band_tables(L):
    out = np.zeros((128, 4, 5, 128), np.float32)
    for g, w in enumerate(POOL_WINDOWS):
        before = w // 2
        after = w - 1 - before

        def fill(kind, t_tile, s_tile):
            for tl in range(128):
                t = t_tile * 128 + tl
                lo = max(t - before, 0)
                hi = min(t + after, L - 1)
                cnt = hi - lo + 1
                for s in range(lo, hi + 1):
                    sl = s - s_tile * 128
                    if 0 <= sl < 128:
                        out[sl, g, kind, tl] += 1.0 / cnt
                if s_tile == t_tile:
                    out[tl, g, kind, tl] -= 1.0
        nt = L // 128
        fill(0, 2, 2)
        fill(1, 2, 1)
        fill(2, 2, 3)
        fill(3, 0, 0)
        fill(4, nt - 1, nt - 1)
    return out


def _consts():
    c = {}
    c['ident'] = np.eye(128, dtype=np.float32)
    inv = 10000.0 ** (-np.arange(0, 32, 2, dtype=np.float32) / 32)
    ang = np.arange(LS, dtype=np.float32)[None, :] * inv[:, None]
    cs, sn = np.cos(ang).astype(np.float32), np.sin(ang).astype(np.float32)
    c['ropeC'] = np.concatenate([cs, cs], 0)
    c['ropeS'] = np.concatenate([-sn, sn], 0)
    for nm, L in (('p', LP), ('s', LS)):
        c['band_' + nm] = _band_tables(L)
        N = 2 * L
        N2 = L // 64
        a = np.arange(64)[:, None].astype(np.float64)
        ap = np.arange(64)[None, :].astype(np.float64)
        th = np.pi * (2 * ap + 1) * a / 128.0
        c['F1'] = np.concatenate([np.cos(th), -np.sin(th)], 1).astype(np.float32)
        c['G1re_' + nm] = ((2.0 / N) * np.cos(th).T).astype(np.float32)
        c['G1im_' + nm] = ((2.0 / N) * (-np.sin(th)).T).astype(np.float32)
        PK = 128 // N2
        b = np.arange(N2)[:, None].astype(np.float64)
        tt = np.pi * (2 * ap + 1) * b / N
        tre, tim = np.tile(np.cos(tt), (PK, 1)), np.tile(-np.sin(tt), (PK, 1))
        TA = np.concatenate([tre, tre], 1)
        TB = np.concatenate([tim, tim], 1)
        c['TA_' + nm] = np.repeat(TA[:, None, :], 4, 1).astype(np.float32)
        c['TB_' + nm] = np.repeat(TB[:, None, :], 4, 1).astype(np.float32)
        tcr, tci = np.tile(np.cos(tt).T, (1, PK)), np.tile(np.sin(tt).T, (1, PK))
        c['TcA_' + nm] = np.repeat(np.concatenate([tcr, tcr], 1)[:, None, :], 4, 1).astype(np.float32)
        c['TcB_' + nm] = np.repeat(np.concatenate([tci, tci], 1)[:, None, :], 4, 1).astype(np.float32)
        bb = np.arange(N2)[None, :].astype(np.float64)
        f2 = 2 * np.pi * b * bb / N2
        eye = np.eye(PK)
        f2re, f2im = np.kron(eye, np.cos(f2)), np.kron(eye, -np.sin(f2))
        c['F2_' + nm] = np.stack([f2re, f2im, -f2im], 1).astype(np.float32)
        c['FA_' + nm] = np.concatenate([f2re, -f2im], 1).astype(np.float32)
        c['FB_' + nm] = np.concatenate([f2im, f2re], 1).astype(np.float32)
        tf = 2 * np.pi * b * ap / L
        fre, fim = np.tile(np.cos(tf), (PK, 1)), np.tile(-np.sin(tf), (PK, 1))
        c['TfA_' + nm] = np.repeat(np.concatenate([fre, fre], 1)[:, None, :], 4, 1).astype(np.float32)
        c['TfB_' + nm] = np.repeat(np.concatenate([fim, fim], 1)[:, None, :], 4, 1).astype(np.float32)
        t = np.linspace(0.0, 1.0, L, dtype=np.float32)
        bands = np.linspace(1e-4, 15, 16, dtype=np.float32)
        angf = (np.float32(2.0 * math.pi / L) * np.arange(L, dtype=np.float32)[:, None] * bands[None, :]).astype(np.float32)
        feat = np.concatenate([t[:, None], np.cos(angf), -np.sin(angf)], -1).astype(np.float32)
        c['featT_' + nm] = np.ascontiguousarray(feat.T)
        c['trow_' + nm] = t[None, :].copy()
    ff = 2 * np.pi * a * ap / 64.0
    c['F1fA'] = np.concatenate([np.cos(ff), -np.sin(ff)], 1).astype(np.float32)
    c['F1fB'] = np.concatenate([-np.sin(ff), -np.cos(ff)], 1).astype(np.float32)
    dd = np.arange(64)[:, None] * np.arange(64)[None, :]
    c64, s64 = np.cos(2 * np.pi * dd / 64.0), np.sin(2 * np.pi * dd / 64.0)
    z = np.zeros((64, 64))
    c['DC'] = np.block([[c64, z], [z, c64]]).astype(np.float32)
    c['DS'] = np.block([[s64, z], [z, s64]]).astype(np.float32)
    deltas = np.abs(np.linspace(math.log(1e-2) / 1.5, math.log(1e-2) / 0.3, 768, dtype=np.float32))
    c['negd'] = np.ascontiguousarray((-deltas).reshape(6, 128).T).astype(np.float32)
    return c


_CONST_CACHE = {}


def _get_consts():
    if not _CONST_CACHE:
        _CONST_CACHE.update(_consts())
    return _CONST_CACHE


def build(consts, stop_after=99, dbg=None):
    nc = bass.Bass("TRN2", target_bir_lowering=False)

    def sb(name, shape, dt):
        _UID[0] += 1
        return nc.sbuf_tensor('%s_%d' % (name, _UID[0]), shape, dt)
    I = {}

    def inp(name, shape):
        I[name] = nc.dram_tensor(name, list(shape), F32, kind="ExternalInput").ap()
        return I[name]

    specs = {
        "x_p": (LP, D), "x_s": (LS, D), "norm_mix": (2, 2, D), "norm_mlp": (2, 2, D),
        "w_ff_in": (2, D, 4096), "w_ff_out": (2, 4096, D), "even_w_in": (1, D, 928),
        "pool_w": (1, 4, 128, 128), "pool_scale": (1, 512), "mla_q_norm": (1, 256),
        "mla_w_uq": (1, 256, 768), "mla_kv_norm": (1, 128), "mla_w_ukv": (1, 128, 1024),
        "even_w_out": (1, D, D), "odd_w_in": (1, D, 2560), "short_w": (1, 3, 2304),
        "short_b": (1, 2304), "filt_w1": (1, 33, 64), "filt_b1": (1, 64), "filt_w2": (1, 64, 64),
        "filt_b2": (1, 64), "filt_w3": (1, 64, 3072), "filt_freq": (1, 64), "hyena_bias": (1, 2, 768),
        "fnet_w": (1, 4, 64, 64), "odd_w_out": (1, D, D),
    }
    for k, v in specs.items():
        inp(k, v)
    for k, v in consts.items():
        inp('c_' + k, v.shape)
    y_p = nc.dram_tensor("y_p", [LP, D], F32, kind="ExternalOutput").ap()
    y_s = nc.dram_tensor("y_s", [LS, D], F32, kind="ExternalOutput").ap()

    def scratch(name, shape, dt):
        return nc.dram_tensor(name, list(shape), dt, kind="Internal").ap()

    SEQ = []
    for nm, L, xin, yout in (('p', LP, I['x_p'], y_p), ('s', LS, I['x_s'], y_s)):
        s = dict(nm=nm, L=L, x=xin, y=yout, N2=L // 64)
        s['QT'] = scratch('QT' + nm, [8, 96, L], BF16)
        s['KN'] = scratch('KN' + nm, [512, L], BF16)
        s['KR'] = scratch('KR' + nm, [32, L], BF16)
        s['V'] = scratch('V' + nm, [8, 128, L // 128, 64], BF16)
        s['mixT'] = scratch('mixT' + nm, [D, L], BF16)
        s['X1'] = scratch('X1' + nm, [L, D], F32)
        s['X2'] = scratch('X2' + nm, [L, D], F32)
        s['zcT'] = scratch('zcT' + nm, [2304, L], F32)
        s['fcT'] = scratch('fcT' + nm, [512, L], F32)
        s['hfT'] = scratch('hfT' + nm, [3072, L], F32)
        s['invd'] = scratch('invd' + nm, [128, 32], F32)
        s['mix2'] = scratch('mix2' + nm, [768, L], BF16)
        s['mfT'] = scratch('mfT' + nm, [256, L], BF16)
        s['X3'] = scratch('X3' + nm, [L, D], F32)
        for k in ('QT', 'KN', 'KR', 'V', 'mixT', 'X1', 'X2', 'zcT', 'fcT', 'hfT', 'invd', 'mix2', 'mfT', 'X3'):
            s['b_' + k] = Buf()
        SEQ.append(s)

    P = Prog(nc)
    NOB = Buf

    with contextlib.ExitStack() as top:
        def gt(name, shape, dt):
            return top.enter_context(sb(name, shape, dt))

        PS = RB(top, nc, 'ps', [128, 512], F32, 6, psum=True)
        PST = RB(top, nc, 'pst', [128, 1024], BF16, 2, psum=True)
        ident = gt('ident', [128, 128], BF16)
        onesb = gt('onesb', [128, 128], BF16)
        onesf = gt('onesf', [128, 128], F32)
        epsT = gt('epsT', [128, 2], F32)
        stg = RB(top, nc, 'stg', [128, 1024], F32, 2)
        bconst = Buf()
        P.op('pool', lambda e: e.memset(onesb[:], 1.0), writes=[bconst])
        P.op('pool', lambda e: e.memset(onesf[:], 1.0), writes=[bconst])
        P.op('pool', lambda e: e.memset(epsT[:, 0:1], EPS), writes=[bconst])
        P.op('pool', lambda e: e.memset(epsT[:, 1:2], 96.0 * EPS), writes=[bconst])
        _castc = [0]

        def cast_engine():
            _castc[0] += 1
            return ('act', 'pool')[_castc[0] % 2]

        def copy_op(eng, out, in_, reads, writes, nowaw=False):
            if eng == 'act':
                P.op('act', lambda e: e.copy(out=out, in_=in_), reads, writes, nowaw=nowaw)
            else:
                P.op(eng, lambda e: e.tensor_copy(out=out, in_=in_), reads, writes, nowaw=nowaw)

        def loadw(dst, src, dbuf, np_=128):
            cols = dst.shape[-1]
            assert len(dst.shape) == 2 and len(src.shape) == 2, (dst.shape, src.shape)
            for c0 in range(0, cols, 1024):
                cw = min(1024, cols - c0)
                t, b = stg.next()
                P.dma('sp', t[0:np_, 0:cw], src[:, c0:c0 + cw], writes=[b])
                copy_op(cast_engine(), dst[:, c0:c0 + cw], t[0:np_, 0:cw], [b], [dbuf])

        loadw(ident[:], I['c_ident'], bconst)

        def rms_transpose(ph, xt, bx, nj, gcol, hT, bh, tmp, bg):
            ssq, bs = tmp['ssq'].next()
            junk, bj = tmp['junk'].next()
            for j in range(nj):
                P.op('act', (lambda j: lambda e: e.activation(out=junk[:], in_=xt[:, j, :], func=AF.Square,
                                                               accum_out=ssq[:, j:j + 1]))(j), [bx], [bj, bs])
            P.op('act', lambda e: e.activation(out=ssq[:, 4:4 + nj], in_=ssq[:, 0:nj], func=AF.Sqrt,
                                               bias=epsT[:, 0:1], scale=1.0 / D), [bs, bconst], [bs])
            P.op('dve', lambda e: e.reciprocal(out=ssq[:, 8:8 + nj], in_=ssq[:, 4:4 + nj]), [bs], [bs])
            xn, bn = tmp['xn'].next()
            for j in range(nj):
                P.op('dve', (lambda j: lambda e: e.tensor_scalar(out=xn[:, j, :], in0=xt[:, j, :],
                                                                 scalar1=ssq[:, 8 + j:9 + j], scalar2=None,
                                                                 op0=ALU.mult))(j), [bx, bs], [bn])
            for dc in range(8):
                pt, bp = PST.next()
                for j in range(nj):
                    P.op('pe', (lambda j, dc, pt: lambda e: e.transpose(pt[:, j * 128:(j + 1) * 128],
                                                                        xn[:, j, dc * 128:(dc + 1) * 128], ident[:]))(j, dc, pt),
                         [bn, bconst], [bp])
                P.op('act', (lambda dc, pt: lambda e: e.activation(out=hT[:, dc, :], in_=pt[:, 0:nj * 128], func=AF.Copy,
                                                                   scale=gcol[:, dc:dc + 1]))(dc, pt), [bp, bg], [bh])

        def epilogue(pss, xt, bx, j, grow, tmp, bg):
            ssq, bs = tmp['ssq2'].next()
            junk, bj = tmp['junk'].next()
            for h in range(2):
                P.op('act', (lambda h: lambda e: e.activation(out=junk[:, 0:512], in_=pss[h][0][:], func=AF.Square,
                                                               accum_out=ssq[:, h:h + 1]))(h), [pss[h][1]], [bj, bs])
            P.op('dve', lambda e: e.tensor_tensor(out=ssq[:, 2:3], in0=ssq[:, 0:1], in1=ssq[:, 1:2], op=ALU.add), [bs], [bs])
            P.op('act', lambda e: e.activation(out=ssq[:, 3:4], in_=ssq[:, 2:3], func=AF.Sqrt, bias=epsT[:, 0:1],
                                               scale=1.0 / D), [bs, bconst], [bs])
            P.op('dve', lambda e: e.reciprocal(out=ssq[:, 4:5], in_=ssq[:, 3:4]), [bs], [bs])
            for h in range(2):
                t, bt = tmp['ep'].next()
                P.op('dve', (lambda h, t: lambda e: e.scalar_tensor_tensor(out=t[:], in0=pss[h][0][:], scalar=ssq[:, 4:5],
                                                                           in1=grow[:, h * 512:(h + 1) * 512], op0=ALU.mult,
                                                                           op1=ALU.mult))(h, t), [pss[h][1], bs, bg], [bt])
                P.op('pool', (lambda h, t: lambda e: e.tensor_tensor(out=xt[:, j, h * 512:(h + 1) * 512], in0=t[:],
                                                                     in1=xt[:, j, h * 512:(h + 1) * 512], op=ALU.add))(h, t),
                     [bt, bx], [bx])

        def colvec(ph, name, src1d, ncol, bufc):
            t = ph.enter_context(sb(name, [128, ncol], F32))
            P.dma('pool', t[:], src1d.rearrange("(c p) -> p c", p=128), writes=[bufc])
            return t

        def rowbc(ph, name, src1d, n, bufc, npart=128):
            t = ph.enter_context(sb(name, [npart, n], F32))
            P.dma('pool', t[:], src1d.rearrange("(o n) -> o n", o=1).partition_broadcast(npart), writes=[bufc])
            return t

        def ph1():
            try:
                ph1_()
            except _Stop:
                pass
            P.barrier()

        def ph1_():
            with contextlib.ExitStack() as ph:
                try:
                    ph1_body(ph)
                except _Stop:
                    pass

        def ph1_body(ph):
            if True:
                T = lambda name, shape, dt: ph.enter_context(sb(name, shape, dt))
                bw = Buf()
                win = T('win', [128, 8, 928], BF16)
                winsw = T('winsw', [128, 8, 32], BF16)
                wuq = T('wuq', [128, 2, 768], BF16)
                wuqsw = T('wuqsw', [128, 2, 768], BF16)
                wuk = T('wuk', [128, 512], BF16)
                wuv = T('wuv', [128, 512], BF16)
                poolw = T('poolw', [128, 4, 128], BF16)
                ewi = I['even_w_in'][0].rearrange("(c p) f -> p c f", p=128)
                for dc in range(8):
                    loadw(win[:, dc, :], ewi[:, dc, :], bw)
                    loadw(winsw[:, dc, 0:16], ewi[:, dc, 912:928], bw)
                    loadw(winsw[:, dc, 16:32], ewi[:, dc, 896:912], bw)
                uq = I['mla_w_uq'][0].rearrange("(c p) f -> p c f", p=128)
                for rc in range(2):
                    loadw(wuq[:, rc, :], uq[:, rc, :], bw)
                    loadw(wuqsw[:, rc, :], uq[:, rc, :], bw)
                    for h in range(8):
                        loadw(wuqsw[:, rc, h * 96 + 64:h * 96 + 80], uq[:, rc, h * 96 + 80:h * 96 + 96], bw)
                        loadw(wuqsw[:, rc, h * 96 + 80:h * 96 + 96], uq[:, rc, h * 96 + 64:h * 96 + 80], bw)
                ukv = I['mla_w_ukv'][0]
                for h in range(8):
                    loadw(wuk[:, h * 64:(h + 1) * 64], ukv[:, h * 128:h * 128 + 64], bw)
                    loadw(wuv[:, h * 64:(h + 1) * 64], ukv[:, h * 128 + 64:h * 128 + 128], bw)
                for g in range(4):
                    loadw(poolw[:, g, :], I['pool_w'][0, g], bw)
                g0col = colvec(ph, 'g0col', I['norm_mix'][0, 0], 8, bw)
                pscol = colvec(ph, 'pscol', I['pool_scale'][0], 4, bw)
                qncol = colvec(ph, 'qncol', I['mla_q_norm'][0], 2, bw)
                kvcol = colvec(ph, 'kvcol', I['mla_kv_norm'][0], 1, bw)
                band = T('band', [128, 4 * 5 * 128], BF16)
                _chk('w')
                tmp = dict(ssq=RB(ph, nc, 'ssq', [128, 12], F32, 2), junk=RB(ph, nc, 'junk', [128, 1024], F32, 1),
                           xn=RB(ph, nc, 'xn', [128, 4, 1024], BF16, 1))
                XT = RB(ph, nc, 'xt', [128, 4, 1024], F32, 1)
                HT = RB(ph, nc, 'hT', [128, 8, 512], BF16, 1)
                atok = T('atok', [128, 64, 512], BF16)
                SQ = RB(ph, nc, 'sq', [128, 3, 512], BF16, 2)
                CG = RB(ph, nc, 'cg', [128, 3, 512], BF16, 2)
                RQ = RB(ph, nc, 'rq', [128, 2, 512], F32, 1)
                RT = RB(ph, nc, 'rt', [96, 4, 512], F32, 1)
                RK = RB(ph, nc, 'rk', [32, 2, 512], F32, 1)
                KT = RB(ph, nc, 'kt', [32, 2, 512], F32, 1)
                KRO = RB(ph, nc, 'kro', [32, 512], BF16, 2)
                QTs = RB(ph, nc, 'qTs', [96, 512], BF16, 3)
                QTm = RB(ph, nc, 'qTm', [96, 2, 512], F32, 1)
                KS = RB(ph, nc, 'ks', [128, 512], BF16, 2)
                VS = RB(ph, nc, 'vs', [128, 512], BF16, 2)
                RC = RB(ph, nc, 'rc', [128, 12], F32, 2)
                RTs = RB(ph, nc, 'rTs', [128, 512], BF16, 2)
                MS = RB(ph, nc, 'ms', [128, 512], BF16, 2)
                for s in SEQ:
                    L = s['L']
                    nt = L // 128
                    batok = [Buf() for _ in range(nt)]
                    bband = Buf()
                    loadw(band[:], I['c_band_' + s['nm']].rearrange("p g k t -> p (g k t)"), bband)
                    for ci in range(L // 512):
                        c0 = ci * 512
                        xt, bx = XT.next()
                        P.dma('sp', xt[:], s['x'][c0:c0 + 512, :].rearrange("(j p) d -> p j d", p=128), writes=[bx])
                        hT, bh = HT.next()
                        rms_transpose(ph, xt, bx, 4, g0col, hT, bh, tmp, bw)
                        _chk('rms')
                        for j in range(4):
                            ps, bp = PS.next()
                            for dc in range(8):
                                P.op('pe', (lambda j, dc, ps: lambda e: e.matmul(ps[:], lhsT=hT[:, dc, j * 128:(j + 1) * 128],
                                                                                 rhs=win[:, dc, 0:512], start=(dc == 0), stop=(dc == 7)))(j, dc, ps),
                                     [bh, bw], [bp])
                            copy_op('act', atok[:, ci * 4 + j, :], ps[:], [bp], [batok[ci * 4 + j]])
                        _chk('atok')
                        sq, bsq = SQ.next()
                        cg, bcg = CG.next()
                        for k3, (lo, ncol) in enumerate(((512, qncol[:, 0:1]), (640, qncol[:, 1:2]), (768, kvcol[:, 0:1]))):
                            ps, bp = PS.next()
                            for dc in range(8):
                                P.op('pe', (lambda dc, ps, lo: lambda e: e.matmul(ps[:], lhsT=win[:, dc, lo:lo + 128], rhs=hT[:, dc, :],
                                                                                  start=(dc == 0), stop=(dc == 7)))(dc, ps, lo), [bh, bw], [bp])
                            P.op('act', (lambda k3, ps: lambda e: e.activation(out=sq[:, k3, :], in_=ps[:], func=AF.Square))(k3, ps), [bp], [bsq])
                            P.op('dve', (lambda k3, ps, ncol: lambda e: e.tensor_scalar(out=cg[:, k3, :], in0=ps[:], scalar1=ncol, scalar2=None,
                                                                                        op0=ALU.mult))(k3, ps, ncol), [bp, bw, bsq], [bcg])
                        _chk('lat')
                        rk, brk = RK.next()
                        P.dma('pool', rk[:, 0, :], I['c_ropeC'][:, c0:c0 + 512], writes=[brk])
                        P.dma('pool', rk[:, 1, :], I['c_ropeS'][:, c0:c0 + 512], writes=[brk])
                        kt, bkt = KT.next()
                        for k2, (wt, lo) in enumerate(((win, 896), (winsw, 0))):
                            ps, bp = PS.next()
                            for dc in range(8):
                                P.op('pe', (lambda dc, ps, wt, lo: lambda e: e.matmul(ps[0:32, :], lhsT=wt[:, dc, lo:lo + 32], rhs=hT[:, dc, :],
                                                                                      start=(dc == 0), stop=(dc == 7)))(dc, ps, wt, lo), [bh, bw], [bp])
                            P.op('dve', (lambda k2, ps: lambda e: e.tensor_tensor(out=kt[:, k2, :], in0=ps[0:32, :], in1=rk[:, k2, :],
                                                                                  op=ALU.mult))(k2, ps), [bp, brk], [bkt])
                        kro, bkro = KRO.next()
                        P.op('pool', lambda e, kt=kt, kro=kro: e.tensor_tensor(out=kro[:], in0=kt[:, 0, :], in1=kt[:, 1, :], op=ALU.add), [bkt], [bkro])
                        P.dma('pool', s['KR'][:, c0:c0 + 512], kro[:], reads=[bkro], writes=[s['b_KR']])
                        _chk('krope')
                        rq, brq = RQ.next()
                        ps, bp = PS.next()
                        for rc in range(2):
                            P.op('pe', (lambda rc, ps: lambda e: e.matmul(ps[:], lhsT=onesb[:], rhs=sq[:, rc, :], start=(rc == 0), stop=(rc == 1)))(rc, ps),
                                 [bsq, bconst], [bp])
                        P.op('act', lambda e, ps=ps, rq=rq: e.activation(out=rq[:, 0, :], in_=ps[:], func=AF.Sqrt, bias=epsT[:, 1:2], scale=96.0 / 256.0),
                             [bp, bconst], [brq])
                        P.op('dve', lambda e, rq=rq: e.reciprocal(out=rq[:, 0, :], in_=rq[:, 0, :]), [brq], [brq])
                        ps, bp = PS.next()
                        P.op('pe', lambda e, ps=ps, sq=sq: e.matmul(ps[:], lhsT=onesb[:], rhs=sq[:, 2, :], start=True, stop=True), [bsq, bconst], [bp])
                        P.op('act', lambda e, ps=ps, rq=rq: e.activation(out=rq[:, 1, :], in_=ps[:], func=AF.Sqrt, bias=epsT[:, 0:1], scale=1.0 / 128.0),
                             [bp, bconst], [brq])
                        P.op('dve', lambda e, rq=rq: e.reciprocal(out=rq[:, 1, :], in_=rq[:, 1, :]), [brq], [brq])
                        rc_, brc = RC.next()
                        ps, bp = PS.next()
                        for j in range(4):
                            P.op('pe', (lambda j, ps: lambda e: e.matmul(ps[:, j:j + 1], lhsT=sq[:, 2, j * 128:(j + 1) * 128], rhs=onesb[:, 0:1],
                                                                         start=True, stop=True))(j, ps), [bsq, bconst], [bp])
                        P.op('act', lambda e, ps=ps, rc_=rc_: e.activation(out=rc_[:, 0:4], in_=ps[:, 0:4], func=AF.Sqrt, bias=epsT[:, 0:1], scale=1.0 / 128.0),
                             [bp, bconst], [brc])
                        P.op('dve', lambda e, rc_=rc_: e.reciprocal(out=rc_[:, 4:8], in_=rc_[:, 0:4]), [brc], [brc])
                        rt, brt = RT.next()
                        P.dma('pool', rt[64:96, 0, :], I['c_ropeC'][:, c0:c0 + 512], writes=[brt])
                        P.dma('pool', rt[64:96, 1, :], I['c_ropeS'][:, c0:c0 + 512], writes=[brt])
                        for k2 in range(2):
                            P.op('pool', (lambda k2: lambda e, rt=rt, rq=rq: e.tensor_tensor(out=rt[64:96, 2 + k2, :], in0=rt[64:96, k2, :],
                                                                                             in1=rq[64:96, 0, :], op=ALU.mult))(k2), [brt, brq], [brt])
                        _chk('rstd')
                        for h in range(8):
                            pq, bpq = PS.next()
                            pw, bpw = PS.next()
                            for rc in range(2):
                                P.op('pe', (lambda rc, pq, h: lambda e: e.matmul(pq[0:96, :], lhsT=wuq[:, rc, h * 96:(h + 1) * 96], rhs=cg[:, rc, :],
                                                                                 start=(rc == 0), stop=(rc == 1)))(rc, pq, h), [bcg, bw], [bpq])
                            for rc in range(2):
                                P.op('pe', (lambda rc, pw, h: lambda e: e.matmul(pw[0:96, :], lhsT=wuqsw[:, rc, h * 96:(h + 1) * 96], rhs=cg[:, rc, :],
                                                                                 start=(rc == 0), stop=(rc == 1)))(rc, pw, h), [bcg, bw], [bpw])
                            qs, bqs = QTs.next()
                            qm, bqm = QTm.next()
                            P.op('dve', lambda e, qs=qs, pq=pq, rq=rq: e.tensor_tensor(out=qs[0:64, :], in0=pq[0:64, :], in1=rq[0:64, 0, :], op=ALU.mult),
                                 [bpq, brq], [bqs])
                            P.op('dve', lambda e, qm=qm, pq=pq, rt=rt: e.tensor_tensor(out=qm[64:96, 0, :], in0=pq[64:96, :], in1=rt[64:96, 2, :], op=ALU.mult),
                                 [bpq, brt], [bqm])
                            P.op('dve', lambda e, qm=qm, pw=pw, rt=rt: e.tensor_tensor(out=qm[64:96, 1, :], in0=pw[64:96, :], in1=rt[64:96, 3, :], op=ALU.mult),
                                 [bpw, brt], [bqm])
                            P.op('pool', lambda e, qs=qs, qm=qm: e.tensor_tensor(out=qs[64:96, :], in0=qm[64:96, 0, :], in1=qm[64:96, 1, :], op=ALU.add),
                                 [bqm], [bqs])
                            P.dma('sp', s['QT'][h, :, c0:c0 + 512], qs[:], reads=[bqs], writes=[s['b_QT']])
                        _chk('q')
                        for hp in range(4):
                            ps, bp = PS.next()
                            P.op('pe', lambda e, ps=ps, hp=hp, cg=cg: e.matmul(ps[:], lhsT=wuk[:, hp * 128:(hp + 1) * 128], rhs=cg[:, 2, :], start=True, stop=True),
                                 [bcg, bw], [bp])
                            ks, bks = KS.next()
                            P.op('dve', lambda e, ks=ks, ps=ps, rq=rq: e.tensor_tensor(out=ks[:], in0=ps[:], in1=rq[:, 1, :], op=ALU.mult), [bp, brq], [bks])
                            P.dma('sp', s['KN'][hp * 128:(hp + 1) * 128, c0:c0 + 512], ks[:], reads=[bks], writes=[s['b_KN']])
                        _chk('k')
                        for j in range(4):
                            ps, bp = PS.next()
                            P.op('pe', lambda e, ps=ps, j=j, cg=cg: e.matmul(ps[:], lhsT=cg[:, 2, j * 128:(j + 1) * 128], rhs=wuv[:], start=True, stop=True),
                                 [bcg, bw], [bp])
                            vs, bvs = VS.next()
                            P.op('act', lambda e, vs=vs, ps=ps, rc_=rc_, j=j: e.activation(out=vs[:], in_=ps[:], func=AF.Copy, scale=rc_[:, 4 + j:5 + j]),
                                 [bp, brc], [bvs])
                            P.dma('sp', s['V'][:, :, ci * 4 + j, :].rearrange("h p d -> p h d"), vs[:].rearrange("p (h d) -> p h d", h=8), reads=[bvs], writes=[s['b_V']])
                    _chk('v')
                    for sp in range(L // 512):
                        for g in range(4):
                            ps, bp = PS.next()
                            for jt in range(4):
                                t = sp * 4 + jt
                                terms = []
                                if t > 0:
                                    terms.append((t - 1, 1))
                                terms.append((t, 3 if t == 0 else (4 if t == nt - 1 else 0)))
                                if t < nt - 1:
                                    terms.append((t + 1, 2))
                                for k, (st_, kind) in enumerate(terms):
                                    off = (g * 5 + kind) * 128
                                    P.op('pe', lambda e, ps=ps, jt=jt, st_=st_, g=g, off=off, k=k, n=len(terms): e.matmul(
                                        ps[:, jt * 128:(jt + 1) * 128], lhsT=atok[:, st_, g * 128:(g + 1) * 128], rhs=band[:, off:off + 128],
                                        start=(k == 0), stop=(k == n - 1)), [batok[st_], bband], [bp])
                            rts, brts = RTs.next()
                            copy_op('dve', rts[:], ps[:], [bp], [brts])
                            ps2, bp2 = PS.next()
                            P.op('pe', lambda e, ps2=ps2, g=g, rts=rts: e.matmul(ps2[:], lhsT=poolw[:, g, :], rhs=rts[:], start=True, stop=True), [brts, bw], [bp2])
                            ms, bms = MS.next()
                            P.op('act', lambda e, ms=ms, ps2=ps2, g=g: e.activation(out=ms[:], in_=ps2[:], func=AF.Copy, scale=pscol[:, g:g + 1]), [bp2, bw], [bms])
                            P.dma('sp', s['mixT'][g * 128:(g + 1) * 128, sp * 512:(sp + 1) * 512], ms[:], reads=[bms], writes=[s['b_mixT']])
            P.barrier()

        def ph2():
            with contextlib.ExitStack() as ph:
                KT = RB(ph, nc, 'aK', [96, LS], BF16, 2)
                QT = RB(ph, nc, 'aQ', [96, LS], BF16, 2)
                VT = RB(ph, nc, 'aV', [128, LS // 128, 128], BF16, 2)
                PT = RB(ph, nc, 'aP', [128, 512], BF16, 3)
                RS = RB(ph, nc, 'aR', [128, 512], F32, 2)
                BC = RB(ph, nc, 'aB', [64, 512], F32, 2)
                OS = RB(ph, nc, 'aO', [64, 512], BF16, 2)
                for i in range(2):
                    P.op('pool', lambda e, i=i: e.memset(VT.t[i][:, :, 64:128], 1.0), writes=[VT.b[i]])

                class _Sub:
                    def __init__(self, lo, hi):
                        self.t = PS.t[lo:hi]
                        self.b = PS.b[lo:hi]
                        self.i = 0
                    next = RB.next
                POs = _Sub(0, 2)
                PSs = _Sub(2, 6)
                for s in SEQ:
                    L = s['L']
                    nk = L // 128
                    for h in range(8):
                        kt, bk = KT.next()
                        qt, bq = QT.next()
                        vt, bv = VT.next()
                        P.dma('sp', kt[0:64, 0:L], s['KN'][h * 64:(h + 1) * 64, :], reads=[s['b_KN']], writes=[bk])
                        P.dma('sp', kt[64:96, 0:L], s['KR'][:, :], reads=[s['b_KR']], writes=[bk])
                        P.dma('sp', qt[:, 0:L], s['QT'][h], reads=[s['b_QT']], writes=[bq])
                        P.dma('pool', vt[:, 0:nk, 0:64], s['V'][h],
                              reads=[s['b_V']], writes=[bv])
                        for qc in range(L // 512):
                            po, bpo = POs.next()
                            pend = []

                            def issue_s(k, kt=kt, qt=qt, qc=qc):
                                pss, bps = PSs.next()
                                P.op('pe', lambda e: e.matmul(pss[:], lhsT=kt[:, k * 128:(k + 1) * 128], rhs=qt[:, qc * 512:(qc + 1) * 512], start=True, stop=True),
                                     [bk, bq], [bps])
                                pend.append((pss, bps))
                            for k in range(min(2, nk)):
                                issue_s(k)
                            for k in range(nk):
                                if k + 2 < nk:
                                    issue_s(k + 2)
                                pss, bps = pend.pop(0)
                                pt, bpt = PT.next()
                                P.op('act', lambda e, pt=pt, pss=pss: e.activation(out=pt[:], in_=pss[:], func=AF.Exp), [bps], [bpt])
                                P.op('pe', lambda e, po=po, vt=vt, pt=pt, k=k, nk=nk: e.matmul(po[:], lhsT=vt[:, k, :], rhs=pt[:], start=(k == 0), stop=(k == nk - 1)),
                                     [bv, bpt], [bpo])
                            rs, brs = RS.next()
                            P.op('dve', lambda e, rs=rs, po=po: e.reciprocal(out=rs[64:65, :], in_=po[64:65, :]), [bpo], [brs])
                            pb, bpb = PSs.next()
                            P.op('pe', lambda e, pb=pb, rs=rs: e.matmul(pb[0:64, :], lhsT=onesf[64:65, 0:64], rhs=rs[64:65, :], start=True, stop=True),
                                 [brs, bconst], [bpb])
                            bc, bbc = BC.next()
                            copy_op('act', bc[:], pb[0:64, :], [bpb], [bbc])
                            os_, bos = OS.next()
                            P.op('dve', lambda e, os_=os_, po=po, bc=bc: e.tensor_tensor(out=os_[:], in0=po[0:64, :], in1=bc[:], op=ALU.mult), [bpo, bbc], [bos])
                            P.dma('pool', s['mixT'][512 + h * 64:512 + (h + 1) * 64, qc * 512:(qc + 1) * 512], os_[:], reads=[bos], writes=[s['b_mixT']])
            P.barrier()

        def ph_out(layer):
            with contextlib.ExitStack() as ph:
                T = lambda name, shape, dt: ph.enter_context(sb(name, shape, dt))
                bw = Buf()
                wout = T('wout', [128, 8, 1024], BF16)
                wsrc = (I['even_w_out'] if layer == 0 else I['odd_w_out'])[0].rearrange("(c p) f -> p c f", p=128)
                for c in range(8):
                    loadw(wout[:, c, :], wsrc[:, c, :], bw)
                grow = rowbc(ph, 'grow', I['norm_mix'][layer, 1], D, bw)
                if layer == 1:
                    fwb = T('fwb', [128, 2, 128], BF16)
                    P.op('pool', lambda e: e.memset(fwb[:], 0.0), writes=[bw])
                    for g in range(4):
                        pp = (g % 2) * 64
                        loadw(fwb[pp:pp + 64, g // 2, pp:pp + 64], I['fnet_w'][0, g], bw, np_=64)
                tmp = dict(ssq2=RB(ph, nc, 'ssq2', [128, 8], F32, 2), junk=RB(ph, nc, 'junk', [128, 1024], F32, 1),
                           ep=RB(ph, nc, 'ep', [128, 512], F32, 3))
                XT = RB(ph, nc, 'xt', [128, 4, 1024], F32, 2)
                MX = RB(ph, nc, 'mx', [128, 8, 512], BF16, 2)
                MF = RB(ph, nc, 'mf', [128, 2, 512], BF16, 2)
                for s in SEQ:
                    L = s['L']
                    xin, bxin = (s['x'], Buf()) if layer == 0 else (s['X2'], s['b_X2'])
                    xout, bxout = (s['X1'], s['b_X1']) if layer == 0 else (s['X3'], s['b_X3'])
                    for ci in range(L // 512):
                        c0 = ci * 512
                        xt, bx = XT.next()
                        P.dma('sp', xt[:], xin[c0:c0 + 512, :].rearrange("(j p) d -> p j d", p=128), reads=[bxin], writes=[bx])
                        mx, bm = MX.next()
                        if layer == 0:
                            P.dma('pool', mx[:], s['mixT'][:, c0:c0 + 512].rearrange("(c p) t -> p c t", p=128), reads=[s['b_mixT']], writes=[bm])
                        else:
                            P.dma('pool', mx[:, 0:6, :], s['mix2'][:, c0:c0 + 512].rearrange("(c p) t -> p c t", p=128), reads=[s['b_mix2']], writes=[bm])
                            mf, bmf = MF.next()
                            P.dma('pool', mf[:], s['mfT'][:, c0:c0 + 512].rearrange("(c p) t -> p c t", p=128), reads=[s['b_mfT']], writes=[bmf])
                            for j2 in range(2):
                                ps, bp = PS.next()
                                P.op('pe', lambda e, ps=ps, j2=j2, mf=mf: e.matmul(ps[:], lhsT=fwb[:, j2, :], rhs=mf[:, j2, :], start=True, stop=True), [bmf, bw], [bp])
                                copy_op('act', mx[:, 6 + j2, :], ps[:], [bp], [bm])
                        for j in range(4):
                            pss = []
                            for hf in range(2):
                                ps, bp = PS.next()
                                for c in range(8):
                                    P.op('pe', lambda e, ps=ps, c=c, j=j, hf=hf, mx=mx: e.matmul(ps[:], lhsT=mx[:, c, j * 128:(j + 1) * 128],
                                                                                                 rhs=wout[:, c, hf * 512:(hf + 1) * 512], start=(c == 0), stop=(c == 7)),
                                         [bm, bw], [bp])
                                pss.append((ps, bp))
                            epilogue(pss, xt, bx, j, grow, tmp, bw)
                        P.dma('sp', xout[c0:c0 + 512, :].rearrange("(j p) d -> p j d", p=128), xt[:], reads=[bx], writes=[bxout])
            P.barrier()

        def ph_mlp(layer):
            TT = 256
            nj = TT // 128
            with contextlib.ExitStack() as ph:
                T = lambda name, shape, dt: ph.enter_context(sb(name, shape, dt))
                bw = Buf()
                wfi = T('wfi', [128, 8, 4096], BF16)
                wfo = T('wfo', [128, 32, 1024], BF16)
                s1 = I['w_ff_in'][layer].rearrange("(c p) f -> p c f", p=128)
                s2 = I['w_ff_out'][layer].rearrange("(c p) f -> p c f", p=128)
                for c in range(8):
                    loadw(wfi[:, c, :], s1[:, c, :], bw)
                for c in range(32):
                    loadw(wfo[:, c, :], s2[:, c, :], bw)
                gcol = colvec(ph, 'gcol', I['norm_mlp'][layer, 0], 8, bw)
                grow = rowbc(ph, 'grow', I['norm_mlp'][layer, 1], D, bw)
                tmp = dict(ssq=RB(ph, nc, 'ssq', [128, 12], F32, 2), ssq2=RB(ph, nc, 'ssq2', [128, 8], F32, 2),
                           junk=RB(ph, nc, 'junk', [128, 1024], F32, 1), xn=RB(ph, nc, 'xn', [128, nj, 1024], BF16, 1),
                           ep=RB(ph, nc, 'ep', [128, 512], F32, 2))
                XT = RB(ph, nc, 'xt', [128, nj, 1024], F32, 2)
                HT = RB(ph, nc, 'hT', [128, 8, TT], BF16, 1)
                F1 = RB(ph, nc, 'f1', [128, 32, TT], BF16, 1)
                RL = RB(ph, nc, 'rl', [128, TT], F32, 3)
                for s in SEQ:
                    L = s['L']
                    xin, bxin = (s['X1'], s['b_X1']) if layer == 0 else (s['X3'], s['b_X3'])
                    xout, bxout = (s['X2'], s['b_X2']) if layer == 0 else (s['y'], Buf())
                    for ci in range(L // TT):
                        c0 = ci * TT
                        xt, bx = XT.next()
                        P.dma('sp', xt[:], xin[c0:c0 + TT, :].rearrange("(j p) d -> p j d", p=128), reads=[bxin], writes=[bx])
                        hT, bh = HT.next()
                        rms_transpose(ph, xt, bx, nj, gcol, hT, bh, tmp, bw)
                        f1, bf1 = F1.next()
                        for fc in range(32):
                            ps, bp = PS.next()
                            for dc in range(8):
                                P.op('pe', lambda e, ps=ps, dc=dc, fc=fc, hT=hT: e.matmul(ps[:, 0:TT], lhsT=wfi[:, dc, fc * 128:(fc + 1) * 128], rhs=hT[:, dc, :],
                                                                                          start=(dc == 0), stop=(dc == 7)), [bh, bw], [bp])
                            rl, brl = RL.next()
                            P.op('act', lambda e, rl=rl, ps=ps: e.activation(out=rl[:], in_=ps[:, 0:TT], func=AF.Relu), [bp], [brl])
                            eng = 'pool' if fc % 2 else 'dve'
                            P.op(eng, lambda e, rl=rl, f1=f1, fc=fc: e.tensor_tensor(out=f1[:, fc, :], in0=rl[:], in1=rl[:], op=ALU.mult), [brl], [bf1])
                        for j in range(nj):
                            pss = []
                            for hf in range(2):
                                ps, bp = PS.next()
                                for fc in range(32):
                                    P.op('pe', lambda e, ps=ps, fc=fc, j=j, hf=hf, f1=f1: e.matmul(ps[:], lhsT=f1[:, fc, j * 128:(j + 1) * 128],
                                                                                                   rhs=wfo[:, fc, hf * 512:(hf + 1) * 512], start=(fc == 0), stop=(fc == 31)),
                                         [bf1, bw], [bp])
                                pss.append((ps, bp))
                            epilogue(pss, xt, bx, j, grow, tmp, bw)
                        P.dma('pool', xout[c0:c0 + TT, :].rearrange("(j p) d -> p j d", p=128), xt[:], reads=[bx], writes=[bxout])
            P.barrier()

        def ph5():
            with contextlib.ExitStack() as ph:
                T = lambda name, shape, dt: ph.enter_context(sb(name, shape, dt))
                bw = Buf()
                owin = T('owin', [128, 8, 2560], BF16)
                src = I['odd_w_in'][0].rearrange("(c p) f -> p c f", p=128)
                for c in range(8):
                    loadw(owin[:, c, :], src[:, c, :], bw)
                gcol = colvec(ph, 'gcol', I['norm_mix'][1, 0], 8, bw)
                swc = T('swc', [128, 3, 18], F32)
                for j in range(3):
                    P.dma('pool', swc[:, j, :], I['short_w'][0, j].rearrange("(c p) -> p c", p=128), writes=[bw])
                sbc = colvec(ph, 'sbc', I['short_b'][0], 18, bw)
                DCt = T('DCt', [128, 2, 128], F32)
                loadw(DCt[:, 0, :], I['c_DC'], bw)
                loadw(DCt[:, 1, :], I['c_DS'], bw)
                tmp = dict(ssq=RB(ph, nc, 'ssq', [128, 12], F32, 2), junk=RB(ph, nc, 'junk', [128, 1024], F32, 1),
                           xn=RB(ph, nc, 'xn', [128, 4, 1024], BF16, 1))
                XT = RB(ph, nc, 'xt', [128, 4, 1024], F32, 2)
                HT = RB(ph, nc, 'hT', [128, 8, 512], BF16, 2)
                WN = RB(ph, nc, 'wn', [128, 514], F32, 3)
                OT = RB(ph, nc, 'ot', [128, 512], F32, 3)
                FS = RB(ph, nc, 'fs', [128, 512], F32, 2)
                FO = RB(ph, nc, 'fo', [128, 512], F32, 3)
                carry = T('carry', [128, 18, 2], F32)
                bcar = Buf()
                for s in SEQ:
                    L = s['L']
                    fsc = 1.0 / math.sqrt(64.0 * L)
                    P.op('pool', lambda e: e.memset(carry[:], 0.0), writes=[bcar])
                    nch = L // 512
                    for ci in range(nch + 1):
                        c0 = ci * 512
                        last = (ci == nch)
                        if not last:
                            xt, bx = XT.next()
                            P.dma('sp', xt[:], s['X2'][c0:c0 + 512, :].rearrange("(j p) d -> p j d", p=128), reads=[s['b_X2']], writes=[bx])
                            hT, bh = HT.next()
                            rms_transpose(ph, xt, bx, 4, gcol, hT, bh, tmp, bw)
                        for fc in range(18):
                            wn, bwn = WN.next()
                            P.op('pool', lambda e, wn=wn, fc=fc: e.tensor_copy(out=wn[:, 0:2], in_=carry[:, fc, :]), [bcar], [bwn])
                            if not last:
                                ps, bp = PS.next()
                                for dc in range(8):
                                    P.op('pe', lambda e, ps=ps, dc=dc, fc=fc, hT=hT: e.matmul(ps[:], lhsT=owin[:, dc, fc * 128:(fc + 1) * 128], rhs=hT[:, dc, :],
                                                                                              start=(dc == 0), stop=(dc == 7)), [bh, bw], [bp])
                                copy_op('act', wn[:, 2:514], ps[:], [bp], [bwn])
                                nw = 512
                            else:
                                P.op('pool', lambda e, wn=wn: e.memset(wn[:, 2:3], 0.0), [], [bwn])
                                nw = 1
                            ot, bo = OT.next()
                            P.op('dve', lambda e, ot=ot, wn=wn, fc=fc, nw=nw: e.tensor_scalar(out=ot[:, 0:nw], in0=wn[:, 0:nw], scalar1=swc[:, 0, fc:fc + 1],
                                                                                              scalar2=sbc[:, fc:fc + 1], op0=ALU.mult, op1=ALU.add), [bwn, bw], [bo])
                            for j in (1, 2):
                                P.op('dve', lambda e, ot=ot, wn=wn, fc=fc, nw=nw, j=j: e.scalar_tensor_tensor(out=ot[:, 0:nw], in0=wn[:, j:j + nw], scalar=swc[:, j, fc:fc + 1],
                                                                                                              in1=ot[:, 0:nw], op0=ALU.mult, op1=ALU.add), [bwn, bw, bo], [bo])
                            if not last:
                                P.op('pool', lambda e, wn=wn, fc=fc: e.tensor_copy(out=carry[:, fc, :], in_=wn[:, 512:514]), [bwn], [bcar])
                            if ci == 0:
                                P.dma('sp', s['zcT'][fc * 128:(fc + 1) * 128, 0:511], ot[:, 1:512], reads=[bo], writes=[s['b_zcT']])
                            elif not last:
                                P.dma('sp', s['zcT'][fc * 128:(fc + 1) * 128, c0 - 1:c0 + 511], ot[:, 0:512], reads=[bo], writes=[s['b_zcT']])
                            else:
                                P.dma('sp', s['zcT'][fc * 128:(fc + 1) * 128, L - 1:L], ot[:, 0:1], reads=[bo], writes=[s['b_zcT']])
                        if last:
                            continue
                        for f2 in range(2):
                            ps, bp = PS.next()
                            for dc in range(8):
                                P.op('pe', lambda e, ps=ps, dc=dc, f2=f2, hT=hT: e.matmul(ps[:], lhsT=owin[:, dc, 2304 + f2 * 128:2304 + (f2 + 1) * 128], rhs=hT[:, dc, :],
                                                                                          start=(dc == 0), stop=(dc == 7)), [bh, bw], [bp])
                            fs, bfs = FS.next()
                            copy_op('act', fs[:], ps[:], [bp], [bfs])
                            for k2 in range(2):
                                ps2, bp2 = PS.next()
                                P.op('pe', lambda e, ps2=ps2, k2=k2, fs=fs: e.matmul(ps2[:], lhsT=DCt[:, k2, :], rhs=fs[:], start=True, stop=True), [bfs, bw], [bp2])
                                fo, bfo = FO.next()
                                P.op('act', lambda e, fo=fo, ps2=ps2: e.activation(out=fo[:], in_=ps2[:], func=AF.Copy, scale=fsc), [bp2], [bfo])
                                r0 = k2 * 256 + f2 * 128
                                P.dma('pool', s['fcT'][r0:r0 + 128, c0:c0 + 512], fo[:], reads=[bfo], writes=[s['b_fcT']])
            P.barrier()

        def ph6():
            PI = math.pi
            with contextlib.ExitStack() as ph:
                T = lambda name, shape, dt: ph.enter_context(sb(name, shape, dt))
                bw = Buf()
                w1 = T('fw1', [33, 64], F32)
                w2 = T('fw2', [64, 64], F32)
                w3 = T('fw3', [64, 3072], F32)
                P.dma('pool', w1[:], I['filt_w1'][0], writes=[bw])
                P.dma('pool', w2[:], I['filt_w2'][0], writes=[bw])
                P.dma('pool', w3[:], I['filt_w3'][0], writes=[bw])
                cols = T('fcols', [64, 8], F32)
                for k, nm_ in enumerate(('filt_b1', 'filt_b2', 'filt_freq')):
                    P.dma('pool', cols[:, k:k + 1], I[nm_][0].rearrange("(p o) -> p o", o=1), writes=[bw])
                P.op('dve', lambda e: e.tensor_tensor(out=cols[:, 3:4], in0=cols[:, 2:3], in1=cols[:, 0:1], op=ALU.mult), [bw], [bw])
                P.op('dve', lambda e: e.tensor_tensor(out=cols[:, 4:5], in0=cols[:, 2:3], in1=cols[:, 1:2], op=ALU.mult), [bw], [bw])
                negd = T('negd', [128, 6], F32)
                P.dma('pool', negd[:], I['c_negd'], writes=[bw])
                FT = RB(ph, nc, 'ft', [33, 512], F32, 2)
                TB = RB(ph, nc, 'tb', [128, 512], F32, 2)
                DT = RB(ph, nc, 'dt', [128, 6, 512], F32, 2)
                AR = RB(ph, nc, 'ar', [64, 3, 512], F32, 2)
                H1 = RB(ph, nc, 'h1', [64, 512], F32, 2)
                H2 = RB(ph, nc, 'h2', [64, 512], F32, 2)
                HF = RB(ph, nc, 'hf', [128, 512], F32, 3)
                junk = T('fjunk', [128, 512], F32)
                bjunk = Buf()
                ssq = T('fssq', [128, 24, 16], F32)
                tot = T('ftot', [128, 3, 24], F32)
                bss = Buf()

                def sinlayer(ps, bp, fb, out, bo):
                    ar, ba = AR.next()
                    P.op('dve', lambda e: e.tensor_scalar(out=ar[:, 0, :], in0=ps[0:64, :], scalar1=cols[:, 2:3], scalar2=cols[:, fb:fb + 1],
                                                          op0=ALU.mult, op1=ALU.add), [bp, bw], [ba])
                    P.op('dve', lambda e: e.tensor_scalar(out=ar[:, 1, :], in0=ar[:, 0, :], scalar1=PI, scalar2=2 * PI, op0=ALU.is_gt, op1=ALU.mult), [ba], [ba])
                    P.op('dve', lambda e: e.tensor_tensor(out=ar[:, 0, :], in0=ar[:, 0, :], in1=ar[:, 1, :], op=ALU.subtract), [ba], [ba])
                    P.op('dve', lambda e: e.tensor_scalar(out=ar[:, 1, :], in0=ar[:, 0, :], scalar1=-PI, scalar2=2 * PI, op0=ALU.is_lt, op1=ALU.mult), [ba], [ba])
                    P.op('dve', lambda e: e.tensor_tensor(out=ar[:, 0, :], in0=ar[:, 0, :], in1=ar[:, 1, :], op=ALU.add), [ba], [ba])
                    P.op('act', lambda e: e.activation(out=out[:], in_=ar[:, 0, :], func=AF.Sin, scale=0.999999), [ba], [bo])

                for s in SEQ:
                    L = s['L']
                    nch = L // 512
                    for ci in range(nch):
                        c0 = ci * 512
                        ft, bft = FT.next()
                        P.dma('sp', ft[:], I['c_featT_' + s['nm']][:, c0:c0 + 512], writes=[bft])
                        tb, btb = TB.next()
                        P.dma('sp', tb[:], I['c_trow_' + s['nm']][:, c0:c0 + 512].partition_broadcast(128), writes=[btb])
                        dt_, bdt = DT.next()
                        for k in range(6):
                            P.op('act', lambda e, k=k, dt_=dt_, tb=tb: e.activation(out=dt_[:, k, :], in_=tb[:], func=AF.Exp, scale=negd[:, k:k + 1]), [btb, bw], [bdt])
                        ps, bp = PS.next()
                        P.op('pe', lambda e, ps=ps, ft=ft: e.matmul(ps[0:64, :], lhsT=w1[:], rhs=ft[:], start=True, stop=True), [bft, bw], [bp])
                        h1, bh1 = H1.next()
                        sinlayer(ps, bp, 3, h1, bh1)
                        ps, bp = PS.next()
                        P.op('pe', lambda e, ps=ps, h1=h1: e.matmul(ps[0:64, :], lhsT=w2[:], rhs=h1[:], start=True, stop=True), [bh1, bw], [bp])
                        h2, bh2 = H2.next()
                        sinlayer(ps, bp, 4, h2, bh2)
                        for jc in range(24):
                            ps, bp = PS.next()
                            P.op('pe', lambda e, ps=ps, jc=jc, h2=h2: e.matmul(ps[:], lhsT=w3[:, jc * 128:(jc + 1) * 128], rhs=h2[:], start=True, stop=True), [bh2, bw], [bp])
                            hf, bhf = HF.next()
                            P.op('dve', lambda e, hf=hf, ps=ps, dt_=dt_, jc=jc: e.tensor_tensor(out=hf[:], in0=ps[:], in1=dt_[:, jc % 6, :], op=ALU.mult), [bp, bdt], [bhf])
                            P.op('act', lambda e, hf=hf, jc=jc, ci=ci: e.activation(out=junk[:], in_=hf[:], func=AF.Square, accum_out=ssq[:, jc, ci:ci + 1]), [bhf], [bjunk, bss])
                            P.dma('sp', s['hfT'][jc * 128:(jc + 1) * 128, c0:c0 + 512], hf[:], reads=[bhf], writes=[s['b_hfT']])
                    P.op('dve', lambda e, nch=nch: e.tensor_reduce(out=tot[:, 0, :], in_=ssq[:, :, 0:nch], axis=AX.X, op=ALU.add), [bss], [bss])
                    P.op('act', lambda e: e.activation(out=tot[:, 1, :], in_=tot[:, 0, :], func=AF.Sqrt, bias=epsT[:, 0:1], scale=1.0), [bss, bconst], [bss])
                    P.op('dve', lambda e: e.reciprocal(out=tot[:, 2, :], in_=tot[:, 1, :]), [bss], [bss])
                    P.dma('sp', s['invd'][:, 0:24], tot[:, 2, :], reads=[bss], writes=[s['b_invd']])
            P.barrier()

        def ph7():
            CB = 4
            with contextlib.ExitStack() as ph:
                T = lambda name, shape, dt: ph.enter_context(sb(name, shape, dt))
                bw = Buf()
                F1 = T('cF1', [64, 128], BF16)
                loadw(F1[:], I['c_F1'], bw, np_=64)
                F1fA = T('cF1fA', [64, 128], BF16)
                F1fB = T('cF1fB', [64, 128], BF16)
                loadw(F1fA[:], I['c_F1fA'], bw, np_=64)
                loadw(F1fB[:], I['c_F1fB'], bw, np_=64)
                hbb = rowbc(ph, 'hbb', I['hyena_bias'][0].rearrange("o c -> (o c)"), 1536, bw, npart=64)
                for s in SEQ:
                    L, N2r, nm = s['L'], s['N2'], s['nm']
                    N2 = 128
                    PK = 128 // N2r
                    CBc = CB * PK
                    G = 3
                    with contextlib.ExitStack() as p2:
                        T2 = lambda name, shape, dt: p2.enter_context(sb(name + nm, shape, dt))
                        bc_ = Buf()
                        G1re = T2('G1re', [64, 64], BF16)
                        G1im = T2('G1im', [64, 64], BF16)
                        loadw(G1re[:], I['c_G1re_' + nm], bc_, np_=64)
                        loadw(G1im[:], I['c_G1im_' + nm], bc_, np_=64)
                        TA = T2('TA', [N2, 4, 128], F32)
                        TBt = T2('TB', [N2, 4, 128], F32)
                        TfA = T2('TfA', [N2, 4, 128], F32)
                        TfB = T2('TfB', [N2, 4, 128], F32)
                        for t_, k_ in ((TA, 'TA_'), (TBt, 'TB_'), (TfA, 'TfA_'), (TfB, 'TfB_')):
                            P.dma('pool', t_[:], I['c_' + k_ + nm], writes=[bc_])
                        TcA = T2('TcA', [64, 4, 2 * N2], F32)
                        TcB = T2('TcB', [64, 4, 2 * N2], F32)
                        P.dma('pool', TcA[:], I['c_TcA_' + nm], writes=[bc_])
                        P.dma('pool', TcB[:], I['c_TcB_' + nm], writes=[bc_])
                        F2 = T2('F2', [N2, 3, N2], BF16)
                        FA = T2('FA', [N2, 2 * N2], BF16)
                        FB = T2('FB', [N2, 2 * N2], BF16)
                        loadw(F2[:].rearrange("p a b -> p (a b)"), I['c_F2_' + nm].rearrange("p a b -> p (a b)"), bc_, np_=N2)
                        loadw(FA[:], I['c_FA_' + nm], bc_, np_=N2)
                        loadw(FB[:], I['c_FB_' + nm], bc_, np_=N2)
                        XI = RB(p2, nc, 'xi' + nm, [64, 3, CB, N2], F32, G)
                        HI = RB(p2, nc, 'hi' + nm, [64, 4, CB, N2], F32, 2)
                        IV = RB(p2, nc, 'iv' + nm, [64, 4, CBc], F32, 3)
                        HB = RB(p2, nc, 'hb' + nm, [64, 4, CB, N2], BF16, G)
                        UB = RB(p2, nc, 'ub' + nm, [64, CB, N2], BF16, 2 * G)
                        PPa = RB(p2, nc, 'ppa' + nm, [N2, 4, 128], F32, 3)
                        PPb = RB(p2, nc, 'ppb' + nm, [N2, 4, 128], F32, 3)
                        YT = RB(p2, nc, 'yt' + nm, [N2, 4, 128], BF16, 2 * G + 2)
                        YF = RB(p2, nc, 'yf' + nm, [N2, 4, 128], F32, 2 * G + 2)
                        KK = RB(p2, nc, 'kk' + nm, [N2, 2, 2, CB, 128], F32, G)
                        GG = RB(p2, nc, 'gg' + nm, [N2, 4, 128], BF16, G)
                        VPa = RB(p2, nc, 'vpa' + nm, [64, 4, 2 * N2], F32, 2)
                        VPb = RB(p2, nc, 'vpb' + nm, [64, 4, 2 * N2], F32, 2)
                        VT_ = RB(p2, nc, 'vt' + nm, [64, 4, 2, N2], BF16, G)
                        UT = RB(p2, nc, 'ut' + nm, [64, CB, N2], F32, 3 * G)
                        MO = RB(p2, nc, 'mo' + nm, [N2, CB, 64], BF16, 3)

                        def tview(ps, n):
                            return ps[0:n, :].rearrange("p (t c) -> p t c", t=4)

                        def twiddle(ps, bp, A, B, n, w, outre, outim, bo, rd):
                            pa, bpa = PPa.next()
                            pb, bpb = PPb.next()
                            src = ps
                            P.op('dve', lambda e: e.tensor_tensor(out=pa[0:n], in0=src, in1=A, op=ALU.mult), [bp] + rd, [bpa])
                            P.op('dve', lambda e: e.tensor_tensor(out=pb[0:n], in0=src, in1=B, op=ALU.mult), [bp] + rd, [bpb])
                            P.op('pool', lambda e: e.tensor_tensor(out=outre, in0=pa[0:n, :, 0:w], in1=pb[0:n, :, w:2 * w], op=ALU.subtract), [bpa, bpb], [bo])
                            P.op('pool', lambda e: e.tensor_tensor(out=outim, in0=pb[0:n, :, 0:w], in1=pa[0:n, :, w:2 * w], op=ALU.add), [bpa, bpb], [bo], nowaw=True)

                        def tform_fwd(tiles, rd, A, B, out, bo):
                            ps, bp = PS.next()
                            pv = tview(ps, N2)
                            for t, tl in enumerate(tiles):
                                P.op('pe', lambda e, t=t, tl=tl: e.matmul(pv[:, t, :], lhsT=tl, rhs=F1[:], start=True, stop=True), rd + [bw], [bp])
                            twiddle(pv, bp, A, B, N2, 64, out[:, :, 0:64], out[:, :, 64:128], bo, [bc_])

                        def nform_fwd(yre_src, yim_src, rd, imsrc_re=None, imsrc_im=None):
                            ps, bp = PS.next()
                            pv = tview(ps, N2)
                            ire = yre_src if imsrc_re is None else imsrc_re
                            iim = yim_src if imsrc_im is None else imsrc_im
                            P.op('pe', lambda e: e.matmul(pv[:, :, 0:64], lhsT=F2[:, 0, :], rhs=yre_src, start=True, stop=False), rd + [bc_], [bp])
                            P.op('pe', lambda e: e.matmul(pv[:, :, 0:64], lhsT=F2[:, 2, :], rhs=yim_src, start=False, stop=True), rd + [bc_], [bp])
                            P.op('pe', lambda e: e.matmul(pv[:, :, 64:128], lhsT=F2[:, 0, :], rhs=iim, start=True, stop=False), rd + [bc_], [bp])
                            P.op('pe', lambda e: e.matmul(pv[:, :, 64:128], lhsT=F2[:, 1, :], rhs=ire, start=False, stop=True), rd + [bc_], [bp])
                            return pv, bp

                        def conv(ub, bub, kk, bkk, o, res):
                            yt, byt = YT.next()
                            tform_fwd([ub[:, t, :] for t in range(CB)], [bub], TA[:], TBt[:], yt, byt)
                            yield
                            pv, bp = nform_fwd(yt[:, :, 0:64], yt[:, :, 64:128], [byt])
                            gg, bgg = GG.next()
                            twiddle(pv, bp, kk[:, o, 0], kk[:, o, 1], N2, 64, gg[:, :, 0:64], gg[:, :, 64:128], bgg, [bkk])
                            yield
                            vt, bvt = VT_.next()
                            per = min(512 // (2 * N2), CB)
                            for g0 in range(0, CB, per):
                                ps, bp = PS.next()
                                pv2 = ps[0:64, 0:per * 2 * N2].rearrange("p (t c) -> p t c", t=per)
                                for t in range(per):
                                    P.op('pe', lambda e, t=t, g0=g0: e.matmul(pv2[:, t, :], lhsT=gg[:, g0 + t, 0:64], rhs=FA[:], start=True, stop=False), [bgg, bc_], [bp])
                                    P.op('pe', lambda e, t=t, g0=g0: e.matmul(pv2[:, t, :], lhsT=gg[:, g0 + t, 64:128], rhs=FB[:], start=False, stop=True), [bgg, bc_], [bp])
                                va, bva = VPa.next()
                                vb, bvb = VPb.next()
                                P.op('dve', lambda e, pv2=pv2, va=va: e.tensor_tensor(out=va[:, 0:per, :], in0=pv2, in1=TcA[:, 0:per, :], op=ALU.mult), [bp, bc_], [bva])
                                P.op('dve', lambda e, pv2=pv2, vb=vb: e.tensor_tensor(out=vb[:, 0:per, :], in0=pv2, in1=TcB[:, 0:per, :], op=ALU.mult), [bp, bc_], [bvb])
                                P.op('pool', lambda e, g0=g0, va=va, vb=vb: e.tensor_tensor(out=vt[:, g0:g0 + per, 0, :], in0=va[:, 0:per, 0:N2], in1=vb[:, 0:per, N2:2 * N2], op=ALU.subtract), [bva, bvb], [bvt], nowaw=(g0 > 0))
                                P.op('pool', lambda e, g0=g0, va=va, vb=vb: e.tensor_tensor(out=vt[:, g0:g0 + per, 1, :], in0=vb[:, 0:per, 0:N2], in1=va[:, 0:per, N2:2 * N2], op=ALU.add), [bva, bvb], [bvt], nowaw=True)
                                yield
                            ps, bp = PS.next()
                            pvo = ps[0:64, 0:CB * N2].rearrange("p (t c) -> p t c", t=CB)
                            P.op('pe', lambda e: e.matmul(pvo, lhsT=G1re[:], rhs=vt[:, :, 0, :], start=True, stop=False), [bvt, bc_], [bp])
                            P.op('pe', lambda e: e.matmul(pvo, lhsT=G1im[:], rhs=vt[:, :, 1, :], start=False, stop=True), [bvt, bc_], [bp])
                            res['pvo'] = (pvo, bp)

                        def zc_view(r0):
                            return s['zcT'][r0:r0 + CBc, :].rearrange("c (a b) -> a c b", b=N2r)

                        def cview(ap_):
                            return ap_.rearrange("p t (c b) -> p (t c) b", b=N2r)

                        def hyena_gen(cb):
                            ch0 = cb * CBc
                            xi, bxi = XI.next()
                            for k in range(3):
                                P.dma('sp', cview(xi[:, k]), zc_view(k * 768 + ch0), reads=[s['b_zcT']], writes=[bxi])
                            hi, bhi = HI.next()
                            for od in range(4):
                                P.dma('pool', cview(hi[:, od]), s['hfT'][od * 768 + ch0:od * 768 + ch0 + CBc, :].rearrange("c (a b) -> a c b", b=N2r),
                                      reads=[s['b_hfT']], writes=[bhi])
                            iv, biv = IV.next()
                            for od in range(4):
                                col = od * 768 + ch0
                                P.dma('pool', iv[:, od, :], s['invd'][col % 128:col % 128 + CBc, col // 128:col // 128 + 1].rearrange("c o -> o c").partition_broadcast(64),
                                      reads=[s['b_invd']], writes=[biv])
                            hb, bhb = HB.next()
                            P.op('dve', lambda e: e.tensor_tensor(out=hb[:].rearrange("p o t (c b) -> p o (t c) b", b=N2r), in0=hi[:].rearrange("p o t (c b) -> p o (t c) b", b=N2r),
                                                                  in1=iv[:].unsqueeze(3).to_broadcast([64, 4, CBc, N2r]), op=ALU.mult),
                                 [bhi, biv], [bhb])
                            for od in (1, 3):
                                P.op('pool', lambda e, od=od: e.memset(cview(hb[0:1, od])[:, :, 0:1], 0.0), [], [bhb])
                            yield
                            kk, bkk = KK.next()
                            for o in range(2):
                                yf, byf = YF.next()
                                yb, byb = YF.next()
                                tform_fwd([hb[:, 2 * o, t, :] for t in range(CB)], [bhb], TA[:], TBt[:], yf, byf)
                                yield
                                tform_fwd([hb[:, 2 * o + 1, t, :] for t in range(CB)], [bhb], TA[:], TBt[:], yb, byb)
                                ysum, bys = YT.next()
                                ydif, byd = YT.next()
                                P.op('pool', lambda e: e.tensor_tensor(out=ysum[:], in0=yf[:], in1=yb[:], op=ALU.add), [byf, byb], [bys])
                                P.op('pool', lambda e: e.tensor_tensor(out=ydif[:], in0=yf[:], in1=yb[:], op=ALU.subtract), [byf, byb], [byd])
                                yield
                                pv, bp = nform_fwd(ysum[:, :, 0:64], ysum[:, :, 64:128], [bys, byd], imsrc_re=ydif[:, :, 0:64], imsrc_im=ydif[:, :, 64:128])
                                for half in range(2):
                                    copy_op('act', kk[:, o, 0, :, half * 64:(half + 1) * 64], pv[:, :, 0:64], [bp], [bkk], nowaw=(o > 0 or half > 0))
                                    copy_op('act', kk[:, o, 1, :, half * 64:(half + 1) * 64], pv[:, :, 64:128], [bp], [bkk], nowaw=True)
                                yield
                            ub, bub = UB.next()
                            copy_op('act', ub[:], xi[:, 0], [bxi], [bub])
                            ucur, bucur = xi[:, 0], bxi
                            for o in range(2):
                                res = {}
                                yield from conv(ub, bub, kk, bkk, o, res)
                                pvo, bp = res['pvo']
                                ut, but = UT.next()
                                bia = hbb[:, o * 768 + ch0:o * 768 + ch0 + CBc].unsqueeze(2).to_broadcast([64, CBc, N2r])
                                P.op('pool', lambda e, ut=ut, ucur=ucur, bia=bia: e.tensor_tensor(out=cview(ut[:]), in0=cview(ucur), in1=bia, op=ALU.mult), [bucur, bw], [but])
                                P.op('dve', lambda e, ut=ut, pvo=pvo: e.tensor_tensor(out=ut[:], in0=pvo, in1=ut[:], op=ALU.add), [bp, but], [but])
                                ut2, but2 = UT.next()
                                P.op('pool', lambda e, ut=ut, ut2=ut2, o=o: e.tensor_tensor(out=ut2[:], in0=ut[:], in1=xi[:, 1 + o], op=ALU.mult), [but, bxi], [but2])
                                ub, bub = UB.next()
                                copy_op('act', ub[:], ut2[:], [but2], [bub])
                                ucur, bucur = ut2[:], but2
                                yield
                            P.dma('sp', s['mix2'][ch0:ch0 + CBc, :].rearrange("c (a b) -> a c b", b=N2r), cview(ub[:]), reads=[bub], writes=[s['b_mix2']])

                        def fnet_gen(cb):
                            q0 = cb * CBc
                            xi, bxi = XI.next()
                            P.dma('sp', cview(xi[:, 0]), s['fcT'][q0:q0 + CBc, :].rearrange("c (a b) -> a c b", b=N2r), reads=[s['b_fcT']], writes=[bxi])
                            P.dma('sp', cview(xi[:, 1]), s['fcT'][256 + q0:256 + q0 + CBc, :].rearrange("c (a b) -> a c b", b=N2r), reads=[s['b_fcT']], writes=[bxi])
                            xc, bxc = UB.next()
                            xs_, bxs = UB.next()
                            copy_op('act', xc[:], xi[:, 0], [bxi], [bxc])
                            copy_op('pool', xs_[:], xi[:, 1], [bxi], [bxs])
                            yield
                            ps, bp = PS.next()
                            pv = tview(ps, N2)
                            for t in range(CB):
                                P.op('pe', lambda e, t=t: e.matmul(pv[:, t, :], lhsT=xc[:, t, :], rhs=F1fA[:], start=True, stop=False), [bxc, bw], [bp])
                                P.op('pe', lambda e, t=t: e.matmul(pv[:, t, :], lhsT=xs_[:, t, :], rhs=F1fB[:], start=False, stop=True), [bxs, bw], [bp])
                            yt, byt = YT.next()
                            twiddle(pv, bp, TfA[:], TfB[:], N2, 64, yt[:, :, 0:64], yt[:, :, 64:128], byt, [bc_])
                            yield
                            ps, bp = PS.next()
                            pz = ps[0:N2, 0:CB * 64].rearrange("p (t c) -> p t c", t=CB)
                            P.op('pe', lambda e: e.matmul(pz, lhsT=F2[:, 0, :], rhs=yt[:, :, 0:64], start=True, stop=False), [byt, bc_], [bp])
                            P.op('pe', lambda e: e.matmul(pz, lhsT=F2[:, 2, :], rhs=yt[:, :, 64:128], start=False, stop=True), [byt, bc_], [bp])
                            mo, bmo = MO.next()
                            copy_op('act', mo[:], pz, [bp], [bmo])
                            P.dma('sp', s['mfT'][q0:q0 + CBc, :].rearrange("(t c) (b a) -> (c b) t a", c=PK, a=64), mo[:], reads=[bmo], writes=[s['b_mfT']])

                        def lockstep(gens, g):
                            it = iter(gens)
                            active = []
                            done = False
                            while True:
                                while not done and len(active) < g:
                                    nx = next(it, None)
                                    if nx is None:
                                        done = True
                                    else:
                                        active.append(nx)
                                if not active:
                                    break
                                for gg_ in list(active):
                                    try:
                                        next(gg_)
                                    except StopIteration:
                                        active.remove(gg_)

                        lockstep((hyena_gen(cb) for cb in range(768 // CBc)), G)
                        lockstep((fnet_gen(cb) for cb in range(256 // CBc)), G)
                    P.barrier()

        phases = [ph1, ph2, lambda: ph_out(0), lambda: ph_mlp(0), ph5, ph6, ph7, lambda: ph_out(1), lambda: ph_mlp(1)]
        for i, f in enumerate(phases):
            if i >= stop_after:
                break
            f()
        if dbg is not None:
            dbg(nc, P, SEQ, I)
        P.emit()
    return nc


_NC_CACHE = {}


def _prep_inputs(inputs):
    consts = _get_consts()
    base = {}
    for k, v in inputs.items():
        if k in ('x_prompt', 'x_sample'):
            continue
        base[k] = np.ascontiguousarray(np.asarray(v, dtype=np.float32))
    for k, v in consts.items():
        base['c_' + k] = np.ascontiguousarray(v)
    xp = np.asarray(inputs['x_prompt'], dtype=np.float32)
    xs = np.asarray(inputs['x_sample'], dtype=np.float32)
    in_maps = []
    for i in range(8):
        m = dict(base)
        m['x_p'] = np.ascontiguousarray(xp[i])
        m['x_s'] = np.ascontiguousarray(xs[i])
        in_maps.append(m)
    return consts, in_maps


def kernel(**inputs):
    consts, in_maps = _prep_inputs(inputs)
    if 'nc' not in _NC_CACHE:
        _NC_CACHE['nc'] = build(consts)
    nc = _NC_CACHE['nc']
    res = run_bass_kernel_spmd(nc, in_maps, core_ids=list(range(8)))
    yp = np.stack([np.asarray(r['y_p'], dtype=np.float32) for r in res.results], 0)
    ys = np.stack([np.asarray(r['y_s'], dtype=np.float32) for r in res.results], 0)
    return (yp, ys)
```

```python
import contextlib
import math
import os
import numpy as np
import concourse.bass as bass
import concourse.mybir as mybir
from concourse.bass_utils import run_bass_kernel_spmd

F32 = mybir.dt.float32
BF16 = mybir.dt.bfloat16
AF = mybir.ActivationFunctionType
ALU = mybir.AluOpType
AX = mybir.AxisListType

D = 1024
EPS = 1e-6
LP, LS = 2048, 8192
ENGS = ('pe', 'act', 'dve', 'pool', 'sp')
NRING = 8


class Buf:
    __slots__ = ('w', 'r', 'chain', 'pending')

    def __init__(self, chain=True):
        self.w = None
        self.r = []
        self.chain = chain
        self.pending = None


class WBuf(Buf):
    __slots__ = ()

    def __init__(self):
        Buf.__init__(self)
        self.pending = []

    def piece(self):
        b = Buf()
        self.pending.append(b)
        return b


class Op:
    __slots__ = ('eng', 'fn', 'deps', 'signal', 'sigval', 'is_dma', 'dsem', 'dval', 'ringdep')

    def __init__(self, eng, fn, is_dma):
        self.eng = eng
        self.fn = fn
        self.deps = []
        self.signal = False
        self.sigval = 0
        self.is_dma = is_dma
        self.dsem = None
        self.dval = 0
        self.ringdep = None


class _Rec:
    def __getattr__(self, name):
        def f(*a, **k):
            self.__dict__['call'] = (name, a, k)
            return None
        return f


class Prog:
    def __init__(self, nc):
        self.nc = nc
        self.ops = {e: [] for e in ENGS}
        self.ndma = {e: 0 for e in ENGS}
        self.ringops = {e: [None] * NRING for e in ENGS}

    def _rec(self, eng, fn, reads, writes, is_dma, nowaw=False):
        for t in list(reads) + list(writes):
            if t.pending:
                pend, t.pending = t.pending, []
                self._rec('pool', self.join_fn, pend, [t], False)
        r = _Rec()
        fn(r)
        op = Op(eng, r.call, is_dma)
        deps = []
        for t in reads:
            if t.w is not None:
                deps.append(t.w)
        for t in writes:
            if t.w is not None and not (nowaw and t.w.eng == eng and not t.w.is_dma) \
                    and not ((not t.chain) and is_dma and t.w.is_dma):
                deps.append(t.w)
            deps.extend(t.r)
        for t in reads:
            t.r.append(op)
        for t in writes:
            t.w = op
            t.r = []
        seen = set()
        for d in deps:
            if id(d) in seen or d is op:
                continue
            seen.add(id(d))
            if (not d.is_dma) and d.eng == eng and eng == 'pe':
                continue
            op.deps.append(d)
            if not d.is_dma:
                d.signal = True
        if is_dma:
            j = self.ndma[eng]
            self.ndma[eng] = j + 1
            op.dsem = (eng, j % NRING)
            op.dval = 16 * (j // NRING + 1)
            op.ringdep = self.ringops[eng][j % NRING]
            self.ringops[eng][j % NRING] = op
        self.ops[eng].append(op)
        return op

    def op(self, eng, fn, reads=(), writes=(), nowaw=False):
        return self._rec(eng, fn, reads, writes, False, nowaw)

    def dma(self, eng, out, in_, reads=(), writes=()):
        return self._rec(eng, lambda e: e.dma_start(out=out, in_=in_), reads, writes, True)

    def barrier(self):
        lasts = []
        for e in ENGS:
            for o in reversed(self.ops[e]):
                if o.fn is not None and not o.is_dma:
                    lasts.append(o)
                    break
            for d in self.ringops[e]:
                if d is not None:
                    lasts.append(d)
        for e in ENGS:
            op = Op(e, None, False)
            for d in lasts:
                if d.is_dma or d.eng != e:
                    op.deps.append(d)
                    if not d.is_dma:
                        d.signal = True
            self.ops[e].append(op)

    def emit(self):
        nc = self.nc
        with contextlib.ExitStack() as st:
            csem = {e: st.enter_context(nc.semaphore('c_' + e)) for e in ENGS}
            dsem = {}
            for e in ENGS:
                for i in range(min(NRING, self.ndma[e])):
                    dsem[(e, i)] = st.enter_context(nc.semaphore('d_%s%d' % (e, i)))
            for e in ENGS:
                c = 0
                for o in self.ops[e]:
                    if o.signal:
                        c += 1
                        o.sigval = c
            st.enter_context(nc.allow_non_contiguous_dma(reason='small strided tables / layout loads'))
            block = st.enter_context(nc.Block())
            handles = {'pe': 'tensor', 'act': 'scalar', 'dve': 'vector', 'pool': 'gpsimd', 'sp': 'sync'}

            def make(e):
                def body(eng):
                    waited = {}
                    for o in self.ops[e]:
                        ws = []
                        for d in o.deps:
                            if d.is_dma:
                                ws.append((('d',) + d.dsem, dsem[d.dsem], d.dval))
                            else:
                                ws.append((('c', d.eng), csem[d.eng], d.sigval))
                        if o.ringdep is not None:
                            d = o.ringdep
                            ws.append((('d',) + d.dsem, dsem[d.dsem], d.dval))
                        for key, sem, val in ws:
                            if waited.get(key, 0) >= val:
                                continue
                            waited[key] = val
                            eng.wait_ge(sem, val)
                        if o.fn is None:
                            continue
                        name_, a_, k_ = o.fn
                        ins = getattr(eng, name_)(*a_, **k_)
                        if o.is_dma:
                            ins.then_inc(dsem[o.dsem], 16)
                        elif o.signal:
                            ins.then_inc(csem[e], 1)
                    for i in range(NRING):
                        d = self.ringops[e][i]
                        if d is not None and waited.get(('d',) + d.dsem, 0) < d.dval:
                            eng.wait_ge(dsem[d.dsem], d.dval)
                return body

            for e in ENGS:
                if self.ops[e]:
                    getattr(block, handles[e])(make(e))


_UID = [0]


class _Stop(Exception):
    pass


def _chk(tag):
    if os.environ.get('KSTOP') == tag:
        raise _Stop()


class RB:
    def __init__(self, st, nc, name, shape, dt, n, psum=False):
        alloc = nc.psum_tensor if psum else nc.sbuf_tensor
        _UID[0] += 1
        self.t = [st.enter_context(alloc('%s_%d_%d' % (name, _UID[0], i), shape, dt)) for i in range(n)]
        self.b = [Buf() for _ in range(n)]
        self.i = 0

    def next(self):
        k = self.i % len(self.t)
        self.i += 1
        return self.t[k], self.b[k]


POOL_WINDOWS = (2, 4, 8, 16)


def _band_tables(L):
    out = np.zeros((128, 4, 5, 128), np.float32)
    for g, w in enumerate(POOL_WINDOWS):
        before = w // 2
        after = w - 1 - before

        def fill(kind, t_tile, s_tile):
            for tl in range(128):
                t = t_tile * 128 + tl
                lo = max(t - before, 0)
                hi = min(t + after, L - 1)
                cnt = hi - lo + 1
                for s in range(lo, hi + 1):
                    sl = s - s_tile * 128
                    if 0 <= sl < 128:
                        out[sl, g, kind, tl] += 1.0 / cnt
                if s_tile == t_tile:
                    out[tl, g, kind, tl] -= 1.0
        nt = L // 128
        fill(0, 2, 2)
        fill(1, 2, 1)
        fill(2, 2, 3)
        fill(3, 0, 0)
        fill(4, nt - 1, nt - 1)
    return out


def _consts():
    c = {}
    c['ident'] = np.eye(128, dtype=np.float32)
    inv = 10000.0 ** (-np.arange(0, 32, 2, dtype=np.float32) / 32)
    ang = np.arange(LS, dtype=np.float32)[None, :] * inv[:, None]
    cs, sn = np.cos(ang).astype(np.float32), np.sin(ang).astype(np.float32)
    c['ropeC'] = np.concatenate([cs, cs], 0)
    c['ropeS'] = np.concatenate([-sn, sn], 0)
    for nm, L in (('p', LP), ('s', LS)):
        c['band_' + nm] = _band_tables(L)
        N = 2 * L
        N2 = L // 64
        a = np.arange(64)[:, None].astype(np.float64)
        ap = np.arange(64)[None, :].astype(np.float64)
        th = np.pi * (2 * ap + 1) * a / 128.0
        c['F1'] = np.concatenate([np.cos(th), -np.sin(th)], 1).astype(np.float32)
        c['G1re_' + nm] = ((2.0 / N) * np.cos(th).T).astype(np.float32)
        c['G1im_' + nm] = ((2.0 / N) * (-np.sin(th)).T).astype(np.float32)
        PK = 128 // N2
        b = np.arange(N2)[:, None].astype(np.float64)
        tt = np.pi * (2 * ap + 1) * b / N
        tre, tim = np.tile(np.cos(tt), (PK, 1)), np.tile(-np.sin(tt), (PK, 1))
        TA = np.concatenate([tre, tre], 1)
        TB = np.concatenate([tim, tim], 1)
        c['TA_' + nm] = np.repeat(TA[:, None, :], 4, 1).astype(np.float32)
        c['TB_' + nm] = np.repeat(TB[:, None, :], 4, 1).astype(np.float32)
        tcr, tci = np.tile(np.cos(tt).T, (1, PK)), np.tile(np.sin(tt).T, (1, PK))
        c['TcA_' + nm] = np.repeat(np.concatenate([tcr, tcr], 1)[:, None, :], 4, 1).astype(np.float32)
        c['TcB_' + nm] = np.repeat(np.concatenate([tci, tci], 1)[:, None, :], 4, 1).astype(np.float32)
        bb = np.arange(N2)[None, :].astype(np.float64)
        f2 = 2 * np.pi * b * bb / N2
        eye = np.eye(PK)
        f2re, f2im = np.kron(eye, np.cos(f2)), np.kron(eye, -np.sin(f2))
        c['F2_' + nm] = np.stack([f2re, f2im, -f2im], 1).astype(np.float32)
        c['FA_' + nm] = np.concatenate([f2re, -f2im], 1).astype(np.float32)
        c['FB_' + nm] = np.concatenate([f2im, f2re], 1).astype(np.float32)
        tf = 2 * np.pi * b * ap / L
        fre, fim = np.tile(np.cos(tf), (PK, 1)), np.tile(-np.sin(tf), (PK, 1))
        c['TfA_' + nm] = np.repeat(np.concatenate([fre, fre], 1)[:, None, :], 4, 1).astype(np.float32)
        c['TfB_' + nm] = np.repeat(np.concatenate([fim, fim], 1)[:, None, :], 4, 1).astype(np.float32)
        t = np.linspace(0.0, 1.0, L, dtype=np.float32)
        bands = np.linspace(1e-4, 15, 16, dtype=np.float32)
        angf = (np.float32(2.0 * math.pi / L) * np.arange(L, dtype=np.float32)[:, None] * bands[None, :]).astype(np.float32)
        feat = np.concatenate([t[:, None], np.cos(angf), -np.sin(angf)], -1).astype(np.float32)
        c['featT_' + nm] = np.ascontiguousarray(feat.T)
        c['trow_' + nm] = t[None, :].copy()
    ff = 2 * np.pi * a * ap / 64.0
    c['F1fA'] = np.concatenate([np.cos(ff), -np.sin(ff)], 1).astype(np.float32)
    c['F1fB'] = np.concatenate([-np.sin(ff), -np.cos(ff)], 1).astype(np.float32)
    dd = np.arange(64)[:, None] * np.arange(64)[None, :]
    c64, s64 = np.cos(2 * np.pi * dd / 64.0), np.sin(2 * np.pi * dd / 64.0)
    z = np.zeros((64, 64))
    c['DC'] = np.block([[c64, z], [z, c64]]).astype(np.float32)
    c['DS'] = np.block([[s64, z], [z, s64]]).astype(np.float32)
    deltas = np.abs(np.linspace(math.log(1e-2) / 1.5, math.log(1e-2) / 0.3, 768, dtype=np.float32))
    c['negd'] = np.ascontiguousarray((-deltas).reshape(6, 128).T).astype(np.float32)
    return c


_CONST_CACHE = {}


def _get_consts():
    if not _CONST_CACHE:
        _CONST_CACHE.update(_consts())
    return _CONST_CACHE


def build(consts, stop_after=99, dbg=None):
    nc = bass.Bass("TRN2", target_bir_lowering=False)

    def sb(name, shape, dt):
        _UID[0] += 1
        return nc.sbuf_tensor('%s_%d' % (name, _UID[0]), shape, dt)
    I = {}

    def inp(name, shape):
        I[name] = nc.dram_tensor(name, list(shape), F32, kind="ExternalInput").ap()
        return I[name]

    specs = {
        "x_p": (LP, D), "x_s": (LS, D), "norm_mix": (2, 2, D), "norm_mlp": (2, 2, D),
        "w_ff_in": (2, D, 4096), "w_ff_out": (2, 4096, D), "even_w_in": (1, D, 928),
        "pool_w": (1, 4, 128, 128), "pool_scale": (1, 512), "mla_q_norm": (1, 256),
        "mla_w_uq": (1, 256, 768), "mla_kv_norm": (1, 128), "mla_w_ukv": (1, 128, 1024),
        "even_w_out": (1, D, D), "odd_w_in": (1, D, 2560), "short_w": (1, 3, 2304),
        "short_b": (1, 2304), "filt_w1": (1, 33, 64), "filt_b1": (1, 64), "filt_w2": (1, 64, 64),
        "filt_b2": (1, 64), "filt_w3": (1, 64, 3072), "filt_freq": (1, 64), "hyena_bias": (1, 2, 768),
        "fnet_w": (1, 4, 64, 64), "odd_w_out": (1, D, D),
    }
    for k, v in specs.items():
        inp(k, v)
    for k, v in consts.items():
        inp('c_' + k, v.shape)
    y_p = nc.dram_tensor("y_p", [LP, D], F32, kind="ExternalOutput").ap()
    y_s = nc.dram_tensor("y_s", [LS, D], F32, kind="ExternalOutput").ap()

    def scratch(name, shape, dt):
        return nc.dram_tensor(name, list(shape), dt, kind="Internal").ap()

    SEQ = []
    for nm, L, xin, yout in (('p', LP, I['x_p'], y_p), ('s', LS, I['x_s'], y_s)):
        s = dict(nm=nm, L=L, x=xin, y=yout, N2=L // 64)
        s['QT'] = scratch('QT' + nm, [8, 96, L], BF16)
        s['KN'] = scratch('KN' + nm, [512, L], BF16)
        s['KR'] = scratch('KR' + nm, [32, L], BF16)
        s['V'] = scratch('V' + nm, [8, 128, L // 128, 64], BF16)
        s['mixT'] = scratch('mixT' + nm, [D, L], BF16)
        s['X1'] = scratch('X1' + nm, [L, D], F32)
        s['X2'] = scratch('X2' + nm, [L, D], F32)
        s['zcT'] = scratch('zcT' + nm, [2304, L], F32)
        s['fcT'] = scratch('fcT' + nm, [512, L], F32)
        s['hfT'] = scratch('hfT' + nm, [3072, L], F32)
        s['invd'] = scratch('invd' + nm, [128, 32], F32)
        s['mix2'] = scratch('mix2' + nm, [768, L], BF16)
        s['mfT'] = scratch('mfT' + nm, [256, L], BF16)
        s['X3'] = scratch('X3' + nm, [L, D], F32)
        for k in ('QT', 'KN', 'KR', 'V', 'mixT', 'X1', 'X2', 'zcT', 'fcT', 'hfT', 'invd', 'mix2', 'mfT', 'X3'):
            s['b_' + k] = Buf(chain=(k == 'zcT'))
        SEQ.append(s)

    P = Prog(nc)
    NOB = Buf

    with contextlib.ExitStack() as top:
        def gt(name, shape, dt):
            return top.enter_context(sb(name, shape, dt))

        PS = RB(top, nc, 'ps', [128, 512], F32, 6, psum=True)
        PST = RB(top, nc, 'pst', [128, 1024], BF16, 2, psum=True)
        ident = gt('ident', [128, 128], BF16)
        onesb = gt('onesb', [128, 128], BF16)
        onesf = gt('onesf', [128, 128], F32)
        epsT = gt('epsT', [128, 2], F32)
        stg = RB(top, nc, 'stg', [128, 1024], F32, 2)
        bconst = Buf()
        dummy = gt('dummyj', [128, 4], F32)
        P.join_fn = lambda e: e.memset(dummy[:], 0.0)
        P.op('pool', lambda e: e.memset(onesb[:], 1.0), writes=[bconst])
        P.op('pool', lambda e: e.memset(onesf[:], 1.0), writes=[bconst])
        P.op('pool', lambda e: e.memset(epsT[:, 0:1], EPS), writes=[bconst])
        P.op('pool', lambda e: e.memset(epsT[:, 1:2], 96.0 * EPS), writes=[bconst])
        _castc = [0]

        def cast_engine():
            _castc[0] += 1
            return ('act', 'pool')[_castc[0] % 2]

        def copy_op(eng, out, in_, reads, writes, nowaw=False):
            if eng == 'act':
                P.op('act', lambda e: e.copy(out=out, in_=in_), reads, writes, nowaw=nowaw)
            else:
                P.op(eng, lambda e: e.tensor_copy(out=out, in_=in_), reads, writes, nowaw=nowaw)

        def loadw(dst, src, dbuf, np_=128):
            cols = dst.shape[-1]
            assert len(dst.shape) == 2 and len(src.shape) == 2, (dst.shape, src.shape)
            for c0 in range(0, cols, 1024):
                cw = min(1024, cols - c0)
                t, b = stg.next()
                P.dma('sp', t[0:np_, 0:cw], src[:, c0:c0 + cw], writes=[b])
                copy_op(cast_engine(), dst[:, c0:c0 + cw], t[0:np_, 0:cw], [b], [dbuf.piece() if isinstance(dbuf, WBuf) else dbuf])

        loadw(ident[:], I['c_ident'], bconst)

        def rms_norm(xt, bx, nj, tmp):
            ssq, bs = tmp['ssq'].next()
            junk, bj = tmp['junk'].next()
            for j in range(nj):
                P.op('act', (lambda j: lambda e: e.activation(out=junk[:], in_=xt[:, j, :], func=AF.Square,
                                                               accum_out=ssq[:, j:j + 1]))(j), [bx], [bj, bs])
            P.op('act', lambda e: e.activation(out=ssq[:, 4:4 + nj], in_=ssq[:, 0:nj], func=AF.Sqrt,
                                               bias=epsT[:, 0:1], scale=1.0 / D), [bs, bconst], [bs])
            P.op('dve', lambda e: e.reciprocal(out=ssq[:, 8:8 + nj], in_=ssq[:, 4:4 + nj]), [bs], [bs])
            xn, bn = tmp['xn'].next()
            for j in range(nj):
                P.op('dve', (lambda j: lambda e: e.tensor_scalar(out=xn[:, j, :], in0=xt[:, j, :],
                                                                 scalar1=ssq[:, 8 + j:9 + j], scalar2=None,
                                                                 op0=ALU.mult))(j), [bx, bs], [bn], nowaw=(j > 0))
            return xn, bn

        def rms_tr(xn, bn, nj, gcol, hT, bh, bg):
            for dc in range(8):
                pt, bp = PST.next()
                for j in range(nj):
                    P.op('pe', (lambda j, dc, pt: lambda e: e.transpose(pt[:, j * 128:(j + 1) * 128],
                                                                        xn[:, j, dc * 128:(dc + 1) * 128], ident[:]))(j, dc, pt),
                         [bn, bconst], [bp])
                P.op('act', (lambda dc, pt: lambda e: e.activation(out=hT[:, dc, :], in_=pt[:, 0:nj * 128], func=AF.Copy,
                                                                   scale=gcol[:, dc:dc + 1]))(dc, pt), [bp, bg], [bh], nowaw=(dc > 0))

        def rms_transpose(ph, xt, bx, nj, gcol, hT, bh, tmp, bg):
            xn, bn = rms_norm(xt, bx, nj, tmp)
            rms_tr(xn, bn, nj, gcol, hT, bh, bg)

        def epilogue(pss, xt, bx, j, grow, tmp, bg):
            ssq, bs = tmp['ssq2'].next()
            junk, bj = tmp['junk'].next()
            for h in range(2):
                P.op('act', (lambda h: lambda e: e.activation(out=junk[:, 0:512], in_=pss[h][0][:], func=AF.Square,
                                                               accum_out=ssq[:, h:h + 1]))(h), [pss[h][1]], [bj, bs])
            P.op('dve', lambda e: e.tensor_tensor(out=ssq[:, 2:3], in0=ssq[:, 0:1], in1=ssq[:, 1:2], op=ALU.add), [bs], [bs])
            P.op('act', lambda e: e.activation(out=ssq[:, 3:4], in_=ssq[:, 2:3], func=AF.Sqrt, bias=epsT[:, 0:1],
                                               scale=1.0 / D), [bs, bconst], [bs])
            P.op('dve', lambda e: e.reciprocal(out=ssq[:, 4:5], in_=ssq[:, 3:4]), [bs], [bs])
            for h in range(2):
                t, bt = tmp['ep'].next()
                P.op('dve', (lambda h, t: lambda e: e.scalar_tensor_tensor(out=t[:], in0=pss[h][0][:], scalar=ssq[:, 4:5],
                                                                           in1=grow[:, h * 512:(h + 1) * 512], op0=ALU.mult,
                                                                           op1=ALU.mult))(h, t), [pss[h][1], bs, bg], [bt])
                P.op('pool', (lambda h, t: lambda e: e.tensor_tensor(out=xt[:, j, h * 512:(h + 1) * 512], in0=t[:],
                                                                     in1=xt[:, j, h * 512:(h + 1) * 512], op=ALU.add))(h, t),
                     [bt, bx], [bx])

        def colvec(ph, name, src1d, ncol, bufc):
            t = ph.enter_context(sb(name, [128, ncol], F32))
            P.dma('pool', t[:], src1d.rearrange("(c p) -> p c", p=128), writes=[bufc.piece() if isinstance(bufc, WBuf) else bufc])
            return t

        def rowbc(ph, name, src1d, n, bufc, npart=128):
            t = ph.enter_context(sb(name, [npart, n], F32))
            P.dma('pool', t[:], src1d.rearrange("(o n) -> o n", o=1).partition_broadcast(npart), writes=[bufc.piece() if isinstance(bufc, WBuf) else bufc])
            return t

        def ph1():
            try:
                ph1_()
            except _Stop:
                pass
            P.barrier()

        def ph1_():
            with contextlib.ExitStack() as ph:
                try:
                    ph1_body(ph)
                except _Stop:
                    pass

        def ph1_body(ph):
            if True:
                T = lambda name, shape, dt: ph.enter_context(sb(name, shape, dt))
                bw = WBuf()
                win = T('win', [128, 8, 928], BF16)
                winsw = T('winsw', [128, 8, 32], BF16)
                wuq = T('wuq', [128, 2, 768], BF16)
                wuqsw = T('wuqsw', [128, 2, 768], BF16)
                wuk = T('wuk', [128, 512], BF16)
                wuv = T('wuv', [128, 512], BF16)
                poolw = T('poolw', [128, 4, 128], BF16)
                ewi = I['even_w_in'][0].rearrange("(c p) f -> p c f", p=128)
                for dc in range(8):
                    loadw(win[:, dc, :], ewi[:, dc, :], bw)
                    loadw(winsw[:, dc, 0:16], ewi[:, dc, 912:928], bw)
                    loadw(winsw[:, dc, 16:32], ewi[:, dc, 896:912], bw)
                uq = I['mla_w_uq'][0].rearrange("(c p) f -> p c f", p=128)
                for rc in range(2):
                    loadw(wuq[:, rc, :], uq[:, rc, :], bw)
                    loadw(wuqsw[:, rc, :], uq[:, rc, :], bw)
                    for h in range(8):
                        loadw(wuqsw[:, rc, h * 96 + 64:h * 96 + 80], uq[:, rc, h * 96 + 80:h * 96 + 96], bw)
                        loadw(wuqsw[:, rc, h * 96 + 80:h * 96 + 96], uq[:, rc, h * 96 + 64:h * 96 + 80], bw)
                ukv = I['mla_w_ukv'][0]
                for h in range(8):
                    loadw(wuk[:, h * 64:(h + 1) * 64], ukv[:, h * 128:h * 128 + 64], bw)
                    loadw(wuv[:, h * 64:(h + 1) * 64], ukv[:, h * 128 + 64:h * 128 + 128], bw)
                for g in range(4):
                    loadw(poolw[:, g, :], I['pool_w'][0, g], bw)
                g0col = colvec(ph, 'g0col', I['norm_mix'][0, 0], 8, bw)
                pscol = colvec(ph, 'pscol', I['pool_scale'][0], 4, bw)
                qncol = colvec(ph, 'qncol', I['mla_q_norm'][0], 2, bw)
                kvcol = colvec(ph, 'kvcol', I['mla_kv_norm'][0], 1, bw)
                band = T('band', [128, 4 * 5 * 128], BF16)
                _chk('w')
                tmp = dict(ssq=RB(ph, nc, 'ssq', [128, 12], F32, 2), junk=RB(ph, nc, 'junk', [128, 1024], F32, 1),
                           xn=RB(ph, nc, 'xn', [128, 4, 1024], BF16, 1))
                XT = RB(ph, nc, 'xt', [128, 4, 1024], F32, 1)
                HT = RB(ph, nc, 'hT', [128, 8, 512], BF16, 1)
                atok = T('atok', [128, 64, 512], BF16)
                SQ = RB(ph, nc, 'sq', [128, 3, 512], BF16, 2)
                CG = RB(ph, nc, 'cg', [128, 3, 512], BF16, 2)
                RQ = RB(ph, nc, 'rq', [128, 2, 512], F32, 1)
                RT = RB(ph, nc, 'rt', [96, 4, 512], F32, 1)
                RK = RB(ph, nc, 'rk', [32, 2, 512], F32, 1)
                KT = RB(ph, nc, 'kt', [32, 2, 512], F32, 1)
                KRO = RB(ph, nc, 'kro', [32, 512], BF16, 2)
                QTs = RB(ph, nc, 'qTs', [96, 512], BF16, 3)
                QTm = RB(ph, nc, 'qTm', [96, 2, 512], F32, 1)
                KS = RB(ph, nc, 'ks', [128, 512], BF16, 2)
                VS = RB(ph, nc, 'vs', [128, 512], BF16, 2)
                RC = RB(ph, nc, 'rc', [128, 12], F32, 2)
                RTs = RB(ph, nc, 'rTs', [128, 512], BF16, 2)
                MS = RB(ph, nc, 'ms', [128, 512], BF16, 2)
                for s in SEQ:
                    L = s['L']
                    nt = L // 128
                    batok = [Buf() for _ in range(nt)]
                    bband = Buf()
                    loadw(band[:], I['c_band_' + s['nm']].rearrange("p g k t -> p (g k t)"), bband)
                    for ci in range(L // 512):
                        c0 = ci * 512
                        xt, bx = XT.next()
                        P.dma('sp', xt[:], s['x'][c0:c0 + 512, :].rearrange("(j p) d -> p j d", p=128), writes=[bx])
                        hT, bh = HT.next()
                        rms_transpose(ph, xt, bx, 4, g0col, hT, bh, tmp, bw)
                        _chk('rms')
                        for j in range(4):
                            ps, bp = PS.next()
                            for dc in range(8):
                                P.op('pe', (lambda j, dc, ps: lambda e: e.matmul(ps[:], lhsT=hT[:, dc, j * 128:(j + 1) * 128],
                                                                                 rhs=win[:, dc, 0:512], start=(dc == 0), stop=(dc == 7)))(j, dc, ps),
                                     [bh, bw], [bp])
                            copy_op('act', atok[:, ci * 4 + j, :], ps[:], [bp], [batok[ci * 4 + j]])
                        _chk('atok')
                        sq, bsq = SQ.next()
                        cg, bcg = CG.next()
                        for k3, (lo, ncol) in enumerate(((512, qncol[:, 0:1]), (640, qncol[:, 1:2]), (768, kvcol[:, 0:1]))):
                            ps, bp = PS.next()
                            for dc in range(8):
                                P.op('pe', (lambda dc, ps, lo: lambda e: e.matmul(ps[:], lhsT=win[:, dc, lo:lo + 128], rhs=hT[:, dc, :],
                                                                                  start=(dc == 0), stop=(dc == 7)))(dc, ps, lo), [bh, bw], [bp])
                            P.op('act', (lambda k3, ps: lambda e: e.activation(out=sq[:, k3, :], in_=ps[:], func=AF.Square))(k3, ps), [bp], [bsq])
                            P.op('dve', (lambda k3, ps, ncol: lambda e: e.tensor_scalar(out=cg[:, k3, :], in0=ps[:], scalar1=ncol, scalar2=None,
                                                                                        op0=ALU.mult))(k3, ps, ncol), [bp, bw, bsq], [bcg])
                        _chk('lat')
                        rk, brk = RK.next()
                        P.dma('pool', rk[:, 0, :], I['c_ropeC'][:, c0:c0 + 512], writes=[brk])
                        P.dma('pool', rk[:, 1, :], I['c_ropeS'][:, c0:c0 + 512], writes=[brk])
                        kt, bkt = KT.next()
                        for k2, (wt, lo) in enumerate(((win, 896), (winsw, 0))):
                            ps, bp = PS.next()
                            for dc in range(8):
                                P.op('pe', (lambda dc, ps, wt, lo: lambda e: e.matmul(ps[0:32, :], lhsT=wt[:, dc, lo:lo + 32], rhs=hT[:, dc, :],
                                                                                      start=(dc == 0), stop=(dc == 7)))(dc, ps, wt, lo), [bh, bw], [bp])
                            P.op('dve', (lambda k2, ps: lambda e: e.tensor_tensor(out=kt[:, k2, :], in0=ps[0:32, :], in1=rk[:, k2, :],
                                                                                  op=ALU.mult))(k2, ps), [bp, brk], [bkt])
                        kro, bkro = KRO.next()
                        P.op('pool', lambda e, kt=kt, kro=kro: e.tensor_tensor(out=kro[:], in0=kt[:, 0, :], in1=kt[:, 1, :], op=ALU.add), [bkt], [bkro])
                        P.dma('pool', s['KR'][:, c0:c0 + 512], kro[:], reads=[bkro], writes=[s['b_KR']])
                        _chk('krope')
                        rq, brq = RQ.next()
                        ps, bp = PS.next()
                        for rc in range(2):
                            P.op('pe', (lambda rc, ps: lambda e: e.matmul(ps[:], lhsT=onesb[:], rhs=sq[:, rc, :], start=(rc == 0), stop=(rc == 1)))(rc, ps),
                                 [bsq, bconst], [bp])
                        P.op('act', lambda e, ps=ps, rq=rq: e.activation(out=rq[:, 0, :], in_=ps[:], func=AF.Sqrt, bias=epsT[:, 1:2], scale=96.0 / 256.0),
                             [bp, bconst], [brq])
                        P.op('dve', lambda e, rq=rq: e.reciprocal(out=rq[:, 0, :], in_=rq[:, 0, :]), [brq], [brq])
                        ps, bp = PS.next()
                        P.op('pe', lambda e, ps=ps, sq=sq: e.matmul(ps[:], lhsT=onesb[:], rhs=sq[:, 2, :], start=True, stop=True), [bsq, bconst], [bp])
                        P.op('act', lambda e, ps=ps, rq=rq: e.activation(out=rq[:, 1, :], in_=ps[:], func=AF.Sqrt, bias=epsT[:, 0:1], scale=1.0 / 128.0),
                             [bp, bconst], [brq])
                        P.op('dve', lambda e, rq=rq: e.reciprocal(out=rq[:, 1, :], in_=rq[:, 1, :]), [brq], [brq])
                        rc_, brc = RC.next()
                        ps, bp = PS.next()
                        for j in range(4):
                            P.op('pe', (lambda j, ps: lambda e: e.matmul(ps[:, j:j + 1], lhsT=sq[:, 2, j * 128:(j + 1) * 128], rhs=onesb[:, 0:1],
                                                                         start=True, stop=True))(j, ps), [bsq, bconst], [bp])
                        P.op('act', lambda e, ps=ps, rc_=rc_: e.activation(out=rc_[:, 0:4], in_=ps[:, 0:4], func=AF.Sqrt, bias=epsT[:, 0:1], scale=1.0 / 128.0),
                             [bp, bconst], [brc])
                        P.op('dve', lambda e, rc_=rc_: e.reciprocal(out=rc_[:, 4:8], in_=rc_[:, 0:4]), [brc], [brc])
                        rt, brt = RT.next()
                        P.dma('pool', rt[64:96, 0, :], I['c_ropeC'][:, c0:c0 + 512], writes=[brt])
                        P.dma('pool', rt[64:96, 1, :], I['c_ropeS'][:, c0:c0 + 512], writes=[brt])
                        for k2 in range(2):
                            P.op('pool', (lambda k2: lambda e, rt=rt, rq=rq: e.tensor_tensor(out=rt[64:96, 2 + k2, :], in0=rt[64:96, k2, :],
                                                                                             in1=rq[64:96, 0, :], op=ALU.mult))(k2), [brt, brq], [brt])
                        _chk('rstd')
                        for h in range(8):
                            pq, bpq = PS.next()
                            pw, bpw = PS.next()
                            for rc in range(2):
                                P.op('pe', (lambda rc, pq, h: lambda e: e.matmul(pq[0:96, :], lhsT=wuq[:, rc, h * 96:(h + 1) * 96], rhs=cg[:, rc, :],
                                                                                 start=(rc == 0), stop=(rc == 1)))(rc, pq, h), [bcg, bw], [bpq])
                            for rc in range(2):
                                P.op('pe', (lambda rc, pw, h: lambda e: e.matmul(pw[0:96, :], lhsT=wuqsw[:, rc, h * 96:(h + 1) * 96], rhs=cg[:, rc, :],
                                                                                 start=(rc == 0), stop=(rc == 1)))(rc, pw, h), [bcg, bw], [bpw])
                            qs, bqs = QTs.next()
                            qm, bqm = QTm.next()
                            P.op('dve', lambda e, qs=qs, pq=pq, rq=rq: e.tensor_tensor(out=qs[0:64, :], in0=pq[0:64, :], in1=rq[0:64, 0, :], op=ALU.mult),
                                 [bpq, brq], [bqs])
                            P.op('dve', lambda e, qm=qm, pq=pq, rt=rt: e.tensor_tensor(out=qm[64:96, 0, :], in0=pq[64:96, :], in1=rt[64:96, 2, :], op=ALU.mult),
                                 [bpq, brt], [bqm])
                            P.op('dve', lambda e, qm=qm, pw=pw, rt=rt: e.tensor_tensor(out=qm[64:96, 1, :], in0=pw[64:96, :], in1=rt[64:96, 3, :], op=ALU.mult),
                                 [bpw, brt], [bqm])
                            P.op('pool', lambda e, qs=qs, qm=qm: e.tensor_tensor(out=qs[64:96, :], in0=qm[64:96, 0, :], in1=qm[64:96, 1, :], op=ALU.add),
                                 [bqm], [bqs])
                            P.dma('sp', s['QT'][h, :, c0:c0 + 512], qs[:], reads=[bqs], writes=[s['b_QT']])
                        _chk('q')
                        for hp in range(4):
                            ps, bp = PS.next()
                            P.op('pe', lambda e, ps=ps, hp=hp, cg=cg: e.matmul(ps[:], lhsT=wuk[:, hp * 128:(hp + 1) * 128], rhs=cg[:, 2, :], start=True, stop=True),
                                 [bcg, bw], [bp])
                            ks, bks = KS.next()
                            P.op('dve', lambda e, ks=ks, ps=ps, rq=rq: e.tensor_tensor(out=ks[:], in0=ps[:], in1=rq[:, 1, :], op=ALU.mult), [bp, brq], [bks])
                            P.dma('sp', s['KN'][hp * 128:(hp + 1) * 128, c0:c0 + 512], ks[:], reads=[bks], writes=[s['b_KN']])
                        _chk('k')
                        for j in range(4):
                            ps, bp = PS.next()
                            P.op('pe', lambda e, ps=ps, j=j, cg=cg: e.matmul(ps[:], lhsT=cg[:, 2, j * 128:(j + 1) * 128], rhs=wuv[:], start=True, stop=True),
                                 [bcg, bw], [bp])
                            vs, bvs = VS.next()
                            P.op('act', lambda e, vs=vs, ps=ps, rc_=rc_, j=j: e.activation(out=vs[:], in_=ps[:], func=AF.Copy, scale=rc_[:, 4 + j:5 + j]),
                                 [bp, brc], [bvs])
                            P.dma('sp', s['V'][:, :, ci * 4 + j, :].rearrange("h p d -> p h d"), vs[:].rearrange("p (h d) -> p h d", h=8), reads=[bvs], writes=[s['b_V']])
                    _chk('v')
                    for sp in range(L // 512):
                        for g in range(4):
                            ps, bp = PS.next()
                            for jt in range(4):
                                t = sp * 4 + jt
                                terms = []
                                if t > 0:
                                    terms.append((t - 1, 1))
                                terms.append((t, 3 if t == 0 else (4 if t == nt - 1 else 0)))
                                if t < nt - 1:
                                    terms.append((t + 1, 2))
                                for k, (st_, kind) in enumerate(terms):
                                    off = (g * 5 + kind) * 128
                                    P.op('pe', lambda e, ps=ps, jt=jt, st_=st_, g=g, off=off, k=k, n=len(terms): e.matmul(
                                        ps[:, jt * 128:(jt + 1) * 128], lhsT=atok[:, st_, g * 128:(g + 1) * 128], rhs=band[:, off:off + 128],
                                        start=(k == 0), stop=(k == n - 1)), [batok[st_], bband], [bp])
                            rts, brts = RTs.next()
                            copy_op('dve', rts[:], ps[:], [bp], [brts])
                            ps2, bp2 = PS.next()
                            P.op('pe', lambda e, ps2=ps2, g=g, rts=rts: e.matmul(ps2[:], lhsT=poolw[:, g, :], rhs=rts[:], start=True, stop=True), [brts, bw], [bp2])
                            ms, bms = MS.next()
                            P.op('act', lambda e, ms=ms, ps2=ps2, g=g: e.activation(out=ms[:], in_=ps2[:], func=AF.Copy, scale=pscol[:, g:g + 1]), [bp2, bw], [bms])
                            P.dma('sp', s['mixT'][g * 128:(g + 1) * 128, sp * 512:(sp + 1) * 512], ms[:], reads=[bms], writes=[s['b_mixT']])
            P.barrier()

        def ph2():
            with contextlib.ExitStack() as ph:
                KT = RB(ph, nc, 'aK', [96, LS], BF16, 2)
                QT = RB(ph, nc, 'aQ', [96, LS], BF16, 2)
                VT = RB(ph, nc, 'aV', [128, LS // 128, 128], BF16, 2)
                PT = RB(ph, nc, 'aP', [128, 512], BF16, 3)
                RS = RB(ph, nc, 'aR', [128, 512], F32, 2)
                BC = RB(ph, nc, 'aB', [64, 512], F32, 2)
                OS = RB(ph, nc, 'aO', [64, 512], BF16, 2)
                for i in range(2):
                    P.op('pool', lambda e, i=i: e.memset(VT.t[i][:, :, 64:128], 1.0), writes=[VT.b[i]])

                class _Sub:
                    def __init__(self, lo, hi):
                        self.t = PS.t[lo:hi]
                        self.b = PS.b[lo:hi]
                        self.i = 0
                    next = RB.next
                POs = _Sub(0, 2)
                PSs = _Sub(2, 6)
                for s in SEQ:
                    L = s['L']
                    nk = L // 128
                    for h in range(8):
                        kt, bk = KT.next()
                        qt, bq = QT.next()
                        vt, bv = VT.next()
                        P.dma('sp', kt[0:64, 0:L], s['KN'][h * 64:(h + 1) * 64, :], reads=[s['b_KN']], writes=[bk])
                        P.dma('sp', kt[64:96, 0:L], s['KR'][:, :], reads=[s['b_KR']], writes=[bk])
                        P.dma('sp', qt[:, 0:L], s['QT'][h], reads=[s['b_QT']], writes=[bq])
                        P.dma('pool', vt[:, 0:nk, 0:64], s['V'][h],
                              reads=[s['b_V']], writes=[bv])
                        for qc in range(L // 512):
                            po, bpo = POs.next()
                            pend = []

                            def issue_s(k, kt=kt, qt=qt, qc=qc):
                                pss, bps = PSs.next()
                                P.op('pe', lambda e: e.matmul(pss[:], lhsT=kt[:, k * 128:(k + 1) * 128], rhs=qt[:, qc * 512:(qc + 1) * 512], start=True, stop=True),
                                     [bk, bq], [bps])
                                pend.append((pss, bps))
                            for k in range(min(2, nk)):
                                issue_s(k)
                            for k in range(nk):
                                if k + 2 < nk:
                                    issue_s(k + 2)
                                pss, bps = pend.pop(0)
                                pt, bpt = PT.next()
                                P.op('act', lambda e, pt=pt, pss=pss: e.activation(out=pt[:], in_=pss[:], func=AF.Exp), [bps], [bpt])
                                P.op('pe', lambda e, po=po, vt=vt, pt=pt, k=k, nk=nk: e.matmul(po[:], lhsT=vt[:, k, :], rhs=pt[:], start=(k == 0), stop=(k == nk - 1)),
                                     [bv, bpt], [bpo])
                            rs, brs = RS.next()
                            P.op('dve', lambda e, rs=rs, po=po: e.reciprocal(out=rs[64:65, :], in_=po[64:65, :]), [bpo], [brs])
                            pb, bpb = PSs.next()
                            P.op('pe', lambda e, pb=pb, rs=rs: e.matmul(pb[0:64, :], lhsT=onesf[64:65, 0:64], rhs=rs[64:65, :], start=True, stop=True),
                                 [brs, bconst], [bpb])
                            bc, bbc = BC.next()
                            copy_op('act', bc[:], pb[0:64, :], [bpb], [bbc])
                            os_, bos = OS.next()
                            P.op('dve', lambda e, os_=os_, po=po, bc=bc: e.tensor_tensor(out=os_[:], in0=po[0:64, :], in1=bc[:], op=ALU.mult), [bpo, bbc], [bos])
                            P.dma('pool', s['mixT'][512 + h * 64:512 + (h + 1) * 64, qc * 512:(qc + 1) * 512], os_[:], reads=[bos], writes=[s['b_mixT']])
            P.barrier()

        def ph_out(layer):
            with contextlib.ExitStack() as ph:
                T = lambda name, shape, dt: ph.enter_context(sb(name, shape, dt))
                bw = WBuf()
                wout = T('wout', [128, 8, 1024], BF16)
                wsrc = (I['even_w_out'] if layer == 0 else I['odd_w_out'])[0].rearrange("(c p) f -> p c f", p=128)
                for c in range(8):
                    loadw(wout[:, c, :], wsrc[:, c, :], bw)
                grow = rowbc(ph, 'grow', I['norm_mix'][layer, 1], D, bw)
                if layer == 1:
                    fwb = T('fwb', [128, 2, 128], BF16)
                    bfw = Buf()
                    P.op('pool', lambda e: e.memset(fwb[:], 0.0), writes=[bfw])
                    for g in range(4):
                        pp = (g % 2) * 64
                        loadw(fwb[pp:pp + 64, g // 2, pp:pp + 64], I['fnet_w'][0, g], bfw, np_=64)
                tmp = dict(ssq2=RB(ph, nc, 'ssq2', [128, 8], F32, 2), junk=RB(ph, nc, 'junk', [128, 1024], F32, 1),
                           ep=RB(ph, nc, 'ep', [128, 512], F32, 3))
                XT = RB(ph, nc, 'xt', [128, 4, 1024], F32, 2)
                MX = RB(ph, nc, 'mx', [128, 8, 512], BF16, 2)
                MF = RB(ph, nc, 'mf', [128, 2, 512], BF16, 2)
                for s in SEQ:
                    L = s['L']
                    xin, bxin = (s['x'], Buf()) if layer == 0 else (s['X2'], s['b_X2'])
                    xout, bxout = (s['X1'], s['b_X1']) if layer == 0 else (s['X3'], s['b_X3'])
                    for ci in range(L // 512):
                        c0 = ci * 512
                        xt, bx = XT.next()
                        P.dma('sp', xt[:], xin[c0:c0 + 512, :].rearrange("(j p) d -> p j d", p=128), reads=[bxin], writes=[bx])
                        mx, bm = MX.next()
                        if layer == 0:
                            P.dma('pool', mx[:], s['mixT'][:, c0:c0 + 512].rearrange("(c p) t -> p c t", p=128), reads=[s['b_mixT']], writes=[bm])
                        else:
                            P.dma('pool', mx[:, 0:6, :], s['mix2'][:, c0:c0 + 512].rearrange("(c p) t -> p c t", p=128), reads=[s['b_mix2']], writes=[bm])
                            mf, bmf = MF.next()
                            P.dma('pool', mf[:], s['mfT'][:, c0:c0 + 512].rearrange("(c p) t -> p c t", p=128), reads=[s['b_mfT']], writes=[bmf])
                            for j2 in range(2):
                                ps, bp = PS.next()
                                P.op('pe', lambda e, ps=ps, j2=j2, mf=mf: e.matmul(ps[:], lhsT=fwb[:, j2, :], rhs=mf[:, j2, :], start=True, stop=True), [bmf, bfw], [bp])
                                copy_op('act', mx[:, 6 + j2, :], ps[:], [bp], [bm])
                        for j in range(4):
                            pss = []
                            for hf in range(2):
                                ps, bp = PS.next()
                                for c in range(8):
                                    P.op('pe', lambda e, ps=ps, c=c, j=j, hf=hf, mx=mx: e.matmul(ps[:], lhsT=mx[:, c, j * 128:(j + 1) * 128],
                                                                                                 rhs=wout[:, c, hf * 512:(hf + 1) * 512], start=(c == 0), stop=(c == 7)),
                                         [bm, bw], [bp])
                                pss.append((ps, bp))
                            epilogue(pss, xt, bx, j, grow, tmp, bw)
                        P.dma('sp', xout[c0:c0 + 512, :].rearrange("(j p) d -> p j d", p=128), xt[:], reads=[bx], writes=[bxout])
            P.barrier()

        def ph_mlp(layer):
            TT = 256
            nj = TT // 128
            with contextlib.ExitStack() as ph:
                T = lambda name, shape, dt: ph.enter_context(sb(name, shape, dt))
                bw = WBuf()
                wfi = T('wfi', [128, 8, 4096], BF16)
                wfo = T('wfo', [128, 32, 1024], BF16)
                s1 = I['w_ff_in'][layer].rearrange("(c p) f -> p c f", p=128)
                s2 = I['w_ff_out'][layer].rearrange("(c p) f -> p c f", p=128)
                for c in range(8):
                    loadw(wfi[:, c, :], s1[:, c, :], bw)
                for c in range(32):
                    loadw(wfo[:, c, :], s2[:, c, :], bw)
                gcol = colvec(ph, 'gcol', I['norm_mlp'][layer, 0], 8, bw)
                grow = rowbc(ph, 'grow', I['norm_mlp'][layer, 1], D, bw)
                tmp = dict(ssq=RB(ph, nc, 'ssq', [128, 12], F32, 2), ssq2=RB(ph, nc, 'ssq2', [128, 8], F32, 2),
                           junk=RB(ph, nc, 'junk', [128, 1024], F32, 1), xn=RB(ph, nc, 'xn', [128, nj, 1024], BF16, 1),
                           ep=RB(ph, nc, 'ep', [128, 512], F32, 2))
                XT = RB(ph, nc, 'xt', [128, nj, 1024], F32, 2)
                HT = RB(ph, nc, 'hT', [128, 8, TT], BF16, 2)
                F1 = RB(ph, nc, 'f1', [128, 32, TT], BF16, 1)
                RL = RB(ph, nc, 'rl', [128, TT], F32, 3)
                for s in SEQ:
                    L = s['L']
                    xin, bxin = (s['X1'], s['b_X1']) if layer == 0 else (s['X3'], s['b_X3'])
                    xout, bxout = (s['X2'], s['b_X2']) if layer == 0 else (s['y'], Buf())
                    nchunk = L // TT

                    def load_norm(ci):
                        c0 = ci * TT
                        xt, bx = XT.next()
                        P.dma('sp', xt[:], xin[c0:c0 + TT, :].rearrange("(j p) d -> p j d", p=128), reads=[bxin], writes=[bx])
                        xn, bn = rms_norm(xt, bx, nj, tmp)
                        return xt, bx, xn, bn
                    cur = load_norm(0)
                    hT, bh = HT.next()
                    rms_tr(cur[2], cur[3], nj, gcol, hT, bh, bw)
                    for ci in range(nchunk):
                        c0 = ci * TT
                        xt, bx = cur[0], cur[1]
                        f1, bf1 = F1.next()
                        nxt = None
                        for fc in range(32):
                            ps, bp = PS.next()
                            for dc in range(8):
                                P.op('pe', lambda e, ps=ps, dc=dc, fc=fc, hT=hT: e.matmul(ps[:, 0:TT], lhsT=wfi[:, dc, fc * 128:(fc + 1) * 128], rhs=hT[:, dc, :],
                                                                                          start=(dc == 0), stop=(dc == 7)), [bh, bw], [bp])
                            rl, brl = RL.next()
                            P.op('act', lambda e, rl=rl, ps=ps: e.activation(out=rl[:], in_=ps[:, 0:TT], func=AF.Relu), [bp], [brl])
                            eng = 'pool' if fc % 2 else 'dve'
                            P.op(eng, lambda e, rl=rl, f1=f1, fc=fc: e.tensor_tensor(out=f1[:, fc, :], in0=rl[:], in1=rl[:], op=ALU.mult), [brl], [bf1], nowaw=(fc > 1))
                            if fc == 6 and ci + 1 < nchunk:
                                nxt = load_norm(ci + 1)
                        if nxt is not None:
                            hTn, bhn = HT.next()
                            rms_tr(nxt[2], nxt[3], nj, gcol, hTn, bhn, bw)
                        for j in range(nj):
                            pss = []
                            for hf in range(2):
                                ps, bp = PS.next()
                                for fc in range(32):
                                    P.op('pe', lambda e, ps=ps, fc=fc, j=j, hf=hf, f1=f1: e.matmul(ps[:], lhsT=f1[:, fc, j * 128:(j + 1) * 128],
                                                                                                   rhs=wfo[:, fc, hf * 512:(hf + 1) * 512], start=(fc == 0), stop=(fc == 31)),
                                         [bf1, bw], [bp])
                                pss.append((ps, bp))
                            epilogue(pss, xt, bx, j, grow, tmp, bw)
                        P.dma('pool', xout[c0:c0 + TT, :].rearrange("(j p) d -> p j d", p=128), xt[:], reads=[bx], writes=[bxout])
                        if nxt is not None:
                            cur = nxt
                            hT, bh = hTn, bhn
            P.barrier()

        def ph5():
            with contextlib.ExitStack() as ph:
                T = lambda name, shape, dt: ph.enter_context(sb(name, shape, dt))
                bw = WBuf()
                owin = T('owin', [128, 8, 2560], BF16)
                src = I['odd_w_in'][0].rearrange("(c p) f -> p c f", p=128)
                for c in range(8):
                    loadw(owin[:, c, :], src[:, c, :], bw)
                gcol = colvec(ph, 'gcol', I['norm_mix'][1, 0], 8, bw)
                swc = T('swc', [128, 3, 18], F32)
                for j in range(3):
                    P.dma('pool', swc[:, j, :], I['short_w'][0, j].rearrange("(c p) -> p c", p=128), writes=[bw])
                sbc = colvec(ph, 'sbc', I['short_b'][0], 18, bw)
                DCt = T('DCt', [128, 2, 128], F32)
                loadw(DCt[:, 0, :], I['c_DC'], bw)
                loadw(DCt[:, 1, :], I['c_DS'], bw)
                tmp = dict(ssq=RB(ph, nc, 'ssq', [128, 12], F32, 2), junk=RB(ph, nc, 'junk', [128, 1024], F32, 1),
                           xn=RB(ph, nc, 'xn', [128, 4, 1024], BF16, 1))
                XT = RB(ph, nc, 'xt', [128, 4, 1024], F32, 2)
                HT = RB(ph, nc, 'hT', [128, 8, 512], BF16, 2)
                WN = RB(ph, nc, 'wn', [128, 514], F32, 3)
                OT = RB(ph, nc, 'ot', [128, 512], F32, 3)
                FS = RB(ph, nc, 'fs', [128, 512], F32, 2)
                FO = RB(ph, nc, 'fo', [128, 512], F32, 3)
                carry = T('carry', [128, 18, 2], F32)
                bcar = Buf()
                for s in SEQ:
                    L = s['L']
                    fsc = 1.0 / math.sqrt(64.0 * L)
                    P.op('pool', lambda e: e.memset(carry[:], 0.0), writes=[bcar])
                    nch = L // 512
                    for ci in range(nch + 1):
                        c0 = ci * 512
                        last = (ci == nch)
                        if not last:
                            xt, bx = XT.next()
                            P.dma('sp', xt[:], s['X2'][c0:c0 + 512, :].rearrange("(j p) d -> p j d", p=128), reads=[s['b_X2']], writes=[bx])
                            hT, bh = HT.next()
                            rms_transpose(ph, xt, bx, 4, gcol, hT, bh, tmp, bw)
                        for fc in range(18):
                            wn, bwn = WN.next()
                            P.op('pool', lambda e, wn=wn, fc=fc: e.tensor_copy(out=wn[:, 0:2], in_=carry[:, fc, :]), [bcar], [bwn])
                            if not last:
                                ps, bp = PS.next()
                                for dc in range(8):
                                    P.op('pe', lambda e, ps=ps, dc=dc, fc=fc, hT=hT: e.matmul(ps[:], lhsT=owin[:, dc, fc * 128:(fc + 1) * 128], rhs=hT[:, dc, :],
                                                                                              start=(dc == 0), stop=(dc == 7)), [bh, bw], [bp])
                                copy_op('act', wn[:, 2:514], ps[:], [bp], [bwn])
                                nw = 512
                            else:
                                P.op('pool', lambda e, wn=wn: e.memset(wn[:, 2:3], 0.0), [], [bwn])
                                nw = 1
                            ot, bo = OT.next()
                            P.op('dve', lambda e, ot=ot, wn=wn, fc=fc, nw=nw: e.tensor_scalar(out=ot[:, 0:nw], in0=wn[:, 0:nw], scalar1=swc[:, 0, fc:fc + 1],
                                                                                              scalar2=sbc[:, fc:fc + 1], op0=ALU.mult, op1=ALU.add), [bwn, bw], [bo])
                            for j in (1, 2):
                                P.op('dve', lambda e, ot=ot, wn=wn, fc=fc, nw=nw, j=j: e.scalar_tensor_tensor(out=ot[:, 0:nw], in0=wn[:, j:j + nw], scalar=swc[:, j, fc:fc + 1],
                                                                                                              in1=ot[:, 0:nw], op0=ALU.mult, op1=ALU.add), [bwn, bw, bo], [bo])
                            if not last:
                                P.op('pool', lambda e, wn=wn, fc=fc: e.tensor_copy(out=carry[:, fc, :], in_=wn[:, 512:514]), [bwn], [bcar])
                            if ci == 0:
                                P.dma('sp', s['zcT'][fc * 128:(fc + 1) * 128, 0:511], ot[:, 1:512], reads=[bo], writes=[s['b_zcT']])
                            elif not last:
                                P.dma('sp', s['zcT'][fc * 128:(fc + 1) * 128, c0 - 1:c0 + 511], ot[:, 0:512], reads=[bo], writes=[s['b_zcT']])
                            else:
                                P.dma('sp', s['zcT'][fc * 128:(fc + 1) * 128, L - 1:L], ot[:, 0:1], reads=[bo], writes=[s['b_zcT']])
                        if last:
                            continue
                        for f2 in range(2):
                            ps, bp = PS.next()
                            for dc in range(8):
                                P.op('pe', lambda e, ps=ps, dc=dc, f2=f2, hT=hT: e.matmul(ps[:], lhsT=owin[:, dc, 2304 + f2 * 128:2304 + (f2 + 1) * 128], rhs=hT[:, dc, :],
                                                                                          start=(dc == 0), stop=(dc == 7)), [bh, bw], [bp])
                            fs, bfs = FS.next()
                            copy_op('act', fs[:], ps[:], [bp], [bfs])
                            for k2 in range(2):
                                ps2, bp2 = PS.next()
                                P.op('pe', lambda e, ps2=ps2, k2=k2, fs=fs: e.matmul(ps2[:], lhsT=DCt[:, k2, :], rhs=fs[:], start=True, stop=True), [bfs, bw], [bp2])
                                fo, bfo = FO.next()
                                P.op('act', lambda e, fo=fo, ps2=ps2: e.activation(out=fo[:], in_=ps2[:], func=AF.Copy, scale=fsc), [bp2], [bfo])
                                r0 = k2 * 256 + f2 * 128
                                P.dma('pool', s['fcT'][r0:r0 + 128, c0:c0 + 512], fo[:], reads=[bfo], writes=[s['b_fcT']])
            P.barrier()

        def ph6():
            PI = math.pi
            with contextlib.ExitStack() as ph:
                T = lambda name, shape, dt: ph.enter_context(sb(name, shape, dt))
                bw = Buf()
                w1 = T('fw1', [33, 64], F32)
                w2 = T('fw2', [64, 64], F32)
                w3 = T('fw3', [64, 3072], F32)
                P.dma('pool', w1[:], I['filt_w1'][0], writes=[bw])
                P.dma('pool', w2[:], I['filt_w2'][0], writes=[bw])
                P.dma('pool', w3[:], I['filt_w3'][0], writes=[bw])
                cols = T('fcols', [64, 8], F32)
                for k, nm_ in enumerate(('filt_b1', 'filt_b2', 'filt_freq')):
                    P.dma('pool', cols[:, k:k + 1], I[nm_][0].rearrange("(p o) -> p o", o=1), writes=[bw])
                P.op('dve', lambda e: e.tensor_tensor(out=cols[:, 3:4], in0=cols[:, 2:3], in1=cols[:, 0:1], op=ALU.mult), [bw], [bw])
                P.op('dve', lambda e: e.tensor_tensor(out=cols[:, 4:5], in0=cols[:, 2:3], in1=cols[:, 1:2], op=ALU.mult), [bw], [bw])
                negd = T('negd', [128, 6], F32)
                P.dma('pool', negd[:], I['c_negd'], writes=[bw])
                FT = RB(ph, nc, 'ft', [33, 512], F32, 2)
                TB = RB(ph, nc, 'tb', [128, 512], F32, 2)
                DT = RB(ph, nc, 'dt', [128, 6, 512], F32, 2)
                AR = RB(ph, nc, 'ar', [64, 3, 512], F32, 2)
                H1 = RB(ph, nc, 'h1', [64, 512], F32, 2)
                H2 = RB(ph, nc, 'h2', [64, 512], F32, 2)
                HF = RB(ph, nc, 'hf', [128, 512], F32, 3)
                junk = T('fjunk', [128, 512], F32)
                bjunk = Buf()
                ssq = T('fssq', [128, 24, 16], F32)
                tot = T('ftot', [128, 3, 24], F32)
                bss = Buf()

                def sinlayer(ps, bp, fb, out, bo):
                    ar, ba = AR.next()
                    P.op('dve', lambda e: e.tensor_scalar(out=ar[:, 0, :], in0=ps[0:64, :], scalar1=cols[:, 2:3], scalar2=cols[:, fb:fb + 1],
                                                          op0=ALU.mult, op1=ALU.add), [bp, bw], [ba])
                    P.op('dve', lambda e: e.tensor_scalar(out=ar[:, 1, :], in0=ar[:, 0, :], scalar1=PI, scalar2=2 * PI, op0=ALU.is_gt, op1=ALU.mult), [ba], [ba])
                    P.op('dve', lambda e: e.tensor_tensor(out=ar[:, 0, :], in0=ar[:, 0, :], in1=ar[:, 1, :], op=ALU.subtract), [ba], [ba])
                    P.op('dve', lambda e: e.tensor_scalar(out=ar[:, 1, :], in0=ar[:, 0, :], scalar1=-PI, scalar2=2 * PI, op0=ALU.is_lt, op1=ALU.mult), [ba], [ba])
                    P.op('dve', lambda e: e.tensor_tensor(out=ar[:, 0, :], in0=ar[:, 0, :], in1=ar[:, 1, :], op=ALU.add), [ba], [ba])
                    P.op('act', lambda e: e.activation(out=out[:], in_=ar[:, 0, :], func=AF.Sin, scale=0.999999), [ba], [bo])

                for s in SEQ:
                    L = s['L']
                    nch = L // 512
                    for ci in range(nch):
                        c0 = ci * 512
                        ft, bft = FT.next()
                        P.dma('sp', ft[:], I['c_featT_' + s['nm']][:, c0:c0 + 512], writes=[bft])
                        tb, btb = TB.next()
                        P.dma('sp', tb[:], I['c_trow_' + s['nm']][:, c0:c0 + 512].partition_broadcast(128), writes=[btb])
                        dt_, bdt = DT.next()
                        for k in range(6):
                            P.op('act', lambda e, k=k, dt_=dt_, tb=tb: e.activation(out=dt_[:, k, :], in_=tb[:], func=AF.Exp, scale=negd[:, k:k + 1]), [btb, bw], [bdt])
                        ps, bp = PS.next()
                        P.op('pe', lambda e, ps=ps, ft=ft: e.matmul(ps[0:64, :], lhsT=w1[:], rhs=ft[:], start=True, stop=True), [bft, bw], [bp])
                        h1, bh1 = H1.next()
                        sinlayer(ps, bp, 3, h1, bh1)
                        ps, bp = PS.next()
                        P.op('pe', lambda e, ps=ps, h1=h1: e.matmul(ps[0:64, :], lhsT=w2[:], rhs=h1[:], start=True, stop=True), [bh1, bw], [bp])
                        h2, bh2 = H2.next()
                        sinlayer(ps, bp, 4, h2, bh2)
                        for jc in range(24):
                            ps, bp = PS.next()
                            P.op('pe', lambda e, ps=ps, jc=jc, h2=h2: e.matmul(ps[:], lhsT=w3[:, jc * 128:(jc + 1) * 128], rhs=h2[:], start=True, stop=True), [bh2, bw], [bp])
                            hf, bhf = HF.next()
                            P.op('dve', lambda e, hf=hf, ps=ps, dt_=dt_, jc=jc: e.tensor_tensor(out=hf[:], in0=ps[:], in1=dt_[:, jc % 6, :], op=ALU.mult), [bp, bdt], [bhf])
                            P.op('act', lambda e, hf=hf, jc=jc, ci=ci: e.activation(out=junk[:], in_=hf[:], func=AF.Square, accum_out=ssq[:, jc, ci:ci + 1]), [bhf], [bjunk, bss])
                            P.dma('sp', s['hfT'][jc * 128:(jc + 1) * 128, c0:c0 + 512], hf[:], reads=[bhf], writes=[s['b_hfT']])
                    P.op('dve', lambda e, nch=nch: e.tensor_reduce(out=tot[:, 0, :], in_=ssq[:, :, 0:nch], axis=AX.X, op=ALU.add), [bss], [bss])
                    P.op('act', lambda e: e.activation(out=tot[:, 1, :], in_=tot[:, 0, :], func=AF.Sqrt, bias=epsT[:, 0:1], scale=1.0), [bss, bconst], [bss])
                    P.op('dve', lambda e: e.reciprocal(out=tot[:, 2, :], in_=tot[:, 1, :]), [bss], [bss])
                    P.dma('sp', s['invd'][:, 0:24], tot[:, 2, :], reads=[bss], writes=[s['b_invd']])
            P.barrier()

        def ph7():
            CB = 4
            with contextlib.ExitStack() as ph:
                T = lambda name, shape, dt: ph.enter_context(sb(name, shape, dt))
                bw = WBuf()
                F1 = T('cF1', [64, 128], BF16)
                loadw(F1[:], I['c_F1'], bw, np_=64)
                F1fA = T('cF1fA', [64, 128], BF16)
                F1fB = T('cF1fB', [64, 128], BF16)
                loadw(F1fA[:], I['c_F1fA'], bw, np_=64)
                loadw(F1fB[:], I['c_F1fB'], bw, np_=64)
                hbb = rowbc(ph, 'hbb', I['hyena_bias'][0].rearrange("o c -> (o c)"), 1536, bw, npart=64)
                for s in SEQ:
                    L, N2r, nm = s['L'], s['N2'], s['nm']
                    N2 = 128
                    PK = 128 // N2r
                    CBc = CB * PK
                    G = 3
                    with contextlib.ExitStack() as p2:
                        T2 = lambda name, shape, dt: p2.enter_context(sb(name + nm, shape, dt))
                        bc_ = Buf()
                        G1re = T2('G1re', [64, 64], BF16)
                        G1im = T2('G1im', [64, 64], BF16)
                        loadw(G1re[:], I['c_G1re_' + nm], bc_, np_=64)
                        loadw(G1im[:], I['c_G1im_' + nm], bc_, np_=64)
                        TA = T2('TA', [N2, 4, 128], F32)
                        TBt = T2('TB', [N2, 4, 128], F32)
                        TfA = T2('TfA', [N2, 4, 128], F32)
                        TfB = T2('TfB', [N2, 4, 128], F32)
                        for t_, k_ in ((TA, 'TA_'), (TBt, 'TB_'), (TfA, 'TfA_'), (TfB, 'TfB_')):
                            P.dma('pool', t_[:], I['c_' + k_ + nm], writes=[bc_])
                        TcA = T2('TcA', [64, 4, 2 * N2], F32)
                        TcB = T2('TcB', [64, 4, 2 * N2], F32)
                        P.dma('pool', TcA[:], I['c_TcA_' + nm], writes=[bc_])
                        P.dma('pool', TcB[:], I['c_TcB_' + nm], writes=[bc_])
                        F2 = T2('F2', [N2, 3, N2], BF16)
                        FA = T2('FA', [N2, 2 * N2], BF16)
                        FB = T2('FB', [N2, 2 * N2], BF16)
                        loadw(F2[:].rearrange("p a b -> p (a b)"), I['c_F2_' + nm].rearrange("p a b -> p (a b)"), bc_, np_=N2)
                        loadw(FA[:], I['c_FA_' + nm], bc_, np_=N2)
                        loadw(FB[:], I['c_FB_' + nm], bc_, np_=N2)
                        XI = RB(p2, nc, 'xi' + nm, [64, 3, CB, N2], F32, G)
                        HI = RB(p2, nc, 'hi' + nm, [64, 4, CB, N2], F32, 2)
                        IV = RB(p2, nc, 'iv' + nm, [64, 4, CBc], F32, 3)
                        HB = RB(p2, nc, 'hb' + nm, [64, 4, CB, N2], BF16, G)
                        UB = RB(p2, nc, 'ub' + nm, [64, CB, N2], BF16, 2 * G)
                        PPa = RB(p2, nc, 'ppa' + nm, [N2, 4, 128], F32, 3)
                        PPb = RB(p2, nc, 'ppb' + nm, [N2, 4, 128], F32, 3)
                        YT = RB(p2, nc, 'yt' + nm, [N2, 4, 128], BF16, 2 * G + 2)
                        YF = RB(p2, nc, 'yf' + nm, [N2, 4, 128], F32, 2 * G + 2)
                        KK = RB(p2, nc, 'kk' + nm, [N2, 2, 2, CB, 128], F32, G)
                        GG = RB(p2, nc, 'gg' + nm, [N2, 4, 128], BF16, G)
                        VPa = RB(p2, nc, 'vpa' + nm, [64, 4, 2 * N2], F32, 2)
                        VPb = RB(p2, nc, 'vpb' + nm, [64, 4, 2 * N2], F32, 2)
                        VT_ = RB(p2, nc, 'vt' + nm, [64, 4, 2, N2], BF16, G)
                        UT = RB(p2, nc, 'ut' + nm, [64, CB, N2], F32, 3 * G)
                        MO = RB(p2, nc, 'mo' + nm, [N2, CB, 64], BF16, 3)

                        def tview(ps, n):
                            return ps[0:n, :].rearrange("p (t c) -> p t c", t=4)

                        def twiddle(ps, bp, A, B, n, w, outre, outim, bo, rd):
                            pa, bpa = PPa.next()
                            pb, bpb = PPb.next()
                            src = ps
                            P.op('dve', lambda e: e.tensor_tensor(out=pa[0:n], in0=src, in1=A, op=ALU.mult), [bp] + rd, [bpa])
                            P.op('dve', lambda e: e.tensor_tensor(out=pb[0:n], in0=src, in1=B, op=ALU.mult), [bp] + rd, [bpb])
                            P.op('pool', lambda e: e.tensor_tensor(out=outre, in0=pa[0:n, :, 0:w], in1=pb[0:n, :, w:2 * w], op=ALU.subtract), [bpa, bpb], [bo])
                            P.op('pool', lambda e: e.tensor_tensor(out=outim, in0=pb[0:n, :, 0:w], in1=pa[0:n, :, w:2 * w], op=ALU.add), [bpa, bpb], [bo], nowaw=True)

                        def tform_fwd(tiles, rd, A, B, out, bo):
                            ps, bp = PS.next()
                            pv = tview(ps, N2)
                            for t, tl in enumerate(tiles):
                                P.op('pe', lambda e, t=t, tl=tl: e.matmul(pv[:, t, :], lhsT=tl, rhs=F1[:], start=True, stop=True), rd + [bw], [bp])
                            twiddle(pv, bp, A, B, N2, 64, out[:, :, 0:64], out[:, :, 64:128], bo, [bc_])

                        def nform_fwd(yre_src, yim_src, rd, imsrc_re=None, imsrc_im=None):
                            ps, bp = PS.next()
                            pv = tview(ps, N2)
                            ire = yre_src if imsrc_re is None else imsrc_re
                            iim = yim_src if imsrc_im is None else imsrc_im
                            P.op('pe', lambda e: e.matmul(pv[:, :, 0:64], lhsT=F2[:, 0, :], rhs=yre_src, start=True, stop=False), rd + [bc_], [bp])
                            P.op('pe', lambda e: e.matmul(pv[:, :, 0:64], lhsT=F2[:, 2, :], rhs=yim_src, start=False, stop=True), rd + [bc_], [bp])
                            P.op('pe', lambda e: e.matmul(pv[:, :, 64:128], lhsT=F2[:, 0, :], rhs=iim, start=True, stop=False), rd + [bc_], [bp])
                            P.op('pe', lambda e: e.matmul(pv[:, :, 64:128], lhsT=F2[:, 1, :], rhs=ire, start=False, stop=True), rd + [bc_], [bp])
                            return pv, bp

                        def conv(ub, bub, kk, bkk, o, res):
                            yt, byt = YT.next()
                            tform_fwd([ub[:, t, :] for t in range(CB)], [bub], TA[:], TBt[:], yt, byt)
                            yield
                            pv, bp = nform_fwd(yt[:, :, 0:64], yt[:, :, 64:128], [byt])
                            gg, bgg = GG.next()
                            twiddle(pv, bp, kk[:, o, 0], kk[:, o, 1], N2, 64, gg[:, :, 0:64], gg[:, :, 64:128], bgg, [bkk])
                            yield
                            vt, bvt = VT_.next()
                            per = min(512 // (2 * N2), CB)
                            for g0 in range(0, CB, per):
                                ps, bp = PS.next()
                                pv2 = ps[0:64, 0:per * 2 * N2].rearrange("p (t c) -> p t c", t=per)
                                for t in range(per):
                                    P.op('pe', lambda e, t=t, g0=g0: e.matmul(pv2[:, t, :], lhsT=gg[:, g0 + t, 0:64], rhs=FA[:], start=True, stop=False), [bgg, bc_], [bp])
                                    P.op('pe', lambda e, t=t, g0=g0: e.matmul(pv2[:, t, :], lhsT=gg[:, g0 + t, 64:128], rhs=FB[:], start=False, stop=True), [bgg, bc_], [bp])
                                va, bva = VPa.next()
                                vb, bvb = VPb.next()
                                P.op('dve', lambda e, pv2=pv2, va=va: e.tensor_tensor(out=va[:, 0:per, :], in0=pv2, in1=TcA[:, 0:per, :], op=ALU.mult), [bp, bc_], [bva])
                                P.op('dve', lambda e, pv2=pv2, vb=vb: e.tensor_tensor(out=vb[:, 0:per, :], in0=pv2, in1=TcB[:, 0:per, :], op=ALU.mult), [bp, bc_], [bvb])
                                P.op('pool', lambda e, g0=g0, va=va, vb=vb: e.tensor_tensor(out=vt[:, g0:g0 + per, 0, :], in0=va[:, 0:per, 0:N2], in1=vb[:, 0:per, N2:2 * N2], op=ALU.subtract), [bva, bvb], [bvt], nowaw=(g0 > 0))
                                P.op('pool', lambda e, g0=g0, va=va, vb=vb: e.tensor_tensor(out=vt[:, g0:g0 + per, 1, :], in0=vb[:, 0:per, 0:N2], in1=va[:, 0:per, N2:2 * N2], op=ALU.add), [bva, bvb], [bvt], nowaw=True)
                                yield
                            ps, bp = PS.next()
                            pvo = ps[0:64, 0:CB * N2].rearrange("p (t c) -> p t c", t=CB)
                            P.op('pe', lambda e: e.matmul(pvo, lhsT=G1re[:], rhs=vt[:, :, 0, :], start=True, stop=False), [bvt, bc_], [bp])
                            P.op('pe', lambda e: e.matmul(pvo, lhsT=G1im[:], rhs=vt[:, :, 1, :], start=False, stop=True), [bvt, bc_], [bp])
                            res['pvo'] = (pvo, bp)

                        def zc_view(r0):
                            return s['zcT'][r0:r0 + CBc, :].rearrange("c (a b) -> a c b", b=N2r)

                        def cview(ap_):
                            return ap_.rearrange("p t (c b) -> p (t c) b", b=N2r)

                        def hyena_gen(cb):
                            ch0 = cb * CBc
                            xi, bxi = XI.next()
                            for k in range(3):
                                P.dma('sp', cview(xi[:, k]), zc_view(k * 768 + ch0), reads=[s['b_zcT']], writes=[bxi])
                            hi, bhi = HI.next()
                            for od in range(4):
                                P.dma('pool', cview(hi[:, od]), s['hfT'][od * 768 + ch0:od * 768 + ch0 + CBc, :].rearrange("c (a b) -> a c b", b=N2r),
                                      reads=[s['b_hfT']], writes=[bhi])
                            iv, biv = IV.next()
                            for od in range(4):
                                col = od * 768 + ch0
                                P.dma('pool', iv[:, od, :], s['invd'][col % 128:col % 128 + CBc, col // 128:col // 128 + 1].rearrange("c o -> o c").partition_broadcast(64),
                                      reads=[s['b_invd']], writes=[biv])
                            hb, bhb = HB.next()
                            P.op('dve', lambda e: e.tensor_tensor(out=hb[:].rearrange("p o t (c b) -> p o (t c) b", b=N2r), in0=hi[:].rearrange("p o t (c b) -> p o (t c) b", b=N2r),
                                                                  in1=iv[:].unsqueeze(3).to_broadcast([64, 4, CBc, N2r]), op=ALU.mult),
                                 [bhi, biv], [bhb])
                            for od in (1, 3):
                                P.op('pool', lambda e, od=od: e.memset(cview(hb[0:1, od])[:, :, 0:1], 0.0), [], [bhb])
                            yield
                            kk, bkk = KK.next()
                            for o in range(2):
                                yf, byf = YF.next()
                                yb, byb = YF.next()
                                tform_fwd([hb[:, 2 * o, t, :] for t in range(CB)], [bhb], TA[:], TBt[:], yf, byf)
                                yield
                                tform_fwd([hb[:, 2 * o + 1, t, :] for t in range(CB)], [bhb], TA[:], TBt[:], yb, byb)
                                ysum, bys = YT.next()
                                ydif, byd = YT.next()
                                P.op('pool', lambda e: e.tensor_tensor(out=ysum[:], in0=yf[:], in1=yb[:], op=ALU.add), [byf, byb], [bys])
                                P.op('pool', lambda e: e.tensor_tensor(out=ydif[:], in0=yf[:], in1=yb[:], op=ALU.subtract), [byf, byb], [byd])
                                yield
                                pv, bp = nform_fwd(ysum[:, :, 0:64], ysum[:, :, 64:128], [bys, byd], imsrc_re=ydif[:, :, 0:64], imsrc_im=ydif[:, :, 64:128])
                                for half in range(2):
                                    copy_op('act', kk[:, o, 0, :, half * 64:(half + 1) * 64], pv[:, :, 0:64], [bp], [bkk], nowaw=(o > 0 or half > 0))
                                    copy_op('act', kk[:, o, 1, :, half * 64:(half + 1) * 64], pv[:, :, 64:128], [bp], [bkk], nowaw=True)
                                yield
                            ub, bub = UB.next()
                            copy_op('act', ub[:], xi[:, 0], [bxi], [bub])
                            ucur, bucur = xi[:, 0], bxi
                            for o in range(2):
                                res = {}
                                yield from conv(ub, bub, kk, bkk, o, res)
                                pvo, bp = res['pvo']
                                ut, but = UT.next()
                                bia = hbb[:, o * 768 + ch0:o * 768 + ch0 + CBc].unsqueeze(2).to_broadcast([64, CBc, N2r])
                                P.op('pool', lambda e, ut=ut, ucur=ucur, bia=bia: e.tensor_tensor(out=cview(ut[:]), in0=cview(ucur), in1=bia, op=ALU.mult), [bucur, bw], [but])
                                P.op('dve', lambda e, ut=ut, pvo=pvo: e.tensor_tensor(out=ut[:], in0=pvo, in1=ut[:], op=ALU.add), [bp, but], [but])
                                ut2, but2 = UT.next()
                                P.op('pool', lambda e, ut=ut, ut2=ut2, o=o: e.tensor_tensor(out=ut2[:], in0=ut[:], in1=xi[:, 1 + o], op=ALU.mult), [but, bxi], [but2])
                                ub, bub = UB.next()
                                copy_op('act', ub[:], ut2[:], [but2], [bub])
                                ucur, bucur = ut2[:], but2
                                yield
                            P.dma('sp', s['mix2'][ch0:ch0 + CBc, :].rearrange("c (a b) -> a c b", b=N2r), cview(ub[:]), reads=[bub], writes=[s['b_mix2']])

                        def fnet_gen(cb):
                            q0 = cb * CBc
                            xi, bxi = XI.next()
                            P.dma('sp', cview(xi[:, 0]), s['fcT'][q0:q0 + CBc, :].rearrange("c (a b) -> a c b", b=N2r), reads=[s['b_fcT']], writes=[bxi])
                            P.dma('sp', cview(xi[:, 1]), s['fcT'][256 + q0:256 + q0 + CBc, :].rearrange("c (a b) -> a c b", b=N2r), reads=[s['b_fcT']], writes=[bxi])
                            xc, bxc = UB.next()
                            xs_, bxs = UB.next()
                            copy_op('act', xc[:], xi[:, 0], [bxi], [bxc])
                            copy_op('pool', xs_[:], xi[:, 1], [bxi], [bxs])
                            yield
                            ps, bp = PS.next()
                            pv = tview(ps, N2)
                            for t in range(CB):
                                P.op('pe', lambda e, t=t: e.matmul(pv[:, t, :], lhsT=xc[:, t, :], rhs=F1fA[:], start=True, stop=False), [bxc, bw], [bp])
                                P.op('pe', lambda e, t=t: e.matmul(pv[:, t, :], lhsT=xs_[:, t, :], rhs=F1fB[:], start=False, stop=True), [bxs, bw], [bp])
                            yt, byt = YT.next()
                            twiddle(pv, bp, TfA[:], TfB[:], N2, 64, yt[:, :, 0:64], yt[:, :, 64:128], byt, [bc_])
                            yield
                            ps, bp = PS.next()
                            pz = ps[0:N2, 0:CB * 64].rearrange("p (t c) -> p t c", t=CB)
                            P.op('pe', lambda e: e.matmul(pz, lhsT=F2[:, 0, :], rhs=yt[:, :, 0:64], start=True, stop=False), [byt, bc_], [bp])
                            P.op('pe', lambda e: e.matmul(pz, lhsT=F2[:, 2, :], rhs=yt[:, :, 64:128], start=False, stop=True), [byt, bc_], [bp])
                            mo, bmo = MO.next()
                            copy_op('act', mo[:], pz, [bp], [bmo])
                            P.dma('sp', s['mfT'][q0:q0 + CBc, :].rearrange("(t c) (b a) -> (c b) t a", c=PK, a=64), mo[:], reads=[bmo], writes=[s['b_mfT']])

                        def lockstep(gens, g):
                            it = iter(gens)
                            active = []
                            done = False
                            while True:
                                while not done and len(active) < g:
                                    nx = next(it, None)
                                    if nx is None:
                                        done = True
                                    else:
                                        active.append(nx)
                                if not active:
                                    break
                                for gg_ in list(active):
                                    try:
                                        next(gg_)
                                    except StopIteration:
                                        active.remove(gg_)

                        lockstep((hyena_gen(cb) for cb in range(768 // CBc)), G)
                        lockstep((fnet_gen(cb) for cb in range(256 // CBc)), G)
                    P.barrier()

        phases = [ph1, ph2, lambda: ph_out(0), lambda: ph_mlp(0), ph5, ph6, ph7, lambda: ph_out(1), lambda: ph_mlp(1)]
        for i, f in enumerate(phases):
            if i >= stop_after:
                break
            f()
        if dbg is not None:
            dbg(nc, P, SEQ, I)
        P.emit()
    return nc


_NC_CACHE = {}


def _prep_inputs(inputs):
    consts = _get_consts()
    base = {}
    for k, v in inputs.items():
        if k in ('x_prompt', 'x_sample'):
            continue
        base[k] = np.ascontiguousarray(np.asarray(v, dtype=np.float32))
    for k, v in consts.items():
        base['c_' + k] = np.ascontiguousarray(v)
    xp = np.asarray(inputs['x_prompt'], dtype=np.float32)
    xs = np.asarray(inputs['x_sample'], dtype=np.float32)
    in_maps = []
    for i in range(8):
        m = dict(base)
        m['x_p'] = np.ascontiguousarray(xp[i])
        m['x_s'] = np.ascontiguousarray(xs[i])
        in_maps.append(m)
    return consts, in_maps


def kernel(**inputs):
    consts, in_maps = _prep_inputs(inputs)
    if 'nc' not in _NC_CACHE:
        _NC_CACHE['nc'] = build(consts)
    nc = _NC_CACHE['nc']
    res = run_bass_kernel_spmd(nc, in_maps, core_ids=list(range(8)))
    yp = np.stack([np.asarray(r['y_p'], dtype=np.float32) for r in res.results], 0)
    ys = np.stack([np.asarray(r['y_s'], dtype=np.float32) for r in res.results], 0)
    return (yp, ys)
```

```python
import contextlib
import math
import os
import numpy as np
import concourse.bass as bass
import concourse.mybir as mybir
from concourse.bass_utils import run_bass_kernel_spmd

F32 = mybir.dt.float32
BF16 = mybir.dt.bfloat16
AF = mybir.ActivationFunctionType
ALU = mybir.AluOpType
AX = mybir.AxisListType

D = 1024
EPS = 1e-6
LP, LS = 2048, 8192
ENGS = ('pe', 'act', 'dve', 'pool', 'sp')
NRING = 8


class Buf:
    __slots__ = ('w', 'r', 'chain', 'pending')

    def __init__(self, chain=True):
        self.w = None
        self.r = []
        self.chain = chain
        self.pending = None


class WBuf(Buf):
    __slots__ = ()

    def __init__(self):
        Buf.__init__(self)
        self.pending = []

    def piece(self):
        b = Buf()
        self.pending.append(b)
        return b


class Op:
    __slots__ = ('eng', 'fn', 'deps', 'signal', 'sigval', 'is_dma', 'dsem', 'dval', 'ringdep')

    def __init__(self, eng, fn, is_dma):
        self.eng = eng
        self.fn = fn
        self.deps = []
        self.signal = False
        self.sigval = 0
        self.is_dma = is_dma
        self.dsem = None
        self.dval = 0
        self.ringdep = None


class _Rec:
    def __getattr__(self, name):
        def f(*a, **k):
            self.__dict__['call'] = (name, a, k)
            return None
        return f


class Prog:
    def __init__(self, nc):
        self.nc = nc
        self.ops = {e: [] for e in ENGS}
        self.ndma = {e: 0 for e in ENGS}
        self.ringops = {e: [None] * NRING for e in ENGS}

    def _rec(self, eng, fn, reads, writes, is_dma, nowaw=False):
        for t in list(reads) + list(writes):
            if t.pending:
                pend, t.pending = t.pending, []
                self._rec('pool', self.join_fn, pend, [t], False)
        r = _Rec()
        fn(r)
        op = Op(eng, r.call, is_dma)
        deps = []
        for t in reads:
            if t.w is not None:
                deps.append(t.w)
        for t in writes:
            if t.w is not None and not (nowaw and t.w.eng == eng and not t.w.is_dma) \
                    and not ((not t.chain) and is_dma and t.w.is_dma):
                deps.append(t.w)
            deps.extend(t.r)
        for t in reads:
            t.r.append(op)
        for t in writes:
            t.w = op
            t.r = []
        seen = set()
        for d in deps:
            if id(d) in seen or d is op:
                continue
            seen.add(id(d))
            if (not d.is_dma) and d.eng == eng and eng == 'pe':
                continue
            op.deps.append(d)
            if not d.is_dma:
                d.signal = True
        if is_dma:
            j = self.ndma[eng]
            self.ndma[eng] = j + 1
            op.dsem = (eng, j % NRING)
            op.dval = 16 * (j // NRING + 1)
            op.ringdep = self.ringops[eng][j % NRING]
            self.ringops[eng][j % NRING] = op
        self.ops[eng].append(op)
        return op

    def op(self, eng, fn, reads=(), writes=(), nowaw=False):
        return self._rec(eng, fn, reads, writes, False, nowaw)

    def dma(self, eng, out, in_, reads=(), writes=()):
        return self._rec(eng, lambda e: e.dma_start(out=out, in_=in_), reads, writes, True)

    def barrier(self):
        lasts = []
        for e in ENGS:
            for o in reversed(self.ops[e]):
                if o.fn is not None and not o.is_dma:
                    lasts.append(o)
                    break
            for d in self.ringops[e]:
                if d is not None:
                    lasts.append(d)
        for e in ENGS:
            op = Op(e, None, False)
            for d in lasts:
                if d.is_dma or d.eng != e:
                    op.deps.append(d)
                    if not d.is_dma:
                        d.signal = True
            self.ops[e].append(op)

    def emit(self):
        nc = self.nc
        with contextlib.ExitStack() as st:
            csem = {e: st.enter_context(nc.semaphore('c_' + e)) for e in ENGS}
            dsem = {}
            for e in ENGS:
                for i in range(min(NRING, self.ndma[e])):
                    dsem[(e, i)] = st.enter_context(nc.semaphore('d_%s%d' % (e, i)))
            for e in ENGS:
                c = 0
                for o in self.ops[e]:
                    if o.signal:
                        c += 1
                        o.sigval = c
            st.enter_context(nc.allow_non_contiguous_dma(reason='small strided tables / layout loads'))
            block = st.enter_context(nc.Block())
            handles = {'pe': 'tensor', 'act': 'scalar', 'dve': 'vector', 'pool': 'gpsimd', 'sp': 'sync'}

            def make(e):
                def body(eng):
                    waited = {}
                    for o in self.ops[e]:
                        ws = []
                        for d in o.deps:
                            if d.is_dma:
                                ws.append((('d',) + d.dsem, dsem[d.dsem], d.dval))
                            else:
                                ws.append((('c', d.eng), csem[d.eng], d.sigval))
                        if o.ringdep is not None:
                            d = o.ringdep
                            ws.append((('d',) + d.dsem, dsem[d.dsem], d.dval))
                        for key, sem, val in ws:
                            if waited.get(key, 0) >= val:
                                continue
                            waited[key] = val
                            eng.wait_ge(sem, val)
                        if o.fn is None:
                            continue
                        name_, a_, k_ = o.fn
                        ins = getattr(eng, name_)(*a_, **k_)
                        if o.is_dma:
                            ins.then_inc(dsem[o.dsem], 16)
                        elif o.signal:
                            ins.then_inc(csem[e], 1)
                    for i in range(NRING):
                        d = self.ringops[e][i]
                        if d is not None and waited.get(('d',) + d.dsem, 0) < d.dval:
                            eng.wait_ge(dsem[d.dsem], d.dval)
                return body

            for e in ENGS:
                if self.ops[e]:
                    getattr(block, handles[e])(make(e))


_UID = [0]


class _Stop(Exception):
    pass


def _chk(tag):
    if os.environ.get('KSTOP') == tag:
        raise _Stop()


class RB:
    def __init__(self, st, nc, name, shape, dt, n, psum=False):
        alloc = nc.psum_tensor if psum else nc.sbuf_tensor
        _UID[0] += 1
        self.t = [st.enter_context(alloc('%s_%d_%d' % (name, _UID[0], i), shape, dt)) for i in range(n)]
        self.b = [Buf() for _ in range(n)]
        self.i = 0

    def next(self):
        k = self.i % len(self.t)
        self.i += 1
        return self.t[k], self.b[k]


POOL_WINDOWS = (2, 4, 8, 16)


def _band_tables(L):
    out = np.zeros((128, 4, 5, 128), np.float32)
    for g, w in enumerate(POOL_WINDOWS):
        before = w // 2
        after = w - 1 - before

        def fill(kind, t_tile, s_tile):
            for tl in range(128):
                t = t_tile * 128 + tl
                lo = max(t - before, 0)
                hi = min(t + after, L - 1)
                cnt = hi - lo + 1
                for s in range(lo, hi + 1):
                    sl = s - s_tile * 128
                    if 0 <= sl < 128:
                        out[sl, g, kind, tl] += 1.0 / cnt
                if s_tile == t_tile:
                    out[tl, g, kind, tl] -= 1.0
        nt = L // 128
        fill(0, 2, 2)
        fill(1, 2, 1)
        fill(2, 2, 3)
        fill(3, 0, 0)
        fill(4, nt - 1, nt - 1)
    return out


def _consts():
    c = {}
    c['ident'] = np.eye(128, dtype=np.float32)
    inv = 10000.0 ** (-np.arange(0, 32, 2, dtype=np.float32) / 32)
    ang = np.arange(LS, dtype=np.float32)[None, :] * inv[:, None]
    cs, sn = np.cos(ang).astype(np.float32), np.sin(ang).astype(np.float32)
    c['ropeC'] = np.concatenate([cs, cs], 0)
    c['ropeS'] = np.concatenate([-sn, sn], 0)
    for nm, L in (('p', LP), ('s', LS)):
        c['band_' + nm] = _band_tables(L)
        N = 2 * L
        N2 = L // 64
        a = np.arange(64)[:, None].astype(np.float64)
        ap = np.arange(64)[None, :].astype(np.float64)
        th = np.pi * (2 * ap + 1) * a / 128.0
        c['F1'] = np.concatenate([np.cos(th), -np.sin(th)], 1).astype(np.float32)
        c['G1re_' + nm] = ((2.0 / N) * np.cos(th).T).astype(np.float32)
        c['G1im_' + nm] = ((2.0 / N) * (-np.sin(th)).T).astype(np.float32)
        PK = 128 // N2
        b = np.arange(N2)[:, None].astype(np.float64)
        tt = np.pi * (2 * ap + 1) * b / N
        tre, tim = np.tile(np.cos(tt), (PK, 1)), np.tile(-np.sin(tt), (PK, 1))
        TA = np.concatenate([tre, tre], 1)
        TB = np.concatenate([tim, tim], 1)
        c['TA_' + nm] = np.repeat(TA[:, None, :], 4, 1).astype(np.float32)
        c['TB_' + nm] = np.repeat(TB[:, None, :], 4, 1).astype(np.float32)
        tcr, tci = np.tile(np.cos(tt).T, (1, PK)), np.tile(np.sin(tt).T, (1, PK))
        c['TcA_' + nm] = np.repeat(np.concatenate([tcr, tcr], 1)[:, None, :], 4, 1).astype(np.float32)
        c['TcB_' + nm] = np.repeat(np.concatenate([tci, tci], 1)[:, None, :], 4, 1).astype(np.float32)
        bb = np.arange(N2)[None, :].astype(np.float64)
        f2 = 2 * np.pi * b * bb / N2
        eye = np.eye(PK)
        f2re, f2im = np.kron(eye, np.cos(f2)), np.kron(eye, -np.sin(f2))
        c['F2_' + nm] = np.stack([f2re, f2im, -f2im], 1).astype(np.float32)
        c['FA_' + nm] = np.concatenate([f2re, -f2im], 1).astype(np.float32)
        c['FB_' + nm] = np.concatenate([f2im, f2re], 1).astype(np.float32)
        tf = 2 * np.pi * b * ap / L
        fre, fim = np.tile(np.cos(tf), (PK, 1)), np.tile(-np.sin(tf), (PK, 1))
        c['TfA_' + nm] = np.repeat(np.concatenate([fre, fre], 1)[:, None, :], 4, 1).astype(np.float32)
        c['TfB_' + nm] = np.repeat(np.concatenate([fim, fim], 1)[:, None, :], 4, 1).astype(np.float32)
        t = np.linspace(0.0, 1.0, L, dtype=np.float32)
        bands = np.linspace(1e-4, 15, 16, dtype=np.float32)
        angf = (np.float32(2.0 * math.pi / L) * np.arange(L, dtype=np.float32)[:, None] * bands[None, :]).astype(np.float32)
        feat = np.concatenate([t[:, None], np.cos(angf), -np.sin(angf)], -1).astype(np.float32)
        c['featT_' + nm] = np.ascontiguousarray(feat.T)
        c['trow_' + nm] = t[None, :].copy()
    ff = 2 * np.pi * a * ap / 64.0
    c['F1fA'] = np.concatenate([np.cos(ff), -np.sin(ff)], 1).astype(np.float32)
    c['F1fB'] = np.concatenate([-np.sin(ff), -np.cos(ff)], 1).astype(np.float32)
    dd = np.arange(64)[:, None] * np.arange(64)[None, :]
    c64, s64 = np.cos(2 * np.pi * dd / 64.0), np.sin(2 * np.pi * dd / 64.0)
    z = np.zeros((64, 64))
    c['DC'] = np.block([[c64, z], [z, c64]]).astype(np.float32)
    c['DS'] = np.block([[s64, z], [z, s64]]).astype(np.float32)
    deltas = np.abs(np.linspace(math.log(1e-2) / 1.5, math.log(1e-2) / 0.3, 768, dtype=np.float32))
    c['negd'] = np.ascontiguousarray((-deltas).reshape(6, 128).T).astype(np.float32)
    return c


_CONST_CACHE = {}


def _get_consts():
    if not _CONST_CACHE:
        _CONST_CACHE.update(_consts())
    return _CONST_CACHE


def build(consts, stop_after=99, dbg=None):
    nc = bass.Bass("TRN2", target_bir_lowering=False)

    def sb(name, shape, dt):
        _UID[0] += 1
        return nc.sbuf_tensor('%s_%d' % (name, _UID[0]), shape, dt)
    I = {}

    def inp(name, shape):
        I[name] = nc.dram_tensor(name, list(shape), F32, kind="ExternalInput").ap()
        return I[name]

    specs = {
        "x_p": (LP, D), "x_s": (LS, D), "norm_mix": (2, 2, D), "norm_mlp": (2, 2, D),
        "w_ff_in": (2, D, 4096), "w_ff_out": (2, 4096, D), "even_w_in": (1, D, 928),
        "pool_w": (1, 4, 128, 128), "pool_scale": (1, 512), "mla_q_norm": (1, 256),
        "mla_w_uq": (1, 256, 768), "mla_kv_norm": (1, 128), "mla_w_ukv": (1, 128, 1024),
        "even_w_out": (1, D, D), "odd_w_in": (1, D, 2560), "short_w": (1, 3, 2304),
        "short_b": (1, 2304), "filt_w1": (1, 33, 64), "filt_b1": (1, 64), "filt_w2": (1, 64, 64),
        "filt_b2": (1, 64), "filt_w3": (1, 64, 3072), "filt_freq": (1, 64), "hyena_bias": (1, 2, 768),
        "fnet_w": (1, 4, 64, 64), "odd_w_out": (1, D, D),
    }
    for k, v in specs.items():
        inp(k, v)
    for k, v in consts.items():
        inp('c_' + k, v.shape)
    y_p = nc.dram_tensor("y_p", [LP, D], F32, kind="ExternalOutput").ap()
    y_s = nc.dram_tensor("y_s", [LS, D], F32, kind="ExternalOutput").ap()

    def scratch(name, shape, dt):
        return nc.dram_tensor(name, list(shape), dt, kind="Internal").ap()

    SEQ = []
    for nm, L, xin, yout in (('p', LP, I['x_p'], y_p), ('s', LS, I['x_s'], y_s)):
        s = dict(nm=nm, L=L, x=xin, y=yout, N2=L // 64)
        s['QT'] = scratch('QT' + nm, [8, 96, L], BF16)
        s['KN'] = scratch('KN' + nm, [512, L], BF16)
        s['KR'] = scratch('KR' + nm, [32, L], BF16)
        s['V'] = scratch('V' + nm, [8, 128, L // 128, 64], BF16)
        s['mixT'] = scratch('mixT' + nm, [D, L], BF16)
        s['X1'] = scratch('X1' + nm, [L, D], F32)
        s['X2'] = scratch('X2' + nm, [L, D], F32)
        s['zcT'] = scratch('zcT' + nm, [2304, L], F32)
        s['fcT'] = scratch('fcT' + nm, [512, L], F32)
        s['hfT'] = scratch('hfT' + nm, [3072, L], F32)
        s['invd'] = scratch('invd' + nm, [128, 32], F32)
        s['mix2'] = scratch('mix2' + nm, [768, L], BF16)
        s['mfT'] = scratch('mfT' + nm, [256, L], BF16)
        s['X3'] = scratch('X3' + nm, [L, D], F32)
        for k in ('QT', 'KN', 'KR', 'V', 'mixT', 'X1', 'X2', 'zcT', 'fcT', 'hfT', 'invd', 'mix2', 'mfT', 'X3'):
            s['b_' + k] = Buf(chain=(k == 'zcT'))
        SEQ.append(s)

    P = Prog(nc)
    NOB = Buf

    with contextlib.ExitStack() as top:
        def gt(name, shape, dt):
            return top.enter_context(sb(name, shape, dt))

        PS = RB(top, nc, 'ps', [128, 512], F32, 6, psum=True)
        PST = RB(top, nc, 'pst', [128, 1024], BF16, 2, psum=True)
        ident = gt('ident', [128, 128], BF16)
        onesb = gt('onesb', [128, 128], BF16)
        onesf = gt('onesf', [128, 128], F32)
        epsT = gt('epsT', [128, 2], F32)
        stg = RB(top, nc, 'stg', [128, 1024], F32, 2)
        bconst = Buf()
        dummy = gt('dummyj', [128, 4], F32)
        P.join_fn = lambda e: e.memset(dummy[:], 0.0)
        P.op('pool', lambda e: e.memset(onesb[:], 1.0), writes=[bconst])
        P.op('pool', lambda e: e.memset(onesf[:], 1.0), writes=[bconst])
        P.op('pool', lambda e: e.memset(epsT[:, 0:1], EPS), writes=[bconst])
        P.op('pool', lambda e: e.memset(epsT[:, 1:2], 96.0 * EPS), writes=[bconst])
        _castc = [0]

        def cast_engine():
            _castc[0] += 1
            return ('act', 'pool')[_castc[0] % 2]

        def copy_op(eng, out, in_, reads, writes, nowaw=False):
            if eng == 'act':
                P.op('act', lambda e: e.copy(out=out, in_=in_), reads, writes, nowaw=nowaw)
            else:
                P.op(eng, lambda e: e.tensor_copy(out=out, in_=in_), reads, writes, nowaw=nowaw)

        def loadw(dst, src, dbuf, np_=128):
            cols = dst.shape[-1]
            assert len(dst.shape) == 2 and len(src.shape) == 2, (dst.shape, src.shape)
            for c0 in range(0, cols, 1024):
                cw = min(1024, cols - c0)
                t, b = stg.next()
                P.dma('sp', t[0:np_, 0:cw], src[:, c0:c0 + cw], writes=[b])
                copy_op(cast_engine(), dst[:, c0:c0 + cw], t[0:np_, 0:cw], [b], [dbuf.piece() if isinstance(dbuf, WBuf) else dbuf])

        loadw(ident[:], I['c_ident'], bconst)

        def rms_norm(xt, bx, nj, tmp):
            ssq, bs = tmp['ssq'].next()
            junk, bj = tmp['junk'].next()
            for j in range(nj):
                P.op('act', (lambda j: lambda e: e.activation(out=junk[:], in_=xt[:, j, :], func=AF.Square,
                                                               accum_out=ssq[:, j:j + 1]))(j), [bx], [bj, bs])
            P.op('act', lambda e: e.activation(out=ssq[:, 4:4 + nj], in_=ssq[:, 0:nj], func=AF.Sqrt,
                                               bias=epsT[:, 0:1], scale=1.0 / D), [bs, bconst], [bs])
            P.op('dve', lambda e: e.reciprocal(out=ssq[:, 8:8 + nj], in_=ssq[:, 4:4 + nj]), [bs], [bs])
            xn, bn = tmp['xn'].next()
            for j in range(nj):
                P.op('dve', (lambda j: lambda e: e.tensor_scalar(out=xn[:, j, :], in0=xt[:, j, :],
                                                                 scalar1=ssq[:, 8 + j:9 + j], scalar2=None,
                                                                 op0=ALU.mult))(j), [bx, bs], [bn], nowaw=(j > 0))
            return xn, bn

        def rms_tr(xn, bn, nj, gcol, hT, bh, bg):
            for dc in range(8):
                pt, bp = PST.next()
                for j in range(nj):
                    P.op('pe', (lambda j, dc, pt: lambda e: e.transpose(pt[:, j * 128:(j + 1) * 128],
                                                                        xn[:, j, dc * 128:(dc + 1) * 128], ident[:]))(j, dc, pt),
                         [bn, bconst], [bp])
                P.op('act', (lambda dc, pt: lambda e: e.activation(out=hT[:, dc, :], in_=pt[:, 0:nj * 128], func=AF.Copy,
                                                                   scale=gcol[:, dc:dc + 1]))(dc, pt), [bp, bg], [bh], nowaw=(dc > 0))

        def rms_transpose(ph, xt, bx, nj, gcol, hT, bh, tmp, bg):
            xn, bn = rms_norm(xt, bx, nj, tmp)
            rms_tr(xn, bn, nj, gcol, hT, bh, bg)

        def epilogue(pss, xt, bx, j, grow, tmp, bg):
            ssq, bs = tmp['ssq2'].next()
            junk, bj = tmp['junk'].next()
            for h in range(2):
                P.op('act', (lambda h: lambda e: e.activation(out=junk[:, 0:512], in_=pss[h][0][:], func=AF.Square,
                                                               accum_out=ssq[:, h:h + 1]))(h), [pss[h][1]], [bj, bs])
            P.op('dve', lambda e: e.tensor_tensor(out=ssq[:, 2:3], in0=ssq[:, 0:1], in1=ssq[:, 1:2], op=ALU.add), [bs], [bs])
            P.op('act', lambda e: e.activation(out=ssq[:, 3:4], in_=ssq[:, 2:3], func=AF.Sqrt, bias=epsT[:, 0:1],
                                               scale=1.0 / D), [bs, bconst], [bs])
            P.op('dve', lambda e: e.reciprocal(out=ssq[:, 4:5], in_=ssq[:, 3:4]), [bs], [bs])
            for h in range(2):
                t, bt = tmp['ep'].next()
                P.op('dve', (lambda h, t: lambda e: e.scalar_tensor_tensor(out=t[:], in0=pss[h][0][:], scalar=ssq[:, 4:5],
                                                                           in1=grow[:, h * 512:(h + 1) * 512], op0=ALU.mult,
                                                                           op1=ALU.mult))(h, t), [pss[h][1], bs, bg], [bt])
                P.op('pool', (lambda h, t: lambda e: e.tensor_tensor(out=xt[:, j, h * 512:(h + 1) * 512], in0=t[:],
                                                                     in1=xt[:, j, h * 512:(h + 1) * 512], op=ALU.add))(h, t),
                     [bt, bx], [bx])

        def colvec(ph, name, src1d, ncol, bufc):
            t = ph.enter_context(sb(name, [128, ncol], F32))
            P.dma('pool', t[:], src1d.rearrange("(c p) -> p c", p=128), writes=[bufc.piece() if isinstance(bufc, WBuf) else bufc])
            return t

        def rowbc(ph, name, src1d, n, bufc, npart=128):
            t = ph.enter_context(sb(name, [npart, n], F32))
            P.dma('pool', t[:], src1d.rearrange("(o n) -> o n", o=1).partition_broadcast(npart), writes=[bufc.piece() if isinstance(bufc, WBuf) else bufc])
            return t

        def ph1():
            try:
                ph1_()
            except _Stop:
                pass
            P.barrier()

        def ph1_():
            with contextlib.ExitStack() as ph:
                try:
                    ph1_body(ph)
                except _Stop:
                    pass

        def ph1_body(ph):
            if True:
                T = lambda name, shape, dt: ph.enter_context(sb(name, shape, dt))
                bw = WBuf()
                win = T('win', [128, 8, 928], BF16)
                winsw = T('winsw', [128, 8, 32], BF16)
                wuq = T('wuq', [128, 2, 768], BF16)
                wuqsw = T('wuqsw', [128, 2, 768], BF16)
                wuk = T('wuk', [128, 512], BF16)
                wuv = T('wuv', [128, 512], BF16)
                poolw = T('poolw', [128, 4, 128], BF16)
                ewi = I['even_w_in'][0].rearrange("(c p) f -> p c f", p=128)
                for dc in range(8):
                    loadw(win[:, dc, :], ewi[:, dc, :], bw)
                    loadw(winsw[:, dc, 0:16], ewi[:, dc, 912:928], bw)
                    loadw(winsw[:, dc, 16:32], ewi[:, dc, 896:912], bw)
                uq = I['mla_w_uq'][0].rearrange("(c p) f -> p c f", p=128)
                for rc in range(2):
                    loadw(wuq[:, rc, :], uq[:, rc, :], bw)
                    loadw(wuqsw[:, rc, :], uq[:, rc, :], bw)
                    for h in range(8):
                        loadw(wuqsw[:, rc, h * 96 + 64:h * 96 + 80], uq[:, rc, h * 96 + 80:h * 96 + 96], bw)
                        loadw(wuqsw[:, rc, h * 96 + 80:h * 96 + 96], uq[:, rc, h * 96 + 64:h * 96 + 80], bw)
                ukv = I['mla_w_ukv'][0]
                for h in range(8):
                    loadw(wuk[:, h * 64:(h + 1) * 64], ukv[:, h * 128:h * 128 + 64], bw)
                    loadw(wuv[:, h * 64:(h + 1) * 64], ukv[:, h * 128 + 64:h * 128 + 128], bw)
                for g in range(4):
                    loadw(poolw[:, g, :], I['pool_w'][0, g], bw)
                g0col = colvec(ph, 'g0col', I['norm_mix'][0, 0], 8, bw)
                pscol = colvec(ph, 'pscol', I['pool_scale'][0], 4, bw)
                qncol = colvec(ph, 'qncol', I['mla_q_norm'][0], 2, bw)
                kvcol = colvec(ph, 'kvcol', I['mla_kv_norm'][0], 1, bw)
                band = T('band', [128, 4 * 5 * 128], BF16)
                _chk('w')
                tmp = dict(ssq=RB(ph, nc, 'ssq', [128, 12], F32, 2), junk=RB(ph, nc, 'junk', [128, 1024], F32, 1),
                           xn=RB(ph, nc, 'xn', [128, 4, 1024], BF16, 1))
                XT = RB(ph, nc, 'xt', [128, 4, 1024], F32, 1)
                HT = RB(ph, nc, 'hT', [128, 8, 512], BF16, 1)
                atok = T('atok', [128, 64, 512], BF16)
                SQ = RB(ph, nc, 'sq', [128, 3, 512], BF16, 2)
                CG = RB(ph, nc, 'cg', [128, 3, 512], BF16, 2)
                RQ = RB(ph, nc, 'rq', [128, 2, 512], F32, 1)
                RT = RB(ph, nc, 'rt', [96, 4, 512], F32, 1)
                RK = RB(ph, nc, 'rk', [32, 2, 512], F32, 1)
                KT = RB(ph, nc, 'kt', [32, 2, 512], F32, 1)
                KRO = RB(ph, nc, 'kro', [32, 512], BF16, 2)
                QTs = RB(ph, nc, 'qTs', [96, 512], BF16, 3)
                QTm = RB(ph, nc, 'qTm', [96, 2, 512], F32, 1)
                KS = RB(ph, nc, 'ks', [128, 512], BF16, 2)
                VS = RB(ph, nc, 'vs', [128, 512], BF16, 2)
                RC = RB(ph, nc, 'rc', [128, 12], F32, 2)
                RTs = RB(ph, nc, 'rTs', [128, 512], BF16, 2)
                MS = RB(ph, nc, 'ms', [128, 512], BF16, 2)
                for s in SEQ:
                    L = s['L']
                    nt = L // 128
                    batok = [Buf() for _ in range(nt)]
                    bband = Buf()
                    loadw(band[:], I['c_band_' + s['nm']].rearrange("p g k t -> p (g k t)"), bband)
                    for ci in range(L // 512):
                        c0 = ci * 512
                        xt, bx = XT.next()
                        P.dma('sp', xt[:], s['x'][c0:c0 + 512, :].rearrange("(j p) d -> p j d", p=128), writes=[bx])
                        hT, bh = HT.next()
                        rms_transpose(ph, xt, bx, 4, g0col, hT, bh, tmp, bw)
                        _chk('rms')
                        for j in range(4):
                            ps, bp = PS.next()
                            for dc in range(8):
                                P.op('pe', (lambda j, dc, ps: lambda e: e.matmul(ps[:], lhsT=hT[:, dc, j * 128:(j + 1) * 128],
                                                                                 rhs=win[:, dc, 0:512], start=(dc == 0), stop=(dc == 7)))(j, dc, ps),
                                     [bh, bw], [bp])
                            copy_op('act', atok[:, ci * 4 + j, :], ps[:], [bp], [batok[ci * 4 + j]])
                        _chk('atok')
                        sq, bsq = SQ.next()
                        cg, bcg = CG.next()
                        for k3, (lo, ncol) in enumerate(((512, qncol[:, 0:1]), (640, qncol[:, 1:2]), (768, kvcol[:, 0:1]))):
                            ps, bp = PS.next()
                            for dc in range(8):
                                P.op('pe', (lambda dc, ps, lo: lambda e: e.matmul(ps[:], lhsT=win[:, dc, lo:lo + 128], rhs=hT[:, dc, :],
                                                                                  start=(dc == 0), stop=(dc == 7)))(dc, ps, lo), [bh, bw], [bp])
                            P.op('act', (lambda k3, ps: lambda e: e.activation(out=sq[:, k3, :], in_=ps[:], func=AF.Square))(k3, ps), [bp], [bsq])
                            P.op('dve', (lambda k3, ps, ncol: lambda e: e.tensor_scalar(out=cg[:, k3, :], in0=ps[:], scalar1=ncol, scalar2=None,
                                                                                        op0=ALU.mult))(k3, ps, ncol), [bp, bw, bsq], [bcg])
                        _chk('lat')
                        rk, brk = RK.next()
                        P.dma('pool', rk[:, 0, :], I['c_ropeC'][:, c0:c0 + 512], writes=[brk])
                        P.dma('pool', rk[:, 1, :], I['c_ropeS'][:, c0:c0 + 512], writes=[brk])
                        kt, bkt = KT.next()
                        for k2, (wt, lo) in enumerate(((win, 896), (winsw, 0))):
                            ps, bp = PS.next()
                            for dc in range(8):
                                P.op('pe', (lambda dc, ps, wt, lo: lambda e: e.matmul(ps[0:32, :], lhsT=wt[:, dc, lo:lo + 32], rhs=hT[:, dc, :],
                                                                                      start=(dc == 0), stop=(dc == 7)))(dc, ps, wt, lo), [bh, bw], [bp])
                            P.op('dve', (lambda k2, ps: lambda e: e.tensor_tensor(out=kt[:, k2, :], in0=ps[0:32, :], in1=rk[:, k2, :],
                                                                                  op=ALU.mult))(k2, ps), [bp, brk], [bkt])
                        kro, bkro = KRO.next()
                        P.op('pool', lambda e, kt=kt, kro=kro: e.tensor_tensor(out=kro[:], in0=kt[:, 0, :], in1=kt[:, 1, :], op=ALU.add), [bkt], [bkro])
                        P.dma('pool', s['KR'][:, c0:c0 + 512], kro[:], reads=[bkro], writes=[s['b_KR']])
                        _chk('krope')
                        rq, brq = RQ.next()
                        ps, bp = PS.next()
                        for rc in range(2):
                            P.op('pe', (lambda rc, ps: lambda e: e.matmul(ps[:], lhsT=onesb[:], rhs=sq[:, rc, :], start=(rc == 0), stop=(rc == 1)))(rc, ps),
                                 [bsq, bconst], [bp])
                        P.op('act', lambda e, ps=ps, rq=rq: e.activation(out=rq[:, 0, :], in_=ps[:], func=AF.Sqrt, bias=epsT[:, 1:2], scale=96.0 / 256.0),
                             [bp, bconst], [brq])
                        P.op('dve', lambda e, rq=rq: e.reciprocal(out=rq[:, 0, :], in_=rq[:, 0, :]), [brq], [brq])
                        ps, bp = PS.next()
                        P.op('pe', lambda e, ps=ps, sq=sq: e.matmul(ps[:], lhsT=onesb[:], rhs=sq[:, 2, :], start=True, stop=True), [bsq, bconst], [bp])
                        P.op('act', lambda e, ps=ps, rq=rq: e.activation(out=rq[:, 1, :], in_=ps[:], func=AF.Sqrt, bias=epsT[:, 0:1], scale=1.0 / 128.0),
                             [bp, bconst], [brq])
                        P.op('dve', lambda e, rq=rq: e.reciprocal(out=rq[:, 1, :], in_=rq[:, 1, :]), [brq], [brq])
                        rc_, brc = RC.next()
                        ps, bp = PS.next()
                        for j in range(4):
                            P.op('pe', (lambda j, ps: lambda e: e.matmul(ps[:, j:j + 1], lhsT=sq[:, 2, j * 128:(j + 1) * 128], rhs=onesb[:, 0:1],
                                                                         start=True, stop=True))(j, ps), [bsq, bconst], [bp])
                        P.op('act', lambda e, ps=ps, rc_=rc_: e.activation(out=rc_[:, 0:4], in_=ps[:, 0:4], func=AF.Sqrt, bias=epsT[:, 0:1], scale=1.0 / 128.0),
                             [bp, bconst], [brc])
                        P.op('dve', lambda e, rc_=rc_: e.reciprocal(out=rc_[:, 4:8], in_=rc_[:, 0:4]), [brc], [brc])
                        rt, brt = RT.next()
                        P.dma('pool', rt[64:96, 0, :], I['c_ropeC'][:, c0:c0 + 512], writes=[brt])
                        P.dma('pool', rt[64:96, 1, :], I['c_ropeS'][:, c0:c0 + 512], writes=[brt])
                        for k2 in range(2):
                            P.op('pool', (lambda k2: lambda e, rt=rt, rq=rq: e.tensor_tensor(out=rt[64:96, 2 + k2, :], in0=rt[64:96, k2, :],
                                                                                             in1=rq[64:96, 0, :], op=ALU.mult))(k2), [brt, brq], [brt])
                        _chk('rstd')
                        for h in range(8):
                            pq, bpq = PS.next()
                            pw, bpw = PS.next()
                            for rc in range(2):
                                P.op('pe', (lambda rc, pq, h: lambda e: e.matmul(pq[0:96, :], lhsT=wuq[:, rc, h * 96:(h + 1) * 96], rhs=cg[:, rc, :],
                                                                                 start=(rc == 0), stop=(rc == 1)))(rc, pq, h), [bcg, bw], [bpq])
                            for rc in range(2):
                                P.op('pe', (lambda rc, pw, h: lambda e: e.matmul(pw[0:96, :], lhsT=wuqsw[:, rc, h * 96:(h + 1) * 96], rhs=cg[:, rc, :],
                                                                                 start=(rc == 0), stop=(rc == 1)))(rc, pw, h), [bcg, bw], [bpw])
                            qs, bqs = QTs.next()
                            qm, bqm = QTm.next()
                            P.op('dve', lambda e, qs=qs, pq=pq, rq=rq: e.tensor_tensor(out=qs[0:64, :], in0=pq[0:64, :], in1=rq[0:64, 0, :], op=ALU.mult),
                                 [bpq, brq], [bqs])
                            P.op('dve', lambda e, qm=qm, pq=pq, rt=rt: e.tensor_tensor(out=qm[64:96, 0, :], in0=pq[64:96, :], in1=rt[64:96, 2, :], op=ALU.mult),
                                 [bpq, brt], [bqm])
                            P.op('dve', lambda e, qm=qm, pw=pw, rt=rt: e.tensor_tensor(out=qm[64:96, 1, :], in0=pw[64:96, :], in1=rt[64:96, 3, :], op=ALU.mult),
                                 [bpw, brt], [bqm])
                            P.op('pool', lambda e, qs=qs, qm=qm: e.tensor_tensor(out=qs[64:96, :], in0=qm[64:96, 0, :], in1=qm[64:96, 1, :], op=ALU.add),
                                 [bqm], [bqs])
                            P.dma('sp', s['QT'][h, :, c0:c0 + 512], qs[:], reads=[bqs], writes=[s['b_QT']])
                        _chk('q')
                        for hp in range(4):
                            ps, bp = PS.next()
                            P.op('pe', lambda e, ps=ps, hp=hp, cg=cg: e.matmul(ps[:], lhsT=wuk[:, hp * 128:(hp + 1) * 128], rhs=cg[:, 2, :], start=True, stop=True),
                                 [bcg, bw], [bp])
                            ks, bks = KS.next()
                            P.op('dve', lambda e, ks=ks, ps=ps, rq=rq: e.tensor_tensor(out=ks[:], in0=ps[:], in1=rq[:, 1, :], op=ALU.mult), [bp, brq], [bks])
                            P.dma('sp', s['KN'][hp * 128:(hp + 1) * 128, c0:c0 + 512], ks[:], reads=[bks], writes=[s['b_KN']])
                        _chk('k')
                        for j in range(4):
                            ps, bp = PS.next()
                            P.op('pe', lambda e, ps=ps, j=j, cg=cg: e.matmul(ps[:], lhsT=cg[:, 2, j * 128:(j + 1) * 128], rhs=wuv[:], start=True, stop=True),
                                 [bcg, bw], [bp])
                            vs, bvs = VS.next()
                            P.op('act', lambda e, vs=vs, ps=ps, rc_=rc_, j=j: e.activation(out=vs[:], in_=ps[:], func=AF.Copy, scale=rc_[:, 4 + j:5 + j]),
                                 [bp, brc], [bvs])
                            P.dma('sp', s['V'][:, :, ci * 4 + j, :].rearrange("h p d -> p h d"), vs[:].rearrange("p (h d) -> p h d", h=8), reads=[bvs], writes=[s['b_V']])
                    _chk('v')
                    for sp in range(L // 512):
                        for g in range(4):
                            ps, bp = PS.next()
                            for jt in range(4):
                                t = sp * 4 + jt
                                terms = []
                                if t > 0:
                                    terms.append((t - 1, 1))
                                terms.append((t, 3 if t == 0 else (4 if t == nt - 1 else 0)))
                                if t < nt - 1:
                                    terms.append((t + 1, 2))
                                for k, (st_, kind) in enumerate(terms):
                                    off = (g * 5 + kind) * 128
                                    P.op('pe', lambda e, ps=ps, jt=jt, st_=st_, g=g, off=off, k=k, n=len(terms): e.matmul(
                                        ps[:, jt * 128:(jt + 1) * 128], lhsT=atok[:, st_, g * 128:(g + 1) * 128], rhs=band[:, off:off + 128],
                                        start=(k == 0), stop=(k == n - 1)), [batok[st_], bband], [bp])
                            rts, brts = RTs.next()
                            copy_op('dve', rts[:], ps[:], [bp], [brts])
                            ps2, bp2 = PS.next()
                            P.op('pe', lambda e, ps2=ps2, g=g, rts=rts: e.matmul(ps2[:], lhsT=poolw[:, g, :], rhs=rts[:], start=True, stop=True), [brts, bw], [bp2])
                            ms, bms = MS.next()
                            P.op('act', lambda e, ms=ms, ps2=ps2, g=g: e.activation(out=ms[:], in_=ps2[:], func=AF.Copy, scale=pscol[:, g:g + 1]), [bp2, bw], [bms])
                            P.dma('sp', s['mixT'][g * 128:(g + 1) * 128, sp * 512:(sp + 1) * 512], ms[:], reads=[bms], writes=[s['b_mixT']])
            P.barrier()

        def ph2():
            with contextlib.ExitStack() as ph:
                KT = RB(ph, nc, 'aK', [96, LS], BF16, 2)
                QT = RB(ph, nc, 'aQ', [96, LS], BF16, 2)
                VT = RB(ph, nc, 'aV', [128, LS // 128, 128], BF16, 2)
                PT = RB(ph, nc, 'aP', [128, 512], BF16, 3)
                RS = RB(ph, nc, 'aR', [128, 512], F32, 2)
                BC = RB(ph, nc, 'aB', [64, 512], F32, 2)
                OS = RB(ph, nc, 'aO', [64, 512], BF16, 2)
                for i in range(2):
                    P.op('pool', lambda e, i=i: e.memset(VT.t[i][:, :, 64:128], 1.0), writes=[VT.b[i]])

                class _Sub:
                    def __init__(self, lo, hi):
                        self.t = PS.t[lo:hi]
                        self.b = PS.b[lo:hi]
                        self.i = 0
                    next = RB.next
                POs = _Sub(0, 2)
                PSs = _Sub(2, 6)
                for s in SEQ:
                    L = s['L']
                    nk = L // 128
                    for h in range(8):
                        kt, bk = KT.next()
                        qt, bq = QT.next()
                        vt, bv = VT.next()
                        P.dma('sp', kt[0:64, 0:L], s['KN'][h * 64:(h + 1) * 64, :], reads=[s['b_KN']], writes=[bk])
                        P.dma('sp', kt[64:96, 0:L], s['KR'][:, :], reads=[s['b_KR']], writes=[bk])
                        P.dma('sp', qt[:, 0:L], s['QT'][h], reads=[s['b_QT']], writes=[bq])
                        P.dma('pool', vt[:, 0:nk, 0:64], s['V'][h],
                              reads=[s['b_V']], writes=[bv])
                        for qc in range(L // 512):
                            po, bpo = POs.next()
                            pend = []

                            def issue_s(k, kt=kt, qt=qt, qc=qc):
                                pss, bps = PSs.next()
                                P.op('pe', lambda e: e.matmul(pss[:], lhsT=kt[:, k * 128:(k + 1) * 128], rhs=qt[:, qc * 512:(qc + 1) * 512], start=True, stop=True),
                                     [bk, bq], [bps])
                                pend.append((pss, bps))
                            for k in range(min(2, nk)):
                                issue_s(k)
                            for k in range(nk):
                                if k + 2 < nk:
                                    issue_s(k + 2)
                                pss, bps = pend.pop(0)
                                pt, bpt = PT.next()
                                P.op('act', lambda e, pt=pt, pss=pss: e.activation(out=pt[:], in_=pss[:], func=AF.Exp), [bps], [bpt])
                                P.op('pe', lambda e, po=po, vt=vt, pt=pt, k=k, nk=nk: e.matmul(po[:], lhsT=vt[:, k, :], rhs=pt[:], start=(k == 0), stop=(k == nk - 1)),
                                     [bv, bpt], [bpo])
                            rs, brs = RS.next()
                            P.op('dve', lambda e, rs=rs, po=po: e.reciprocal(out=rs[64:65, :], in_=po[64:65, :]), [bpo], [brs])
                            pb, bpb = PSs.next()
                            P.op('pe', lambda e, pb=pb, rs=rs: e.matmul(pb[0:64, :], lhsT=onesf[64:65, 0:64], rhs=rs[64:65, :], start=True, stop=True),
                                 [brs, bconst], [bpb])
                            bc, bbc = BC.next()
                            copy_op('act', bc[:], pb[0:64, :], [bpb], [bbc])
                            os_, bos = OS.next()
                            P.op('dve', lambda e, os_=os_, po=po, bc=bc: e.tensor_tensor(out=os_[:], in0=po[0:64, :], in1=bc[:], op=ALU.mult), [bpo, bbc], [bos])
                            P.dma('pool', s['mixT'][512 + h * 64:512 + (h + 1) * 64, qc * 512:(qc + 1) * 512], os_[:], reads=[bos], writes=[s['b_mixT']])
            P.barrier()

        def ph_out(layer):
            with contextlib.ExitStack() as ph:
                T = lambda name, shape, dt: ph.enter_context(sb(name, shape, dt))
                bw = WBuf()
                wout = T('wout', [128, 8, 1024], BF16)
                wsrc = (I['even_w_out'] if layer == 0 else I['odd_w_out'])[0].rearrange("(c p) f -> p c f", p=128)
                for c in range(8):
                    loadw(wout[:, c, :], wsrc[:, c, :], bw)
                grow = rowbc(ph, 'grow', I['norm_mix'][layer, 1], D, bw)
                if layer == 1:
                    fwb = T('fwb', [128, 2, 128], BF16)
                    bfw = Buf()
                    P.op('pool', lambda e: e.memset(fwb[:], 0.0), writes=[bfw])
                    for g in range(4):
                        pp = (g % 2) * 64
                        loadw(fwb[pp:pp + 64, g // 2, pp:pp + 64], I['fnet_w'][0, g], bfw, np_=64)
                tmp = dict(ssq2=RB(ph, nc, 'ssq2', [128, 8], F32, 2), junk=RB(ph, nc, 'junk', [128, 1024], F32, 1),
                           ep=RB(ph, nc, 'ep', [128, 512], F32, 3))
                XT = RB(ph, nc, 'xt', [128, 4, 1024], F32, 2)
                MX = RB(ph, nc, 'mx', [128, 8, 512], BF16, 2)
                MF = RB(ph, nc, 'mf', [128, 2, 512], BF16, 2)
                for s in SEQ:
                    L = s['L']
                    xin, bxin = (s['x'], Buf()) if layer == 0 else (s['X2'], s['b_X2'])
                    xout, bxout = (s['X1'], s['b_X1']) if layer == 0 else (s['X3'], s['b_X3'])
                    for ci in range(L // 512):
                        c0 = ci * 512
                        xt, bx = XT.next()
                        P.dma('sp', xt[:], xin[c0:c0 + 512, :].rearrange("(j p) d -> p j d", p=128), reads=[bxin], writes=[bx])
                        mx, bm = MX.next()
                        if layer == 0:
                            P.dma('pool', mx[:], s['mixT'][:, c0:c0 + 512].rearrange("(c p) t -> p c t", p=128), reads=[s['b_mixT']], writes=[bm])
                        else:
                            P.dma('pool', mx[:, 0:6, :], s['mix2'][:, c0:c0 + 512].rearrange("(c p) t -> p c t", p=128), reads=[s['b_mix2']], writes=[bm])
                            mf, bmf = MF.next()
                            P.dma('pool', mf[:], s['mfT'][:, c0:c0 + 512].rearrange("(c p) t -> p c t", p=128), reads=[s['b_mfT']], writes=[bmf])
                            for j2 in range(2):
                                ps, bp = PS.next()
                                P.op('pe', lambda e, ps=ps, j2=j2, mf=mf: e.matmul(ps[:], lhsT=fwb[:, j2, :], rhs=mf[:, j2, :], start=True, stop=True), [bmf, bfw], [bp])
                                copy_op('act', mx[:, 6 + j2, :], ps[:], [bp], [bm])
                        for j in range(4):
                            pss = []
                            for hf in range(2):
                                ps, bp = PS.next()
                                for c in range(8):
                                    P.op('pe', lambda e, ps=ps, c=c, j=j, hf=hf, mx=mx: e.matmul(ps[:], lhsT=mx[:, c, j * 128:(j + 1) * 128],
                                                                                                 rhs=wout[:, c, hf * 512:(hf + 1) * 512], start=(c == 0), stop=(c == 7)),
                                         [bm, bw], [bp])
                                pss.append((ps, bp))
                            epilogue(pss, xt, bx, j, grow, tmp, bw)
                        P.dma('sp', xout[c0:c0 + 512, :].rearrange("(j p) d -> p j d", p=128), xt[:], reads=[bx], writes=[bxout])
            P.barrier()

        def ph_mlp(layer):
            TT = 256
            nj = TT // 128
            with contextlib.ExitStack() as ph:
                T = lambda name, shape, dt: ph.enter_context(sb(name, shape, dt))
                bw = WBuf()
                wfi = T('wfi', [128, 8, 4096], BF16)
                wfo = T('wfo', [128, 32, 1024], BF16)
                s1 = I['w_ff_in'][layer].rearrange("(c p) f -> p c f", p=128)
                s2 = I['w_ff_out'][layer].rearrange("(c p) f -> p c f", p=128)
                for c in range(8):
                    loadw(wfi[:, c, :], s1[:, c, :], bw)
                for c in range(32):
                    loadw(wfo[:, c, :], s2[:, c, :], bw)
                gcol = colvec(ph, 'gcol', I['norm_mlp'][layer, 0], 8, bw)
                grow = rowbc(ph, 'grow', I['norm_mlp'][layer, 1], D, bw)
                tmp = dict(ssq=RB(ph, nc, 'ssq', [128, 12], F32, 2), ssq2=RB(ph, nc, 'ssq2', [128, 8], F32, 2),
                           junk=RB(ph, nc, 'junk', [128, 1024], F32, 1), xn=RB(ph, nc, 'xn', [128, nj, 1024], BF16, 1),
                           ep=RB(ph, nc, 'ep', [128, 512], F32, 2))
                XT = RB(ph, nc, 'xt', [128, nj, 1024], F32, 2)
                HT = RB(ph, nc, 'hT', [128, 8, TT], BF16, 2)
                F1 = RB(ph, nc, 'f1', [128, 32, TT], BF16, 1)
                RL = RB(ph, nc, 'rl', [128, TT], F32, 3)
                for s in SEQ:
                    L = s['L']
                    xin, bxin = (s['X1'], s['b_X1']) if layer == 0 else (s['X3'], s['b_X3'])
                    xout, bxout = (s['X2'], s['b_X2']) if layer == 0 else (s['y'], Buf())
                    nchunk = L // TT

                    def load_norm(ci):
                        c0 = ci * TT
                        xt, bx = XT.next()
                        P.dma('sp', xt[:], xin[c0:c0 + TT, :].rearrange("(j p) d -> p j d", p=128), reads=[bxin], writes=[bx])
                        xn, bn = rms_norm(xt, bx, nj, tmp)
                        return xt, bx, xn, bn
                    cur = load_norm(0)
                    hT, bh = HT.next()
                    rms_tr(cur[2], cur[3], nj, gcol, hT, bh, bw)
                    for ci in range(nchunk):
                        c0 = ci * TT
                        xt, bx = cur[0], cur[1]
                        f1, bf1 = F1.next()
                        nxt = None
                        for fc in range(32):
                            ps, bp = PS.next()
                            for dc in range(8):
                                P.op('pe', lambda e, ps=ps, dc=dc, fc=fc, hT=hT: e.matmul(ps[:, 0:TT], lhsT=wfi[:, dc, fc * 128:(fc + 1) * 128], rhs=hT[:, dc, :],
                                                                                          start=(dc == 0), stop=(dc == 7)), [bh, bw], [bp])
                            rl, brl = RL.next()
                            P.op('act', lambda e, rl=rl, ps=ps: e.activation(out=rl[:], in_=ps[:, 0:TT], func=AF.Relu), [bp], [brl])
                            eng = 'pool' if fc % 2 else 'dve'
                            P.op(eng, lambda e, rl=rl, f1=f1, fc=fc: e.tensor_tensor(out=f1[:, fc, :], in0=rl[:], in1=rl[:], op=ALU.mult), [brl], [bf1], nowaw=(fc > 1))
                            if fc == 6 and ci + 1 < nchunk:
                                nxt = load_norm(ci + 1)
                        if nxt is not None:
                            hTn, bhn = HT.next()
                            rms_tr(nxt[2], nxt[3], nj, gcol, hTn, bhn, bw)
                        for j in range(nj):
                            pss = []
                            for hf in range(2):
                                ps, bp = PS.next()
                                for fc in range(32):
                                    P.op('pe', lambda e, ps=ps, fc=fc, j=j, hf=hf, f1=f1: e.matmul(ps[:], lhsT=f1[:, fc, j * 128:(j + 1) * 128],
                                                                                                   rhs=wfo[:, fc, hf * 512:(hf + 1) * 512], start=(fc == 0), stop=(fc == 31)),
                                         [bf1, bw], [bp])
                                pss.append((ps, bp))
                            epilogue(pss, xt, bx, j, grow, tmp, bw)
                        P.dma('pool', xout[c0:c0 + TT, :].rearrange("(j p) d -> p j d", p=128), xt[:], reads=[bx], writes=[bxout])
                        if nxt is not None:
                            cur = nxt
                            hT, bh = hTn, bhn
            P.barrier()

        def ph5():
            with contextlib.ExitStack() as ph:
                T = lambda name, shape, dt: ph.enter_context(sb(name, shape, dt))
                bw = WBuf()
                owin = T('owin', [128, 8, 2560], BF16)
                src = I['odd_w_in'][0].rearrange("(c p) f -> p c f", p=128)
                for c in range(8):
                    loadw(owin[:, c, :], src[:, c, :], bw)
                gcol = colvec(ph, 'gcol', I['norm_mix'][1, 0], 8, bw)
                swc = T('swc', [128, 3, 18], F32)
                for j in range(3):
                    P.dma('pool', swc[:, j, :], I['short_w'][0, j].rearrange("(c p) -> p c", p=128), writes=[bw])
                sbc = colvec(ph, 'sbc', I['short_b'][0], 18, bw)
                DCt = T('DCt', [128, 2, 128], F32)
                loadw(DCt[:, 0, :], I['c_DC'], bw)
                loadw(DCt[:, 1, :], I['c_DS'], bw)
                tmp = dict(ssq=RB(ph, nc, 'ssq', [128, 12], F32, 2), junk=RB(ph, nc, 'junk', [128, 1024], F32, 1),
                           xn=RB(ph, nc, 'xn', [128, 4, 1024], BF16, 1))
                XT = RB(ph, nc, 'xt', [128, 4, 1024], F32, 2)
                HT = RB(ph, nc, 'hT', [128, 8, 512], BF16, 2)
                WN = RB(ph, nc, 'wn', [128, 514], F32, 3)
                OT = RB(ph, nc, 'ot', [128, 512], F32, 3)
                FS = RB(ph, nc, 'fs', [128, 512], F32, 2)
                FO = RB(ph, nc, 'fo', [128, 512], F32, 3)
                carry = T('carry', [128, 18, 2], F32)
                bcar = Buf()
                for s in SEQ:
                    L = s['L']
                    fsc = 1.0 / math.sqrt(64.0 * L)
                    P.op('pool', lambda e: e.memset(carry[:], 0.0), writes=[bcar])
                    nch = L // 512
                    for ci in range(nch + 1):
                        c0 = ci * 512
                        last = (ci == nch)
                        if not last:
                            xt, bx = XT.next()
                            P.dma('sp', xt[:], s['X2'][c0:c0 + 512, :].rearrange("(j p) d -> p j d", p=128), reads=[s['b_X2']], writes=[bx])
                            hT, bh = HT.next()
                            rms_transpose(ph, xt, bx, 4, gcol, hT, bh, tmp, bw)
                        for fc in range(18):
                            wn, bwn = WN.next()
                            P.op('pool', lambda e, wn=wn, fc=fc: e.tensor_copy(out=wn[:, 0:2], in_=carry[:, fc, :]), [bcar], [bwn])
                            if not last:
                                ps, bp = PS.next()
                                for dc in range(8):
                                    P.op('pe', lambda e, ps=ps, dc=dc, fc=fc, hT=hT: e.matmul(ps[:], lhsT=owin[:, dc, fc * 128:(fc + 1) * 128], rhs=hT[:, dc, :],
                                                                                              start=(dc == 0), stop=(dc == 7)), [bh, bw], [bp])
                                copy_op('act', wn[:, 2:514], ps[:], [bp], [bwn])
                                nw = 512
                            else:
                                P.op('pool', lambda e, wn=wn: e.memset(wn[:, 2:3], 0.0), [], [bwn])
                                nw = 1
                            ot, bo = OT.next()
                            P.op('dve', lambda e, ot=ot, wn=wn, fc=fc, nw=nw: e.tensor_scalar(out=ot[:, 0:nw], in0=wn[:, 0:nw], scalar1=swc[:, 0, fc:fc + 1],
                                                                                              scalar2=sbc[:, fc:fc + 1], op0=ALU.mult, op1=ALU.add), [bwn, bw], [bo])
                            for j in (1, 2):
                                P.op('dve', lambda e, ot=ot, wn=wn, fc=fc, nw=nw, j=j: e.scalar_tensor_tensor(out=ot[:, 0:nw], in0=wn[:, j:j + nw], scalar=swc[:, j, fc:fc + 1],
                                                                                                              in1=ot[:, 0:nw], op0=ALU.mult, op1=ALU.add), [bwn, bw, bo], [bo])
                            if not last:
                                P.op('pool', lambda e, wn=wn, fc=fc: e.tensor_copy(out=carry[:, fc, :], in_=wn[:, 512:514]), [bwn], [bcar])
                            if ci == 0:
                                P.dma('sp', s['zcT'][fc * 128:(fc + 1) * 128, 0:511], ot[:, 1:512], reads=[bo], writes=[s['b_zcT']])
                            elif not last:
                                P.dma('sp', s['zcT'][fc * 128:(fc + 1) * 128, c0 - 1:c0 + 511], ot[:, 0:512], reads=[bo], writes=[s['b_zcT']])
                            else:
                                P.dma('sp', s['zcT'][fc * 128:(fc + 1) * 128, L - 1:L], ot[:, 0:1], reads=[bo], writes=[s['b_zcT']])
                        if last:
                            continue
                        for f2 in range(2):
                            ps, bp = PS.next()
                            for dc in range(8):
                                P.op('pe', lambda e, ps=ps, dc=dc, f2=f2, hT=hT: e.matmul(ps[:], lhsT=owin[:, dc, 2304 + f2 * 128:2304 + (f2 + 1) * 128], rhs=hT[:, dc, :],
                                                                                          start=(dc == 0), stop=(dc == 7)), [bh, bw], [bp])
                            fs, bfs = FS.next()
                            copy_op('act', fs[:], ps[:], [bp], [bfs])
                            for k2 in range(2):
                                ps2, bp2 = PS.next()
                                P.op('pe', lambda e, ps2=ps2, k2=k2, fs=fs: e.matmul(ps2[:], lhsT=DCt[:, k2, :], rhs=fs[:], start=True, stop=True), [bfs, bw], [bp2])
                                fo, bfo = FO.next()
                                P.op('act', lambda e, fo=fo, ps2=ps2: e.activation(out=fo[:], in_=ps2[:], func=AF.Copy, scale=fsc), [bp2], [bfo])
                                r0 = k2 * 256 + f2 * 128
                                P.dma('pool', s['fcT'][r0:r0 + 128, c0:c0 + 512], fo[:], reads=[bfo], writes=[s['b_fcT']])
            P.barrier()

        def ph6():
            PI = math.pi
            with contextlib.ExitStack() as ph:
                T = lambda name, shape, dt: ph.enter_context(sb(name, shape, dt))
                bw = Buf()
                w1 = T('fw1', [33, 64], F32)
                w2 = T('fw2', [64, 64], F32)
                w3 = T('fw3', [64, 3072], F32)
                P.dma('pool', w1[:], I['filt_w1'][0], writes=[bw])
                P.dma('pool', w2[:], I['filt_w2'][0], writes=[bw])
                P.dma('pool', w3[:], I['filt_w3'][0], writes=[bw])
                cols = T('fcols', [64, 8], F32)
                for k, nm_ in enumerate(('filt_b1', 'filt_b2', 'filt_freq')):
                    P.dma('pool', cols[:, k:k + 1], I[nm_][0].rearrange("(p o) -> p o", o=1), writes=[bw])
                P.op('dve', lambda e: e.tensor_tensor(out=cols[:, 3:4], in0=cols[:, 2:3], in1=cols[:, 0:1], op=ALU.mult), [bw], [bw])
                P.op('dve', lambda e: e.tensor_tensor(out=cols[:, 4:5], in0=cols[:, 2:3], in1=cols[:, 1:2], op=ALU.mult), [bw], [bw])
                negd = T('negd', [128, 6], F32)
                P.dma('pool', negd[:], I['c_negd'], writes=[bw])
                FT = RB(ph, nc, 'ft', [33, 512], F32, 2)
                TB = RB(ph, nc, 'tb', [128, 512], F32, 2)
                DT = RB(ph, nc, 'dt', [128, 6, 512], F32, 2)
                AR = RB(ph, nc, 'ar', [64, 3, 512], F32, 2)
                H1 = RB(ph, nc, 'h1', [64, 512], F32, 2)
                H2 = RB(ph, nc, 'h2', [64, 512], F32, 2)
                HF = RB(ph, nc, 'hf', [128, 512], F32, 3)
                junk = T('fjunk', [128, 512], F32)
                bjunk = Buf()
                ssq = T('fssq', [128, 24, 16], F32)
                tot = T('ftot', [128, 3, 24], F32)
                bss = Buf()

                def sinlayer(ps, bp, fb, out, bo):
                    ar, ba = AR.next()
                    P.op('dve', lambda e: e.tensor_scalar(out=ar[:, 0, :], in0=ps[0:64, :], scalar1=cols[:, 2:3], scalar2=cols[:, fb:fb + 1],
                                                          op0=ALU.mult, op1=ALU.add), [bp, bw], [ba])
                    P.op('dve', lambda e: e.tensor_scalar(out=ar[:, 1, :], in0=ar[:, 0, :], scalar1=PI, scalar2=2 * PI, op0=ALU.is_gt, op1=ALU.mult), [ba], [ba])
                    P.op('dve', lambda e: e.tensor_tensor(out=ar[:, 0, :], in0=ar[:, 0, :], in1=ar[:, 1, :], op=ALU.subtract), [ba], [ba])
                    P.op('dve', lambda e: e.tensor_scalar(out=ar[:, 1, :], in0=ar[:, 0, :], scalar1=-PI, scalar2=2 * PI, op0=ALU.is_lt, op1=ALU.mult), [ba], [ba])
                    P.op('dve', lambda e: e.tensor_tensor(out=ar[:, 0, :], in0=ar[:, 0, :], in1=ar[:, 1, :], op=ALU.add), [ba], [ba])
                    P.op('act', lambda e: e.activation(out=out[:], in_=ar[:, 0, :], func=AF.Sin, scale=0.999999), [ba], [bo])

                for s in SEQ:
                    L = s['L']
                    nch = L // 512
                    for ci in range(nch):
                        c0 = ci * 512
                        ft, bft = FT.next()
                        P.dma('sp', ft[:], I['c_featT_' + s['nm']][:, c0:c0 + 512], writes=[bft])
                        tb, btb = TB.next()
                        P.dma('sp', tb[:], I['c_trow_' + s['nm']][:, c0:c0 + 512].partition_broadcast(128), writes=[btb])
                        dt_, bdt = DT.next()
                        for k in range(6):
                            P.op('act', lambda e, k=k, dt_=dt_, tb=tb: e.activation(out=dt_[:, k, :], in_=tb[:], func=AF.Exp, scale=negd[:, k:k + 1]), [btb, bw], [bdt])
                        ps, bp = PS.next()
                        P.op('pe', lambda e, ps=ps, ft=ft: e.matmul(ps[0:64, :], lhsT=w1[:], rhs=ft[:], start=True, stop=True), [bft, bw], [bp])
                        h1, bh1 = H1.next()
                        sinlayer(ps, bp, 3, h1, bh1)
                        ps, bp = PS.next()
                        P.op('pe', lambda e, ps=ps, h1=h1: e.matmul(ps[0:64, :], lhsT=w2[:], rhs=h1[:], start=True, stop=True), [bh1, bw], [bp])
                        h2, bh2 = H2.next()
                        sinlayer(ps, bp, 4, h2, bh2)
                        for jc in range(24):
                            ps, bp = PS.next()
                            P.op('pe', lambda e, ps=ps, jc=jc, h2=h2: e.matmul(ps[:], lhsT=w3[:, jc * 128:(jc + 1) * 128], rhs=h2[:], start=True, stop=True), [bh2, bw], [bp])
                            hf, bhf = HF.next()
                            P.op('dve', lambda e, hf=hf, ps=ps, dt_=dt_, jc=jc: e.tensor_tensor(out=hf[:], in0=ps[:], in1=dt_[:, jc % 6, :], op=ALU.mult), [bp, bdt], [bhf])
                            P.op('act', lambda e, hf=hf, jc=jc, ci=ci: e.activation(out=junk[:], in_=hf[:], func=AF.Square, accum_out=ssq[:, jc, ci:ci + 1]), [bhf], [bjunk, bss])
                            P.dma('sp', s['hfT'][jc * 128:(jc + 1) * 128, c0:c0 + 512], hf[:], reads=[bhf], writes=[s['b_hfT']])
                    P.op('dve', lambda e, nch=nch: e.tensor_reduce(out=tot[:, 0, :], in_=ssq[:, :, 0:nch], axis=AX.X, op=ALU.add), [bss], [bss])
                    P.op('act', lambda e: e.activation(out=tot[:, 1, :], in_=tot[:, 0, :], func=AF.Sqrt, bias=epsT[:, 0:1], scale=1.0), [bss, bconst], [bss])
                    P.op('dve', lambda e: e.reciprocal(out=tot[:, 2, :], in_=tot[:, 1, :]), [bss], [bss])
                    P.dma('sp', s['invd'][:, 0:24], tot[:, 2, :], reads=[bss], writes=[s['b_invd']])
            P.barrier()

        def ph7():
            CB = 4
            with contextlib.ExitStack() as ph:
                T = lambda name, shape, dt: ph.enter_context(sb(name, shape, dt))
                bw = WBuf()
                F1 = T('cF1', [64, 128], BF16)
                loadw(F1[:], I['c_F1'], bw, np_=64)
                F1fA = T('cF1fA', [64, 128], BF16)
                F1fB = T('cF1fB', [64, 128], BF16)
                loadw(F1fA[:], I['c_F1fA'], bw, np_=64)
                loadw(F1fB[:], I['c_F1fB'], bw, np_=64)
                hbb = rowbc(ph, 'hbb', I['hyena_bias'][0].rearrange("o c -> (o c)"), 1536, bw, npart=64)
                for s in SEQ:
                    L, N2r, nm = s['L'], s['N2'], s['nm']
                    N2 = 128
                    PK = 128 // N2r
                    CBc = CB * PK
                    G = 3
                    with contextlib.ExitStack() as p2:
                        T2 = lambda name, shape, dt: p2.enter_context(sb(name + nm, shape, dt))
                        bc_ = Buf()
                        G1re = T2('G1re', [64, 64], BF16)
                        G1im = T2('G1im', [64, 64], BF16)
                        loadw(G1re[:], I['c_G1re_' + nm], bc_, np_=64)
                        loadw(G1im[:], I['c_G1im_' + nm], bc_, np_=64)
                        TA = T2('TA', [N2, 4, 128], F32)
                        TBt = T2('TB', [N2, 4, 128], F32)
                        TfA = T2('TfA', [N2, 4, 128], F32)
                        TfB = T2('TfB', [N2, 4, 128], F32)
                        for t_, k_ in ((TA, 'TA_'), (TBt, 'TB_'), (TfA, 'TfA_'), (TfB, 'TfB_')):
                            P.dma('pool', t_[:], I['c_' + k_ + nm], writes=[bc_])
                        TcA = T2('TcA', [64, 4, 2 * N2], F32)
                        TcB = T2('TcB', [64, 4, 2 * N2], F32)
                        P.dma('pool', TcA[:], I['c_TcA_' + nm], writes=[bc_])
                        P.dma('pool', TcB[:], I['c_TcB_' + nm], writes=[bc_])
                        F2 = T2('F2', [N2, 3, N2], BF16)
                        FA = T2('FA', [N2, 2 * N2], BF16)
                        FB = T2('FB', [N2, 2 * N2], BF16)
                        loadw(F2[:].rearrange("p a b -> p (a b)"), I['c_F2_' + nm].rearrange("p a b -> p (a b)"), bc_, np_=N2)
                        loadw(FA[:], I['c_FA_' + nm], bc_, np_=N2)
                        loadw(FB[:], I['c_FB_' + nm], bc_, np_=N2)
                        XI = RB(p2, nc, 'xi' + nm, [64, 3, CB, N2], F32, G)
                        HI = RB(p2, nc, 'hi' + nm, [64, 4, CB, N2], F32, 1)
                        IV = RB(p2, nc, 'iv' + nm, [64, 4, CBc], F32, 3)
                        HB = RB(p2, nc, 'hb' + nm, [64, 4, CB, N2], BF16, G)
                        UB = RB(p2, nc, 'ub' + nm, [64, CB, N2], BF16, 2 * G)
                        PPa = RB(p2, nc, 'ppa' + nm, [N2, 4, 128], F32, G + 1)
                        PPb = RB(p2, nc, 'ppb' + nm, [N2, 4, 128], F32, G + 1)
                        YT = RB(p2, nc, 'yt' + nm, [N2, 4, 128], BF16, 2 * G + 2)
                        YF = RB(p2, nc, 'yf' + nm, [N2, 4, 128], F32, 2 * G + 2)
                        KK = RB(p2, nc, 'kk' + nm, [N2, 2, 2, CB, 128], F32, G)
                        GG = RB(p2, nc, 'gg' + nm, [N2, 4, 128], BF16, G)
                        VPa = RB(p2, nc, 'vpa' + nm, [64, 4, 2 * N2], F32, 2)
                        VPb = RB(p2, nc, 'vpb' + nm, [64, 4, 2 * N2], F32, 2)
                        VT_ = RB(p2, nc, 'vt' + nm, [64, 4, 2, N2], BF16, G)
                        UT = RB(p2, nc, 'ut' + nm, [64, CB, N2], F32, 3 * G)
                        MO = RB(p2, nc, 'mo' + nm, [N2, CB, 64], BF16, 3)

                        AUX = {}

                        def aux(b):
                            k = id(b)
                            if k not in AUX:
                                AUX[k] = (b, Buf())
                            return AUX[k][1]

                        def tview(ps, n):
                            return ps[0:n, :].rearrange("p (t c) -> p t c", t=4)

                        def twiddle(ps, bp, A, B, n, w, outre, outim, bo, rd):
                            pa, bpa = PPa.next()
                            pb, bpb = PPb.next()
                            src = ps
                            P.op('dve', lambda e: e.tensor_tensor(out=pa[0:n], in0=src, in1=A, op=ALU.mult), [bp] + rd, [bpa])
                            P.op('dve', lambda e: e.tensor_tensor(out=pb[0:n], in0=src, in1=B, op=ALU.mult), [bp] + rd, [bpb])
                            yield
                            P.op('pool', lambda e: e.tensor_tensor(out=outre, in0=pa[0:n, :, 0:w], in1=pb[0:n, :, w:2 * w], op=ALU.subtract), [bpa, bpb], [bo])
                            P.op('dve', lambda e: e.tensor_tensor(out=outim, in0=pb[0:n, :, 0:w], in1=pa[0:n, :, w:2 * w], op=ALU.add), [bpa, bpb], [aux(bo)])

                        def tform_fwd(tiles, rd, A, B, out, bo):
                            ps, bp = PS.next()
                            pv = tview(ps, N2)
                            for t, tl in enumerate(tiles):
                                P.op('pe', lambda e, t=t, tl=tl: e.matmul(pv[:, t, :], lhsT=tl, rhs=F1[:], start=True, stop=True), rd + [bw], [bp])
                            yield from twiddle(pv, bp, A, B, N2, 64, out[:, :, 0:64], out[:, :, 64:128], bo, [bc_])

                        def nform_fwd(yre_src, yim_src, rd, imsrc_re=None, imsrc_im=None):
                            ps, bp = PS.next()
                            pv = tview(ps, N2)
                            ire = yre_src if imsrc_re is None else imsrc_re
                            iim = yim_src if imsrc_im is None else imsrc_im
                            P.op('pe', lambda e: e.matmul(pv[:, :, 0:64], lhsT=F2[:, 0, :], rhs=yre_src, start=True, stop=False), rd + [bc_], [bp])
                            P.op('pe', lambda e: e.matmul(pv[:, :, 0:64], lhsT=F2[:, 2, :], rhs=yim_src, start=False, stop=True), rd + [bc_], [bp])
                            P.op('pe', lambda e: e.matmul(pv[:, :, 64:128], lhsT=F2[:, 0, :], rhs=iim, start=True, stop=False), rd + [bc_], [bp])
                            P.op('pe', lambda e: e.matmul(pv[:, :, 64:128], lhsT=F2[:, 1, :], rhs=ire, start=False, stop=True), rd + [bc_], [bp])
                            return pv, bp

                        def conv(ub, bub, kk, bkk, o, res):
                            yt, byt = YT.next()
                            yield from tform_fwd([ub[:, t, :] for t in range(CB)], [bub], TA[:], TBt[:], yt, byt)
                            yield
                            pv, bp = nform_fwd(yt[:, :, 0:64], yt[:, :, 64:128], [byt, aux(byt)])
                            gg, bgg = GG.next()
                            yield from twiddle(pv, bp, kk[:, o, 0], kk[:, o, 1], N2, 64, gg[:, :, 0:64], gg[:, :, 64:128], bgg, [bkk])
                            yield
                            vt, bvt = VT_.next()
                            per = min(512 // (2 * N2), CB)
                            for g0 in range(0, CB, per):
                                ps, bp = PS.next()
                                pv2 = ps[0:64, 0:per * 2 * N2].rearrange("p (t c) -> p t c", t=per)
                                for t in range(per):
                                    P.op('pe', lambda e, t=t, g0=g0: e.matmul(pv2[:, t, :], lhsT=gg[:, g0 + t, 0:64], rhs=FA[:], start=True, stop=False), [bgg, aux(bgg), bc_], [bp])
                                    P.op('pe', lambda e, t=t, g0=g0: e.matmul(pv2[:, t, :], lhsT=gg[:, g0 + t, 64:128], rhs=FB[:], start=False, stop=True), [bgg, aux(bgg), bc_], [bp])
                                va, bva = VPa.next()
                                vb, bvb = VPb.next()
                                P.op('dve', lambda e, pv2=pv2, va=va: e.tensor_tensor(out=va[:, 0:per, :], in0=pv2, in1=TcA[:, 0:per, :], op=ALU.mult), [bp, bc_], [bva])
                                P.op('dve', lambda e, pv2=pv2, vb=vb: e.tensor_tensor(out=vb[:, 0:per, :], in0=pv2, in1=TcB[:, 0:per, :], op=ALU.mult), [bp, bc_], [bvb])
                                P.op('pool', lambda e, g0=g0, va=va, vb=vb: e.tensor_tensor(out=vt[:, g0:g0 + per, 0, :], in0=va[:, 0:per, 0:N2], in1=vb[:, 0:per, N2:2 * N2], op=ALU.subtract), [bva, bvb], [bvt], nowaw=(g0 > 0))
                                P.op('dve', lambda e, g0=g0, va=va, vb=vb: e.tensor_tensor(out=vt[:, g0:g0 + per, 1, :], in0=vb[:, 0:per, 0:N2], in1=va[:, 0:per, N2:2 * N2], op=ALU.add), [bva, bvb], [aux(bvt)], nowaw=(g0 > 0))
                                yield
                            ps, bp = PS.next()
                            pvo = ps[0:64, 0:CB * N2].rearrange("p (t c) -> p t c", t=CB)
                            P.op('pe', lambda e: e.matmul(pvo, lhsT=G1re[:], rhs=vt[:, :, 0, :], start=True, stop=False), [bvt, aux(bvt), bc_], [bp])
                            P.op('pe', lambda e: e.matmul(pvo, lhsT=G1im[:], rhs=vt[:, :, 1, :], start=False, stop=True), [bvt, aux(bvt), bc_], [bp])
                            res['pvo'] = (pvo, bp)

                        def zc_view(r0):
                            return s['zcT'][r0:r0 + CBc, :].rearrange("c (a b) -> a c b", b=N2r)

                        def cview(ap_):
                            return ap_.rearrange("p t (c b) -> p (t c) b", b=N2r)

                        def hyena_gen(cb):
                            ch0 = cb * CBc
                            xi, bxi = XI.next()
                            for k in range(3):
                                P.dma('sp', cview(xi[:, k]), zc_view(k * 768 + ch0), reads=[s['b_zcT']], writes=[bxi])
                            hi, bhi = HI.next()
                            for od in range(4):
                                P.dma('act', cview(hi[:, od]), s['hfT'][od * 768 + ch0:od * 768 + ch0 + CBc, :].rearrange("c (a b) -> a c b", b=N2r),
                                      reads=[s['b_hfT']], writes=[bhi])
                            iv, biv = IV.next()
                            for od in range(4):
                                col = od * 768 + ch0
                                P.dma('sp', iv[:, od, :], s['invd'][col % 128:col % 128 + CBc, col // 128:col // 128 + 1].rearrange("c o -> o c").partition_broadcast(64),
                                      reads=[s['b_invd']], writes=[biv])
                            hb, bhb = HB.next()
                            P.op('dve', lambda e: e.tensor_tensor(out=hb[:].rearrange("p o t (c b) -> p o (t c) b", b=N2r), in0=hi[:].rearrange("p o t (c b) -> p o (t c) b", b=N2r),
                                                                  in1=iv[:].unsqueeze(3).to_broadcast([64, 4, CBc, N2r]), op=ALU.mult),
                                 [bhi, biv], [bhb])
                            for od in (1, 3):
                                P.op('pool', lambda e, od=od: e.memset(cview(hb[0:1, od])[:, :, 0:1], 0.0), [], [bhb])
                            yield
                            kk, bkk = KK.next()
                            for o in range(2):
                                yf, byf = YF.next()
                                yb, byb = YF.next()
                                yield from tform_fwd([hb[:, 2 * o, t, :] for t in range(CB)], [bhb], TA[:], TBt[:], yf, byf)
                                yield
                                yield from tform_fwd([hb[:, 2 * o + 1, t, :] for t in range(CB)], [bhb], TA[:], TBt[:], yb, byb)
                                yield
                                ysum, bys = YT.next()
                                ydif, byd = YT.next()
                                P.op('pool', lambda e: e.tensor_tensor(out=ysum[:], in0=yf[:], in1=yb[:], op=ALU.add), [byf, byb, aux(byf), aux(byb)], [bys])
                                P.op('dve', lambda e: e.tensor_tensor(out=ydif[:], in0=yf[:], in1=yb[:], op=ALU.subtract), [byf, byb, aux(byf), aux(byb)], [byd])
                                yield
                                pv, bp = nform_fwd(ysum[:, :, 0:64], ysum[:, :, 64:128], [bys, byd], imsrc_re=ydif[:, :, 0:64], imsrc_im=ydif[:, :, 64:128])
                                for half in range(2):
                                    copy_op('act', kk[:, o, 0, :, half * 64:(half + 1) * 64], pv[:, :, 0:64], [bp], [bkk], nowaw=(o > 0 or half > 0))
                                    copy_op('act', kk[:, o, 1, :, half * 64:(half + 1) * 64], pv[:, :, 64:128], [bp], [bkk], nowaw=True)
                                yield
                            ub, bub = UB.next()
                            copy_op('act', ub[:], xi[:, 0], [bxi], [bub])
                            ucur, bucur = xi[:, 0], bxi
                            for o in range(2):
                                res = {}
                                yield from conv(ub, bub, kk, bkk, o, res)
                                pvo, bp = res['pvo']
                                ut, but = UT.next()
                                bia = hbb[:, o * 768 + ch0:o * 768 + ch0 + CBc].unsqueeze(2).to_broadcast([64, CBc, N2r])
                                P.op('pool', lambda e, ut=ut, ucur=ucur, bia=bia: e.tensor_tensor(out=cview(ut[:]), in0=cview(ucur), in1=bia, op=ALU.mult), [bucur, bw], [but])
                                P.op('dve', lambda e, ut=ut, pvo=pvo: e.tensor_tensor(out=ut[:], in0=pvo, in1=ut[:], op=ALU.add), [bp, but], [but])
                                ut2, but2 = UT.next()
                                P.op('pool', lambda e, ut=ut, ut2=ut2, o=o: e.tensor_tensor(out=ut2[:], in0=ut[:], in1=xi[:, 1 + o], op=ALU.mult), [but, bxi], [but2])
                                ub, bub = UB.next()
                                copy_op('act', ub[:], ut2[:], [but2], [bub])
                                ucur, bucur = ut2[:], but2
                                yield
                            P.dma('sp', s['mix2'][ch0:ch0 + CBc, :].rearrange("c (a b) -> a c b", b=N2r), cview(ub[:]), reads=[bub], writes=[s['b_mix2']])

                        def fnet_gen(cb):
                            q0 = cb * CBc
                            xi, bxi = XI.next()
                            P.dma('sp', cview(xi[:, 0]), s['fcT'][q0:q0 + CBc, :].rearrange("c (a b) -> a c b", b=N2r), reads=[s['b_fcT']], writes=[bxi])
                            P.dma('sp', cview(xi[:, 1]), s['fcT'][256 + q0:256 + q0 + CBc, :].rearrange("c (a b) -> a c b", b=N2r), reads=[s['b_fcT']], writes=[bxi])
                            xc, bxc = UB.next()
                            xs_, bxs = UB.next()
                            copy_op('act', xc[:], xi[:, 0], [bxi], [bxc])
                            copy_op('pool', xs_[:], xi[:, 1], [bxi], [bxs])
                            yield
                            ps, bp = PS.next()
                            pv = tview(ps, N2)
                            for t in range(CB):
                                P.op('pe', lambda e, t=t: e.matmul(pv[:, t, :], lhsT=xc[:, t, :], rhs=F1fA[:], start=True, stop=False), [bxc, bw], [bp])
                                P.op('pe', lambda e, t=t: e.matmul(pv[:, t, :], lhsT=xs_[:, t, :], rhs=F1fB[:], start=False, stop=True), [bxs, bw], [bp])
                            yt, byt = YT.next()
                            yield from twiddle(pv, bp, TfA[:], TfB[:], N2, 64, yt[:, :, 0:64], yt[:, :, 64:128], byt, [bc_])
                            yield
                            ps, bp = PS.next()
                            pz = ps[0:N2, 0:CB * 64].rearrange("p (t c) -> p t c", t=CB)
                            P.op('pe', lambda e: e.matmul(pz, lhsT=F2[:, 0, :], rhs=yt[:, :, 0:64], start=True, stop=False), [byt, aux(byt), bc_], [bp])
                            P.op('pe', lambda e: e.matmul(pz, lhsT=F2[:, 2, :], rhs=yt[:, :, 64:128], start=False, stop=True), [byt, aux(byt), bc_], [bp])
                            mo, bmo = MO.next()
                            copy_op('act', mo[:], pz, [bp], [bmo])
                            P.dma('sp', s['mfT'][q0:q0 + CBc, :].rearrange("(t c) (b a) -> (c b) t a", c=PK, a=64), mo[:], reads=[bmo], writes=[s['b_mfT']])

                        def lockstep(gens, g):
                            it = iter(gens)
                            active = []
                            done = False
                            while True:
                                while not done and len(active) < g:
                                    nx = next(it, None)
                                    if nx is None:
                                        done = True
                                    else:
                                        active.append(nx)
                                if not active:
                                    break
                                for gg_ in list(active):
                                    try:
                                        next(gg_)
                                    except StopIteration:
                                        active.remove(gg_)

                        lockstep((hyena_gen(cb) for cb in range(768 // CBc)), G)
                        lockstep((fnet_gen(cb) for cb in range(256 // CBc)), G)
                    P.barrier()

        phases = [ph1, ph2, lambda: ph_out(0), lambda: ph_mlp(0), ph5, ph6, ph7, lambda: ph_out(1), lambda: ph_mlp(1)]
        for i, f in enumerate(phases):
            if i >= stop_after:
                break
            f()
        if dbg is not None:
            dbg(nc, P, SEQ, I)
        P.emit()
    return nc


_NC_CACHE = {}


def _prep_inputs(inputs):
    consts = _get_consts()
    base = {}
    for k, v in inputs.items():
        if k in ('x_prompt', 'x_sample'):
            continue
        base[k] = np.ascontiguousarray(np.asarray(v, dtype=np.float32))
    for k, v in consts.items():
        base['c_' + k] = np.ascontiguousarray(v)
    xp = np.asarray(inputs['x_prompt'], dtype=np.float32)
    xs = np.asarray(inputs['x_sample'], dtype=np.float32)
    in_maps = []
    for i in range(8):
        m = dict(base)
        m['x_p'] = np.ascontiguousarray(xp[i])
        m['x_s'] = np.ascontiguousarray(xs[i])
        in_maps.append(m)
    return consts, in_maps


def kernel(**inputs):
    consts, in_maps = _prep_inputs(inputs)
    if 'nc' not in _NC_CACHE:
        _NC_CACHE['nc'] = build(consts)
    nc = _NC_CACHE['nc']
    res = run_bass_kernel_spmd(nc, in_maps, core_ids=list(range(8)))
    yp = np.stack([np.asarray(r['y_p'], dtype=np.float32) for r in res.results], 0)
    ys = np.stack([np.asarray(r['y_s'], dtype=np.float32) for r in res.results], 0)
    return (yp, ys)
```

```python
import contextlib
import math
import os
import numpy as np
import concourse.bass as bass
import concourse.mybir as mybir
from concourse.bass_utils import run_bass_kernel_spmd

F32 = mybir.dt.float32
BF16 = mybir.dt.bfloat16
AF = mybir.ActivationFunctionType
ALU = mybir.AluOpType
AX = mybir.AxisListType

D = 1024
EPS = 1e-6
LP, LS = 2048, 8192
ENGS = ('pe', 'act', 'dve', 'pool', 'sp')
NRING = 8


class Buf:
    __slots__ = ('w', 'r', 'chain', 'pending')

    def __init__(self, chain=True):
        self.w = None
        self.r = []
        self.chain = chain
        self.pending = None


class WBuf(Buf):
    __slots__ = ()

    def __init__(self):
        Buf.__init__(self)
        self.pending = []

    def piece(self):
        b = Buf()
        self.pending.append(b)
        return b


class Op:
    __slots__ = ('eng', 'fn', 'deps', 'signal', 'sigval', 'is_dma', 'dsem', 'dval', 'ringdep')

    def __init__(self, eng, fn, is_dma):
        self.eng = eng
        self.fn = fn
        self.deps = []
        self.signal = False
        self.sigval = 0
        self.is_dma = is_dma
        self.dsem = None
        self.dval = 0
        self.ringdep = None


class _Rec:
    def __getattr__(self, name):
        def f(*a, **k):
            self.__dict__['call'] = (name, a, k)
            return None
        return f


class Prog:
    def __init__(self, nc):
        self.nc = nc
        self.ops = {e: [] for e in ENGS}
        self.ndma = {e: 0 for e in ENGS}
        self.ringops = {e: [None] * NRING for e in ENGS}

    def _rec(self, eng, fn, reads, writes, is_dma, nowaw=False):
        for t in list(reads) + list(writes):
            if t.pending:
                pend, t.pending = t.pending, []
                self._rec('pool', self.join_fn, pend, [t], False)
        r = _Rec()
        fn(r)
        op = Op(eng, r.call, is_dma)
        deps = []
        for t in reads:
            if t.w is not None:
                deps.append(t.w)
        for t in writes:
            if t.w is not None and not (nowaw and t.w.eng == eng and not t.w.is_dma) \
                    and not ((not t.chain) and is_dma and t.w.is_dma):
                deps.append(t.w)
            deps.extend(t.r)
        for t in reads:
            t.r.append(op)
        for t in writes:
            t.w = op
            t.r = []
        seen = set()
        for d in deps:
            if id(d) in seen or d is op:
                continue
            seen.add(id(d))
            if (not d.is_dma) and d.eng == eng and eng == 'pe':
                continue
            op.deps.append(d)
            if not d.is_dma:
                d.signal = True
        if is_dma:
            j = self.ndma[eng]
            self.ndma[eng] = j + 1
            op.dsem = (eng, j % NRING)
            op.dval = 16 * (j // NRING + 1)
            op.ringdep = self.ringops[eng][j % NRING]
            self.ringops[eng][j % NRING] = op
        self.ops[eng].append(op)
        return op

    def op(self, eng, fn, reads=(), writes=(), nowaw=False):
        return self._rec(eng, fn, reads, writes, False, nowaw)

    def dma(self, eng, out, in_, reads=(), writes=()):
        return self._rec(eng, lambda e: e.dma_start(out=out, in_=in_), reads, writes, True)

    def barrier(self):
        lasts = []
        for e in ENGS:
            for o in reversed(self.ops[e]):
                if o.fn is not None and not o.is_dma:
                    lasts.append(o)
                    break
            for d in self.ringops[e]:
                if d is not None:
                    lasts.append(d)
        for e in ENGS:
            op = Op(e, None, False)
            for d in lasts:
                if d.is_dma or d.eng != e:
                    op.deps.append(d)
                    if not d.is_dma:
                        d.signal = True
            self.ops[e].append(op)

    def emit(self):
        nc = self.nc
        with contextlib.ExitStack() as st:
            csem = {e: st.enter_context(nc.semaphore('c_' + e)) for e in ENGS}
            dsem = {}
            for e in ENGS:
                for i in range(min(NRING, self.ndma[e])):
                    dsem[(e, i)] = st.enter_context(nc.semaphore('d_%s%d' % (e, i)))
            for e in ENGS:
                c = 0
                for o in self.ops[e]:
                    if o.signal:
                        c += 1
                        o.sigval = c
            st.enter_context(nc.allow_non_contiguous_dma(reason='small strided tables / layout loads'))
            block = st.enter_context(nc.Block())
            handles = {'pe': 'tensor', 'act': 'scalar', 'dve': 'vector', 'pool': 'gpsimd', 'sp': 'sync'}

            def make(e):
                def body(eng):
                    waited = {}
                    for o in self.ops[e]:
                        ws = []
                        for d in o.deps:
                            if d.is_dma:
                                ws.append((('d',) + d.dsem, dsem[d.dsem], d.dval))
                            else:
                                ws.append((('c', d.eng), csem[d.eng], d.sigval))
                        if o.ringdep is not None:
                            d = o.ringdep
                            ws.append((('d',) + d.dsem, dsem[d.dsem], d.dval))
                        for key, sem, val in ws:
                            if waited.get(key, 0) >= val:
                                continue
                            waited[key] = val
                            eng.wait_ge(sem, val)
                        if o.fn is None:
                            continue
                        name_, a_, k_ = o.fn
                        ins = getattr(eng, name_)(*a_, **k_)
                        if o.is_dma:
                            ins.then_inc(dsem[o.dsem], 16)
                        elif o.signal:
                            ins.then_inc(csem[e], 1)
                    for i in range(NRING):
                        d = self.ringops[e][i]
                        if d is not None and waited.get(('d',) + d.dsem, 0) < d.dval:
                            eng.wait_ge(dsem[d.dsem], d.dval)
                return body

            for e in ENGS:
                if self.ops[e]:
                    getattr(block, handles[e])(make(e))


_UID = [0]


class _Stop(Exception):
    pass


def _chk(tag):
    if os.environ.get('KSTOP') == tag:
        raise _Stop()


class RB:
    def __init__(self, st, nc, name, shape, dt, n, psum=False):
        alloc = nc.psum_tensor if psum else nc.sbuf_tensor
        _UID[0] += 1
        self.t = [st.enter_context(alloc('%s_%d_%d' % (name, _UID[0], i), shape, dt)) for i in range(n)]
        self.b = [Buf() for _ in range(n)]
        self.i = 0

    def next(self):
        k = self.i % len(self.t)
        self.i += 1
        return self.t[k], self.b[k]


POOL_WINDOWS = (2, 4, 8, 16)


def _band_tables(L):
    out = np.zeros((128, 4, 5, 128), np.float32)
    for g, w in enumerate(POOL_WINDOWS):
        before = w // 2
        after = w - 1 - before

        def fill(kind, t_tile, s_tile):
            for tl in range(128):
                t = t_tile * 128 + tl
                lo = max(t - before, 0)
                hi = min(t + after, L - 1)
                cnt = hi - lo + 1
                for s in range(lo, hi + 1):
                    sl = s - s_tile * 128
                    if 0 <= sl < 128:
                        out[sl, g, kind, tl] += 1.0 / cnt
                if s_tile == t_tile:
                    out[tl, g, kind, tl] -= 1.0
        nt = L // 128
        fill(0, 2, 2)
        fill(1, 2, 1)
        fill(2, 2, 3)
        fill(3, 0, 0)
        fill(4, nt - 1, nt - 1)
    return out


def _consts():
    c = {}
    c['ident'] = np.eye(128, dtype=np.float32)
    inv = 10000.0 ** (-np.arange(0, 32, 2, dtype=np.float32) / 32)
    ang = np.arange(LS, dtype=np.float32)[None, :] * inv[:, None]
    cs, sn = np.cos(ang).astype(np.float32), np.sin(ang).astype(np.float32)
    c['ropeC'] = np.concatenate([cs, cs], 0)
    c['ropeS'] = np.concatenate([-sn, sn], 0)
    for nm, L in (('p', LP), ('s', LS)):
        c['band_' + nm] = _band_tables(L)
        N = 2 * L
        N2 = L // 64
        a = np.arange(64)[:, None].astype(np.float64)
        ap = np.arange(64)[None, :].astype(np.float64)
        th = np.pi * (2 * ap + 1) * a / 128.0
        c['F1'] = np.concatenate([np.cos(th), -np.sin(th)], 1).astype(np.float32)
        c['G1re_' + nm] = ((2.0 / N) * np.cos(th).T).astype(np.float32)
        c['G1im_' + nm] = ((2.0 / N) * (-np.sin(th)).T).astype(np.float32)
        PK = 128 // N2
        b = np.arange(N2)[:, None].astype(np.float64)
        tt = np.pi * (2 * ap + 1) * b / N
        tre, tim = np.tile(np.cos(tt), (PK, 1)), np.tile(-np.sin(tt), (PK, 1))
        TA = np.concatenate([tre, tre], 1)
        TB = np.concatenate([tim, tim], 1)
        c['TA_' + nm] = np.repeat(TA[:, None, :], 4, 1).astype(np.float32)
        c['TB_' + nm] = np.repeat(TB[:, None, :], 4, 1).astype(np.float32)
        tcr, tci = np.tile(np.cos(tt).T, (1, PK)), np.tile(np.sin(tt).T, (1, PK))
        c['TcA_' + nm] = np.repeat(np.concatenate([tcr, tcr], 1)[:, None, :], 4, 1).astype(np.float32)
        c['TcB_' + nm] = np.repeat(np.concatenate([tci, tci], 1)[:, None, :], 4, 1).astype(np.float32)
        bb = np.arange(N2)[None, :].astype(np.float64)
        f2 = 2 * np.pi * b * bb / N2
        eye = np.eye(PK)
        f2re, f2im = np.kron(eye, np.cos(f2)), np.kron(eye, -np.sin(f2))
        c['F2_' + nm] = np.stack([f2re, f2im, -f2im], 1).astype(np.float32)
        c['FA_' + nm] = np.concatenate([f2re, -f2im], 1).astype(np.float32)
        c['FB_' + nm] = np.concatenate([f2im, f2re], 1).astype(np.float32)
        tf = 2 * np.pi * b * ap / L
        fre, fim = np.tile(np.cos(tf), (PK, 1)), np.tile(-np.sin(tf), (PK, 1))
        c['TfA_' + nm] = np.repeat(np.concatenate([fre, fre], 1)[:, None, :], 4, 1).astype(np.float32)
        c['TfB_' + nm] = np.repeat(np.concatenate([fim, fim], 1)[:, None, :], 4, 1).astype(np.float32)
        t = np.linspace(0.0, 1.0, L, dtype=np.float32)
        bands = np.linspace(1e-4, 15, 16, dtype=np.float32)
        angf = (np.float32(2.0 * math.pi / L) * np.arange(L, dtype=np.float32)[:, None] * bands[None, :]).astype(np.float32)
        feat = np.concatenate([t[:, None], np.cos(angf), -np.sin(angf)], -1).astype(np.float32)
        c['featT_' + nm] = np.ascontiguousarray(feat.T)
        c['trow_' + nm] = t[None, :].copy()
    ff = 2 * np.pi * a * ap / 64.0
    c['F1fA'] = np.concatenate([np.cos(ff), -np.sin(ff)], 1).astype(np.float32)
    c['F1fB'] = np.concatenate([-np.sin(ff), -np.cos(ff)], 1).astype(np.float32)
    dd = np.arange(64)[:, None] * np.arange(64)[None, :]
    c64, s64 = np.cos(2 * np.pi * dd / 64.0), np.sin(2 * np.pi * dd / 64.0)
    z = np.zeros((64, 64))
    c['DC'] = np.block([[c64, z], [z, c64]]).astype(np.float32)
    c['DS'] = np.block([[s64, z], [z, s64]]).astype(np.float32)
    deltas = np.abs(np.linspace(math.log(1e-2) / 1.5, math.log(1e-2) / 0.3, 768, dtype=np.float32))
    c['negd'] = np.ascontiguousarray((-deltas).reshape(6, 128).T).astype(np.float32)
    return c


_CONST_CACHE = {}


def _get_consts():
    if not _CONST_CACHE:
        _CONST_CACHE.update(_consts())
    return _CONST_CACHE


def build(consts, stop_after=99, dbg=None):
    nc = bass.Bass("TRN2", target_bir_lowering=False)

    def sb(name, shape, dt):
        _UID[0] += 1
        return nc.sbuf_tensor('%s_%d' % (name, _UID[0]), shape, dt)
    I = {}

    def inp(name, shape):
        I[name] = nc.dram_tensor(name, list(shape), F32, kind="ExternalInput").ap()
        return I[name]

    specs = {
        "x_p": (LP, D), "x_s": (LS, D), "norm_mix": (2, 2, D), "norm_mlp": (2, 2, D),
        "w_ff_in": (2, D, 4096), "w_ff_out": (2, 4096, D), "even_w_in": (1, D, 928),
        "pool_w": (1, 4, 128, 128), "pool_scale": (1, 512), "mla_q_norm": (1, 256),
        "mla_w_uq": (1, 256, 768), "mla_kv_norm": (1, 128), "mla_w_ukv": (1, 128, 1024),
        "even_w_out": (1, D, D), "odd_w_in": (1, D, 2560), "short_w": (1, 3, 2304),
        "short_b": (1, 2304), "filt_w1": (1, 33, 64), "filt_b1": (1, 64), "filt_w2": (1, 64, 64),
        "filt_b2": (1, 64), "filt_w3": (1, 64, 3072), "filt_freq": (1, 64), "hyena_bias": (1, 2, 768),
        "fnet_w": (1, 4, 64, 64), "odd_w_out": (1, D, D),
    }
    for k, v in specs.items():
        inp(k, v)
    for k, v in consts.items():
        inp('c_' + k, v.shape)
    y_p = nc.dram_tensor("y_p", [LP, D], F32, kind="ExternalOutput").ap()
    y_s = nc.dram_tensor("y_s", [LS, D], F32, kind="ExternalOutput").ap()

    def scratch(name, shape, dt):
        return nc.dram_tensor(name, list(shape), dt, kind="Internal").ap()

    SEQ = []
    for nm, L, xin, yout in (('p', LP, I['x_p'], y_p), ('s', LS, I['x_s'], y_s)):
        s = dict(nm=nm, L=L, x=xin, y=yout, N2=L // 64)
        s['QT'] = scratch('QT' + nm, [8, 96, L], BF16)
        s['KN'] = scratch('KN' + nm, [512, L], BF16)
        s['KR'] = scratch('KR' + nm, [32, L], BF16)
        s['V'] = scratch('V' + nm, [8, 128, L // 128, 64], BF16)
        s['mixT'] = scratch('mixT' + nm, [D, L], BF16)
        s['X1'] = scratch('X1' + nm, [L, D], F32)
        s['X2'] = scratch('X2' + nm, [L, D], F32)
        s['zcT'] = scratch('zcT' + nm, [2304, L], F32)
        s['fcT'] = scratch('fcT' + nm, [512, L], F32)
        s['hfT'] = scratch('hfT' + nm, [3072, L], F32)
        s['invd'] = scratch('invd' + nm, [128, 32], F32)
        s['mix2'] = scratch('mix2' + nm, [768, L], BF16)
        s['mfT'] = scratch('mfT' + nm, [256, L], BF16)
        s['X3'] = scratch('X3' + nm, [L, D], F32)
        for k in ('QT', 'KN', 'KR', 'V', 'mixT', 'X1', 'X2', 'zcT', 'fcT', 'hfT', 'invd', 'mix2', 'mfT', 'X3'):
            s['b_' + k] = Buf(chain=(k == 'zcT'))
        SEQ.append(s)

    P = Prog(nc)
    NOB = Buf

    with contextlib.ExitStack() as top:
        def gt(name, shape, dt):
            return top.enter_context(sb(name, shape, dt))

        PS = RB(top, nc, 'ps', [128, 512], F32, 6, psum=True)
        PST = RB(top, nc, 'pst', [128, 1024], BF16, 2, psum=True)
        ident = gt('ident', [128, 128], BF16)
        onesb = gt('onesb', [128, 128], BF16)
        onesf = gt('onesf', [128, 128], F32)
        epsT = gt('epsT', [128, 2], F32)
        stg = RB(top, nc, 'stg', [128, 1024], F32, 2)
        bconst = Buf()
        dummy = gt('dummyj', [128, 4], F32)
        P.join_fn = lambda e: e.memset(dummy[:], 0.0)
        P.op('pool', lambda e: e.memset(onesb[:], 1.0), writes=[bconst])
        P.op('pool', lambda e: e.memset(onesf[:], 1.0), writes=[bconst])
        P.op('pool', lambda e: e.memset(epsT[:, 0:1], EPS), writes=[bconst])
        P.op('pool', lambda e: e.memset(epsT[:, 1:2], 96.0 * EPS), writes=[bconst])
        _castc = [0]

        def cast_engine():
            _castc[0] += 1
            return ('act', 'pool')[_castc[0] % 2]

        def copy_op(eng, out, in_, reads, writes, nowaw=False):
            if eng == 'act':
                P.op('act', lambda e: e.copy(out=out, in_=in_), reads, writes, nowaw=nowaw)
            else:
                P.op(eng, lambda e: e.tensor_copy(out=out, in_=in_), reads, writes, nowaw=nowaw)

        def loadw(dst, src, dbuf, np_=128):
            cols = dst.shape[-1]
            assert len(dst.shape) == 2 and len(src.shape) == 2, (dst.shape, src.shape)
            for c0 in range(0, cols, 1024):
                cw = min(1024, cols - c0)
                t, b = stg.next()
                P.dma('sp', t[0:np_, 0:cw], src[:, c0:c0 + cw], writes=[b])
                copy_op(cast_engine(), dst[:, c0:c0 + cw], t[0:np_, 0:cw], [b], [dbuf.piece() if isinstance(dbuf, WBuf) else dbuf])

        loadw(ident[:], I['c_ident'], bconst)

        def rms_norm(xt, bx, nj, tmp):
            ssq, bs = tmp['ssq'].next()
            junk, bj = tmp['junk'].next()
            for j in range(nj):
                P.op('act', (lambda j: lambda e: e.activation(out=junk[:], in_=xt[:, j, :], func=AF.Square,
                                                               accum_out=ssq[:, j:j + 1]))(j), [bx], [bj, bs])
            P.op('act', lambda e: e.activation(out=ssq[:, 4:4 + nj], in_=ssq[:, 0:nj], func=AF.Sqrt,
                                               bias=epsT[:, 0:1], scale=1.0 / D), [bs, bconst], [bs])
            P.op('dve', lambda e: e.reciprocal(out=ssq[:, 8:8 + nj], in_=ssq[:, 4:4 + nj]), [bs], [bs])
            xn, bn = tmp['xn'].next()
            for j in range(nj):
                P.op('dve', (lambda j: lambda e: e.tensor_scalar(out=xn[:, j, :], in0=xt[:, j, :],
                                                                 scalar1=ssq[:, 8 + j:9 + j], scalar2=None,
                                                                 op0=ALU.mult))(j), [bx, bs], [bn], nowaw=(j > 0))
            return xn, bn

        def rms_tr(xn, bn, nj, gcol, hT, bh, bg):
            for dc in range(8):
                pt, bp = PST.next()
                for j in range(nj):
                    P.op('pe', (lambda j, dc, pt: lambda e: e.transpose(pt[:, j * 128:(j + 1) * 128],
                                                                        xn[:, j, dc * 128:(dc + 1) * 128], ident[:]))(j, dc, pt),
                         [bn, bconst], [bp])
                P.op('act', (lambda dc, pt: lambda e: e.activation(out=hT[:, dc, :], in_=pt[:, 0:nj * 128], func=AF.Copy,
                                                                   scale=gcol[:, dc:dc + 1]))(dc, pt), [bp, bg], [bh], nowaw=(dc > 0))

        def rms_transpose(ph, xt, bx, nj, gcol, hT, bh, tmp, bg):
            xn, bn = rms_norm(xt, bx, nj, tmp)
            rms_tr(xn, bn, nj, gcol, hT, bh, bg)

        def epilogue(pss, xt, bx, j, grow, tmp, bg):
            ssq, bs = tmp['ssq2'].next()
            junk, bj = tmp['junk'].next()
            for h in range(2):
                P.op('act', (lambda h: lambda e: e.activation(out=junk[:, 0:512], in_=pss[h][0][:], func=AF.Square,
                                                               accum_out=ssq[:, h:h + 1]))(h), [pss[h][1]], [bj, bs])
            P.op('dve', lambda e: e.tensor_tensor(out=ssq[:, 2:3], in0=ssq[:, 0:1], in1=ssq[:, 1:2], op=ALU.add), [bs], [bs])
            P.op('act', lambda e: e.activation(out=ssq[:, 3:4], in_=ssq[:, 2:3], func=AF.Sqrt, bias=epsT[:, 0:1],
                                               scale=1.0 / D), [bs, bconst], [bs])
            P.op('dve', lambda e: e.reciprocal(out=ssq[:, 4:5], in_=ssq[:, 3:4]), [bs], [bs])
            for h in range(2):
                t, bt = tmp['ep'].next()
                P.op('dve', (lambda h, t: lambda e: e.scalar_tensor_tensor(out=t[:], in0=pss[h][0][:], scalar=ssq[:, 4:5],
                                                                           in1=grow[:, h * 512:(h + 1) * 512], op0=ALU.mult,
                                                                           op1=ALU.mult))(h, t), [pss[h][1], bs, bg], [bt])
                P.op('pool', (lambda h, t: lambda e: e.tensor_tensor(out=xt[:, j, h * 512:(h + 1) * 512], in0=t[:],
                                                                     in1=xt[:, j, h * 512:(h + 1) * 512], op=ALU.add))(h, t),
                     [bt, bx], [bx])

        def colvec(ph, name, src1d, ncol, bufc):
            t = ph.enter_context(sb(name, [128, ncol], F32))
            P.dma('pool', t[:], src1d.rearrange("(c p) -> p c", p=128), writes=[bufc.piece() if isinstance(bufc, WBuf) else bufc])
            return t

        def rowbc(ph, name, src1d, n, bufc, npart=128):
            t = ph.enter_context(sb(name, [npart, n], F32))
            P.dma('pool', t[:], src1d.rearrange("(o n) -> o n", o=1).partition_broadcast(npart), writes=[bufc.piece() if isinstance(bufc, WBuf) else bufc])
            return t

        def ph1():
            try:
                ph1_()
            except _Stop:
                pass
            P.barrier()

        def ph1_():
            with contextlib.ExitStack() as ph:
                try:
                    ph1_body(ph)
                except _Stop:
                    pass

        def ph1_body(ph):
            if True:
                T = lambda name, shape, dt: ph.enter_context(sb(name, shape, dt))
                bw = WBuf()
                win = T('win', [128, 8, 928], BF16)
                winsw = T('winsw', [128, 8, 32], BF16)
                wuq = T('wuq', [128, 2, 768], BF16)
                wuqsw = T('wuqsw', [128, 2, 768], BF16)
                wuk = T('wuk', [128, 512], BF16)
                wuv = T('wuv', [128, 512], BF16)
                poolw = T('poolw', [128, 4, 128], BF16)
                ewi = I['even_w_in'][0].rearrange("(c p) f -> p c f", p=128)
                for dc in range(8):
                    loadw(win[:, dc, :], ewi[:, dc, :], bw)
                    loadw(winsw[:, dc, 0:16], ewi[:, dc, 912:928], bw)
                    loadw(winsw[:, dc, 16:32], ewi[:, dc, 896:912], bw)
                uq = I['mla_w_uq'][0].rearrange("(c p) f -> p c f", p=128)
                for rc in range(2):
                    loadw(wuq[:, rc, :], uq[:, rc, :], bw)
                    loadw(wuqsw[:, rc, :], uq[:, rc, :], bw)
                    for h in range(8):
                        loadw(wuqsw[:, rc, h * 96 + 64:h * 96 + 80], uq[:, rc, h * 96 + 80:h * 96 + 96], bw)
                        loadw(wuqsw[:, rc, h * 96 + 80:h * 96 + 96], uq[:, rc, h * 96 + 64:h * 96 + 80], bw)
                ukv = I['mla_w_ukv'][0]
                for h in range(8):
                    loadw(wuk[:, h * 64:(h + 1) * 64], ukv[:, h * 128:h * 128 + 64], bw)
                    loadw(wuv[:, h * 64:(h + 1) * 64], ukv[:, h * 128 + 64:h * 128 + 128], bw)
                for g in range(4):
                    loadw(poolw[:, g, :], I['pool_w'][0, g], bw)
                g0col = colvec(ph, 'g0col', I['norm_mix'][0, 0], 8, bw)
                pscol = colvec(ph, 'pscol', I['pool_scale'][0], 4, bw)
                qncol = colvec(ph, 'qncol', I['mla_q_norm'][0], 2, bw)
                kvcol = colvec(ph, 'kvcol', I['mla_kv_norm'][0], 1, bw)
                band = T('band', [128, 4 * 5 * 128], BF16)
                _chk('w')
                tmp = dict(ssq=RB(ph, nc, 'ssq', [128, 12], F32, 2), junk=RB(ph, nc, 'junk', [128, 1024], F32, 1),
                           xn=RB(ph, nc, 'xn', [128, 4, 1024], BF16, 1))
                XT = RB(ph, nc, 'xt', [128, 4, 1024], F32, 1)
                HT = RB(ph, nc, 'hT', [128, 8, 512], BF16, 1)
                atok = T('atok', [128, 64, 512], BF16)
                SQ = RB(ph, nc, 'sq', [128, 3, 512], BF16, 2)
                CG = RB(ph, nc, 'cg', [128, 3, 512], BF16, 2)
                RQ = RB(ph, nc, 'rq', [128, 2, 512], F32, 1)
                RT = RB(ph, nc, 'rt', [96, 4, 512], F32, 1)
                RK = RB(ph, nc, 'rk', [32, 2, 512], F32, 1)
                KT = RB(ph, nc, 'kt', [32, 2, 512], F32, 1)
                KRO = RB(ph, nc, 'kro', [32, 512], BF16, 2)
                QTs = RB(ph, nc, 'qTs', [96, 512], BF16, 3)
                QTm = RB(ph, nc, 'qTm', [96, 2, 512], F32, 1)
                KS = RB(ph, nc, 'ks', [128, 512], BF16, 2)
                VS = RB(ph, nc, 'vs', [128, 512], BF16, 2)
                RC = RB(ph, nc, 'rc', [128, 12], F32, 2)
                RTs = RB(ph, nc, 'rTs', [128, 512], BF16, 2)
                MS = RB(ph, nc, 'ms', [128, 512], BF16, 2)
                for s in SEQ:
                    L = s['L']
                    nt = L // 128
                    batok = [Buf() for _ in range(nt)]
                    bband = Buf()
                    loadw(band[:], I['c_band_' + s['nm']].rearrange("p g k t -> p (g k t)"), bband)
                    for ci in range(L // 512):
                        c0 = ci * 512
                        xt, bx = XT.next()
                        P.dma('sp', xt[:], s['x'][c0:c0 + 512, :].rearrange("(j p) d -> p j d", p=128), writes=[bx])
                        hT, bh = HT.next()
                        rms_transpose(ph, xt, bx, 4, g0col, hT, bh, tmp, bw)
                        _chk('rms')
                        for j in range(4):
                            ps, bp = PS.next()
                            for dc in range(8):
                                P.op('pe', (lambda j, dc, ps: lambda e: e.matmul(ps[:], lhsT=hT[:, dc, j * 128:(j + 1) * 128],
                                                                                 rhs=win[:, dc, 0:512], start=(dc == 0), stop=(dc == 7)))(j, dc, ps),
                                     [bh, bw], [bp])
                            copy_op('act', atok[:, ci * 4 + j, :], ps[:], [bp], [batok[ci * 4 + j]])
                        _chk('atok')
                        sq, bsq = SQ.next()
                        cg, bcg = CG.next()
                        for k3, (lo, ncol) in enumerate(((512, qncol[:, 0:1]), (640, qncol[:, 1:2]), (768, kvcol[:, 0:1]))):
                            ps, bp = PS.next()
                            for dc in range(8):
                                P.op('pe', (lambda dc, ps, lo: lambda e: e.matmul(ps[:], lhsT=win[:, dc, lo:lo + 128], rhs=hT[:, dc, :],
                                                                                  start=(dc == 0), stop=(dc == 7)))(dc, ps, lo), [bh, bw], [bp])
                            P.op('act', (lambda k3, ps: lambda e: e.activation(out=sq[:, k3, :], in_=ps[:], func=AF.Square))(k3, ps), [bp], [bsq])
                            P.op('dve', (lambda k3, ps, ncol: lambda e: e.tensor_scalar(out=cg[:, k3, :], in0=ps[:], scalar1=ncol, scalar2=None,
                                                                                        op0=ALU.mult))(k3, ps, ncol), [bp, bw, bsq], [bcg])
                        _chk('lat')
                        rk, brk = RK.next()
                        P.dma('pool', rk[:, 0, :], I['c_ropeC'][:, c0:c0 + 512], writes=[brk])
                        P.dma('pool', rk[:, 1, :], I['c_ropeS'][:, c0:c0 + 512], writes=[brk])
                        kt, bkt = KT.next()
                        for k2, (wt, lo) in enumerate(((win, 896), (winsw, 0))):
                            ps, bp = PS.next()
                            for dc in range(8):
                                P.op('pe', (lambda dc, ps, wt, lo: lambda e: e.matmul(ps[0:32, :], lhsT=wt[:, dc, lo:lo + 32], rhs=hT[:, dc, :],
                                                                                      start=(dc == 0), stop=(dc == 7)))(dc, ps, wt, lo), [bh, bw], [bp])
                            P.op('dve', (lambda k2, ps: lambda e: e.tensor_tensor(out=kt[:, k2, :], in0=ps[0:32, :], in1=rk[:, k2, :],
                                                                                  op=ALU.mult))(k2, ps), [bp, brk], [bkt])
                        kro, bkro = KRO.next()
                        P.op('pool', lambda e, kt=kt, kro=kro: e.tensor_tensor(out=kro[:], in0=kt[:, 0, :], in1=kt[:, 1, :], op=ALU.add), [bkt], [bkro])
                        P.dma('pool', s['KR'][:, c0:c0 + 512], kro[:], reads=[bkro], writes=[s['b_KR']])
                        _chk('krope')
                        rq, brq = RQ.next()
                        ps, bp = PS.next()
                        for rc in range(2):
                            P.op('pe', (lambda rc, ps: lambda e: e.matmul(ps[:], lhsT=onesb[:], rhs=sq[:, rc, :], start=(rc == 0), stop=(rc == 1)))(rc, ps),
                                 [bsq, bconst], [bp])
                        P.op('act', lambda e, ps=ps, rq=rq: e.activation(out=rq[:, 0, :], in_=ps[:], func=AF.Sqrt, bias=epsT[:, 1:2], scale=96.0 / 256.0),
                             [bp, bconst], [brq])
                        P.op('dve', lambda e, rq=rq: e.reciprocal(out=rq[:, 0, :], in_=rq[:, 0, :]), [brq], [brq])
                        ps, bp = PS.next()
                        P.op('pe', lambda e, ps=ps, sq=sq: e.matmul(ps[:], lhsT=onesb[:], rhs=sq[:, 2, :], start=True, stop=True), [bsq, bconst], [bp])
                        P.op('act', lambda e, ps=ps, rq=rq: e.activation(out=rq[:, 1, :], in_=ps[:], func=AF.Sqrt, bias=epsT[:, 0:1], scale=1.0 / 128.0),
                             [bp, bconst], [brq])
                        P.op('dve', lambda e, rq=rq: e.reciprocal(out=rq[:, 1, :], in_=rq[:, 1, :]), [brq], [brq])
                        rc_, brc = RC.next()
                        ps, bp = PS.next()
                        for j in range(4):
                            P.op('pe', (lambda j, ps: lambda e: e.matmul(ps[:, j:j + 1], lhsT=sq[:, 2, j * 128:(j + 1) * 128], rhs=onesb[:, 0:1],
                                                                         start=True, stop=True))(j, ps), [bsq, bconst], [bp])
                        P.op('act', lambda e, ps=ps, rc_=rc_: e.activation(out=rc_[:, 0:4], in_=ps[:, 0:4], func=AF.Sqrt, bias=epsT[:, 0:1], scale=1.0 / 128.0),
                             [bp, bconst], [brc])
                        P.op('dve', lambda e, rc_=rc_: e.reciprocal(out=rc_[:, 4:8], in_=rc_[:, 0:4]), [brc], [brc])
                        rt, brt = RT.next()
                        P.dma('pool', rt[64:96, 0, :], I['c_ropeC'][:, c0:c0 + 512], writes=[brt])
                        P.dma('pool', rt[64:96, 1, :], I['c_ropeS'][:, c0:c0 + 512], writes=[brt])
                        for k2 in range(2):
                            P.op('pool', (lambda k2: lambda e, rt=rt, rq=rq: e.tensor_tensor(out=rt[64:96, 2 + k2, :], in0=rt[64:96, k2, :],
                                                                                             in1=rq[64:96, 0, :], op=ALU.mult))(k2), [brt, brq], [brt])
                        _chk('rstd')
                        for h in range(8):
                            pq, bpq = PS.next()
                            pw, bpw = PS.next()
                            for rc in range(2):
                                P.op('pe', (lambda rc, pq, h: lambda e: e.matmul(pq[0:96, :], lhsT=wuq[:, rc, h * 96:(h + 1) * 96], rhs=cg[:, rc, :],
                                                                                 start=(rc == 0), stop=(rc == 1)))(rc, pq, h), [bcg, bw], [bpq])
                            for rc in range(2):
                                P.op('pe', (lambda rc, pw, h: lambda e: e.matmul(pw[0:96, :], lhsT=wuqsw[:, rc, h * 96:(h + 1) * 96], rhs=cg[:, rc, :],
                                                                                 start=(rc == 0), stop=(rc == 1)))(rc, pw, h), [bcg, bw], [bpw])
                            qs, bqs = QTs.next()
                            qm, bqm = QTm.next()
                            P.op('dve', lambda e, qs=qs, pq=pq, rq=rq: e.tensor_tensor(out=qs[0:64, :], in0=pq[0:64, :], in1=rq[0:64, 0, :], op=ALU.mult),
                                 [bpq, brq], [bqs])
                            P.op('dve', lambda e, qm=qm, pq=pq, rt=rt: e.tensor_tensor(out=qm[64:96, 0, :], in0=pq[64:96, :], in1=rt[64:96, 2, :], op=ALU.mult),
                                 [bpq, brt], [bqm])
                            P.op('dve', lambda e, qm=qm, pw=pw, rt=rt: e.tensor_tensor(out=qm[64:96, 1, :], in0=pw[64:96, :], in1=rt[64:96, 3, :], op=ALU.mult),
                                 [bpw, brt], [bqm])
                            P.op('pool', lambda e, qs=qs, qm=qm: e.tensor_tensor(out=qs[64:96, :], in0=qm[64:96, 0, :], in1=qm[64:96, 1, :], op=ALU.add),
                                 [bqm], [bqs])
                            P.dma('sp', s['QT'][h, :, c0:c0 + 512], qs[:], reads=[bqs], writes=[s['b_QT']])
                        _chk('q')
                        for hp in range(4):
                            ps, bp = PS.next()
                            P.op('pe', lambda e, ps=ps, hp=hp, cg=cg: e.matmul(ps[:], lhsT=wuk[:, hp * 128:(hp + 1) * 128], rhs=cg[:, 2, :], start=True, stop=True),
                                 [bcg, bw], [bp])
                            ks, bks = KS.next()
                            P.op('dve', lambda e, ks=ks, ps=ps, rq=rq: e.tensor_tensor(out=ks[:], in0=ps[:], in1=rq[:, 1, :], op=ALU.mult), [bp, brq], [bks])
                            P.dma('sp', s['KN'][hp * 128:(hp + 1) * 128, c0:c0 + 512], ks[:], reads=[bks], writes=[s['b_KN']])
                        _chk('k')
                        for j in range(4):
                            ps, bp = PS.next()
                            P.op('pe', lambda e, ps=ps, j=j, cg=cg: e.matmul(ps[:], lhsT=cg[:, 2, j * 128:(j + 1) * 128], rhs=wuv[:], start=True, stop=True),
                                 [bcg, bw], [bp])
                            vs, bvs = VS.next()
                            P.op('act', lambda e, vs=vs, ps=ps, rc_=rc_, j=j: e.activation(out=vs[:], in_=ps[:], func=AF.Copy, scale=rc_[:, 4 + j:5 + j]),
                                 [bp, brc], [bvs])
                            P.dma('sp', s['V'][:, :, ci * 4 + j, :].rearrange("h p d -> p h d"), vs[:].rearrange("p (h d) -> p h d", h=8), reads=[bvs], writes=[s['b_V']])
                    _chk('v')
                    for sp in range(L // 512):
                        for g in range(4):
                            ps, bp = PS.next()
                            for jt in range(4):
                                t = sp * 4 + jt
                                terms = []
                                if t > 0:
                                    terms.append((t - 1, 1))
                                terms.append((t, 3 if t == 0 else (4 if t == nt - 1 else 0)))
                                if t < nt - 1:
                                    terms.append((t + 1, 2))
                                for k, (st_, kind) in enumerate(terms):
                                    off = (g * 5 + kind) * 128
                                    P.op('pe', lambda e, ps=ps, jt=jt, st_=st_, g=g, off=off, k=k, n=len(terms): e.matmul(
                                        ps[:, jt * 128:(jt + 1) * 128], lhsT=atok[:, st_, g * 128:(g + 1) * 128], rhs=band[:, off:off + 128],
                                        start=(k == 0), stop=(k == n - 1)), [batok[st_], bband], [bp])
                            rts, brts = RTs.next()
                            copy_op('dve', rts[:], ps[:], [bp], [brts])
                            ps2, bp2 = PS.next()
                            P.op('pe', lambda e, ps2=ps2, g=g, rts=rts: e.matmul(ps2[:], lhsT=poolw[:, g, :], rhs=rts[:], start=True, stop=True), [brts, bw], [bp2])
                            ms, bms = MS.next()
                            P.op('act', lambda e, ms=ms, ps2=ps2, g=g: e.activation(out=ms[:], in_=ps2[:], func=AF.Copy, scale=pscol[:, g:g + 1]), [bp2, bw], [bms])
                            P.dma('sp', s['mixT'][g * 128:(g + 1) * 128, sp * 512:(sp + 1) * 512], ms[:], reads=[bms], writes=[s['b_mixT']])
            P.barrier()

        def ph2():
            with contextlib.ExitStack() as ph:
                KT = RB(ph, nc, 'aK', [96, LS], BF16, 2)
                QT = RB(ph, nc, 'aQ', [96, LS], BF16, 2)
                VT = RB(ph, nc, 'aV', [128, LS // 128, 128], BF16, 2)
                PT = RB(ph, nc, 'aP', [128, 512], BF16, 4)
                RS = RB(ph, nc, 'aR', [128, 512], F32, 2)
                BC = RB(ph, nc, 'aB', [64, 512], F32, 2)
                OS = RB(ph, nc, 'aO', [64, 512], BF16, 2)
                for i in range(2):
                    P.op('pool', lambda e, i=i: e.memset(VT.t[i][:, :, 64:128], 1.0), writes=[VT.b[i]])

                class _Sub:
                    def __init__(self, lo, hi):
                        self.t = PS.t[lo:hi]
                        self.b = PS.b[lo:hi]
                        self.i = 0
                    next = RB.next
                POs = _Sub(0, 2)
                PSs = _Sub(2, 6)
                for s in SEQ:
                    L = s['L']
                    nk = L // 128
                    for h in range(8):
                        kt, bk = KT.next()
                        qt, bq = QT.next()
                        vt, bv = VT.next()
                        P.dma('sp', kt[0:64, 0:L], s['KN'][h * 64:(h + 1) * 64, :], reads=[s['b_KN']], writes=[bk])
                        P.dma('sp', kt[64:96, 0:L], s['KR'][:, :], reads=[s['b_KR']], writes=[bk])
                        P.dma('sp', qt[:, 0:L], s['QT'][h], reads=[s['b_QT']], writes=[bq])
                        P.dma('pool', vt[:, 0:nk, 0:64], s['V'][h],
                              reads=[s['b_V']], writes=[bv])
                        for qc in range(L // 512):
                            po, bpo = POs.next()
                            pend = []

                            def issue_s(k, kt=kt, qt=qt, qc=qc):
                                pss, bps = PSs.next()
                                P.op('pe', lambda e: e.matmul(pss[:], lhsT=kt[:, k * 128:(k + 1) * 128], rhs=qt[:, qc * 512:(qc + 1) * 512], start=True, stop=True),
                                     [bk, bq], [bps])
                                pend.append((pss, bps))
                            for k in range(min(3, nk)):
                                issue_s(k)
                            for k in range(nk):
                                if k + 3 < nk:
                                    issue_s(k + 3)
                                pss, bps = pend.pop(0)
                                pt, bpt = PT.next()
                                P.op('act', lambda e, pt=pt, pss=pss: e.activation(out=pt[:], in_=pss[:], func=AF.Exp), [bps], [bpt])
                                P.op('pe', lambda e, po=po, vt=vt, pt=pt, k=k, nk=nk: e.matmul(po[:], lhsT=vt[:, k, :], rhs=pt[:], start=(k == 0), stop=(k == nk - 1)),
                                     [bv, bpt], [bpo])
                            rs, brs = RS.next()
                            P.op('dve', lambda e, rs=rs, po=po: e.reciprocal(out=rs[64:65, :], in_=po[64:65, :]), [bpo], [brs])
                            pb, bpb = PSs.next()
                            P.op('pe', lambda e, pb=pb, rs=rs: e.matmul(pb[0:64, :], lhsT=onesf[64:65, 0:64], rhs=rs[64:65, :], start=True, stop=True),
                                 [brs, bconst], [bpb])
                            bc, bbc = BC.next()
                            copy_op('act', bc[:], pb[0:64, :], [bpb], [bbc])
                            os_, bos = OS.next()
                            P.op('dve', lambda e, os_=os_, po=po, bc=bc: e.tensor_tensor(out=os_[:], in0=po[0:64, :], in1=bc[:], op=ALU.mult), [bpo, bbc], [bos])
                            P.dma('pool', s['mixT'][512 + h * 64:512 + (h + 1) * 64, qc * 512:(qc + 1) * 512], os_[:], reads=[bos], writes=[s['b_mixT']])
            P.barrier()

        def ph_out(layer):
            with contextlib.ExitStack() as ph:
                T = lambda name, shape, dt: ph.enter_context(sb(name, shape, dt))
                bw = WBuf()
                wout = T('wout', [128, 8, 1024], BF16)
                wsrc = (I['even_w_out'] if layer == 0 else I['odd_w_out'])[0].rearrange("(c p) f -> p c f", p=128)
                for c in range(8):
                    loadw(wout[:, c, :], wsrc[:, c, :], bw)
                grow = rowbc(ph, 'grow', I['norm_mix'][layer, 1], D, bw)
                if layer == 1:
                    fwb = T('fwb', [128, 2, 128], BF16)
                    bfw = Buf()
                    P.op('pool', lambda e: e.memset(fwb[:], 0.0), writes=[bfw])
                    for g in range(4):
                        pp = (g % 2) * 64
                        loadw(fwb[pp:pp + 64, g // 2, pp:pp + 64], I['fnet_w'][0, g], bfw, np_=64)
                tmp = dict(ssq2=RB(ph, nc, 'ssq2', [128, 8], F32, 2), junk=RB(ph, nc, 'junk', [128, 1024], F32, 1),
                           ep=RB(ph, nc, 'ep', [128, 512], F32, 3))
                XT = RB(ph, nc, 'xt', [128, 4, 1024], F32, 2)
                MX = RB(ph, nc, 'mx', [128, 8, 512], BF16, 2)
                MF = RB(ph, nc, 'mf', [128, 2, 512], BF16, 2)
                for s in SEQ:
                    L = s['L']
                    xin, bxin = (s['x'], Buf()) if layer == 0 else (s['X2'], s['b_X2'])
                    xout, bxout = (s['X1'], s['b_X1']) if layer == 0 else (s['X3'], s['b_X3'])
                    for ci in range(L // 512):
                        c0 = ci * 512
                        xt, bx = XT.next()
                        P.dma('sp', xt[:], xin[c0:c0 + 512, :].rearrange("(j p) d -> p j d", p=128), reads=[bxin], writes=[bx])
                        mx, bm = MX.next()
                        if layer == 0:
                            P.dma('pool', mx[:], s['mixT'][:, c0:c0 + 512].rearrange("(c p) t -> p c t", p=128), reads=[s['b_mixT']], writes=[bm])
                        else:
                            P.dma('pool', mx[:, 0:6, :], s['mix2'][:, c0:c0 + 512].rearrange("(c p) t -> p c t", p=128), reads=[s['b_mix2']], writes=[bm])
                            mf, bmf = MF.next()
                            P.dma('pool', mf[:], s['mfT'][:, c0:c0 + 512].rearrange("(c p) t -> p c t", p=128), reads=[s['b_mfT']], writes=[bmf])
                            for j2 in range(2):
                                ps, bp = PS.next()
                                P.op('pe', lambda e, ps=ps, j2=j2, mf=mf: e.matmul(ps[:], lhsT=fwb[:, j2, :], rhs=mf[:, j2, :], start=True, stop=True), [bmf, bfw], [bp])
                                copy_op('act', mx[:, 6 + j2, :], ps[:], [bp], [bm])
                        for j in range(4):
                            pss = []
                            for hf in range(2):
                                ps, bp = PS.next()
                                for c in range(8):
                                    P.op('pe', lambda e, ps=ps, c=c, j=j, hf=hf, mx=mx: e.matmul(ps[:], lhsT=mx[:, c, j * 128:(j + 1) * 128],
                                                                                                 rhs=wout[:, c, hf * 512:(hf + 1) * 512], start=(c == 0), stop=(c == 7)),
                                         [bm, bw], [bp])
                                pss.append((ps, bp))
                            epilogue(pss, xt, bx, j, grow, tmp, bw)
                        P.dma('sp', xout[c0:c0 + 512, :].rearrange("(j p) d -> p j d", p=128), xt[:], reads=[bx], writes=[bxout])
            P.barrier()

        def ph_mlp(layer):
            TT = 256
            nj = TT // 128
            with contextlib.ExitStack() as ph:
                T = lambda name, shape, dt: ph.enter_context(sb(name, shape, dt))
                bw = WBuf()
                wfi = T('wfi', [128, 8, 4096], BF16)
                wfo = T('wfo', [128, 32, 1024], BF16)
                s1 = I['w_ff_in'][layer].rearrange("(c p) f -> p c f", p=128)
                s2 = I['w_ff_out'][layer].rearrange("(c p) f -> p c f", p=128)
                for c in range(8):
                    loadw(wfi[:, c, :], s1[:, c, :], bw)
                for c in range(32):
                    loadw(wfo[:, c, :], s2[:, c, :], bw)
                gcol = colvec(ph, 'gcol', I['norm_mlp'][layer, 0], 8, bw)
                grow = rowbc(ph, 'grow', I['norm_mlp'][layer, 1], D, bw)
                tmp = dict(ssq=RB(ph, nc, 'ssq', [128, 12], F32, 2), ssq2=RB(ph, nc, 'ssq2', [128, 8], F32, 2),
                           junk=RB(ph, nc, 'junk', [128, 1024], F32, 1), xn=RB(ph, nc, 'xn', [128, nj, 1024], BF16, 1),
                           ep=RB(ph, nc, 'ep', [128, 512], F32, 2))
                XT = RB(ph, nc, 'xt', [128, nj, 1024], F32, 2)
                HT = RB(ph, nc, 'hT', [128, 8, TT], BF16, 2)
                F1 = RB(ph, nc, 'f1', [128, 32, TT], BF16, 1)
                RL = RB(ph, nc, 'rl', [128, TT], F32, 3)
                for s in SEQ:
                    L = s['L']
                    xin, bxin = (s['X1'], s['b_X1']) if layer == 0 else (s['X3'], s['b_X3'])
                    xout, bxout = (s['X2'], s['b_X2']) if layer == 0 else (s['y'], Buf())
                    nchunk = L // TT

                    def load_norm(ci):
                        c0 = ci * TT
                        xt, bx = XT.next()
                        P.dma('sp', xt[:], xin[c0:c0 + TT, :].rearrange("(j p) d -> p j d", p=128), reads=[bxin], writes=[bx])
                        xn, bn = rms_norm(xt, bx, nj, tmp)
                        return xt, bx, xn, bn
                    cur = load_norm(0)
                    hT, bh = HT.next()
                    rms_tr(cur[2], cur[3], nj, gcol, hT, bh, bw)
                    for ci in range(nchunk):
                        c0 = ci * TT
                        xt, bx = cur[0], cur[1]
                        f1, bf1 = F1.next()
                        nxt = None
                        for fc in range(32):
                            ps, bp = PS.next()
                            for dc in range(8):
                                P.op('pe', lambda e, ps=ps, dc=dc, fc=fc, hT=hT: e.matmul(ps[:, 0:TT], lhsT=wfi[:, dc, fc * 128:(fc + 1) * 128], rhs=hT[:, dc, :],
                                                                                          start=(dc == 0), stop=(dc == 7)), [bh, bw], [bp])
                            rl, brl = RL.next()
                            P.op('act', lambda e, rl=rl, ps=ps: e.activation(out=rl[:], in_=ps[:, 0:TT], func=AF.Relu), [bp], [brl])
                            eng = 'pool' if fc % 2 else 'dve'
                            P.op(eng, lambda e, rl=rl, f1=f1, fc=fc: e.tensor_tensor(out=f1[:, fc, :], in0=rl[:], in1=rl[:], op=ALU.mult), [brl], [bf1], nowaw=(fc > 1))
                            if fc == 6 and ci + 1 < nchunk:
                                nxt = load_norm(ci + 1)
                        if nxt is not None:
                            hTn, bhn = HT.next()
                            rms_tr(nxt[2], nxt[3], nj, gcol, hTn, bhn, bw)
                        for j in range(nj):
                            pss = []
                            for hf in range(2):
                                ps, bp = PS.next()
                                for fc in range(32):
                                    P.op('pe', lambda e, ps=ps, fc=fc, j=j, hf=hf, f1=f1: e.matmul(ps[:], lhsT=f1[:, fc, j * 128:(j + 1) * 128],
                                                                                                   rhs=wfo[:, fc, hf * 512:(hf + 1) * 512], start=(fc == 0), stop=(fc == 31)),
                                         [bf1, bw], [bp])
                                pss.append((ps, bp))
                            epilogue(pss, xt, bx, j, grow, tmp, bw)
                        P.dma('pool', xout[c0:c0 + TT, :].rearrange("(j p) d -> p j d", p=128), xt[:], reads=[bx], writes=[bxout])
                        if nxt is not None:
                            cur = nxt
                            hT, bh = hTn, bhn
            P.barrier()

        def ph5():
            with contextlib.ExitStack() as ph:
                T = lambda name, shape, dt: ph.enter_context(sb(name, shape, dt))
                bw = WBuf()
                owin = T('owin', [128, 8, 2560], BF16)
                src = I['odd_w_in'][0].rearrange("(c p) f -> p c f", p=128)
                for c in range(8):
                    loadw(owin[:, c, :], src[:, c, :], bw)
                gcol = colvec(ph, 'gcol', I['norm_mix'][1, 0], 8, bw)
                swc = T('swc', [128, 3, 18], F32)
                for j in range(3):
                    P.dma('pool', swc[:, j, :], I['short_w'][0, j].rearrange("(c p) -> p c", p=128), writes=[bw])
                sbc = colvec(ph, 'sbc', I['short_b'][0], 18, bw)
                DCt = T('DCt', [128, 2, 128], F32)
                loadw(DCt[:, 0, :], I['c_DC'], bw)
                loadw(DCt[:, 1, :], I['c_DS'], bw)
                tmp = dict(ssq=RB(ph, nc, 'ssq', [128, 12], F32, 2), junk=RB(ph, nc, 'junk', [128, 1024], F32, 1),
                           xn=RB(ph, nc, 'xn', [128, 4, 1024], BF16, 1))
                XT = RB(ph, nc, 'xt', [128, 4, 1024], F32, 2)
                HT = RB(ph, nc, 'hT', [128, 8, 512], BF16, 2)
                WN = RB(ph, nc, 'wn', [128, 514], F32, 3)
                OT = RB(ph, nc, 'ot', [128, 512], F32, 3)
                FS = RB(ph, nc, 'fs', [128, 512], F32, 2)
                FO = RB(ph, nc, 'fo', [128, 512], F32, 3)
                carry = T('carry', [128, 18, 2], F32)
                bcar = Buf()
                for s in SEQ:
                    L = s['L']
                    fsc = 1.0 / math.sqrt(64.0 * L)
                    P.op('pool', lambda e: e.memset(carry[:], 0.0), writes=[bcar])
                    nch = L // 512
                    for ci in range(nch + 1):
                        c0 = ci * 512
                        last = (ci == nch)
                        if not last:
                            xt, bx = XT.next()
                            P.dma('sp', xt[:], s['X2'][c0:c0 + 512, :].rearrange("(j p) d -> p j d", p=128), reads=[s['b_X2']], writes=[bx])
                            hT, bh = HT.next()
                            rms_transpose(ph, xt, bx, 4, gcol, hT, bh, tmp, bw)
                        for fc in range(18):
                            wn, bwn = WN.next()
                            P.op('pool', lambda e, wn=wn, fc=fc: e.tensor_copy(out=wn[:, 0:2], in_=carry[:, fc, :]), [bcar], [bwn])
                            if not last:
                                ps, bp = PS.next()
                                for dc in range(8):
                                    P.op('pe', lambda e, ps=ps, dc=dc, fc=fc, hT=hT: e.matmul(ps[:], lhsT=owin[:, dc, fc * 128:(fc + 1) * 128], rhs=hT[:, dc, :],
                                                                                              start=(dc == 0), stop=(dc == 7)), [bh, bw], [bp])
                                copy_op('act', wn[:, 2:514], ps[:], [bp], [bwn])
                                nw = 512
                            else:
                                P.op('pool', lambda e, wn=wn: e.memset(wn[:, 2:3], 0.0), [], [bwn])
                                nw = 1
                            ot, bo = OT.next()
                            P.op('dve', lambda e, ot=ot, wn=wn, fc=fc, nw=nw: e.tensor_scalar(out=ot[:, 0:nw], in0=wn[:, 0:nw], scalar1=swc[:, 0, fc:fc + 1],
                                                                                              scalar2=sbc[:, fc:fc + 1], op0=ALU.mult, op1=ALU.add), [bwn, bw], [bo])
                            for j in (1, 2):
                                P.op('dve', lambda e, ot=ot, wn=wn, fc=fc, nw=nw, j=j: e.scalar_tensor_tensor(out=ot[:, 0:nw], in0=wn[:, j:j + nw], scalar=swc[:, j, fc:fc + 1],
                                                                                                              in1=ot[:, 0:nw], op0=ALU.mult, op1=ALU.add), [bwn, bw, bo], [bo])
                            if not last:
                                P.op('pool', lambda e, wn=wn, fc=fc: e.tensor_copy(out=carry[:, fc, :], in_=wn[:, 512:514]), [bwn], [bcar])
                            if ci == 0:
                                P.dma('sp', s['zcT'][fc * 128:(fc + 1) * 128, 0:511], ot[:, 1:512], reads=[bo], writes=[s['b_zcT']])
                            elif not last:
                                P.dma('sp', s['zcT'][fc * 128:(fc + 1) * 128, c0 - 1:c0 + 511], ot[:, 0:512], reads=[bo], writes=[s['b_zcT']])
                            else:
                                P.dma('sp', s['zcT'][fc * 128:(fc + 1) * 128, L - 1:L], ot[:, 0:1], reads=[bo], writes=[s['b_zcT']])
                        if last:
                            continue
                        for f2 in range(2):
                            ps, bp = PS.next()
                            for dc in range(8):
                                P.op('pe', lambda e, ps=ps, dc=dc, f2=f2, hT=hT: e.matmul(ps[:], lhsT=owin[:, dc, 2304 + f2 * 128:2304 + (f2 + 1) * 128], rhs=hT[:, dc, :],
                                                                                          start=(dc == 0), stop=(dc == 7)), [bh, bw], [bp])
                            fs, bfs = FS.next()
                            copy_op('act', fs[:], ps[:], [bp], [bfs])
                            for k2 in range(2):
                                ps2, bp2 = PS.next()
                                P.op('pe', lambda e, ps2=ps2, k2=k2, fs=fs: e.matmul(ps2[:], lhsT=DCt[:, k2, :], rhs=fs[:], start=True, stop=True), [bfs, bw], [bp2])
                                fo, bfo = FO.next()
                                P.op('act', lambda e, fo=fo, ps2=ps2: e.activation(out=fo[:], in_=ps2[:], func=AF.Copy, scale=fsc), [bp2], [bfo])
                                r0 = k2 * 256 + f2 * 128
                                P.dma('pool', s['fcT'][r0:r0 + 128, c0:c0 + 512], fo[:], reads=[bfo], writes=[s['b_fcT']])
            P.barrier()

        def ph6():
            PI = math.pi
            with contextlib.ExitStack() as ph:
                T = lambda name, shape, dt: ph.enter_context(sb(name, shape, dt))
                bw = Buf()
                w1 = T('fw1', [33, 64], F32)
                w2 = T('fw2', [64, 64], F32)
                w3 = T('fw3', [64, 3072], F32)
                P.dma('pool', w1[:], I['filt_w1'][0], writes=[bw])
                P.dma('pool', w2[:], I['filt_w2'][0], writes=[bw])
                P.dma('pool', w3[:], I['filt_w3'][0], writes=[bw])
                cols = T('fcols', [64, 8], F32)
                for k, nm_ in enumerate(('filt_b1', 'filt_b2', 'filt_freq')):
                    P.dma('pool', cols[:, k:k + 1], I[nm_][0].rearrange("(p o) -> p o", o=1), writes=[bw])
                P.op('dve', lambda e: e.tensor_tensor(out=cols[:, 3:4], in0=cols[:, 2:3], in1=cols[:, 0:1], op=ALU.mult), [bw], [bw])
                P.op('dve', lambda e: e.tensor_tensor(out=cols[:, 4:5], in0=cols[:, 2:3], in1=cols[:, 1:2], op=ALU.mult), [bw], [bw])
                negd = T('negd', [128, 6], F32)
                P.dma('pool', negd[:], I['c_negd'], writes=[bw])
                FT = RB(ph, nc, 'ft', [33, 512], F32, 2)
                TB = RB(ph, nc, 'tb', [128, 512], F32, 2)
                DT = RB(ph, nc, 'dt', [128, 6, 512], F32, 2)
                AR = RB(ph, nc, 'ar', [64, 3, 512], F32, 2)
                H1 = RB(ph, nc, 'h1', [64, 512], F32, 2)
                H2 = RB(ph, nc, 'h2', [64, 512], F32, 2)
                HF = RB(ph, nc, 'hf', [128, 512], F32, 3)
                junk = T('fjunk', [128, 512], F32)
                bjunk = Buf()
                ssq = T('fssq', [128, 24, 16], F32)
                tot = T('ftot', [128, 3, 24], F32)
                bss = Buf()

                def sinlayer(ps, bp, fb, out, bo):
                    ar, ba = AR.next()
                    P.op('dve', lambda e: e.tensor_scalar(out=ar[:, 0, :], in0=ps[0:64, :], scalar1=cols[:, 2:3], scalar2=cols[:, fb:fb + 1],
                                                          op0=ALU.mult, op1=ALU.add), [bp, bw], [ba])
                    P.op('dve', lambda e: e.tensor_scalar(out=ar[:, 1, :], in0=ar[:, 0, :], scalar1=PI, scalar2=2 * PI, op0=ALU.is_gt, op1=ALU.mult), [ba], [ba])
                    P.op('dve', lambda e: e.tensor_tensor(out=ar[:, 0, :], in0=ar[:, 0, :], in1=ar[:, 1, :], op=ALU.subtract), [ba], [ba])
                    P.op('dve', lambda e: e.tensor_scalar(out=ar[:, 1, :], in0=ar[:, 0, :], scalar1=-PI, scalar2=2 * PI, op0=ALU.is_lt, op1=ALU.mult), [ba], [ba])
                    P.op('dve', lambda e: e.tensor_tensor(out=ar[:, 0, :], in0=ar[:, 0, :], in1=ar[:, 1, :], op=ALU.add), [ba], [ba])
                    P.op('act', lambda e: e.activation(out=out[:], in_=ar[:, 0, :], func=AF.Sin, scale=0.999999), [ba], [bo])

                for s in SEQ:
                    L = s['L']
                    nch = L // 512
                    for ci in range(nch):
                        c0 = ci * 512
                        ft, bft = FT.next()
                        P.dma('sp', ft[:], I['c_featT_' + s['nm']][:, c0:c0 + 512], writes=[bft])
                        tb, btb = TB.next()
                        P.dma('sp', tb[:], I['c_trow_' + s['nm']][:, c0:c0 + 512].partition_broadcast(128), writes=[btb])
                        dt_, bdt = DT.next()
                        for k in range(6):
                            P.op('act', lambda e, k=k, dt_=dt_, tb=tb: e.activation(out=dt_[:, k, :], in_=tb[:], func=AF.Exp, scale=negd[:, k:k + 1]), [btb, bw], [bdt])
                        ps, bp = PS.next()
                        P.op('pe', lambda e, ps=ps, ft=ft: e.matmul(ps[0:64, :], lhsT=w1[:], rhs=ft[:], start=True, stop=True), [bft, bw], [bp])
                        h1, bh1 = H1.next()
                        sinlayer(ps, bp, 3, h1, bh1)
                        ps, bp = PS.next()
                        P.op('pe', lambda e, ps=ps, h1=h1: e.matmul(ps[0:64, :], lhsT=w2[:], rhs=h1[:], start=True, stop=True), [bh1, bw], [bp])
                        h2, bh2 = H2.next()
                        sinlayer(ps, bp, 4, h2, bh2)
                        for jc in range(24):
                            ps, bp = PS.next()
                            P.op('pe', lambda e, ps=ps, jc=jc, h2=h2: e.matmul(ps[:], lhsT=w3[:, jc * 128:(jc + 1) * 128], rhs=h2[:], start=True, stop=True), [bh2, bw], [bp])
                            hf, bhf = HF.next()
                            P.op('dve', lambda e, hf=hf, ps=ps, dt_=dt_, jc=jc: e.tensor_tensor(out=hf[:], in0=ps[:], in1=dt_[:, jc % 6, :], op=ALU.mult), [bp, bdt], [bhf])
                            P.op('act', lambda e, hf=hf, jc=jc, ci=ci: e.activation(out=junk[:], in_=hf[:], func=AF.Square, accum_out=ssq[:, jc, ci:ci + 1]), [bhf], [bjunk, bss])
                            P.dma('sp', s['hfT'][jc * 128:(jc + 1) * 128, c0:c0 + 512], hf[:], reads=[bhf], writes=[s['b_hfT']])
                    P.op('dve', lambda e, nch=nch: e.tensor_reduce(out=tot[:, 0, :], in_=ssq[:, :, 0:nch], axis=AX.X, op=ALU.add), [bss], [bss])
                    P.op('act', lambda e: e.activation(out=tot[:, 1, :], in_=tot[:, 0, :], func=AF.Sqrt, bias=epsT[:, 0:1], scale=1.0), [bss, bconst], [bss])
                    P.op('dve', lambda e: e.reciprocal(out=tot[:, 2, :], in_=tot[:, 1, :]), [bss], [bss])
                    P.dma('sp', s['invd'][:, 0:24], tot[:, 2, :], reads=[bss], writes=[s['b_invd']])
            P.barrier()

        def ph7():
            CB = 4
            with contextlib.ExitStack() as ph:
                T = lambda name, shape, dt: ph.enter_context(sb(name, shape, dt))
                bw = WBuf()
                F1 = T('cF1', [64, 128], BF16)
                loadw(F1[:], I['c_F1'], bw, np_=64)
                F1fA = T('cF1fA', [64, 128], BF16)
                F1fB = T('cF1fB', [64, 128], BF16)
                loadw(F1fA[:], I['c_F1fA'], bw, np_=64)
                loadw(F1fB[:], I['c_F1fB'], bw, np_=64)
                hbb = rowbc(ph, 'hbb', I['hyena_bias'][0].rearrange("o c -> (o c)"), 1536, bw, npart=64)
                for s in SEQ:
                    L, N2r, nm = s['L'], s['N2'], s['nm']
                    N2 = 128
                    PK = 128 // N2r
                    CBc = CB * PK
                    G = 3
                    with contextlib.ExitStack() as p2:
                        T2 = lambda name, shape, dt: p2.enter_context(sb(name + nm, shape, dt))
                        bc_ = Buf()
                        G1re = T2('G1re', [64, 64], BF16)
                        G1im = T2('G1im', [64, 64], BF16)
                        loadw(G1re[:], I['c_G1re_' + nm], bc_, np_=64)
                        loadw(G1im[:], I['c_G1im_' + nm], bc_, np_=64)
                        TA = T2('TA', [N2, 4, 128], F32)
                        TBt = T2('TB', [N2, 4, 128], F32)
                        TfA = T2('TfA', [N2, 4, 128], F32)
                        TfB = T2('TfB', [N2, 4, 128], F32)
                        for t_, k_ in ((TA, 'TA_'), (TBt, 'TB_'), (TfA, 'TfA_'), (TfB, 'TfB_')):
                            P.dma('pool', t_[:], I['c_' + k_ + nm], writes=[bc_])
                        TcA = T2('TcA', [64, 4, 2 * N2], F32)
                        TcB = T2('TcB', [64, 4, 2 * N2], F32)
                        P.dma('pool', TcA[:], I['c_TcA_' + nm], writes=[bc_])
                        P.dma('pool', TcB[:], I['c_TcB_' + nm], writes=[bc_])
                        F2 = T2('F2', [N2, 3, N2], BF16)
                        FA = T2('FA', [N2, 2 * N2], BF16)
                        FB = T2('FB', [N2, 2 * N2], BF16)
                        loadw(F2[:].rearrange("p a b -> p (a b)"), I['c_F2_' + nm].rearrange("p a b -> p (a b)"), bc_, np_=N2)
                        loadw(FA[:], I['c_FA_' + nm], bc_, np_=N2)
                        loadw(FB[:], I['c_FB_' + nm], bc_, np_=N2)
                        XI = RB(p2, nc, 'xi' + nm, [64, 3, CB, N2], F32, G)
                        HI = RB(p2, nc, 'hi' + nm, [64, 4, CB, N2], F32, 1)
                        IV = RB(p2, nc, 'iv' + nm, [64, 4, CBc], F32, 3)
                        HB = RB(p2, nc, 'hb' + nm, [64, 4, CB, N2], BF16, G)
                        UB = RB(p2, nc, 'ub' + nm, [64, CB, N2], BF16, 2 * G)
                        PPa = RB(p2, nc, 'ppa' + nm, [N2, 4, 128], F32, G + 1)
                        PPb = RB(p2, nc, 'ppb' + nm, [N2, 4, 128], F32, G + 1)
                        YT = RB(p2, nc, 'yt' + nm, [N2, 4, 128], BF16, 2 * G + 2)
                        YF = RB(p2, nc, 'yf' + nm, [N2, 4, 128], F32, 2 * G + 2)
                        KK = RB(p2, nc, 'kk' + nm, [N2, 2, 2, CB, 128], F32, G)
                        GG = RB(p2, nc, 'gg' + nm, [N2, 4, 128], BF16, G)
                        VPa = RB(p2, nc, 'vpa' + nm, [64, 4, 2 * N2], F32, G)
                        VPb = RB(p2, nc, 'vpb' + nm, [64, 4, 2 * N2], F32, G)
                        VT_ = RB(p2, nc, 'vt' + nm, [64, 4, 2, N2], BF16, G)
                        UT = RB(p2, nc, 'ut' + nm, [64, CB, N2], F32, 3 * G)
                        MO = RB(p2, nc, 'mo' + nm, [N2, CB, 64], BF16, 3)

                        AUX = {}

                        def aux(b):
                            k = id(b)
                            if k not in AUX:
                                AUX[k] = (b, Buf())
                            return AUX[k][1]

                        def tview(ps, n):
                            return ps[0:n, :].rearrange("p (t c) -> p t c", t=4)

                        def twiddle(ps, bp, A, B, n, w, outre, outim, bo, rd):
                            pa, bpa = PPa.next()
                            pb, bpb = PPb.next()
                            src = ps
                            P.op('dve', lambda e: e.tensor_tensor(out=pa[0:n], in0=src, in1=A, op=ALU.mult), [bp] + rd, [bpa])
                            P.op('dve', lambda e: e.tensor_tensor(out=pb[0:n], in0=src, in1=B, op=ALU.mult), [bp] + rd, [bpb])
                            yield
                            P.op('pool', lambda e: e.tensor_tensor(out=outre, in0=pa[0:n, :, 0:w], in1=pb[0:n, :, w:2 * w], op=ALU.subtract), [bpa, bpb], [bo])
                            P.op('dve', lambda e: e.tensor_tensor(out=outim, in0=pb[0:n, :, 0:w], in1=pa[0:n, :, w:2 * w], op=ALU.add), [bpa, bpb], [aux(bo)])

                        def tform_fwd(tiles, rd, A, B, out, bo):
                            ps, bp = PS.next()
                            pv = tview(ps, N2)
                            for t, tl in enumerate(tiles):
                                P.op('pe', lambda e, t=t, tl=tl: e.matmul(pv[:, t, :], lhsT=tl, rhs=F1[:], start=True, stop=True), rd + [bw], [bp])
                            yield from twiddle(pv, bp, A, B, N2, 64, out[:, :, 0:64], out[:, :, 64:128], bo, [bc_])

                        def nform_fwd(yre_src, yim_src, rd, imsrc_re=None, imsrc_im=None):
                            ps, bp = PS.next()
                            pv = tview(ps, N2)
                            ire = yre_src if imsrc_re is None else imsrc_re
                            iim = yim_src if imsrc_im is None else imsrc_im
                            P.op('pe', lambda e: e.matmul(pv[:, :, 0:64], lhsT=F2[:, 0, :], rhs=yre_src, start=True, stop=False), rd + [bc_], [bp])
                            P.op('pe', lambda e: e.matmul(pv[:, :, 0:64], lhsT=F2[:, 2, :], rhs=yim_src, start=False, stop=True), rd + [bc_], [bp])
                            P.op('pe', lambda e: e.matmul(pv[:, :, 64:128], lhsT=F2[:, 0, :], rhs=iim, start=True, stop=False), rd + [bc_], [bp])
                            P.op('pe', lambda e: e.matmul(pv[:, :, 64:128], lhsT=F2[:, 1, :], rhs=ire, start=False, stop=True), rd + [bc_], [bp])
                            return pv, bp

                        def conv(ub, bub, kk, bkk, o, res):
                            yt, byt = YT.next()
                            yield from tform_fwd([ub[:, t, :] for t in range(CB)], [bub], TA[:], TBt[:], yt, byt)
                            yield
                            pv, bp = nform_fwd(yt[:, :, 0:64], yt[:, :, 64:128], [byt, aux(byt)])
                            gg, bgg = GG.next()
                            yield from twiddle(pv, bp, kk[:, o, 0], kk[:, o, 1], N2, 64, gg[:, :, 0:64], gg[:, :, 64:128], bgg, [bkk])
                            yield
                            vt, bvt = VT_.next()
                            per = min(512 // (2 * N2), CB)
                            for g0 in range(0, CB, per):
                                ps, bp = PS.next()
                                pv2 = ps[0:64, 0:per * 2 * N2].rearrange("p (t c) -> p t c", t=per)
                                for t in range(per):
                                    P.op('pe', lambda e, t=t, g0=g0: e.matmul(pv2[:, t, :], lhsT=gg[:, g0 + t, 0:64], rhs=FA[:], start=True, stop=False), [bgg, aux(bgg), bc_], [bp])
                                    P.op('pe', lambda e, t=t, g0=g0: e.matmul(pv2[:, t, :], lhsT=gg[:, g0 + t, 64:128], rhs=FB[:], start=False, stop=True), [bgg, aux(bgg), bc_], [bp])
                                va, bva = VPa.next()
                                vb, bvb = VPb.next()
                                P.op('dve', lambda e, pv2=pv2, va=va: e.tensor_tensor(out=va[:, 0:per, :], in0=pv2, in1=TcA[:, 0:per, :], op=ALU.mult), [bp, bc_], [bva])
                                P.op('dve', lambda e, pv2=pv2, vb=vb: e.tensor_tensor(out=vb[:, 0:per, :], in0=pv2, in1=TcB[:, 0:per, :], op=ALU.mult), [bp, bc_], [bvb])
                                yield
                                P.op('pool', lambda e, g0=g0, va=va, vb=vb: e.tensor_tensor(out=vt[:, g0:g0 + per, 0, :], in0=va[:, 0:per, 0:N2], in1=vb[:, 0:per, N2:2 * N2], op=ALU.subtract), [bva, bvb], [bvt], nowaw=(g0 > 0))
                                P.op('dve', lambda e, g0=g0, va=va, vb=vb: e.tensor_tensor(out=vt[:, g0:g0 + per, 1, :], in0=vb[:, 0:per, 0:N2], in1=va[:, 0:per, N2:2 * N2], op=ALU.add), [bva, bvb], [aux(bvt)], nowaw=(g0 > 0))
                                yield
                            ps, bp = PS.next()
                            pvo = ps[0:64, 0:CB * N2].rearrange("p (t c) -> p t c", t=CB)
                            P.op('pe', lambda e: e.matmul(pvo, lhsT=G1re[:], rhs=vt[:, :, 0, :], start=True, stop=False), [bvt, aux(bvt), bc_], [bp])
                            P.op('pe', lambda e: e.matmul(pvo, lhsT=G1im[:], rhs=vt[:, :, 1, :], start=False, stop=True), [bvt, aux(bvt), bc_], [bp])
                            res['pvo'] = (pvo, bp)

                        def zc_view(r0):
                            return s['zcT'][r0:r0 + CBc, :].rearrange("c (a b) -> a c b", b=N2r)

                        def cview(ap_):
                            return ap_.rearrange("p t (c b) -> p (t c) b", b=N2r)

                        def hyena_gen(cb):
                            ch0 = cb * CBc
                            xi, bxi = XI.next()
                            for k in range(3):
                                P.dma('sp', cview(xi[:, k]), zc_view(k * 768 + ch0), reads=[s['b_zcT']], writes=[bxi])
                            hi, bhi = HI.next()
                            for od in range(4):
                                P.dma('act', cview(hi[:, od]), s['hfT'][od * 768 + ch0:od * 768 + ch0 + CBc, :].rearrange("c (a b) -> a c b", b=N2r),
                                      reads=[s['b_hfT']], writes=[bhi])
                            iv, biv = IV.next()
                            for od in range(4):
                                col = od * 768 + ch0
                                P.dma('sp', iv[:, od, :], s['invd'][col % 128:col % 128 + CBc, col // 128:col // 128 + 1].rearrange("c o -> o c").partition_broadcast(64),
                                      reads=[s['b_invd']], writes=[biv])
                            hb, bhb = HB.next()
                            P.op('dve', lambda e: e.tensor_tensor(out=hb[:].rearrange("p o t (c b) -> p o (t c) b", b=N2r), in0=hi[:].rearrange("p o t (c b) -> p o (t c) b", b=N2r),
                                                                  in1=iv[:].unsqueeze(3).to_broadcast([64, 4, CBc, N2r]), op=ALU.mult),
                                 [bhi, biv], [bhb])
                            for od in (1, 3):
                                P.op('pool', lambda e, od=od: e.memset(cview(hb[0:1, od])[:, :, 0:1], 0.0), [], [bhb])
                            yield
                            kk, bkk = KK.next()
                            for o in range(2):
                                yf, byf = YF.next()
                                yb, byb = YF.next()
                                yield from tform_fwd([hb[:, 2 * o, t, :] for t in range(CB)], [bhb], TA[:], TBt[:], yf, byf)
                                yield
                                yield from tform_fwd([hb[:, 2 * o + 1, t, :] for t in range(CB)], [bhb], TA[:], TBt[:], yb, byb)
                                yield
                                ysum, bys = YT.next()
                                ydif, byd = YT.next()
                                P.op('pool', lambda e: e.tensor_tensor(out=ysum[:], in0=yf[:], in1=yb[:], op=ALU.add), [byf, byb, aux(byf), aux(byb)], [bys])
                                P.op('dve', lambda e: e.tensor_tensor(out=ydif[:], in0=yf[:], in1=yb[:], op=ALU.subtract), [byf, byb, aux(byf), aux(byb)], [byd])
                                yield
                                pv, bp = nform_fwd(ysum[:, :, 0:64], ysum[:, :, 64:128], [bys, byd], imsrc_re=ydif[:, :, 0:64], imsrc_im=ydif[:, :, 64:128])
                                for half in range(2):
                                    copy_op('act', kk[:, o, 0, :, half * 64:(half + 1) * 64], pv[:, :, 0:64], [bp], [bkk], nowaw=(o > 0 or half > 0))
                                    copy_op('act', kk[:, o, 1, :, half * 64:(half + 1) * 64], pv[:, :, 64:128], [bp], [bkk], nowaw=True)
                                yield
                            ub, bub = UB.next()
                            copy_op('act', ub[:], xi[:, 0], [bxi], [bub])
                            ucur, bucur = xi[:, 0], bxi
                            for o in range(2):
                                res = {}
                                yield from conv(ub, bub, kk, bkk, o, res)
                                pvo, bp = res['pvo']
                                ut, but = UT.next()
                                bia = hbb[:, o * 768 + ch0:o * 768 + ch0 + CBc].unsqueeze(2).to_broadcast([64, CBc, N2r])
                                P.op('pool', lambda e, ut=ut, ucur=ucur, bia=bia: e.tensor_tensor(out=cview(ut[:]), in0=cview(ucur), in1=bia, op=ALU.mult), [bucur, bw], [but])
                                P.op('dve', lambda e, ut=ut, pvo=pvo: e.tensor_tensor(out=ut[:], in0=pvo, in1=ut[:], op=ALU.add), [bp, but], [but])
                                ut2, but2 = UT.next()
                                P.op('pool', lambda e, ut=ut, ut2=ut2, o=o: e.tensor_tensor(out=ut2[:], in0=ut[:], in1=xi[:, 1 + o], op=ALU.mult), [but, bxi], [but2])
                                ub, bub = UB.next()
                                copy_op('act', ub[:], ut2[:], [but2], [bub])
                                ucur, bucur = ut2[:], but2
                                yield
                            P.dma('sp', s['mix2'][ch0:ch0 + CBc, :].rearrange("c (a b) -> a c b", b=N2r), cview(ub[:]), reads=[bub], writes=[s['b_mix2']])

                        def fnet_gen(cb):
                            q0 = cb * CBc
                            xi, bxi = XI.next()
                            P.dma('sp', cview(xi[:, 0]), s['fcT'][q0:q0 + CBc, :].rearrange("c (a b) -> a c b", b=N2r), reads=[s['b_fcT']], writes=[bxi])
                            P.dma('sp', cview(xi[:, 1]), s['fcT'][256 + q0:256 + q0 + CBc, :].rearrange("c (a b) -> a c b", b=N2r), reads=[s['b_fcT']], writes=[bxi])
                            xc, bxc = UB.next()
                            xs_, bxs = UB.next()
                            copy_op('act', xc[:], xi[:, 0], [bxi], [bxc])
                            copy_op('pool', xs_[:], xi[:, 1], [bxi], [bxs])
                            yield
                            ps, bp = PS.next()
                            pv = tview(ps, N2)
                            for t in range(CB):
                                P.op('pe', lambda e, t=t: e.matmul(pv[:, t, :], lhsT=xc[:, t, :], rhs=F1fA[:], start=True, stop=False), [bxc, bw], [bp])
                                P.op('pe', lambda e, t=t: e.matmul(pv[:, t, :], lhsT=xs_[:, t, :], rhs=F1fB[:], start=False, stop=True), [bxs, bw], [bp])
                            yt, byt = YT.next()
                            yield from twiddle(pv, bp, TfA[:], TfB[:], N2, 64, yt[:, :, 0:64], yt[:, :, 64:128], byt, [bc_])
                            yield
                            ps, bp = PS.next()
                            pz = ps[0:N2, 0:CB * 64].rearrange("p (t c) -> p t c", t=CB)
                            P.op('pe', lambda e: e.matmul(pz, lhsT=F2[:, 0, :], rhs=yt[:, :, 0:64], start=True, stop=False), [byt, aux(byt), bc_], [bp])
                            P.op('pe', lambda e: e.matmul(pz, lhsT=F2[:, 2, :], rhs=yt[:, :, 64:128], start=False, stop=True), [byt, aux(byt), bc_], [bp])
                            mo, bmo = MO.next()
                            copy_op('act', mo[:], pz, [bp], [bmo])
                            P.dma('sp', s['mfT'][q0:q0 + CBc, :].rearrange("(t c) (b a) -> (c b) t a", c=PK, a=64), mo[:], reads=[bmo], writes=[s['b_mfT']])

                        def lockstep(gens, g):
                            it = iter(gens)
                            active = []
                            done = False
                            while True:
                                while not done and len(active) < g:
                                    nx = next(it, None)
                                    if nx is None:
                                        done = True
                                    else:
                                        active.append(nx)
                                if not active:
                                    break
                                for gg_ in list(active):
                                    try:
                                        next(gg_)
                                    except StopIteration:
                                        active.remove(gg_)

                        lockstep((hyena_gen(cb) for cb in range(768 // CBc)), G)
                        lockstep((fnet_gen(cb) for cb in range(256 // CBc)), G)
                    P.barrier()

        phases = [ph1, ph2, lambda: ph_out(0), lambda: ph_mlp(0), ph5, ph6, ph7, lambda: ph_out(1), lambda: ph_mlp(1)]
        for i, f in enumerate(phases):
            if i >= stop_after:
                break
            f()
        if dbg is not None:
            dbg(nc, P, SEQ, I)
        P.emit()
    return nc


_NC_CACHE = {}


def _prep_inputs(inputs):
    consts = _get_consts()
    base = {}
    for k, v in inputs.items():
        if k in ('x_prompt', 'x_sample'):
            continue
        base[k] = np.ascontiguousarray(np.asarray(v, dtype=np.float32))
    for k, v in consts.items():
        base['c_' + k] = np.ascontiguousarray(v)
    xp = np.asarray(inputs['x_prompt'], dtype=np.float32)
    xs = np.asarray(inputs['x_sample'], dtype=np.float32)
    in_maps = []
    for i in range(8):
        m = dict(base)
        m['x_p'] = np.ascontiguousarray(xp[i])
        m['x_s'] = np.ascontiguousarray(xs[i])
        in_maps.append(m)
    return consts, in_maps


def kernel(**inputs):
    consts, in_maps = _prep_inputs(inputs)
    if 'nc' not in _NC_CACHE:
        _NC_CACHE['nc'] = build(consts)
    nc = _NC_CACHE['nc']
    res = run_bass_kernel_spmd(nc, in_maps, core_ids=list(range(8)))
    yp = np.stack([np.asarray(r['y_p'], dtype=np.float32) for r in res.results], 0)
    ys = np.stack([np.asarray(r['y_s'], dtype=np.float32) for r in res.results], 0)
    return (yp, ys)
```

```python
import contextlib
import math
import os
import numpy as np
import concourse.bass as bass
import concourse.mybir as mybir
from concourse.bass_utils import run_bass_kernel_spmd

F32 = mybir.dt.float32
BF16 = mybir.dt.bfloat16
AF = mybir.ActivationFunctionType
ALU = mybir.AluOpType
AX = mybir.AxisListType

D = 1024
EPS = 1e-6
LP, LS = 2048, 8192
ENGS = ('pe', 'act', 'dve', 'pool', 'sp')
NRING = 8


class Buf:
    __slots__ = ('w', 'r', 'chain', 'pending')

    def __init__(self, chain=True):
        self.w = None
        self.r = []
        self.chain = chain
        self.pending = None


class WBuf(Buf):
    __slots__ = ()

    def __init__(self):
        Buf.__init__(self)
        self.pending = []

    def piece(self):
        b = Buf()
        self.pending.append(b)
        return b


class Op:
    __slots__ = ('eng', 'fn', 'deps', 'signal', 'sigval', 'is_dma', 'dsem', 'dval', 'ringdep')

    def __init__(self, eng, fn, is_dma):
        self.eng = eng
        self.fn = fn
        self.deps = []
        self.signal = False
        self.sigval = 0
        self.is_dma = is_dma
        self.dsem = None
        self.dval = 0
        self.ringdep = None


class _Rec:
    def __getattr__(self, name):
        def f(*a, **k):
            self.__dict__['call'] = (name, a, k)
            return None
        return f


class Prog:
    def __init__(self, nc):
        self.nc = nc
        self.ops = {e: [] for e in ENGS}
        self.ndma = {e: 0 for e in ENGS}
        self.ringops = {e: [None] * NRING for e in ENGS}

    def _rec(self, eng, fn, reads, writes, is_dma, nowaw=False):
        for t in list(reads) + list(writes):
            if t.pending:
                pend, t.pending = t.pending, []
                self._rec('pool', self.join_fn, pend, [t], False)
        r = _Rec()
        fn(r)
        op = Op(eng, r.call, is_dma)
        deps = []
        for t in reads:
            if t.w is not None:
                deps.append(t.w)
        for t in writes:
            if t.w is not None and not (nowaw and t.w.eng == eng and not t.w.is_dma) \
                    and not ((not t.chain) and is_dma and t.w.is_dma):
                deps.append(t.w)
            deps.extend(t.r)
        for t in reads:
            t.r.append(op)
        for t in writes:
            t.w = op
            t.r = []
        seen = set()
        for d in deps:
            if id(d) in seen or d is op:
                continue
            seen.add(id(d))
            if (not d.is_dma) and d.eng == eng and eng == 'pe':
                continue
            op.deps.append(d)
            if not d.is_dma:
                d.signal = True
        if is_dma:
            j = self.ndma[eng]
            self.ndma[eng] = j + 1
            op.dsem = (eng, j % NRING)
            op.dval = 16 * (j // NRING + 1)
            op.ringdep = self.ringops[eng][j % NRING]
            self.ringops[eng][j % NRING] = op
        self.ops[eng].append(op)
        return op

    def op(self, eng, fn, reads=(), writes=(), nowaw=False):
        return self._rec(eng, fn, reads, writes, False, nowaw)

    def dma(self, eng, out, in_, reads=(), writes=()):
        return self._rec(eng, lambda e: e.dma_start(out=out, in_=in_), reads, writes, True)

    def barrier(self):
        lasts = []
        for e in ENGS:
            for o in reversed(self.ops[e]):
                if o.fn is not None and not o.is_dma:
                    lasts.append(o)
                    break
            for d in self.ringops[e]:
                if d is not None:
                    lasts.append(d)
        for e in ENGS:
            op = Op(e, None, False)
            for d in lasts:
                if d.is_dma or d.eng != e:
                    op.deps.append(d)
                    if not d.is_dma:
                        d.signal = True
            self.ops[e].append(op)

    def emit(self):
        nc = self.nc
        with contextlib.ExitStack() as st:
            csem = {e: st.enter_context(nc.semaphore('c_' + e)) for e in ENGS}
            dsem = {}
            for e in ENGS:
                for i in range(min(NRING, self.ndma[e])):
                    dsem[(e, i)] = st.enter_context(nc.semaphore('d_%s%d' % (e, i)))
            for e in ENGS:
                c = 0
                for o in self.ops[e]:
                    if o.signal:
                        c += 1
                        o.sigval = c
            st.enter_context(nc.allow_non_contiguous_dma(reason='small strided tables / layout loads'))
            block = st.enter_context(nc.Block())
            handles = {'pe': 'tensor', 'act': 'scalar', 'dve': 'vector', 'pool': 'gpsimd', 'sp': 'sync'}

            def make(e):
                def body(eng):
                    waited = {}
                    for o in self.ops[e]:
                        ws = []
                        for d in o.deps:
                            if d.is_dma:
                                ws.append((('d',) + d.dsem, dsem[d.dsem], d.dval))
                            else:
                                ws.append((('c', d.eng), csem[d.eng], d.sigval))
                        if o.ringdep is not None:
                            d = o.ringdep
                            ws.append((('d',) + d.dsem, dsem[d.dsem], d.dval))
                        for key, sem, val in ws:
                            if waited.get(key, 0) >= val:
                                continue
                            waited[key] = val
                            eng.wait_ge(sem, val)
                        if o.fn is None:
                            continue
                        name_, a_, k_ = o.fn
                        ins = getattr(eng, name_)(*a_, **k_)
                        if o.is_dma:
                            ins.then_inc(dsem[o.dsem], 16)
                        elif o.signal:
                            ins.then_inc(csem[e], 1)
                    for i in range(NRING):
                        d = self.ringops[e][i]
                        if d is not None and waited.get(('d',) + d.dsem, 0) < d.dval:
                            eng.wait_ge(dsem[d.dsem], d.dval)
                return body

            for e in ENGS:
                if self.ops[e]:
                    getattr(block, handles[e])(make(e))


_UID = [0]


class _Stop(Exception):
    pass


def _chk(tag):
    if os.environ.get('KSTOP') == tag:
        raise _Stop()


class RB:
    def __init__(self, st, nc, name, shape, dt, n, psum=False):
        alloc = nc.psum_tensor if psum else nc.sbuf_tensor
        _UID[0] += 1
        self.t = [st.enter_context(alloc('%s_%d_%d' % (name, _UID[0], i), shape, dt)) for i in range(n)]
        self.b = [Buf() for _ in range(n)]
        self.i = 0

    def next(self):
        k = self.i % len(self.t)
        self.i += 1
        return self.t[k], self.b[k]


POOL_WINDOWS = (2, 4, 8, 16)


def _band_tables(L):
    out = np.zeros((128, 4, 5, 128), np.float32)
    for g, w in enumerate(POOL_WINDOWS):
        before = w // 2
        after = w - 1 - before

        def fill(kind, t_tile, s_tile):
            for tl in range(128):
                t = t_tile * 128 + tl
                lo = max(t - before, 0)
                hi = min(t + after, L - 1)
                cnt = hi - lo + 1
                for s in range(lo, hi + 1):
                    sl = s - s_tile * 128
                    if 0 <= sl < 128:
                        out[sl, g, kind, tl] += 1.0 / cnt
                if s_tile == t_tile:
                    out[tl, g, kind, tl] -= 1.0
        nt = L // 128
        fill(0, 2, 2)
        fill(1, 2, 1)
        fill(2, 2, 3)
        fill(3, 0, 0)
        fill(4, nt - 1, nt - 1)
    return out


def _consts():
    c = {}
    c['ident'] = np.eye(128, dtype=np.float32)
    inv = 10000.0 ** (-np.arange(0, 32, 2, dtype=np.float32) / 32)
    ang = np.arange(LS, dtype=np.float32)[None, :] * inv[:, None]
    cs, sn = np.cos(ang).astype(np.float32), np.sin(ang).astype(np.float32)
    c['ropeC'] = np.concatenate([cs, cs], 0)
    c['ropeS'] = np.concatenate([-sn, sn], 0)
    for nm, L in (('p', LP), ('s', LS)):
        c['band_' + nm] = _band_tables(L)
        N = 2 * L
        N2 = L // 64
        a = np.arange(64)[:, None].astype(np.float64)
        ap = np.arange(64)[None, :].astype(np.float64)
        th = np.pi * (2 * ap + 1) * a / 128.0
        c['F1'] = np.concatenate([np.cos(th), -np.sin(th)], 1).astype(np.float32)
        c['G1re_' + nm] = ((2.0 / N) * np.cos(th).T).astype(np.float32)
        c['G1im_' + nm] = ((2.0 / N) * (-np.sin(th)).T).astype(np.float32)
        PK = 128 // N2
        b = np.arange(N2)[:, None].astype(np.float64)
        tt = np.pi * (2 * ap + 1) * b / N
        tre, tim = np.tile(np.cos(tt), (PK, 1)), np.tile(-np.sin(tt), (PK, 1))
        TA = np.concatenate([tre, tre], 1)
        TB = np.concatenate([tim, tim], 1)
        c['TA_' + nm] = np.repeat(TA[:, None, :], 4, 1).astype(np.float32)
        c['TB_' + nm] = np.repeat(TB[:, None, :], 4, 1).astype(np.float32)
        tcr, tci = np.tile(np.cos(tt).T, (1, PK)), np.tile(np.sin(tt).T, (1, PK))
        c['TcA_' + nm] = np.repeat(np.concatenate([tcr, tcr], 1)[:, None, :], 4, 1).astype(np.float32)
        c['TcB_' + nm] = np.repeat(np.concatenate([tci, tci], 1)[:, None, :], 4, 1).astype(np.float32)
        bb = np.arange(N2)[None, :].astype(np.float64)
        f2 = 2 * np.pi * b * bb / N2
        eye = np.eye(PK)
        f2re, f2im = np.kron(eye, np.cos(f2)), np.kron(eye, -np.sin(f2))
        c['F2_' + nm] = np.stack([f2re, f2im, -f2im], 1).astype(np.float32)
        c['FA_' + nm] = np.concatenate([f2re, -f2im], 1).astype(np.float32)
        c['FB_' + nm] = np.concatenate([f2im, f2re], 1).astype(np.float32)
        tf = 2 * np.pi * b * ap / L
        fre, fim = np.tile(np.cos(tf), (PK, 1)), np.tile(-np.sin(tf), (PK, 1))
        c['TfA_' + nm] = np.repeat(np.concatenate([fre, fre], 1)[:, None, :], 4, 1).astype(np.float32)
        c['TfB_' + nm] = np.repeat(np.concatenate([fim, fim], 1)[:, None, :], 4, 1).astype(np.float32)
        t = np.linspace(0.0, 1.0, L, dtype=np.float32)
        bands = np.linspace(1e-4, 15, 16, dtype=np.float32)
        angf = (np.float32(2.0 * math.pi / L) * np.arange(L, dtype=np.float32)[:, None] * bands[None, :]).astype(np.float32)
        feat = np.concatenate([t[:, None], np.cos(angf), -np.sin(angf)], -1).astype(np.float32)
        c['featT_' + nm] = np.ascontiguousarray(feat.T)
        c['trow_' + nm] = t[None, :].copy()
    ff = 2 * np.pi * a * ap / 64.0
    c['F1fA'] = np.concatenate([np.cos(ff), -np.sin(ff)], 1).astype(np.float32)
    c['F1fB'] = np.concatenate([-np.sin(ff), -np.cos(ff)], 1).astype(np.float32)
    dd = np.arange(64)[:, None] * np.arange(64)[None, :]
    c64, s64 = np.cos(2 * np.pi * dd / 64.0), np.sin(2 * np.pi * dd / 64.0)
    z = np.zeros((64, 64))
    c['DC'] = np.block([[c64, z], [z, c64]]).astype(np.float32)
    c['DS'] = np.block([[s64, z], [z, s64]]).astype(np.float32)
    deltas = np.abs(np.linspace(math.log(1e-2) / 1.5, math.log(1e-2) / 0.3, 768, dtype=np.float32))
    c['negd'] = np.ascontiguousarray((-deltas).reshape(6, 128).T).astype(np.float32)
    return c


_CONST_CACHE = {}


def _get_consts():
    if not _CONST_CACHE:
        _CONST_CACHE.update(_consts())
    return _CONST_CACHE


def build(consts, stop_after=99, dbg=None):
    nc = bass.Bass("TRN2", target_bir_lowering=False)

    def sb(name, shape, dt):
        _UID[0] += 1
        return nc.sbuf_tensor('%s_%d' % (name, _UID[0]), shape, dt)
    I = {}

    def inp(name, shape):
        I[name] = nc.dram_tensor(name, list(shape), F32, kind="ExternalInput").ap()
        return I[name]

    specs = {
        "x_p": (LP, D), "x_s": (LS, D), "norm_mix": (2, 2, D), "norm_mlp": (2, 2, D),
        "w_ff_in": (2, D, 4096), "w_ff_out": (2, 4096, D), "even_w_in": (1, D, 928),
        "pool_w": (1, 4, 128, 128), "pool_scale": (1, 512), "mla_q_norm": (1, 256),
        "mla_w_uq": (1, 256, 768), "mla_kv_norm": (1, 128), "mla_w_ukv": (1, 128, 1024),
        "even_w_out": (1, D, D), "odd_w_in": (1, D, 2560), "short_w": (1, 3, 2304),
        "short_b": (1, 2304), "filt_w1": (1, 33, 64), "filt_b1": (1, 64), "filt_w2": (1, 64, 64),
        "filt_b2": (1, 64), "filt_w3": (1, 64, 3072), "filt_freq": (1, 64), "hyena_bias": (1, 2, 768),
        "fnet_w": (1, 4, 64, 64), "odd_w_out": (1, D, D),
    }
    for k, v in specs.items():
        inp(k, v)
    for k, v in consts.items():
        inp('c_' + k, v.shape)
    y_p = nc.dram_tensor("y_p", [LP, D], F32, kind="ExternalOutput").ap()
    y_s = nc.dram_tensor("y_s", [LS, D], F32, kind="ExternalOutput").ap()

    def scratch(name, shape, dt):
        return nc.dram_tensor(name, list(shape), dt, kind="Internal").ap()

    SEQ = []
    for nm, L, xin, yout in (('p', LP, I['x_p'], y_p), ('s', LS, I['x_s'], y_s)):
        s = dict(nm=nm, L=L, x=xin, y=yout, N2=L // 64)
        s['QT'] = scratch('QT' + nm, [8, 96, L], BF16)
        s['KN'] = scratch('KN' + nm, [512, L], BF16)
        s['KR'] = scratch('KR' + nm, [32, L], BF16)
        s['V'] = scratch('V' + nm, [8, 128, L // 128, 64], BF16)
        s['mixT'] = scratch('mixT' + nm, [D, L], BF16)
        s['X1'] = scratch('X1' + nm, [L, D], F32)
        s['X2'] = scratch('X2' + nm, [L, D], F32)
        s['zcT'] = scratch('zcT' + nm, [2304, L], F32)
        s['fcT'] = scratch('fcT' + nm, [512, L], F32)
        s['hfT'] = scratch('hfT' + nm, [3072, L], F32)
        s['invd'] = scratch('invd' + nm, [128, 32], F32)
        s['mix2'] = scratch('mix2' + nm, [768, L], BF16)
        s['mfT'] = scratch('mfT' + nm, [256, L], BF16)
        s['X3'] = scratch('X3' + nm, [L, D], F32)
        for k in ('QT', 'KN', 'KR', 'V', 'mixT', 'X1', 'X2', 'zcT', 'fcT', 'hfT', 'invd', 'mix2', 'mfT', 'X3'):
            s['b_' + k] = Buf(chain=(k == 'zcT'))
        SEQ.append(s)

    P = Prog(nc)
    NOB = Buf

    with contextlib.ExitStack() as top:
        def gt(name, shape, dt):
            return top.enter_context(sb(name, shape, dt))

        PS = RB(top, nc, 'ps', [128, 512], F32, 6, psum=True)
        PST = RB(top, nc, 'pst', [128, 1024], BF16, 2, psum=True)
        ident = gt('ident', [128, 128], BF16)
        onesb = gt('onesb', [128, 128], BF16)
        onesf = gt('onesf', [128, 128], F32)
        epsT = gt('epsT', [128, 2], F32)
        stg = RB(top, nc, 'stg', [128, 1024], F32, 2)
        bconst = Buf()
        dummy = gt('dummyj', [128, 4], F32)
        P.join_fn = lambda e: e.memset(dummy[:], 0.0)
        P.op('pool', lambda e: e.memset(onesb[:], 1.0), writes=[bconst])
        P.op('pool', lambda e: e.memset(onesf[:], 1.0), writes=[bconst])
        P.op('pool', lambda e: e.memset(epsT[:, 0:1], EPS), writes=[bconst])
        P.op('pool', lambda e: e.memset(epsT[:, 1:2], 96.0 * EPS), writes=[bconst])
        _castc = [0]

        def cast_engine():
            _castc[0] += 1
            return ('act', 'pool')[_castc[0] % 2]

        def copy_op(eng, out, in_, reads, writes, nowaw=False):
            if eng == 'act':
                P.op('act', lambda e: e.copy(out=out, in_=in_), reads, writes, nowaw=nowaw)
            else:
                P.op(eng, lambda e: e.tensor_copy(out=out, in_=in_), reads, writes, nowaw=nowaw)

        def loadw(dst, src, dbuf, np_=128):
            cols = dst.shape[-1]
            assert len(dst.shape) == 2 and len(src.shape) == 2, (dst.shape, src.shape)
            for c0 in range(0, cols, 1024):
                cw = min(1024, cols - c0)
                t, b = stg.next()
                P.dma('sp', t[0:np_, 0:cw], src[:, c0:c0 + cw], writes=[b])
                copy_op(cast_engine(), dst[:, c0:c0 + cw], t[0:np_, 0:cw], [b], [dbuf.piece() if isinstance(dbuf, WBuf) else dbuf])

        loadw(ident[:], I['c_ident'], bconst)

        def rms_norm(xt, bx, nj, tmp):
            ssq, bs = tmp['ssq'].next()
            junk, bj = tmp['junk'].next()
            for j in range(nj):
                P.op('act', (lambda j: lambda e: e.activation(out=junk[:], in_=xt[:, j, :], func=AF.Square,
                                                               accum_out=ssq[:, j:j + 1]))(j), [bx], [bj, bs])
            P.op('act', lambda e: e.activation(out=ssq[:, 4:4 + nj], in_=ssq[:, 0:nj], func=AF.Sqrt,
                                               bias=epsT[:, 0:1], scale=1.0 / D), [bs, bconst], [bs])
            P.op('dve', lambda e: e.reciprocal(out=ssq[:, 8:8 + nj], in_=ssq[:, 4:4 + nj]), [bs], [bs])
            xn, bn = tmp['xn'].next()
            for j in range(nj):
                P.op('dve', (lambda j: lambda e: e.tensor_scalar(out=xn[:, j, :], in0=xt[:, j, :],
                                                                 scalar1=ssq[:, 8 + j:9 + j], scalar2=None,
                                                                 op0=ALU.mult))(j), [bx, bs], [bn], nowaw=(j > 0))
            return xn, bn

        def rms_tr(xn, bn, nj, gcol, hT, bh, bg):
            for dc in range(8):
                pt, bp = PST.next()
                for j in range(nj):
                    P.op('pe', (lambda j, dc, pt: lambda e: e.transpose(pt[:, j * 128:(j + 1) * 128],
                                                                        xn[:, j, dc * 128:(dc + 1) * 128], ident[:]))(j, dc, pt),
                         [bn, bconst], [bp])
                P.op('act', (lambda dc, pt: lambda e: e.activation(out=hT[:, dc, :], in_=pt[:, 0:nj * 128], func=AF.Copy,
                                                                   scale=gcol[:, dc:dc + 1]))(dc, pt), [bp, bg], [bh], nowaw=(dc > 0))

        def rms_transpose(ph, xt, bx, nj, gcol, hT, bh, tmp, bg):
            xn, bn = rms_norm(xt, bx, nj, tmp)
            rms_tr(xn, bn, nj, gcol, hT, bh, bg)

        def epilogue(pss, xt, bx, j, grow, tmp, bg):
            ssq, bs = tmp['ssq2'].next()
            junk, bj = tmp['junk'].next()
            for h in range(2):
                P.op('act', (lambda h: lambda e: e.activation(out=junk[:, 0:512], in_=pss[h][0][:], func=AF.Square,
                                                               accum_out=ssq[:, h:h + 1]))(h), [pss[h][1]], [bj, bs])
            P.op('dve', lambda e: e.tensor_tensor(out=ssq[:, 2:3], in0=ssq[:, 0:1], in1=ssq[:, 1:2], op=ALU.add), [bs], [bs])
            P.op('act', lambda e: e.activation(out=ssq[:, 3:4], in_=ssq[:, 2:3], func=AF.Sqrt, bias=epsT[:, 0:1],
                                               scale=1.0 / D), [bs, bconst], [bs])
            P.op('dve', lambda e: e.reciprocal(out=ssq[:, 4:5], in_=ssq[:, 3:4]), [bs], [bs])
            for h in range(2):
                t, bt = tmp['ep'].next()
                P.op('dve', (lambda h, t: lambda e: e.scalar_tensor_tensor(out=t[:], in0=pss[h][0][:], scalar=ssq[:, 4:5],
                                                                           in1=grow[:, h * 512:(h + 1) * 512], op0=ALU.mult,
                                                                           op1=ALU.mult))(h, t), [pss[h][1], bs, bg], [bt])
                P.op('pool', (lambda h, t: lambda e: e.tensor_tensor(out=xt[:, j, h * 512:(h + 1) * 512], in0=t[:],
                                                                     in1=xt[:, j, h * 512:(h + 1) * 512], op=ALU.add))(h, t),
                     [bt, bx], [bx])

        def colvec(ph, name, src1d, ncol, bufc):
            t = ph.enter_context(sb(name, [128, ncol], F32))
            P.dma('pool', t[:], src1d.rearrange("(c p) -> p c", p=128), writes=[bufc.piece() if isinstance(bufc, WBuf) else bufc])
            return t

        def rowbc(ph, name, src1d, n, bufc, npart=128):
            t = ph.enter_context(sb(name, [npart, n], F32))
            P.dma('pool', t[:], src1d.rearrange("(o n) -> o n", o=1).partition_broadcast(npart), writes=[bufc.piece() if isinstance(bufc, WBuf) else bufc])
            return t

        def ph1():
            try:
                ph1_()
            except _Stop:
                pass
            P.barrier()

        def ph1_():
            with contextlib.ExitStack() as ph:
                try:
                    ph1_body(ph)
                except _Stop:
                    pass

        def ph1_body(ph):
            if True:
                T = lambda name, shape, dt: ph.enter_context(sb(name, shape, dt))
                bw = WBuf()
                win = T('win', [128, 8, 928], BF16)
                winsw = T('winsw', [128, 8, 32], BF16)
                wuq = T('wuq', [128, 2, 768], BF16)
                wuqsw = T('wuqsw', [128, 2, 768], BF16)
                wuk = T('wuk', [128, 512], BF16)
                wuv = T('wuv', [128, 512], BF16)
                poolw = T('poolw', [128, 4, 128], BF16)
                ewi = I['even_w_in'][0].rearrange("(c p) f -> p c f", p=128)
                for dc in range(8):
                    loadw(win[:, dc, :], ewi[:, dc, :], bw)
                    loadw(winsw[:, dc, 0:16], ewi[:, dc, 912:928], bw)
                    loadw(winsw[:, dc, 16:32], ewi[:, dc, 896:912], bw)
                uq = I['mla_w_uq'][0].rearrange("(c p) f -> p c f", p=128)
                for rc in range(2):
                    loadw(wuq[:, rc, :], uq[:, rc, :], bw)
                    loadw(wuqsw[:, rc, :], uq[:, rc, :], bw)
                    for h in range(8):
                        loadw(wuqsw[:, rc, h * 96 + 64:h * 96 + 80], uq[:, rc, h * 96 + 80:h * 96 + 96], bw)
                        loadw(wuqsw[:, rc, h * 96 + 80:h * 96 + 96], uq[:, rc, h * 96 + 64:h * 96 + 80], bw)
                ukv = I['mla_w_ukv'][0]
                for h in range(8):
                    loadw(wuk[:, h * 64:(h + 1) * 64], ukv[:, h * 128:h * 128 + 64], bw)
                    loadw(wuv[:, h * 64:(h + 1) * 64], ukv[:, h * 128 + 64:h * 128 + 128], bw)
                for g in range(4):
                    loadw(poolw[:, g, :], I['pool_w'][0, g], bw)
                g0col = colvec(ph, 'g0col', I['norm_mix'][0, 0], 8, bw)
                pscol = colvec(ph, 'pscol', I['pool_scale'][0], 4, bw)
                qncol = colvec(ph, 'qncol', I['mla_q_norm'][0], 2, bw)
                kvcol = colvec(ph, 'kvcol', I['mla_kv_norm'][0], 1, bw)
                band = T('band', [128, 4 * 5 * 128], BF16)
                _chk('w')
                tmp = dict(ssq=RB(ph, nc, 'ssq', [128, 12], F32, 2), junk=RB(ph, nc, 'junk', [128, 1024], F32, 1),
                           xn=RB(ph, nc, 'xn', [128, 4, 1024], BF16, 1))
                XT = RB(ph, nc, 'xt', [128, 4, 1024], F32, 1)
                HT = RB(ph, nc, 'hT', [128, 8, 512], BF16, 1)
                atok = T('atok', [128, 64, 512], BF16)
                SQ = RB(ph, nc, 'sq', [128, 3, 512], BF16, 2)
                CG = RB(ph, nc, 'cg', [128, 3, 512], BF16, 2)
                RQ = RB(ph, nc, 'rq', [128, 2, 512], F32, 1)
                RT = RB(ph, nc, 'rt', [96, 4, 512], F32, 1)
                RK = RB(ph, nc, 'rk', [32, 2, 512], F32, 1)
                KT = RB(ph, nc, 'kt', [32, 2, 512], F32, 1)
                KRO = RB(ph, nc, 'kro', [32, 512], BF16, 2)
                QTs = RB(ph, nc, 'qTs', [96, 512], BF16, 3)
                QTm = RB(ph, nc, 'qTm', [96, 2, 512], F32, 1)
                KS = RB(ph, nc, 'ks', [128, 512], BF16, 2)
                VS = RB(ph, nc, 'vs', [128, 512], BF16, 2)
                RC = RB(ph, nc, 'rc', [128, 12], F32, 2)
                RTs = RB(ph, nc, 'rTs', [128, 512], BF16, 2)
                MS = RB(ph, nc, 'ms', [128, 512], BF16, 2)
                for s in SEQ:
                    L = s['L']
                    nt = L // 128
                    batok = [Buf() for _ in range(nt)]
                    bband = Buf()
                    loadw(band[:], I['c_band_' + s['nm']].rearrange("p g k t -> p (g k t)"), bband)
                    for ci in range(L // 512):
                        c0 = ci * 512
                        xt, bx = XT.next()
                        P.dma('sp', xt[:], s['x'][c0:c0 + 512, :].rearrange("(j p) d -> p j d", p=128), writes=[bx])
                        hT, bh = HT.next()
                        rms_transpose(ph, xt, bx, 4, g0col, hT, bh, tmp, bw)
                        _chk('rms')
                        for j in range(4):
                            ps, bp = PS.next()
                            for dc in range(8):
                                P.op('pe', (lambda j, dc, ps: lambda e: e.matmul(ps[:], lhsT=hT[:, dc, j * 128:(j + 1) * 128],
                                                                                 rhs=win[:, dc, 0:512], start=(dc == 0), stop=(dc == 7)))(j, dc, ps),
                                     [bh, bw], [bp])
                            copy_op('act', atok[:, ci * 4 + j, :], ps[:], [bp], [batok[ci * 4 + j]])
                        _chk('atok')
                        sq, bsq = SQ.next()
                        cg, bcg = CG.next()
                        for k3, (lo, ncol) in enumerate(((512, qncol[:, 0:1]), (640, qncol[:, 1:2]), (768, kvcol[:, 0:1]))):
                            ps, bp = PS.next()
                            for dc in range(8):
                                P.op('pe', (lambda dc, ps, lo: lambda e: e.matmul(ps[:], lhsT=win[:, dc, lo:lo + 128], rhs=hT[:, dc, :],
                                                                                  start=(dc == 0), stop=(dc == 7)))(dc, ps, lo), [bh, bw], [bp])
                            P.op('act', (lambda k3, ps: lambda e: e.activation(out=sq[:, k3, :], in_=ps[:], func=AF.Square))(k3, ps), [bp], [bsq])
                            P.op('dve', (lambda k3, ps, ncol: lambda e: e.tensor_scalar(out=cg[:, k3, :], in0=ps[:], scalar1=ncol, scalar2=None,
                                                                                        op0=ALU.mult))(k3, ps, ncol), [bp, bw, bsq], [bcg])
                        _chk('lat')
                        rk, brk = RK.next()
                        P.dma('pool', rk[:, 0, :], I['c_ropeC'][:, c0:c0 + 512], writes=[brk])
                        P.dma('pool', rk[:, 1, :], I['c_ropeS'][:, c0:c0 + 512], writes=[brk])
                        kt, bkt = KT.next()
                        for k2, (wt, lo) in enumerate(((win, 896), (winsw, 0))):
                            ps, bp = PS.next()
                            for dc in range(8):
                                P.op('pe', (lambda dc, ps, wt, lo: lambda e: e.matmul(ps[0:32, :], lhsT=wt[:, dc, lo:lo + 32], rhs=hT[:, dc, :],
                                                                                      start=(dc == 0), stop=(dc == 7)))(dc, ps, wt, lo), [bh, bw], [bp])
                            P.op('dve', (lambda k2, ps: lambda e: e.tensor_tensor(out=kt[:, k2, :], in0=ps[0:32, :], in1=rk[:, k2, :],
                                                                                  op=ALU.mult))(k2, ps), [bp, brk], [bkt])
                        kro, bkro = KRO.next()
                        P.op('pool', lambda e, kt=kt, kro=kro: e.tensor_tensor(out=kro[:], in0=kt[:, 0, :], in1=kt[:, 1, :], op=ALU.add), [bkt], [bkro])
                        P.dma('pool', s['KR'][:, c0:c0 + 512], kro[:], reads=[bkro], writes=[s['b_KR']])
                        _chk('krope')
                        rq, brq = RQ.next()
                        ps, bp = PS.next()
                        for rc in range(2):
                            P.op('pe', (lambda rc, ps: lambda e: e.matmul(ps[:], lhsT=onesb[:], rhs=sq[:, rc, :], start=(rc == 0), stop=(rc == 1)))(rc, ps),
                                 [bsq, bconst], [bp])
                        P.op('act', lambda e, ps=ps, rq=rq: e.activation(out=rq[:, 0, :], in_=ps[:], func=AF.Sqrt, bias=epsT[:, 1:2], scale=96.0 / 256.0),
                             [bp, bconst], [brq])
                        P.op('dve', lambda e, rq=rq: e.reciprocal(out=rq[:, 0, :], in_=rq[:, 0, :]), [brq], [brq])
                        ps, bp = PS.next()
                        P.op('pe', lambda e, ps=ps, sq=sq: e.matmul(ps[:], lhsT=onesb[:], rhs=sq[:, 2, :], start=True, stop=True), [bsq, bconst], [bp])
                        P.op('act', lambda e, ps=ps, rq=rq: e.activation(out=rq[:, 1, :], in_=ps[:], func=AF.Sqrt, bias=epsT[:, 0:1], scale=1.0 / 128.0),
                             [bp, bconst], [brq])
                        P.op('dve', lambda e, rq=rq: e.reciprocal(out=rq[:, 1, :], in_=rq[:, 1, :]), [brq], [brq])
                        rc_, brc = RC.next()
                        ps, bp = PS.next()
                        for j in range(4):
                            P.op('pe', (lambda j, ps: lambda e: e.matmul(ps[:, j:j + 1], lhsT=sq[:, 2, j * 128:(j + 1) * 128], rhs=onesb[:, 0:1],
                                                                         start=True, stop=True))(j, ps), [bsq, bconst], [bp])
                        P.op('act', lambda e, ps=ps, rc_=rc_: e.activation(out=rc_[:, 0:4], in_=ps[:, 0:4], func=AF.Sqrt, bias=epsT[:, 0:1], scale=1.0 / 128.0),
                             [bp, bconst], [brc])
                        P.op('dve', lambda e, rc_=rc_: e.reciprocal(out=rc_[:, 4:8], in_=rc_[:, 0:4]), [brc], [brc])
                        rt, brt = RT.next()
                        P.dma('pool', rt[64:96, 0, :], I['c_ropeC'][:, c0:c0 + 512], writes=[brt])
                        P.dma('pool', rt[64:96, 1, :], I['c_ropeS'][:, c0:c0 + 512], writes=[brt])
                        for k2 in range(2):
                            P.op('pool', (lambda k2: lambda e, rt=rt, rq=rq: e.tensor_tensor(out=rt[64:96, 2 + k2, :], in0=rt[64:96, k2, :],
                                                                                             in1=rq[64:96, 0, :], op=ALU.mult))(k2), [brt, brq], [brt])
                        _chk('rstd')
                        for h in range(8):
                            pq, bpq = PS.next()
                            pw, bpw = PS.next()
                            for rc in range(2):
                                P.op('pe', (lambda rc, pq, h: lambda e: e.matmul(pq[0:96, :], lhsT=wuq[:, rc, h * 96:(h + 1) * 96], rhs=cg[:, rc, :],
                                                                                 start=(rc == 0), stop=(rc == 1)))(rc, pq, h), [bcg, bw], [bpq])
                            for rc in range(2):
                                P.op('pe', (lambda rc, pw, h: lambda e: e.matmul(pw[0:96, :], lhsT=wuqsw[:, rc, h * 96:(h + 1) * 96], rhs=cg[:, rc, :],
                                                                                 start=(rc == 0), stop=(rc == 1)))(rc, pw, h), [bcg, bw], [bpw])
                            qs, bqs = QTs.next()
                            qm, bqm = QTm.next()
                            P.op('dve', lambda e, qs=qs, pq=pq, rq=rq: e.tensor_tensor(out=qs[0:64, :], in0=pq[0:64, :], in1=rq[0:64, 0, :], op=ALU.mult),
                                 [bpq, brq], [bqs])
                            P.op('dve', lambda e, qm=qm, pq=pq, rt=rt: e.tensor_tensor(out=qm[64:96, 0, :], in0=pq[64:96, :], in1=rt[64:96, 2, :], op=ALU.mult),
                                 [bpq, brt], [bqm])
                            P.op('dve', lambda e, qm=qm, pw=pw, rt=rt: e.tensor_tensor(out=qm[64:96, 1, :], in0=pw[64:96, :], in1=rt[64:96, 3, :], op=ALU.mult),
                                 [bpw, brt], [bqm])
                            P.op('pool', lambda e, qs=qs, qm=qm: e.tensor_tensor(out=qs[64:96, :], in0=qm[64:96, 0, :], in1=qm[64:96, 1, :], op=ALU.add),
                                 [bqm], [bqs])
                            P.dma('sp', s['QT'][h, :, c0:c0 + 512], qs[:], reads=[bqs], writes=[s['b_QT']])
                        _chk('q')
                        for hp in range(4):
                            ps, bp = PS.next()
                            P.op('pe', lambda e, ps=ps, hp=hp, cg=cg: e.matmul(ps[:], lhsT=wuk[:, hp * 128:(hp + 1) * 128], rhs=cg[:, 2, :], start=True, stop=True),
                                 [bcg, bw], [bp])
                            ks, bks = KS.next()
                            P.op('dve', lambda e, ks=ks, ps=ps, rq=rq: e.tensor_tensor(out=ks[:], in0=ps[:], in1=rq[:, 1, :], op=ALU.mult), [bp, brq], [bks])
                            P.dma('sp', s['KN'][hp * 128:(hp + 1) * 128, c0:c0 + 512], ks[:], reads=[bks], writes=[s['b_KN']])
                        _chk('k')
                        for j in range(4):
                            ps, bp = PS.next()
                            P.op('pe', lambda e, ps=ps, j=j, cg=cg: e.matmul(ps[:], lhsT=cg[:, 2, j * 128:(j + 1) * 128], rhs=wuv[:], start=True, stop=True),
                                 [bcg, bw], [bp])
                            vs, bvs = VS.next()
                            P.op('act', lambda e, vs=vs, ps=ps, rc_=rc_, j=j: e.activation(out=vs[:], in_=ps[:], func=AF.Copy, scale=rc_[:, 4 + j:5 + j]),
                                 [bp, brc], [bvs])
                            P.dma('sp', s['V'][:, :, ci * 4 + j, :].rearrange("h p d -> p h d"), vs[:].rearrange("p (h d) -> p h d", h=8), reads=[bvs], writes=[s['b_V']])
                    _chk('v')
                    for sp in range(L // 512):
                        for g in range(4):
                            ps, bp = PS.next()
                            for jt in range(4):
                                t = sp * 4 + jt
                                terms = []
                                if t > 0:
                                    terms.append((t - 1, 1))
                                terms.append((t, 3 if t == 0 else (4 if t == nt - 1 else 0)))
                                if t < nt - 1:
                                    terms.append((t + 1, 2))
                                for k, (st_, kind) in enumerate(terms):
                                    off = (g * 5 + kind) * 128
                                    P.op('pe', lambda e, ps=ps, jt=jt, st_=st_, g=g, off=off, k=k, n=len(terms): e.matmul(
                                        ps[:, jt * 128:(jt + 1) * 128], lhsT=atok[:, st_, g * 128:(g + 1) * 128], rhs=band[:, off:off + 128],
                                        start=(k == 0), stop=(k == n - 1)), [batok[st_], bband], [bp])
                            rts, brts = RTs.next()
                            copy_op('dve', rts[:], ps[:], [bp], [brts])
                            ps2, bp2 = PS.next()
                            P.op('pe', lambda e, ps2=ps2, g=g, rts=rts: e.matmul(ps2[:], lhsT=poolw[:, g, :], rhs=rts[:], start=True, stop=True), [brts, bw], [bp2])
                            ms, bms = MS.next()
                            P.op('act', lambda e, ms=ms, ps2=ps2, g=g: e.activation(out=ms[:], in_=ps2[:], func=AF.Copy, scale=pscol[:, g:g + 1]), [bp2, bw], [bms])
                            P.dma('sp', s['mixT'][g * 128:(g + 1) * 128, sp * 512:(sp + 1) * 512], ms[:], reads=[bms], writes=[s['b_mixT']])
            P.barrier()

        def ph2():
            with contextlib.ExitStack() as ph:
                KT = RB(ph, nc, 'aK', [96, LS], BF16, 2)
                QT = RB(ph, nc, 'aQ', [96, LS], BF16, 2)
                VT = RB(ph, nc, 'aV', [128, LS // 128, 128], BF16, 2)
                PT = RB(ph, nc, 'aP', [128, 512], BF16, 4)
                RS = RB(ph, nc, 'aR', [128, 512], F32, 2)
                BC = RB(ph, nc, 'aB', [64, 512], F32, 2)
                OS = RB(ph, nc, 'aO', [64, 512], BF16, 2)
                for i in range(2):
                    P.op('pool', lambda e, i=i: e.memset(VT.t[i][:, :, 64:128], 1.0), writes=[VT.b[i]])

                class _Sub:
                    def __init__(self, lo, hi):
                        self.t = PS.t[lo:hi]
                        self.b = PS.b[lo:hi]
                        self.i = 0
                    next = RB.next
                POs = _Sub(0, 2)
                PSs = _Sub(2, 6)
                for s in SEQ:
                    L = s['L']
                    nk = L // 128
                    for h in range(8):
                        kt, bk = KT.next()
                        qt, bq = QT.next()
                        vt, bv = VT.next()
                        P.dma('sp', kt[0:64, 0:L], s['KN'][h * 64:(h + 1) * 64, :], reads=[s['b_KN']], writes=[bk])
                        P.dma('sp', kt[64:96, 0:L], s['KR'][:, :], reads=[s['b_KR']], writes=[bk])
                        P.dma('sp', qt[:, 0:L], s['QT'][h], reads=[s['b_QT']], writes=[bq])
                        P.dma('pool', vt[:, 0:nk, 0:64], s['V'][h],
                              reads=[s['b_V']], writes=[bv])
                        for qc in range(L // 512):
                            po, bpo = POs.next()
                            pend = []

                            def issue_s(k, kt=kt, qt=qt, qc=qc):
                                pss, bps = PSs.next()
                                P.op('pe', lambda e: e.matmul(pss[:], lhsT=kt[:, k * 128:(k + 1) * 128], rhs=qt[:, qc * 512:(qc + 1) * 512], start=True, stop=True),
                                     [bk, bq], [bps])
                                pend.append((pss, bps))
                            for k in range(min(3, nk)):
                                issue_s(k)
                            for k in range(nk):
                                if k + 3 < nk:
                                    issue_s(k + 3)
                                pss, bps = pend.pop(0)
                                pt, bpt = PT.next()
                                P.op('act', lambda e, pt=pt, pss=pss: e.activation(out=pt[:], in_=pss[:], func=AF.Exp), [bps], [bpt])
                                P.op('pe', lambda e, po=po, vt=vt, pt=pt, k=k, nk=nk: e.matmul(po[:], lhsT=vt[:, k, :], rhs=pt[:], start=(k == 0), stop=(k == nk - 1)),
                                     [bv, bpt], [bpo])
                            rs, brs = RS.next()
                            P.op('dve', lambda e, rs=rs, po=po: e.reciprocal(out=rs[64:65, :], in_=po[64:65, :]), [bpo], [brs])
                            pb, bpb = PSs.next()
                            P.op('pe', lambda e, pb=pb, rs=rs: e.matmul(pb[0:64, :], lhsT=onesf[64:65, 0:64], rhs=rs[64:65, :], start=True, stop=True),
                                 [brs, bconst], [bpb])
                            bc, bbc = BC.next()
                            copy_op('act', bc[:], pb[0:64, :], [bpb], [bbc])
                            os_, bos = OS.next()
                            P.op('dve', lambda e, os_=os_, po=po, bc=bc: e.tensor_tensor(out=os_[:], in0=po[0:64, :], in1=bc[:], op=ALU.mult), [bpo, bbc], [bos])
                            P.dma('pool', s['mixT'][512 + h * 64:512 + (h + 1) * 64, qc * 512:(qc + 1) * 512], os_[:], reads=[bos], writes=[s['b_mixT']])
            P.barrier()

        def ph_out(layer):
            with contextlib.ExitStack() as ph:
                T = lambda name, shape, dt: ph.enter_context(sb(name, shape, dt))
                bw = WBuf()
                wout = T('wout', [128, 8, 1024], BF16)
                wsrc = (I['even_w_out'] if layer == 0 else I['odd_w_out'])[0].rearrange("(c p) f -> p c f", p=128)
                for c in range(8):
                    loadw(wout[:, c, :], wsrc[:, c, :], bw)
                grow = rowbc(ph, 'grow', I['norm_mix'][layer, 1], D, bw)
                if layer == 1:
                    fwb = T('fwb', [128, 2, 128], BF16)
                    bfw = Buf()
                    P.op('pool', lambda e: e.memset(fwb[:], 0.0), writes=[bfw])
                    for g in range(4):
                        pp = (g % 2) * 64
                        loadw(fwb[pp:pp + 64, g // 2, pp:pp + 64], I['fnet_w'][0, g], bfw, np_=64)
                tmp = dict(ssq2=RB(ph, nc, 'ssq2', [128, 8], F32, 2), junk=RB(ph, nc, 'junk', [128, 1024], F32, 1),
                           ep=RB(ph, nc, 'ep', [128, 512], F32, 3))
                XT = RB(ph, nc, 'xt', [128, 4, 1024], F32, 2)
                MX = RB(ph, nc, 'mx', [128, 8, 512], BF16, 2)
                MF = RB(ph, nc, 'mf', [128, 2, 512], BF16, 2)
                for s in SEQ:
                    L = s['L']
                    xin, bxin = (s['x'], Buf()) if layer == 0 else (s['X2'], s['b_X2'])
                    xout, bxout = (s['X1'], s['b_X1']) if layer == 0 else (s['X3'], s['b_X3'])
                    for ci in range(L // 512):
                        c0 = ci * 512
                        xt, bx = XT.next()
                        P.dma('sp', xt[:], xin[c0:c0 + 512, :].rearrange("(j p) d -> p j d", p=128), reads=[bxin], writes=[bx])
                        mx, bm = MX.next()
                        if layer == 0:
                            P.dma('pool', mx[:], s['mixT'][:, c0:c0 + 512].rearrange("(c p) t -> p c t", p=128), reads=[s['b_mixT']], writes=[bm])
                        else:
                            P.dma('pool', mx[:, 0:6, :], s['mix2'][:, c0:c0 + 512].rearrange("(c p) t -> p c t", p=128), reads=[s['b_mix2']], writes=[bm])
                            mf, bmf = MF.next()
                            P.dma('pool', mf[:], s['mfT'][:, c0:c0 + 512].rearrange("(c p) t -> p c t", p=128), reads=[s['b_mfT']], writes=[bmf])
                            for j2 in range(2):
                                ps, bp = PS.next()
                                P.op('pe', lambda e, ps=ps, j2=j2, mf=mf: e.matmul(ps[:], lhsT=fwb[:, j2, :], rhs=mf[:, j2, :], start=True, stop=True), [bmf, bfw], [bp])
                                copy_op('act', mx[:, 6 + j2, :], ps[:], [bp], [bm])
                        for j in range(4):
                            pss = []
                            for hf in range(2):
                                ps, bp = PS.next()
                                for c in range(8):
                                    P.op('pe', lambda e, ps=ps, c=c, j=j, hf=hf, mx=mx: e.matmul(ps[:], lhsT=mx[:, c, j * 128:(j + 1) * 128],
                                                                                                 rhs=wout[:, c, hf * 512:(hf + 1) * 512], start=(c == 0), stop=(c == 7)),
                                         [bm, bw], [bp])
                                pss.append((ps, bp))
                            epilogue(pss, xt, bx, j, grow, tmp, bw)
                        P.dma('sp', xout[c0:c0 + 512, :].rearrange("(j p) d -> p j d", p=128), xt[:], reads=[bx], writes=[bxout])
            P.barrier()

        def ph_mlp(layer):
            TT = 256
            nj = TT // 128
            with contextlib.ExitStack() as ph:
                T = lambda name, shape, dt: ph.enter_context(sb(name, shape, dt))
                bw = WBuf()
                wfi = T('wfi', [128, 8, 4096], BF16)
                wfo = T('wfo', [128, 32, 1024], BF16)
                s1 = I['w_ff_in'][layer].rearrange("(c p) f -> p c f", p=128)
                s2 = I['w_ff_out'][layer].rearrange("(c p) f -> p c f", p=128)
                for c in range(8):
                    loadw(wfi[:, c, :], s1[:, c, :], bw)
                for c in range(32):
                    loadw(wfo[:, c, :], s2[:, c, :], bw)
                gcol = colvec(ph, 'gcol', I['norm_mlp'][layer, 0], 8, bw)
                grow = rowbc(ph, 'grow', I['norm_mlp'][layer, 1], D, bw)
                tmp = dict(ssq=RB(ph, nc, 'ssq', [128, 12], F32, 2), ssq2=RB(ph, nc, 'ssq2', [128, 8], F32, 2),
                           junk=RB(ph, nc, 'junk', [128, 1024], F32, 1), xn=RB(ph, nc, 'xn', [128, nj, 1024], BF16, 1),
                           ep=RB(ph, nc, 'ep', [128, 512], F32, 2))
                XT = RB(ph, nc, 'xt', [128, nj, 1024], F32, 2)
                HT = RB(ph, nc, 'hT', [128, 8, TT], BF16, 2)
                F1 = RB(ph, nc, 'f1', [128, 32, TT], BF16, 1)
                RL = RB(ph, nc, 'rl', [128, TT], F32, 3)
                for s in SEQ:
                    L = s['L']
                    xin, bxin = (s['X1'], s['b_X1']) if layer == 0 else (s['X3'], s['b_X3'])
                    xout, bxout = (s['X2'], s['b_X2']) if layer == 0 else (s['y'], Buf())
                    nchunk = L // TT

                    def load_norm(ci):
                        c0 = ci * TT
                        xt, bx = XT.next()
                        P.dma('sp', xt[:], xin[c0:c0 + TT, :].rearrange("(j p) d -> p j d", p=128), reads=[bxin], writes=[bx])
                        xn, bn = rms_norm(xt, bx, nj, tmp)
                        return xt, bx, xn, bn
                    cur = load_norm(0)
                    hT, bh = HT.next()
                    rms_tr(cur[2], cur[3], nj, gcol, hT, bh, bw)
                    for ci in range(nchunk):
                        c0 = ci * TT
                        xt, bx = cur[0], cur[1]
                        f1, bf1 = F1.next()
                        nxt = None
                        for fc in range(32):
                            ps, bp = PS.next()
                            for dc in range(8):
                                P.op('pe', lambda e, ps=ps, dc=dc, fc=fc, hT=hT: e.matmul(ps[:, 0:TT], lhsT=wfi[:, dc, fc * 128:(fc + 1) * 128], rhs=hT[:, dc, :],
                                                                                          start=(dc == 0), stop=(dc == 7)), [bh, bw], [bp])
                            rl, brl = RL.next()
                            P.op('act', lambda e, rl=rl, ps=ps: e.activation(out=rl[:], in_=ps[:, 0:TT], func=AF.Relu), [bp], [brl])
                            eng = 'pool' if fc % 2 else 'dve'
                            P.op(eng, lambda e, rl=rl, f1=f1, fc=fc: e.tensor_tensor(out=f1[:, fc, :], in0=rl[:], in1=rl[:], op=ALU.mult), [brl], [bf1], nowaw=(fc > 1))
                            if fc == 6 and ci + 1 < nchunk:
                                nxt = load_norm(ci + 1)
                        if nxt is not None:
                            hTn, bhn = HT.next()
                            rms_tr(nxt[2], nxt[3], nj, gcol, hTn, bhn, bw)
                        for j in range(nj):
                            pss = []
                            for hf in range(2):
                                ps, bp = PS.next()
                                for fc in range(32):
                                    P.op('pe', lambda e, ps=ps, fc=fc, j=j, hf=hf, f1=f1: e.matmul(ps[:], lhsT=f1[:, fc, j * 128:(j + 1) * 128],
                                                                                                   rhs=wfo[:, fc, hf * 512:(hf + 1) * 512], start=(fc == 0), stop=(fc == 31)),
                                         [bf1, bw], [bp])
                                pss.append((ps, bp))
                            epilogue(pss, xt, bx, j, grow, tmp, bw)
                        P.dma('pool', xout[c0:c0 + TT, :].rearrange("(j p) d -> p j d", p=128), xt[:], reads=[bx], writes=[bxout])
                        if nxt is not None:
                            cur = nxt
                            hT, bh = hTn, bhn
            P.barrier()

        def ph5():
            with contextlib.ExitStack() as ph:
                T = lambda name, shape, dt: ph.enter_context(sb(name, shape, dt))
                bw = WBuf()
                owin = T('owin', [128, 8, 2560], BF16)
                src = I['odd_w_in'][0].rearrange("(c p) f -> p c f", p=128)
                for c in range(8):
                    loadw(owin[:, c, :], src[:, c, :], bw)
                gcol = colvec(ph, 'gcol', I['norm_mix'][1, 0], 8, bw)
                swc = T('swc', [128, 3, 18], F32)
                for j in range(3):
                    P.dma('pool', swc[:, j, :], I['short_w'][0, j].rearrange("(c p) -> p c", p=128), writes=[bw])
                sbc = colvec(ph, 'sbc', I['short_b'][0], 18, bw)
                DCt = T('DCt', [128, 2, 128], F32)
                loadw(DCt[:, 0, :], I['c_DC'], bw)
                loadw(DCt[:, 1, :], I['c_DS'], bw)
                tmp = dict(ssq=RB(ph, nc, 'ssq', [128, 12], F32, 2), junk=RB(ph, nc, 'junk', [128, 1024], F32, 1),
                           xn=RB(ph, nc, 'xn', [128, 4, 1024], BF16, 1))
                XT = RB(ph, nc, 'xt', [128, 4, 1024], F32, 2)
                HT = RB(ph, nc, 'hT', [128, 8, 512], BF16, 2)
                WN = RB(ph, nc, 'wn', [128, 514], F32, 3)
                OT = RB(ph, nc, 'ot', [128, 512], F32, 3)
                FS = RB(ph, nc, 'fs', [128, 512], F32, 2)
                FO = RB(ph, nc, 'fo', [128, 512], F32, 3)
                carry = T('carry', [128, 18, 2], F32)
                bcar = Buf()
                for s in SEQ:
                    L = s['L']
                    fsc = 1.0 / math.sqrt(64.0 * L)
                    P.op('pool', lambda e: e.memset(carry[:], 0.0), writes=[bcar])
                    nch = L // 512
                    for ci in range(nch + 1):
                        c0 = ci * 512
                        last = (ci == nch)
                        if not last:
                            xt, bx = XT.next()
                            P.dma('sp', xt[:], s['X2'][c0:c0 + 512, :].rearrange("(j p) d -> p j d", p=128), reads=[s['b_X2']], writes=[bx])
                            hT, bh = HT.next()
                            rms_transpose(ph, xt, bx, 4, gcol, hT, bh, tmp, bw)
                        for fc in range(18):
                            wn, bwn = WN.next()
                            P.op('pool', lambda e, wn=wn, fc=fc: e.tensor_copy(out=wn[:, 0:2], in_=carry[:, fc, :]), [bcar], [bwn])
                            if not last:
                                ps, bp = PS.next()
                                for dc in range(8):
                                    P.op('pe', lambda e, ps=ps, dc=dc, fc=fc, hT=hT: e.matmul(ps[:], lhsT=owin[:, dc, fc * 128:(fc + 1) * 128], rhs=hT[:, dc, :],
                                                                                              start=(dc == 0), stop=(dc == 7)), [bh, bw], [bp])
                                copy_op('act', wn[:, 2:514], ps[:], [bp], [bwn])
                                nw = 512
                            else:
                                P.op('pool', lambda e, wn=wn: e.memset(wn[:, 2:3], 0.0), [], [bwn])
                                nw = 1
                            ot, bo = OT.next()
                            P.op('dve', lambda e, ot=ot, wn=wn, fc=fc, nw=nw: e.tensor_scalar(out=ot[:, 0:nw], in0=wn[:, 0:nw], scalar1=swc[:, 0, fc:fc + 1],
                                                                                              scalar2=sbc[:, fc:fc + 1], op0=ALU.mult, op1=ALU.add), [bwn, bw], [bo])
                            for j in (1, 2):
                                P.op('dve', lambda e, ot=ot, wn=wn, fc=fc, nw=nw, j=j: e.scalar_tensor_tensor(out=ot[:, 0:nw], in0=wn[:, j:j + nw], scalar=swc[:, j, fc:fc + 1],
                                                                                                              in1=ot[:, 0:nw], op0=ALU.mult, op1=ALU.add), [bwn, bw, bo], [bo])
                            if not last:
                                P.op('pool', lambda e, wn=wn, fc=fc: e.tensor_copy(out=carry[:, fc, :], in_=wn[:, 512:514]), [bwn], [bcar])
                            if ci == 0:
                                P.dma('sp', s['zcT'][fc * 128:(fc + 1) * 128, 0:511], ot[:, 1:512], reads=[bo], writes=[s['b_zcT']])
                            elif not last:
                                P.dma('sp', s['zcT'][fc * 128:(fc + 1) * 128, c0 - 1:c0 + 511], ot[:, 0:512], reads=[bo], writes=[s['b_zcT']])
                            else:
                                P.dma('sp', s['zcT'][fc * 128:(fc + 1) * 128, L - 1:L], ot[:, 0:1], reads=[bo], writes=[s['b_zcT']])
                        if last:
                            continue
                        for f2 in range(2):
                            ps, bp = PS.next()
                            for dc in range(8):
                                P.op('pe', lambda e, ps=ps, dc=dc, f2=f2, hT=hT: e.matmul(ps[:], lhsT=owin[:, dc, 2304 + f2 * 128:2304 + (f2 + 1) * 128], rhs=hT[:, dc, :],
                                                                                          start=(dc == 0), stop=(dc == 7)), [bh, bw], [bp])
                            fs, bfs = FS.next()
                            copy_op('act', fs[:], ps[:], [bp], [bfs])
                            for k2 in range(2):
                                ps2, bp2 = PS.next()
                                P.op('pe', lambda e, ps2=ps2, k2=k2, fs=fs: e.matmul(ps2[:], lhsT=DCt[:, k2, :], rhs=fs[:], start=True, stop=True), [bfs, bw], [bp2])
                                fo, bfo = FO.next()
                                P.op('act', lambda e, fo=fo, ps2=ps2: e.activation(out=fo[:], in_=ps2[:], func=AF.Copy, scale=fsc), [bp2], [bfo])
                                r0 = k2 * 256 + f2 * 128
                                P.dma('pool', s['fcT'][r0:r0 + 128, c0:c0 + 512], fo[:], reads=[bfo], writes=[s['b_fcT']])
            P.barrier()

        def ph6():
            PI = math.pi
            with contextlib.ExitStack() as ph:
                T = lambda name, shape, dt: ph.enter_context(sb(name, shape, dt))
                bw = Buf()
                w1 = T('fw1', [33, 64], F32)
                w2 = T('fw2', [64, 64], F32)
                w3 = T('fw3', [64, 3072], F32)
                P.dma('pool', w1[:], I['filt_w1'][0], writes=[bw])
                P.dma('pool', w2[:], I['filt_w2'][0], writes=[bw])
                P.dma('pool', w3[:], I['filt_w3'][0], writes=[bw])
                cols = T('fcols', [64, 8], F32)
                for k, nm_ in enumerate(('filt_b1', 'filt_b2', 'filt_freq')):
                    P.dma('pool', cols[:, k:k + 1], I[nm_][0].rearrange("(p o) -> p o", o=1), writes=[bw])
                P.op('dve', lambda e: e.tensor_tensor(out=cols[:, 3:4], in0=cols[:, 2:3], in1=cols[:, 0:1], op=ALU.mult), [bw], [bw])
                P.op('dve', lambda e: e.tensor_tensor(out=cols[:, 4:5], in0=cols[:, 2:3], in1=cols[:, 1:2], op=ALU.mult), [bw], [bw])
                negd = T('negd', [128, 6], F32)
                P.dma('pool', negd[:], I['c_negd'], writes=[bw])
                FT = RB(ph, nc, 'ft', [33, 512], F32, 2)
                TB = RB(ph, nc, 'tb', [128, 512], F32, 2)
                DT = RB(ph, nc, 'dt', [128, 6, 512], F32, 2)
                AR = RB(ph, nc, 'ar', [64, 3, 512], F32, 2)
                H1 = RB(ph, nc, 'h1', [64, 512], F32, 2)
                H2 = RB(ph, nc, 'h2', [64, 512], F32, 2)
                HF = RB(ph, nc, 'hf', [128, 512], F32, 3)
                junk = T('fjunk', [128, 512], F32)
                bjunk = Buf()
                ssq = T('fssq', [128, 24, 16], F32)
                tot = T('ftot', [128, 3, 24], F32)
                bss = Buf()

                def sinlayer(ps, bp, fb, out, bo):
                    ar, ba = AR.next()
                    P.op('dve', lambda e: e.tensor_scalar(out=ar[:, 0, :], in0=ps[0:64, :], scalar1=cols[:, 2:3], scalar2=cols[:, fb:fb + 1],
                                                          op0=ALU.mult, op1=ALU.add), [bp, bw], [ba])
                    P.op('dve', lambda e: e.tensor_scalar(out=ar[:, 1, :], in0=ar[:, 0, :], scalar1=PI, scalar2=2 * PI, op0=ALU.is_gt, op1=ALU.mult), [ba], [ba])
                    P.op('dve', lambda e: e.tensor_tensor(out=ar[:, 0, :], in0=ar[:, 0, :], in1=ar[:, 1, :], op=ALU.subtract), [ba], [ba])
                    P.op('dve', lambda e: e.tensor_scalar(out=ar[:, 1, :], in0=ar[:, 0, :], scalar1=-PI, scalar2=2 * PI, op0=ALU.is_lt, op1=ALU.mult), [ba], [ba])
                    P.op('dve', lambda e: e.tensor_tensor(out=ar[:, 0, :], in0=ar[:, 0, :], in1=ar[:, 1, :], op=ALU.add), [ba], [ba])
                    P.op('act', lambda e: e.activation(out=out[:], in_=ar[:, 0, :], func=AF.Sin, scale=0.999999), [ba], [bo])

                for s in SEQ:
                    L = s['L']
                    nch = L // 512
                    for ci in range(nch):
                        c0 = ci * 512
                        ft, bft = FT.next()
                        P.dma('sp', ft[:], I['c_featT_' + s['nm']][:, c0:c0 + 512], writes=[bft])
                        tb, btb = TB.next()
                        P.dma('sp', tb[:], I['c_trow_' + s['nm']][:, c0:c0 + 512].partition_broadcast(128), writes=[btb])
                        dt_, bdt = DT.next()
                        for k in range(6):
                            P.op('act', lambda e, k=k, dt_=dt_, tb=tb: e.activation(out=dt_[:, k, :], in_=tb[:], func=AF.Exp, scale=negd[:, k:k + 1]), [btb, bw], [bdt])
                        ps, bp = PS.next()
                        P.op('pe', lambda e, ps=ps, ft=ft: e.matmul(ps[0:64, :], lhsT=w1[:], rhs=ft[:], start=True, stop=True), [bft, bw], [bp])
                        h1, bh1 = H1.next()
                        sinlayer(ps, bp, 3, h1, bh1)
                        ps, bp = PS.next()
                        P.op('pe', lambda e, ps=ps, h1=h1: e.matmul(ps[0:64, :], lhsT=w2[:], rhs=h1[:], start=True, stop=True), [bh1, bw], [bp])
                        h2, bh2 = H2.next()
                        sinlayer(ps, bp, 4, h2, bh2)
                        for jc in range(24):
                            ps, bp = PS.next()
                            P.op('pe', lambda e, ps=ps, jc=jc, h2=h2: e.matmul(ps[:], lhsT=w3[:, jc * 128:(jc + 1) * 128], rhs=h2[:], start=True, stop=True), [bh2, bw], [bp])
                            hf, bhf = HF.next()
                            P.op('dve', lambda e, hf=hf, ps=ps, dt_=dt_, jc=jc: e.tensor_tensor(out=hf[:], in0=ps[:], in1=dt_[:, jc % 6, :], op=ALU.mult), [bp, bdt], [bhf])
                            P.op('act', lambda e, hf=hf, jc=jc, ci=ci: e.activation(out=junk[:], in_=hf[:], func=AF.Square, accum_out=ssq[:, jc, ci:ci + 1]), [bhf], [bjunk, bss])
                            P.dma('sp', s['hfT'][jc * 128:(jc + 1) * 128, c0:c0 + 512], hf[:], reads=[bhf], writes=[s['b_hfT']])
                    P.op('dve', lambda e, nch=nch: e.tensor_reduce(out=tot[:, 0, :], in_=ssq[:, :, 0:nch], axis=AX.X, op=ALU.add), [bss], [bss])
                    P.op('act', lambda e: e.activation(out=tot[:, 1, :], in_=tot[:, 0, :], func=AF.Sqrt, bias=epsT[:, 0:1], scale=1.0), [bss, bconst], [bss])
                    P.op('dve', lambda e: e.reciprocal(out=tot[:, 2, :], in_=tot[:, 1, :]), [bss], [bss])
                    P.dma('sp', s['invd'][:, 0:24], tot[:, 2, :], reads=[bss], writes=[s['b_invd']])
            P.barrier()

        def ph7():
            CB = 4
            with contextlib.ExitStack() as ph:
                T = lambda name, shape, dt: ph.enter_context(sb(name, shape, dt))
                bw = WBuf()
                F1 = T('cF1', [64, 128], BF16)
                loadw(F1[:], I['c_F1'], bw, np_=64)
                F1fA = T('cF1fA', [64, 128], BF16)
                F1fB = T('cF1fB', [64, 128], BF16)
                loadw(F1fA[:], I['c_F1fA'], bw, np_=64)
                loadw(F1fB[:], I['c_F1fB'], bw, np_=64)
                hbb = rowbc(ph, 'hbb', I['hyena_bias'][0].rearrange("o c -> (o c)"), 1536, bw, npart=64)
                for s in SEQ:
                    L, N2r, nm = s['L'], s['N2'], s['nm']
                    N2 = 128
                    PK = 128 // N2r
                    CBc = CB * PK
                    G = 4
                    with contextlib.ExitStack() as p2:
                        T2 = lambda name, shape, dt: p2.enter_context(sb(name + nm, shape, dt))
                        bc_ = Buf()
                        G1re = T2('G1re', [64, 64], BF16)
                        G1im = T2('G1im', [64, 64], BF16)
                        loadw(G1re[:], I['c_G1re_' + nm], bc_, np_=64)
                        loadw(G1im[:], I['c_G1im_' + nm], bc_, np_=64)
                        TA = T2('TA', [N2, 4, 128], F32)
                        TBt = T2('TB', [N2, 4, 128], F32)
                        TfA = T2('TfA', [N2, 4, 128], F32)
                        TfB = T2('TfB', [N2, 4, 128], F32)
                        for t_, k_ in ((TA, 'TA_'), (TBt, 'TB_'), (TfA, 'TfA_'), (TfB, 'TfB_')):
                            P.dma('pool', t_[:], I['c_' + k_ + nm], writes=[bc_])
                        TcA = T2('TcA', [64, 2, 2 * N2], F32)
                        TcB = T2('TcB', [64, 2, 2 * N2], F32)
                        P.dma('pool', TcA[:], I['c_TcA_' + nm][:, 0:2, :], writes=[bc_])
                        P.dma('pool', TcB[:], I['c_TcB_' + nm][:, 0:2, :], writes=[bc_])
                        F2 = T2('F2', [N2, 3, N2], BF16)
                        FA = T2('FA', [N2, 2 * N2], BF16)
                        FB = T2('FB', [N2, 2 * N2], BF16)
                        loadw(F2[:].rearrange("p a b -> p (a b)"), I['c_F2_' + nm].rearrange("p a b -> p (a b)"), bc_, np_=N2)
                        loadw(FA[:], I['c_FA_' + nm], bc_, np_=N2)
                        loadw(FB[:], I['c_FB_' + nm], bc_, np_=N2)
                        XI = RB(p2, nc, 'xi' + nm, [64, 3, CB, N2], F32, G)
                        HI = RB(p2, nc, 'hi' + nm, [64, 4, CB, N2], F32, 1)
                        IV = RB(p2, nc, 'iv' + nm, [64, 4, CBc], F32, 3)
                        HB = RB(p2, nc, 'hb' + nm, [64, 4, CB, N2], BF16, G)
                        UB = RB(p2, nc, 'ub' + nm, [64, CB, N2], BF16, 2 * G)
                        PPa = RB(p2, nc, 'ppa' + nm, [N2, 4, 128], F32, G + 1)
                        PPb = RB(p2, nc, 'ppb' + nm, [N2, 4, 128], F32, G + 1)
                        YT = RB(p2, nc, 'yt' + nm, [N2, 4, 128], BF16, 2 * G + 2)
                        YF = RB(p2, nc, 'yf' + nm, [N2, 4, 128], BF16, 2 * G + 2)
                        KK = RB(p2, nc, 'kk' + nm, [N2, 2, 2, CB, 128], BF16, G)
                        GG = RB(p2, nc, 'gg' + nm, [N2, 4, 128], BF16, G)
                        VPa = RB(p2, nc, 'vpa' + nm, [64, 2, 2 * N2], F32, G)
                        VPb = RB(p2, nc, 'vpb' + nm, [64, 2, 2 * N2], F32, G)
                        VT_ = RB(p2, nc, 'vt' + nm, [64, 4, 2, N2], BF16, G)
                        UT = RB(p2, nc, 'ut' + nm, [64, CB, N2], F32, 3 * G)
                        MO = RB(p2, nc, 'mo' + nm, [N2, CB, 64], BF16, 3)

                        AUX = {}

                        def aux(b):
                            k = id(b)
                            if k not in AUX:
                                AUX[k] = (b, Buf())
                            return AUX[k][1]

                        def tview(ps, n):
                            return ps[0:n, :].rearrange("p (t c) -> p t c", t=4)

                        def twiddle(ps, bp, A, B, n, w, outre, outim, bo, rd):
                            pa, bpa = PPa.next()
                            pb, bpb = PPb.next()
                            src = ps
                            P.op('dve', lambda e: e.tensor_tensor(out=pa[0:n], in0=src, in1=A, op=ALU.mult), [bp] + rd, [bpa])
                            P.op('dve', lambda e: e.tensor_tensor(out=pb[0:n], in0=src, in1=B, op=ALU.mult), [bp] + rd, [bpb])
                            yield
                            P.op('pool', lambda e: e.tensor_tensor(out=outre, in0=pa[0:n, :, 0:w], in1=pb[0:n, :, w:2 * w], op=ALU.subtract), [bpa, bpb], [bo])
                            P.op('dve', lambda e: e.tensor_tensor(out=outim, in0=pb[0:n, :, 0:w], in1=pa[0:n, :, w:2 * w], op=ALU.add), [bpa, bpb], [aux(bo)])

                        def tform_fwd(tiles, rd, A, B, out, bo):
                            ps, bp = PS.next()
                            pv = tview(ps, N2)
                            for t, tl in enumerate(tiles):
                                P.op('pe', lambda e, t=t, tl=tl: e.matmul(pv[:, t, :], lhsT=tl, rhs=F1[:], start=True, stop=True), rd + [bw], [bp])
                            yield from twiddle(pv, bp, A, B, N2, 64, out[:, :, 0:64], out[:, :, 64:128], bo, [bc_])

                        def nform_fwd(yre_src, yim_src, rd, imsrc_re=None, imsrc_im=None):
                            ps, bp = PS.next()
                            pv = tview(ps, N2)
                            ire = yre_src if imsrc_re is None else imsrc_re
                            iim = yim_src if imsrc_im is None else imsrc_im
                            P.op('pe', lambda e: e.matmul(pv[:, :, 0:64], lhsT=F2[:, 0, :], rhs=yre_src, start=True, stop=False), rd + [bc_], [bp])
                            P.op('pe', lambda e: e.matmul(pv[:, :, 0:64], lhsT=F2[:, 2, :], rhs=yim_src, start=False, stop=True), rd + [bc_], [bp])
                            P.op('pe', lambda e: e.matmul(pv[:, :, 64:128], lhsT=F2[:, 0, :], rhs=iim, start=True, stop=False), rd + [bc_], [bp])
                            P.op('pe', lambda e: e.matmul(pv[:, :, 64:128], lhsT=F2[:, 1, :], rhs=ire, start=False, stop=True), rd + [bc_], [bp])
                            return pv, bp

                        def conv(ub, bub, kk, bkk, o, res):
                            yt, byt = YT.next()
                            yield from tform_fwd([ub[:, t, :] for t in range(CB)], [bub], TA[:], TBt[:], yt, byt)
                            yield
                            pv, bp = nform_fwd(yt[:, :, 0:64], yt[:, :, 64:128], [byt, aux(byt)])
                            gg, bgg = GG.next()
                            yield from twiddle(pv, bp, kk[:, o, 0], kk[:, o, 1], N2, 64, gg[:, :, 0:64], gg[:, :, 64:128], bgg, [bkk])
                            yield
                            vt, bvt = VT_.next()
                            per = min(512 // (2 * N2), CB)
                            for g0 in range(0, CB, per):
                                ps, bp = PS.next()
                                pv2 = ps[0:64, 0:per * 2 * N2].rearrange("p (t c) -> p t c", t=per)
                                for t in range(per):
                                    P.op('pe', lambda e, t=t, g0=g0: e.matmul(pv2[:, t, :], lhsT=gg[:, g0 + t, 0:64], rhs=FA[:], start=True, stop=False), [bgg, aux(bgg), bc_], [bp])
                                    P.op('pe', lambda e, t=t, g0=g0: e.matmul(pv2[:, t, :], lhsT=gg[:, g0 + t, 64:128], rhs=FB[:], start=False, stop=True), [bgg, aux(bgg), bc_], [bp])
                                va, bva = VPa.next()
                                vb, bvb = VPb.next()
                                P.op('dve', lambda e, pv2=pv2, va=va: e.tensor_tensor(out=va[:, 0:per, :], in0=pv2, in1=TcA[:, 0:per, :], op=ALU.mult), [bp, bc_], [bva])
                                P.op('dve', lambda e, pv2=pv2, vb=vb: e.tensor_tensor(out=vb[:, 0:per, :], in0=pv2, in1=TcB[:, 0:per, :], op=ALU.mult), [bp, bc_], [bvb])
                                yield
                                P.op('pool', lambda e, g0=g0, va=va, vb=vb: e.tensor_tensor(out=vt[:, g0:g0 + per, 0, :], in0=va[:, 0:per, 0:N2], in1=vb[:, 0:per, N2:2 * N2], op=ALU.subtract), [bva, bvb], [bvt], nowaw=(g0 > 0))
                                P.op('dve', lambda e, g0=g0, va=va, vb=vb: e.tensor_tensor(out=vt[:, g0:g0 + per, 1, :], in0=vb[:, 0:per, 0:N2], in1=va[:, 0:per, N2:2 * N2], op=ALU.add), [bva, bvb], [aux(bvt)], nowaw=(g0 > 0))
                                yield
                            ps, bp = PS.next()
                            pvo = ps[0:64, 0:CB * N2].rearrange("p (t c) -> p t c", t=CB)
                            P.op('pe', lambda e: e.matmul(pvo, lhsT=G1re[:], rhs=vt[:, :, 0, :], start=True, stop=False), [bvt, aux(bvt), bc_], [bp])
                            P.op('pe', lambda e: e.matmul(pvo, lhsT=G1im[:], rhs=vt[:, :, 1, :], start=False, stop=True), [bvt, aux(bvt), bc_], [bp])
                            res['pvo'] = (pvo, bp)

                        def zc_view(r0):
                            return s['zcT'][r0:r0 + CBc, :].rearrange("c (a b) -> a c b", b=N2r)

                        def cview(ap_):
                            return ap_.rearrange("p t (c b) -> p (t c) b", b=N2r)

                        def hyena_gen(cb):
                            ch0 = cb * CBc
                            xi, bxi = XI.next()
                            for k in range(3):
                                P.dma('sp', cview(xi[:, k]), zc_view(k * 768 + ch0), reads=[s['b_zcT']], writes=[bxi])
                            hi, bhi = HI.next()
                            for od in range(4):
                                P.dma('act', cview(hi[:, od]), s['hfT'][od * 768 + ch0:od * 768 + ch0 + CBc, :].rearrange("c (a b) -> a c b", b=N2r),
                                      reads=[s['b_hfT']], writes=[bhi])
                            iv, biv = IV.next()
                            for od in range(4):
                                col = od * 768 + ch0
                                P.dma('sp', iv[:, od, :], s['invd'][col % 128:col % 128 + CBc, col // 128:col // 128 + 1].rearrange("c o -> o c").partition_broadcast(64),
                                      reads=[s['b_invd']], writes=[biv])
                            hb, bhb = HB.next()
                            P.op('dve', lambda e: e.tensor_tensor(out=hb[:].rearrange("p o t (c b) -> p o (t c) b", b=N2r), in0=hi[:].rearrange("p o t (c b) -> p o (t c) b", b=N2r),
                                                                  in1=iv[:].unsqueeze(3).to_broadcast([64, 4, CBc, N2r]), op=ALU.mult),
                                 [bhi, biv], [bhb])
                            for od in (1, 3):
                                P.op('pool', lambda e, od=od: e.memset(cview(hb[0:1, od])[:, :, 0:1], 0.0), [], [bhb])
                            yield
                            kk, bkk = KK.next()
                            for o in range(2):
                                yf, byf = YF.next()
                                yb, byb = YF.next()
                                yield from tform_fwd([hb[:, 2 * o, t, :] for t in range(CB)], [bhb], TA[:], TBt[:], yf, byf)
                                yield
                                yield from tform_fwd([hb[:, 2 * o + 1, t, :] for t in range(CB)], [bhb], TA[:], TBt[:], yb, byb)
                                yield
                                ysum, bys = YT.next()
                                ydif, byd = YT.next()
                                P.op('pool', lambda e: e.tensor_tensor(out=ysum[:], in0=yf[:], in1=yb[:], op=ALU.add), [byf, byb, aux(byf), aux(byb)], [bys])
                                P.op('dve', lambda e: e.tensor_tensor(out=ydif[:], in0=yf[:], in1=yb[:], op=ALU.subtract), [byf, byb, aux(byf), aux(byb)], [byd])
                                yield
                                pv, bp = nform_fwd(ysum[:, :, 0:64], ysum[:, :, 64:128], [bys, byd], imsrc_re=ydif[:, :, 0:64], imsrc_im=ydif[:, :, 64:128])
                                for half in range(2):
                                    copy_op('act', kk[:, o, 0, :, half * 64:(half + 1) * 64], pv[:, :, 0:64], [bp], [bkk], nowaw=(o > 0 or half > 0))
                                    copy_op('act', kk[:, o, 1, :, half * 64:(half + 1) * 64], pv[:, :, 64:128], [bp], [bkk], nowaw=True)
                                yield
                            ub, bub = UB.next()
                            copy_op('act', ub[:], xi[:, 0], [bxi], [bub])
                            ucur, bucur = xi[:, 0], bxi
                            for o in range(2):
                                res = {}
                                yield from conv(ub, bub, kk, bkk, o, res)
                                pvo, bp = res['pvo']
                                ut, but = UT.next()
                                bia = hbb[:, o * 768 + ch0:o * 768 + ch0 + CBc].unsqueeze(2).to_broadcast([64, CBc, N2r])
                                P.op('pool', lambda e, ut=ut, ucur=ucur, bia=bia: e.tensor_tensor(out=cview(ut[:]), in0=cview(ucur), in1=bia, op=ALU.mult), [bucur, bw], [but])
                                P.op('dve', lambda e, ut=ut, pvo=pvo: e.tensor_tensor(out=ut[:], in0=pvo, in1=ut[:], op=ALU.add), [bp, but], [but])
                                ut2, but2 = UT.next()
                                P.op('pool', lambda e, ut=ut, ut2=ut2, o=o: e.tensor_tensor(out=ut2[:], in0=ut[:], in1=xi[:, 1 + o], op=ALU.mult), [but, bxi], [but2])
                                ub, bub = UB.next()
                                copy_op('act', ub[:], ut2[:], [but2], [bub])
                                ucur, bucur = ut2[:], but2
                                yield
                            P.dma('sp', s['mix2'][ch0:ch0 + CBc, :].rearrange("c (a b) -> a c b", b=N2r), cview(ub[:]), reads=[bub], writes=[s['b_mix2']])

                        def fnet_gen(cb):
                            q0 = cb * CBc
                            xi, bxi = XI.next()
                            P.dma('sp', cview(xi[:, 0]), s['fcT'][q0:q0 + CBc, :].rearrange("c (a b) -> a c b", b=N2r), reads=[s['b_fcT']], writes=[bxi])
                            P.dma('sp', cview(xi[:, 1]), s['fcT'][256 + q0:256 + q0 + CBc, :].rearrange("c (a b) -> a c b", b=N2r), reads=[s['b_fcT']], writes=[bxi])
                            xc, bxc = UB.next()
                            xs_, bxs = UB.next()
                            copy_op('act', xc[:], xi[:, 0], [bxi], [bxc])
                            copy_op('pool', xs_[:], xi[:, 1], [bxi], [bxs])
                            yield
                            ps, bp = PS.next()
                            pv = tview(ps, N2)
                            for t in range(CB):
                                P.op('pe', lambda e, t=t: e.matmul(pv[:, t, :], lhsT=xc[:, t, :], rhs=F1fA[:], start=True, stop=False), [bxc, bw], [bp])
                                P.op('pe', lambda e, t=t: e.matmul(pv[:, t, :], lhsT=xs_[:, t, :], rhs=F1fB[:], start=False, stop=True), [bxs, bw], [bp])
                            yt, byt = YT.next()
                            yield from twiddle(pv, bp, TfA[:], TfB[:], N2, 64, yt[:, :, 0:64], yt[:, :, 64:128], byt, [bc_])
                            yield
                            ps, bp = PS.next()
                            pz = ps[0:N2, 0:CB * 64].rearrange("p (t c) -> p t c", t=CB)
                            P.op('pe', lambda e: e.matmul(pz, lhsT=F2[:, 0, :], rhs=yt[:, :, 0:64], start=True, stop=False), [byt, aux(byt), bc_], [bp])
                            P.op('pe', lambda e: e.matmul(pz, lhsT=F2[:, 2, :], rhs=yt[:, :, 64:128], start=False, stop=True), [byt, aux(byt), bc_], [bp])
                            mo, bmo = MO.next()
                            copy_op('act', mo[:], pz, [bp], [bmo])
                            P.dma('sp', s['mfT'][q0:q0 + CBc, :].rearrange("(t c) (b a) -> (c b) t a", c=PK, a=64), mo[:], reads=[bmo], writes=[s['b_mfT']])

                        def lockstep(gens, g):
                            it = iter(gens)
                            active = []
                            done = False
                            while True:
                                while not done and len(active) < g:
                                    nx = next(it, None)
                                    if nx is None:
                                        done = True
                                    else:
                                        active.append(nx)
                                if not active:
                                    break
                                for gg_ in list(active):
                                    try:
                                        next(gg_)
                                    except StopIteration:
                                        active.remove(gg_)

                        lockstep((hyena_gen(cb) for cb in range(768 // CBc)), G)
                        lockstep((fnet_gen(cb) for cb in range(256 // CBc)), G)
                    P.barrier()

        phases = [ph1, ph2, lambda: ph_out(0), lambda: ph_mlp(0), ph5, ph6, ph7, lambda: ph_out(1), lambda: ph_mlp(1)]
        for i, f in enumerate(phases):
            if i >= stop_after:
                break
            f()
        if dbg is not None:
            dbg(nc, P, SEQ, I)
        P.emit()
    return nc


_NC_CACHE = {}


def _prep_inputs(inputs):
    consts = _get_consts()
    base = {}
    for k, v in inputs.items():
        if k in ('x_prompt', 'x_sample'):
            continue
        base[k] = np.ascontiguousarray(np.asarray(v, dtype=np.float32))
    for k, v in consts.items():
        base['c_' + k] = np.ascontiguousarray(v)
    xp = np.asarray(inputs['x_prompt'], dtype=np.float32)
    xs = np.asarray(inputs['x_sample'], dtype=np.float32)
    in_maps = []
    for i in range(8):
        m = dict(base)
        m['x_p'] = np.ascontiguousarray(xp[i])
        m['x_s'] = np.ascontiguousarray(xs[i])
        in_maps.append(m)
    return consts, in_maps


def kernel(**inputs):
    consts, in_maps = _prep_inputs(inputs)
    if 'nc' not in _NC_CACHE:
        _NC_CACHE['nc'] = build(consts)
    nc = _NC_CACHE['nc']
    res = run_bass_kernel_spmd(nc, in_maps, core_ids=list(range(8)))
    yp = np.stack([np.asarray(r['y_p'], dtype=np.float32) for r in res.results], 0)
    ys = np.stack([np.asarray(r['y_s'], dtype=np.float32) for r in res.results], 0)
    return (yp, ys)
```
